# Optimizing a Trainium2 kernel written in Bass

```python
import jax, jax.numpy as jnp
from jax import lax
import numpy as np

D_MODEL = 1024
BATCH = 4
SEQ = 8192
DEPTH = 4

D_A = D_MODEL // 2
D_B = D_MODEL // 2
D_IN_CONV = 2 * D_A + 3 * D_B
CONV_A_WIDTH = 31
CONV_B_WIDTH = 3
HEAD_DIM = 64
N_HEADS = D_MODEL // HEAD_DIM
BLOCK_Q = 128
D_FF = ((8 * D_MODEL // 3 + 127) // 128) * 128
FFN_CONV_WIDTH = 3
N_EVEN = (DEPTH + 1) // 2
N_ODD = DEPTH // 2
EPS = 1e-6

kernel_name = "hybrid_conformer_shortconv_stickbreaking"


def rmsnorm(x, g):
    xf = x.astype(jnp.float32)
    y = xf * lax.rsqrt(jnp.mean(xf * xf, axis=-1, keepdims=True) + EPS)
    return (y * g.astype(jnp.float32)).astype(x.dtype)


def layernorm(x, g, b):
    xf = x.astype(jnp.float32)
    mu = jnp.mean(xf, axis=-1, keepdims=True)
    xc = xf - mu
    y = xc * lax.rsqrt(jnp.mean(xc * xc, axis=-1, keepdims=True) + EPS)
    return (y * g.astype(jnp.float32) + b.astype(jnp.float32)).astype(x.dtype)


def causal_dwconv(x, w, b=None):
    K, C = w.shape
    y = lax.conv_general_dilated(
        x, w[:, None, :].astype(x.dtype), window_strides=(1,),
        padding=[(K - 1, 0)], dimension_numbers=("NWC", "WIO", "NWC"),
        feature_group_count=C)
    if b is not None:
        y = y + b.astype(x.dtype)
    return y


def conv_mixer(h, w_in, a_dw_w, a_dw_b, a_ln_g, a_ln_b, b_dw_w, w_out):
    p = h @ w_in
    a_val, a_gate, b_gb, b_gc, b_h = jnp.split(
        p, [D_A, 2 * D_A, 2 * D_A + D_B, 2 * D_A + 2 * D_B], axis=-1)
    a = causal_dwconv(a_val * jax.nn.sigmoid(a_gate), a_dw_w, a_dw_b)
    a = jax.nn.silu(layernorm(a, a_ln_g, a_ln_b))
    b = b_gb * causal_dwconv(b_gc * b_h, b_dw_w)
    return jnp.concatenate([a, b], axis=-1) @ w_out


def stick_breaking_attention(q, k, v):
    B, H, S, dh = q.shape
    nb = S // BLOCK_Q
    scale = dh ** -0.5
    qb = q.astype(jnp.float32).reshape(B, H, nb, BLOCK_Q, dh).transpose(2, 0, 1, 3, 4)
    kf = k.astype(jnp.float32)
    starts = jnp.arange(nb, dtype=jnp.int32) * BLOCK_Q
    kpos = jnp.arange(S, dtype=jnp.int32)

    def block(args):
        q_blk, start = args
        z = jnp.einsum("bhqd,bhsd->bhqs", q_blk, kf) * scale
        qpos = start + jnp.arange(BLOCK_Q, dtype=jnp.int32)
        valid = kpos[None, :] < qpos[:, None]
        log_keep = jnp.where(valid, jax.nn.log_sigmoid(-z), 0.0)
        rc = lax.cumsum(log_keep, axis=3, reverse=True)
        after = jnp.concatenate([rc[..., 1:], jnp.zeros_like(rc[..., :1])], axis=-1)
        weight = jnp.where(valid, jnp.exp(jax.nn.log_sigmoid(z) + after), 0.0)
        return jnp.einsum("bhqs,bhsd->bhqd", weight.astype(v.dtype), v)

    out = lax.map(block, (qb, starts))
    return out.transpose(1, 2, 0, 3, 4).reshape(B, H, S, dh)


def attn_mixer(h, w_qkv, q_g, k_g, w_o):
    B, S, _ = h.shape
    qkv = (h @ w_qkv).reshape(B, S, 3, N_HEADS, HEAD_DIM)
    q = rmsnorm(qkv[:, :, 0], q_g).transpose(0, 2, 1, 3)
    k = rmsnorm(qkv[:, :, 1], k_g).transpose(0, 2, 1, 3)
    v = qkv[:, :, 2].transpose(0, 2, 1, 3)
    o = stick_breaking_attention(q, k, v)
    return o.transpose(0, 2, 1, 3).reshape(B, S, N_HEADS * HEAD_DIM) @ w_o


def conv_ffn(h, w_up, dw_w, dw_b, w_down):
    u = causal_dwconv(h @ w_up, dw_w, dw_b)
    gate, val = jnp.split(u, 2, axis=-1)
    return (jax.nn.silu(gate) * val) @ w_down


def setup_inputs(seed: int = 0) -> dict:
    key = jax.random.key(seed)
    ks = jax.random.split(key, 18)
    f32 = jnp.float32

    def nrm(k, shape, scale):
        return jax.random.normal(k, shape, f32) * scale

    def gain(k, shape):
        return 1.0 + 0.02 * jax.random.normal(k, shape, f32)

    return {
        "x": nrm(ks[0], (BATCH, SEQ, D_MODEL), 1.0),
        "mix_norm_g": gain(ks[1], (DEPTH, D_MODEL)),
        "ffn_norm_g": gain(ks[2], (DEPTH, D_MODEL)),
        "conv_w_in": nrm(ks[3], (N_EVEN, D_MODEL, D_IN_CONV), D_MODEL ** -0.5),
        "conv_a_dw_w": nrm(ks[4], (N_EVEN, CONV_A_WIDTH, D_A), CONV_A_WIDTH ** -0.5),
        "conv_a_dw_b": nrm(ks[5], (N_EVEN, D_A), 0.02),
        "conv_a_ln_g": gain(ks[6], (N_EVEN, D_A)),
        "conv_a_ln_b": nrm(ks[7], (N_EVEN, D_A), 0.02),
        "conv_b_dw_w": nrm(ks[8], (N_EVEN, CONV_B_WIDTH, D_B), CONV_B_WIDTH ** -0.5),
        "conv_w_out": nrm(ks[9], (N_EVEN, D_A + D_B, D_MODEL), (D_A + D_B) ** -0.5),
        "attn_w_qkv": nrm(ks[10], (N_ODD, D_MODEL, 3 * N_HEADS * HEAD_DIM), D_MODEL ** -0.5),
        "attn_q_g": gain(ks[11], (N_ODD, HEAD_DIM)),
        "attn_k_g": gain(ks[12], (N_ODD, HEAD_DIM)),
        "attn_w_o": nrm(ks[13], (N_ODD, N_HEADS * HEAD_DIM, D_MODEL), (N_HEADS * HEAD_DIM) ** -0.5),
        "ffn_w_up": nrm(ks[14], (DEPTH, D_MODEL, 2 * D_FF), D_MODEL ** -0.5),
        "ffn_dw_w": nrm(ks[15], (DEPTH, FFN_CONV_WIDTH, 2 * D_FF), FFN_CONV_WIDTH ** -0.5),
        "ffn_dw_b": nrm(ks[16], (DEPTH, 2 * D_FF), 0.02),
        "ffn_w_down": nrm(ks[17], (DEPTH, D_FF, D_MODEL), D_FF ** -0.5),
    }


def reference(x, mix_norm_g, ffn_norm_g, conv_w_in, conv_a_dw_w, conv_a_dw_b,
              conv_a_ln_g, conv_a_ln_b, conv_b_dw_w, conv_w_out, attn_w_qkv,
              attn_q_g, attn_k_g, attn_w_o, ffn_w_up, ffn_dw_w, ffn_dw_b,
              ffn_w_down):
    for layer in range(DEPTH):
        i = layer // 2
        h = rmsnorm(x, mix_norm_g[layer])
        if layer % 2 == 0:
            x = x + conv_mixer(h, conv_w_in[i], conv_a_dw_w[i], conv_a_dw_b[i],
                               conv_a_ln_g[i], conv_a_ln_b[i], conv_b_dw_w[i],
                               conv_w_out[i])
        else:
            x = x + attn_mixer(h, attn_w_qkv[i], attn_q_g[i], attn_k_g[i], attn_w_o[i])
        h = rmsnorm(x, ffn_norm_g[layer])
        x = x + conv_ffn(h, ffn_w_up[layer], ffn_dw_w[layer], ffn_dw_b[layer], ffn_w_down[layer])
    return x
```

```python
import numpy as np
from contextlib import ExitStack
import concourse.bass as bass
import concourse.mybir as mybir
from concourse.bass_utils import run_bass_kernel_spmd

F32 = mybir.dt.float32
BF16 = mybir.dt.bfloat16
AF = mybir.ActivationFunctionType
ALU = mybir.AluOpType

ENGINES = ("pe", "act", "dve", "pool", "sp")
ENG_ATTR = {"pe": "tensor", "act": "scalar", "dve": "vector", "pool": "gpsimd", "sp": "sync"}


class _Op:
    __slots__ = ("fn", "waits", "inc")

    def __init__(self, fn):
        self.fn = fn
        self.waits = []
        self.inc = None


class Sched:
    SEM_ROT = 20000

    def __init__(self):
        self.ops = {e: [] for e in ENGINES}
        self.built = {e: 0 for e in ENGINES}
        self.ms = {e: [] for e in ENGINES}
        self.gen = {e: 0 for e in ENGINES}
        self.cnt = {}
        self.waited = {e: {} for e in ENGINES}
        self.last_w = {}
        self.readers = {}
        self.nwaits = 0
        self.sems = {}

    def _new_ms(self, e, seq, op):
        k = ("eng", e, self.gen[e])
        if self.cnt.get(k, 0) >= self.SEM_ROT:
            self.gen[e] += 1
            k = ("eng", e, self.gen[e])
        self.cnt[k] = self.cnt.get(k, 0) + 1
        op.inc = (k, 1)
        self.ms[e].append((seq, k, self.cnt[k]))
        return k, self.cnt[k]

    def _milestone(self, e, seq):
        lo, hi = 0, len(self.ms[e])
        while lo < hi:
            mid = (lo + hi) // 2
            if self.ms[e][mid][0] >= seq:
                hi = mid
            else:
                lo = mid + 1
        if lo < len(self.ms[e]):
            return self.ms[e][lo][1], self.ms[e][lo][2]
        last = len(self.ops[e]) - 1
        while self.ops[e][last].fn is None:
            last -= 1
        assert last >= seq and last >= self.built[e], (e, seq, last, self.built[e])
        op = self.ops[e][last]
        assert op.inc is None, f"last op on {e} already has an inc"
        return self._new_ms(e, last, op)

    def _resolve(self, tok):
        if tok[0] == "eng":
            return self._milestone(tok[1], tok[2])
        return tok[1], tok[2]

    def _deps(self, engine, reads, writes):
        toks = []
        for k in reads:
            t = self.last_w.get(k)
            if t is not None:
                toks.append(t)
        for k in writes:
            t = self.last_w.get(k)
            if t is not None and not (t[0] == "eng" and t[1] == engine == "pe"):
                toks.append(t)
            for t in self.readers.get(k, {}).values():
                if not (t[0] == "eng" and t[1] == engine):
                    toks.append(t)
        return self._waits(engine, toks)

    def _waits(self, engine, toks):
        waits = {}
        for t in toks:
            sk, v = self._resolve(t)
            if self.waited[engine].get(sk, 0) >= v:
                continue
            waits[sk] = max(waits.get(sk, 0), v)
        for sk, v in waits.items():
            self.waited[engine][sk] = v
        self.nwaits += len(waits)
        return list(waits.items())

    def _track(self, tok, rkey, reads, writes):
        for k in writes:
            self.last_w[k] = tok
            self.readers[k] = {}
        for k in reads:
            self.readers.setdefault(k, {})[rkey] = tok

    def emit(self, engine, fn, reads=(), writes=(), ms=False, banks=()):
        writes = list(writes) + list(banks)
        op = _Op(fn)
        op.waits = self._deps(engine, reads, writes)
        seq = len(self.ops[engine])
        self.ops[engine].append(op)
        if ms or engine == "pool":
            self._new_ms(engine, seq, op)
        tok = ("eng", engine, seq)
        self._track(tok, engine, reads, writes)
        return tok

    def dma(self, queue, sem, fn, reads=(), writes=(), inc=16):
        op = _Op(fn)
        op.waits = self._deps(queue, reads, writes)
        self.ops[queue].append(op)
        k = ("dma", sem.split(".", 1)[-1])
        self.cnt[k] = self.cnt.get(k, 0) + inc
        op.inc = (k, inc)
        tok = ("dma", k, self.cnt[k])
        self._track(tok, k, reads, writes)
        return tok

    def dma_group(self, queue, sem, items):
        toks = [self.dma(queue, sem, fn, reads, writes) for (fn, reads, writes) in items]
        final = toks[-1]
        for (fn, reads, writes) in items:
            for k in writes:
                self.last_w[k] = final
            for k in reads:
                self.readers.setdefault(k, {})[final[1]] = final
        return final

    def wait_all(self, engine, toks):
        op = _Op(None)
        op.waits = self._waits(engine, toks)
        self.ops[engine].append(op)

    def barrier(self, toks=()):
        toks = list(toks)
        for e in ("pe", "act", "dve", "pool"):
            last = len(self.ops[e]) - 1
            while last >= self.built[e] and (self.ops[e][last].fn is None or self.ops[e][last].inc is not None and self.ops[e][last].inc[0][0] == "dma"):
                last -= 1
            if last >= self.built[e]:
                toks.append(("eng", e, last))
        for e in ENGINES:
            self.wait_all(e, toks)

    def build_phase(self, nc, st, outer):
        block = st.enter_context(nc.Block())
        for e in ENGINES:
            deco = getattr(block, ENG_ATTR[e])

            def body(eng, e=e):
                for op in self.ops[e][self.built[e]:]:
                    for (sk, v) in op.waits:
                        eng.wait_ge(self._sem(nc, outer, sk), v)
                    if op.fn is None:
                        continue
                    ins = op.fn(eng)
                    if op.inc is not None:
                        ins.then_inc(self._sem(nc, outer, op.inc[0]), op.inc[1])
                    op.fn = None
                self.built[e] = len(self.ops[e])

            deco(body)

    def _sem(self, nc, outer, k):
        if k not in self.sems:
            self.sems[k] = outer.enter_context(nc.semaphore(f"s{len(self.sems)}"))
        return self.sems[k]

    def build(self, nc, st):
        self.build_phase(nc, st, st)


D = 1024
KD = D // 128
CH = 512
NCHUNK = 8
TOK = CH * NCHUNK
HALO = 32
DFF = 2816
NJ = DFF // 128
EPS = 1e-6


class Ctx:
    def __init__(self, nc, S, st):
        self.nc, self.S, self.st = nc, S, st
        self.n = 0

    def sb(self, name, shape, dt):
        return self.st.enter_context(self.nc.sbuf_tensor(name, list(shape), dt))

    def ps(self, name, shape=(128, 512), dt=F32):
        return self.st.enter_context(self.nc.psum_tensor(name, list(shape), dt))


def emit_rmsnorm(cx, pfx, x_k, h_k, xkeys, hkeys, ncols, sq, ones, pst, pst_bank, tmp, rstd, g_sb, gkey):
    S = cx.S
    for k in range(KD):
        S.emit("act", lambda e, k=k: e.activation(out=sq[:, k, 0:ncols], in_=x_k(k), func=AF.Square),
               reads=[xkeys[k]], writes=[pfx + f"sq{k}"])
    for k in range(KD):
        S.emit("pe", lambda e, k=k: e.matmul(pst, ones[:, :], sq[:, k, 0:ncols], start=(k == 0), stop=(k == KD - 1)),
               reads=[pfx + f"sq{k}", pfx + "ones"], banks=[pst_bank], ms=(k == KD - 1))
    S.emit("dve", lambda e: e.tensor_scalar(out=tmp[:, 0:ncols], in0=pst, scalar1=1.0 / D, scalar2=EPS,
                                            op0=ALU.mult, op1=ALU.add), banks=[pst_bank], writes=[pfx + "tmp"])
    S.emit("act", lambda e: e.activation(out=tmp[:, 0:ncols], in_=tmp[:, 0:ncols], func=AF.Sqrt),
           reads=[pfx + "tmp"], writes=[pfx + "tmp"])
    S.emit("dve", lambda e: e.reciprocal(out=rstd[:, 0:ncols], in_=tmp[:, 0:ncols]), reads=[pfx + "tmp"], writes=[pfx + "rstd"])
    for k in range(KD):
        S.emit("dve", lambda e, k=k: e.scalar_tensor_tensor(out=h_k(k), in0=x_k(k), scalar=g_sb[:, k:k + 1], in1=rstd[:, 0:ncols],
                                                            op0=ALU.mult, op1=ALU.mult),
               reads=[xkeys[k], pfx + "rstd", gkey], writes=[hkeys[k]])


def ffn_phase(cx, pfx, xT, xh, xoT, g_d, wup_d, wdn_d, dw_d, db_d, nst=NCHUNK // 2):
    S, nc = cx.S, cx.nc
    HH = 2
    W = HH + CH
    SC = 2
    x_sb = cx.sb(pfx + "x", [128, KD, SC, W], F32)
    hT = cx.sb(pfx + "hT", [128, KD, SC, W], BF16)
    sq = cx.sb(pfx + "sq", [128, KD, CH], BF16)
    rstd = cx.sb(pfx + "rstd", [128, CH], F32)
    tmp = cx.sb(pfx + "tmp", [128, CH], F32)
    ones = cx.sb(pfx + "ones", [128, 128], BF16)
    g_sb = cx.sb(pfx + "g", [128, KD], F32)
    dw_sb = cx.sb(pfx + "dw", [128, 2 * NJ, 3], F32)
    db_sb = cx.sb(pfx + "db", [128, 2 * NJ], F32)
    wup = [cx.sb(pfx + f"wup{i}", [128, KD, 256], BF16) for i in range(2)]
    wdn = [cx.sb(pfx + f"wdn{i}", [128, NJ, 128], BF16) for i in range(2)]
    gT = cx.sb(pfx + "gT", [128, NJ, SC, CH], BF16)
    gacc = [cx.sb(pfx + f"gacc{i}", [128, CH], F32) for i in range(2)]
    vacc = [cx.sb(pfx + f"vacc{i}", [128, CH], F32) for i in range(2)]
    sg = [cx.sb(pfx + f"sg{i}", [128, CH], F32) for i in range(2)]
    phs = [cx.sb(pfx + f"phs{i}", [128, 4], F32) for i in range(2)]
    xo = [cx.sb(pfx + f"xo{i}", [128, CH], F32) for i in range(2)]
    pg = [cx.ps(pfx + f"pg{i}") for i in range(2)]
    pv = [cx.ps(pfx + f"pv{i}") for i in range(2)]
    pmisc = cx.ps(pfx + "pmisc")
    pstat = cx.ps(pfx + "pstat")
    pd = [cx.ps(pfx + f"pd{i}") for i in range(2)]
    ph = [pmisc[:, 8 * i:8 * i + 4] for i in range(2)]
    pstat_h = pmisc[:, 32:32 + HH]

    xT_r = xT.rearrange("(k p) t -> p k t", p=128)
    xh_r = xh.rearrange("(k p) (c h) -> p k c h", p=128, h=HALO)
    xo_r = xoT.rearrange("(k p) t -> p k t", p=128)

    S.emit("dve", lambda e: e.memset(ones[:], 1.0), writes=[pfx + "ones"])
    S.dma_group("sp", "cst", [(lambda e: e.dma_start(out=g_sb[:], in_=g_d), [], [pfx + "n.g"]),
                              (lambda e: e.dma_start(out=dw_sb[:], in_=dw_d), [], [pfx + "dw"]),
                              (lambda e: e.dma_start(out=db_sb[:], in_=db_d), [], [pfx + "db"])])
    out_toks = []
    wu_i = 0
    wd_i = 0
    it = 0
    for sti in range(nst):
        for c in range(SC):
            gc = sti * SC + c
            S.dma_group("sp", f"lx{c}", [(lambda e, c=c, k=k, gc=gc: e.dma_start(out=x_sb[:, k, c, HH:W], in_=xT_r[:, k, gc * CH:(gc + 1) * CH]),
                                          [], [pfx + f"x{c}.{k}m"]) for k in range(KD)])
            S.dma("sp", pfx + f"lxh{c}",
                  lambda e, c=c, gc=gc: e.dma_start(out=x_sb[:, :, c, 0:HH], in_=xh_r[:, :, gc, HALO - HH:HALO]),
                  writes=[pfx + f"x{c}.h"])
        for c in range(SC):
            xk = [pfx + f"x{c}.{k}m" for k in range(KD)]
            for k in range(KD):
                S.emit("act", lambda e, k=k, c=c: e.activation(out=sq[:, k, :], in_=x_sb[:, k, c, HH:W], func=AF.Square),
                       reads=[xk[k]], writes=[pfx + f"sq{k}"])
            for k in range(KD):
                S.emit("pe", lambda e, k=k: e.matmul(pstat[:, :], ones[:, :], sq[:, k, :], start=(k == 0), stop=(k == KD - 1)),
                       reads=[pfx + f"sq{k}", pfx + "ones"], banks=[pfx + "pstat"], ms=(k == KD - 1))
            S.emit("dve", lambda e: e.tensor_scalar(out=tmp[:, :], in0=pstat[:, :], scalar1=1.0 / D, scalar2=EPS,
                                                    op0=ALU.mult, op1=ALU.add),
                   banks=[pfx + "pstat"], writes=[pfx + "tmp"])
            S.emit("act", lambda e: e.activation(out=tmp[:, :], in_=tmp[:, :], func=AF.Sqrt),
                   reads=[pfx + "tmp"], writes=[pfx + "tmp"])
            S.emit("dve", lambda e: e.reciprocal(out=rstd[:, :], in_=tmp[:, :]), reads=[pfx + "tmp"], writes=[pfx + "rstd"])
            for k in range(KD):
                S.emit("dve", lambda e, k=k, c=c: e.scalar_tensor_tensor(
                    out=hT[:, k, c, HH:W], in0=x_sb[:, k, c, HH:W], scalar=g_sb[:, k:k + 1], in1=rstd[:, :],
                    op0=ALU.mult, op1=ALU.mult),
                    reads=[xk[k], pfx + "rstd", pfx + "n.g"], writes=[pfx + f"h{c}.{k}m"])
            S.emit("act", lambda e, c=c: e.activation(out=sq[:, :, 0:HH], in_=x_sb[:, :, c, 0:HH], func=AF.Square),
                   reads=[pfx + f"x{c}.h"], writes=[pfx + f"sq{k}" for k in range(KD)])
            for k in range(KD):
                S.emit("pe", lambda e, k=k: e.matmul(pstat_h, ones[:, :], sq[:, k, 0:HH], start=(k == 0), stop=(k == KD - 1)),
                       reads=[pfx + f"sq{k}", pfx + "ones"], banks=[pfx + "pmisc"], ms=(k == KD - 1))
            S.emit("dve", lambda e: e.tensor_scalar(out=tmp[:, 0:HH], in0=pstat_h, scalar1=1.0 / D, scalar2=EPS,
                                                    op0=ALU.mult, op1=ALU.add),
                   banks=[pfx + "pmisc"], writes=[pfx + "tmp"])
            S.emit("act", lambda e: e.activation(out=tmp[:, 0:HH], in_=tmp[:, 0:HH], func=AF.Sqrt),
                   reads=[pfx + "tmp"], writes=[pfx + "tmp"])
            S.emit("dve", lambda e: e.reciprocal(out=rstd[:, 0:HH], in_=tmp[:, 0:HH]), reads=[pfx + "tmp"], writes=[pfx + "rstd"])
            for k in range(KD):
                S.emit("dve", lambda e, k=k, c=c: e.scalar_tensor_tensor(
                    out=hT[:, k, c, 0:HH], in0=x_sb[:, k, c, 0:HH], scalar=g_sb[:, k:k + 1], in1=rstd[:, 0:HH],
                    op0=ALU.mult, op1=ALU.mult),
                    reads=[pfx + f"x{c}.h", pfx + "rstd", pfx + "n.g"], writes=[pfx + f"h{c}.{k}h"])
        def load_wup(j, slot):
            S.dma("pool", pfx + f"wu{slot}", lambda e, j=j, slot=slot: e.dma_start(out=wup[slot][:], in_=wup_d[j]),
                  writes=[pfx + f"wup{slot}"])

        def load_wdn(n, slot):
            S.dma("pool", pfx + f"wd{slot}", lambda e, n=n, slot=slot: e.dma_start(out=wdn[slot][:], in_=wdn_d[n]),
                  writes=[pfx + f"wdn{slot}"])

        load_wup(0, wu_i % 2)
        for j in range(NJ):
            slot = wu_i % 2
            if j + 1 < NJ:
                load_wup(j + 1, (wu_i + 1) % 2)
            else:
                load_wdn(0, wd_i % 2)
            wu_i += 1
            for c in range(SC):
                b = it % 2
                it += 1
                hk = [pfx + f"h{c}.{k}m" for k in range(KD)]
                hh = [pfx + f"h{c}.{k}h" for k in range(KD)]
                for half, (pp, acc, jc) in enumerate(((pg[b], gacc[b], j), (pv[b], vacc[b], NJ + j))):
                    co = half * 128
                    pk = pfx + f"p{half}{b}"
                    ak = pfx + f"acc{half}{b}"
                    for k in range(KD):
                        S.emit("pe", lambda e, k=k, c=c, pp=pp, co=co, slot=slot: e.matmul(
                            pp[:, :], wup[slot][:, k, co:co + 128], hT[:, k, c, HH:W], start=(k == 0), stop=(k == KD - 1)),
                            reads=[hk[k], pfx + f"wup{slot}"], banks=[pk], ms=(k == KD - 1))
                    for k in range(KD):
                        S.emit("pe", lambda e, k=k, c=c, co=co, slot=slot, b=b, half=half: e.matmul(
                            ph[b][:, 2 * half:2 * half + 2], wup[slot][:, k, co:co + 128], hT[:, k, c, 0:HH],
                            start=(k == 0), stop=(k == KD - 1)),
                            reads=[hh[k], pfx + f"wup{slot}"], banks=[pfx + "pmisc"], ms=(k == KD - 1))
                    S.emit("act", lambda e, pp=pp, acc=acc, jc=jc: e.activation(
                        out=acc[:, :], in_=pp[:, :], func=AF.Identity, scale=dw_sb[:, jc, 2:3], bias=db_sb[:, jc:jc + 1]),
                        reads=[pfx + "dw", pfx + "db"], writes=[ak], banks=[pk])
                    S.emit("dve", lambda e, pp=pp, acc=acc, jc=jc: e.scalar_tensor_tensor(
                        out=acc[:, 1:CH], in0=pp[:, 0:CH - 1], scalar=dw_sb[:, jc, 1:2], in1=acc[:, 1:CH],
                        op0=ALU.mult, op1=ALU.add), reads=[ak, pfx + "dw"], writes=[ak], banks=[pk])
                    S.emit("dve", lambda e, pp=pp, acc=acc, jc=jc: e.scalar_tensor_tensor(
                        out=acc[:, 2:CH], in0=pp[:, 0:CH - 2], scalar=dw_sb[:, jc, 0:1], in1=acc[:, 2:CH],
                        op0=ALU.mult, op1=ALU.add), reads=[ak, pfx + "dw"], writes=[ak], banks=[pk])
                for half, (acc, jc) in enumerate(((gacc[b], j), (vacc[b], NJ + j))):
                    ak = pfx + f"acc{half}{b}"
                    S.emit("dve", lambda e, acc=acc, jc=jc, b=b, half=half: e.scalar_tensor_tensor(
                        out=acc[:, 0:2], in0=ph[b][:, 2 * half:2 * half + 2], scalar=dw_sb[:, jc, 0:1], in1=acc[:, 0:2],
                        op0=ALU.mult, op1=ALU.add), reads=[ak, pfx + "dw"], writes=[ak], banks=[pfx + "pmisc"])
                    S.emit("dve", lambda e, acc=acc, jc=jc, b=b, half=half: e.scalar_tensor_tensor(
                        out=acc[:, 0:1], in0=ph[b][:, 2 * half + 1:2 * half + 2], scalar=dw_sb[:, jc, 1:2], in1=acc[:, 0:1],
                        op0=ALU.mult, op1=ALU.add), reads=[ak, pfx + "dw"], writes=[ak], banks=[pfx + "pmisc"])
                S.emit("act", lambda e, b=b: e.activation(out=sg[b][:, :], in_=gacc[b][:, :], func=AF.Silu),
                       reads=[pfx + f"acc0{b}"], writes=[pfx + f"sg{b}"])
                S.emit("dve", lambda e, b=b, j=j, c=c: e.tensor_tensor(out=gT[:, j, c, :], in0=sg[b][:, :], in1=vacc[b][:, :],
                                                                      op=ALU.mult),
                       reads=[pfx + f"sg{b}", pfx + f"acc1{b}"], writes=[pfx + f"gT{j}.{c}"])
        for n in range(KD):
            slot = wd_i % 2
            if n + 1 < KD:
                load_wdn(n + 1, (wd_i + 1) % 2)
            wd_i += 1
            for c in range(SC):
                gc = sti * SC + c
                b = it % 2
                it += 1
                for jj in range(NJ):
                    S.emit("pe", lambda e, jj=jj, c=c, b=b, slot=slot: e.matmul(
                        pd[b][:, :], wdn[slot][:, jj, :], gT[:, jj, c, :], start=(jj == 0), stop=(jj == NJ - 1)),
                        reads=[pfx + f"gT{jj}.{c}", pfx + f"wdn{slot}"], banks=[pfx + f"pd{b}"], ms=(jj == NJ - 1))
                S.emit("dve", lambda e, b=b, n=n, c=c: e.tensor_tensor(out=xo[b][:, :], in0=pd[b][:, :], in1=x_sb[:, n, c, HH:W],
                                                                      op=ALU.add),
                       reads=[pfx + f"x{c}.{n}m"], writes=[pfx + f"xo{b}"], banks=[pfx + f"pd{b}"])
                t = S.dma("sp", pfx + f"so{b}", lambda e, b=b, n=n, gc=gc: e.dma_start(
                    out=xo_r[:, n, gc * CH:(gc + 1) * CH], in_=xo[b][:, :]), reads=[pfx + f"xo{b}"], writes=[pfx + f"xoT.{n}.{gc}"])
                out_toks.append(t)
    return out_toks


DA = 512
MA = DA // 128
KA = 31


def conv_phase(cx, pfx, xT, xh, xoT, g_d, win_d, wout_d, adw_d, avec_d, bdw_d, nch=NCHUNK):
    S, nc = cx.S, cx.nc
    H = HALO
    W = H + CH
    x_sb = cx.sb(pfx + "x", [128, KD, W], F32)
    hT = cx.sb(pfx + "hT", [128, KD, W], BF16)
    sq = cx.sb(pfx + "sq", [128, KD, CH], BF16)
    rstd = cx.sb(pfx + "rstd", [128, CH], F32)
    tmp = cx.sb(pfx + "tmp", [128, CH], F32)
    ones = cx.sb(pfx + "ones", [128, 128], BF16)
    onesm = cx.sb(pfx + "onesm", [128, 128], BF16)
    g_sb = cx.sb(pfx + "g", [128, KD], F32)
    adw = cx.sb(pfx + "adw", [128, MA, KA], F32)
    avec = cx.sb(pfx + "avec", [128, 3, MA], F32)
    bdw = cx.sb(pfx + "bdw", [128, MA, 3], F32)
    win = cx.sb(pfx + "win", [128, 20, KD, 128], BF16)
    wout = cx.sb(pfx + "wout", [128, KD, KD, 128], BF16)
    glu = cx.sb(pfx + "glu", [128, MA, W], F32)
    ca = cx.sb(pfx + "ca", [128, MA, CH], F32)
    cab = cx.sb(pfx + "cab", [128, MA, CH], BF16)
    abT = cx.sb(pfx + "abT", [128, 2 * MA, CH], BF16)
    sgm = cx.sb(pfx + "sgm", [128, W], F32)
    chb = cx.sb(pfx + "chb", [128, W], F32)
    accb = cx.sb(pfx + "accb", [128, CH], F32)
    xo = [cx.sb(pfx + f"xo{i}", [128, CH], F32) for i in range(2)]
    pA, pB, pC = cx.ps(pfx + "pA"), cx.ps(pfx + "pB"), cx.ps(pfx + "pC")
    pmisc, pstat, pvar = cx.ps(pfx + "pmisc"), cx.ps(pfx + "pstat"), cx.ps(pfx + "pvar")
    po = [cx.ps(pfx + f"po{i}") for i in range(2)]
    BK = lambda n: pfx + "B." + n

    xT_r = xT.rearrange("(k p) t -> p k t", p=128)
    xh_r = xh.rearrange("(k p) (c h) -> p k c h", p=128, h=HALO)
    xo_r = xoT.rearrange("(k p) t -> p k t", p=128)

    S.emit("dve", lambda e: e.memset(ones[:], 1.0), writes=[pfx + "ones"])
    S.emit("dve", lambda e: e.memset(onesm[:], 1.0 / DA), writes=[pfx + "onesm"])
    S.dma_group("sp", "cst", [(lambda e: e.dma_start(out=g_sb[:], in_=g_d), [], [pfx + "g"]),
                              (lambda e: e.dma_start(out=adw[:], in_=adw_d), [], [pfx + "adw"]),
                              (lambda e: e.dma_start(out=avec[:], in_=avec_d), [], [pfx + "avec"]),
                              (lambda e: e.dma_start(out=bdw[:], in_=bdw_d), [], [pfx + "bdw"])])
    S.dma_group("pool", "wA", [(lambda e, m=m: e.dma_start(out=win[:, m], in_=win_d[m]), [], [pfx + f"win{m}"]) for m in range(20)])
    S.dma_group("pool", "wB", [(lambda e, n=n: e.dma_start(out=wout[:, n], in_=wout_d[n]), [], [pfx + f"wout{n}"]) for n in range(KD)])

    out_toks = []
    it = 0
    for c in range(nch):
        xk = [pfx + f"x{k}" for k in range(KD)]
        hk = [pfx + f"h{k}" for k in range(KD)]
        S.dma_group("sp", "lx", [(lambda e, k=k, c=c: e.dma_start(out=x_sb[:, k, H:W], in_=xT_r[:, k, c * CH:(c + 1) * CH]), [], [xk[k]])
                                 for k in range(KD)])
        S.dma("sp", pfx + "lxh", lambda e, c=c: e.dma_start(out=x_sb[:, :, 0:H], in_=xh_r[:, :, c, :]), writes=[pfx + "xh"])
        emit_rmsnorm(cx, pfx, lambda k: x_sb[:, k, H:W], lambda k: hT[:, k, H:W], xk, hk, CH, sq, ones,
                     pstat[:, :], BK("pstat"), tmp, rstd, g_sb, pfx + "g")
        emit_rmsnorm(cx, pfx, lambda k: x_sb[:, k, 0:H], lambda k: hT[:, k, 0:H], [pfx + "xh"] * KD,
                     [pfx + f"hh{k}" for k in range(KD)], H, sq, ones, pmisc[:, 256:256 + H], BK("pmisc"), tmp, rstd, g_sb, pfx + "g")
        hh = [pfx + f"hh{k}" for k in range(KD)]

        def proj(m, pmain, bank, hcol=None):
            for k in range(KD):
                S.emit("pe", lambda e, k=k, m=m: e.matmul(pmain[:, :], win[:, m, k, :], hT[:, k, H:W], start=(k == 0), stop=(k == KD - 1)),
                       reads=[hk[k], pfx + f"win{m}"], banks=[bank], ms=(k == KD - 1))
            if hcol is not None:
                for k in range(KD):
                    S.emit("pe", lambda e, k=k, m=m: e.matmul(pmisc[:, hcol:hcol + H], win[:, m, k, :], hT[:, k, 0:H],
                                                              start=(k == 0), stop=(k == KD - 1)),
                           reads=[hh[k], pfx + f"win{m}"], banks=[BK("pmisc")], ms=(k == KD - 1))

        for m in range(MA):
            proj(m, pA, BK("pA"), 0)
            proj(MA + m, pB, BK("pB"), H)
            S.emit("act", lambda e: e.activation(out=sgm[:, H:W], in_=pB[:, :], func=AF.Sigmoid), banks=[BK("pB")], writes=[pfx + "sgm"])
            S.emit("act", lambda e: e.activation(out=sgm[:, 0:H], in_=pmisc[:, H:2 * H], func=AF.Sigmoid), banks=[BK("pmisc")], writes=[pfx + "sgm"])
            S.emit("dve", lambda e, m=m: e.tensor_tensor(out=glu[:, m, H:W], in0=pA[:, :], in1=sgm[:, H:W], op=ALU.mult),
                   reads=[pfx + "sgm"], banks=[BK("pA")], writes=[pfx + f"glu{m}"])
            S.emit("dve", lambda e, m=m: e.tensor_tensor(out=glu[:, m, 0:H], in0=pmisc[:, 0:H], in1=sgm[:, 0:H], op=ALU.mult),
                   reads=[pfx + "sgm"], banks=[BK("pmisc")], writes=[pfx + f"glu{m}"])
            S.emit("act", lambda e, m=m: e.activation(out=ca[:, m, :], in_=glu[:, m, H:W], func=AF.Identity,
                                                      scale=adw[:, m, KA - 1:KA], bias=avec[:, 0, m:m + 1]),
                   reads=[pfx + f"glu{m}", pfx + "adw", pfx + "avec"], writes=[pfx + f"ca{m}"])
            for k in range(KA - 1):
                S.emit("dve", lambda e, m=m, k=k: e.scalar_tensor_tensor(
                    out=ca[:, m, :], in0=glu[:, m, 2 + k:2 + k + CH], scalar=adw[:, m, k:k + 1], in1=ca[:, m, :],
                    op0=ALU.mult, op1=ALU.add), reads=[pfx + f"glu{m}", pfx + f"ca{m}", pfx + "adw"], writes=[pfx + f"ca{m}"])
            S.emit("act", lambda e, m=m: e.activation(out=cab[:, m, :], in_=ca[:, m, :], func=AF.Identity),
                   reads=[pfx + f"ca{m}"], writes=[pfx + f"cab{m}"])
        for m in range(MA):
            S.emit("pe", lambda e, m=m: e.matmul(pstat[:, :], onesm[:, :], cab[:, m, :], start=(m == 0), stop=(m == MA - 1)),
                   reads=[pfx + f"cab{m}", pfx + "onesm"], banks=[BK("pstat")], ms=(m == MA - 1))
        for m in range(MA):
            S.emit("dve", lambda e, m=m: e.tensor_tensor(out=ca[:, m, :], in0=ca[:, m, :], in1=pstat[:, :], op=ALU.subtract),
                   reads=[pfx + f"ca{m}"], banks=[BK("pstat")], writes=[pfx + f"ca{m}"])
            S.emit("act", lambda e, m=m: e.activation(out=cab[:, m, :], in_=ca[:, m, :], func=AF.Square),
                   reads=[pfx + f"ca{m}"], writes=[pfx + f"cab{m}"])
        for m in range(MA):
            S.emit("pe", lambda e, m=m: e.matmul(pvar[:, :], onesm[:, :], cab[:, m, :], start=(m == 0), stop=(m == MA - 1)),
                   reads=[pfx + f"cab{m}", pfx + "onesm"], banks=[BK("pvar")], ms=(m == MA - 1))
        S.emit("dve", lambda e: e.tensor_scalar(out=tmp[:, :], in0=pvar[:, :], scalar1=EPS, scalar2=None, op0=ALU.add),
               banks=[BK("pvar")], writes=[pfx + "tmp"])
        S.emit("act", lambda e: e.activation(out=tmp[:, :], in_=tmp[:, :], func=AF.Sqrt), reads=[pfx + "tmp"], writes=[pfx + "tmp"])
        S.emit("dve", lambda e: e.reciprocal(out=rstd[:, :], in_=tmp[:, :]), reads=[pfx + "tmp"], writes=[pfx + "rstd"])
        for m in range(MA):
            S.emit("dve", lambda e, m=m: e.scalar_tensor_tensor(out=ca[:, m, :], in0=ca[:, m, :], scalar=avec[:, 1, m:m + 1], in1=rstd[:, :],
                                                                op0=ALU.mult, op1=ALU.mult),
                   reads=[pfx + f"ca{m}", pfx + "rstd", pfx + "avec"], writes=[pfx + f"ca{m}"])
            S.emit("act", lambda e, m=m: e.activation(out=abT[:, m, :], in_=ca[:, m, :], func=AF.Silu, bias=avec[:, 2, m:m + 1]),
                   reads=[pfx + f"ca{m}", pfx + "avec"], writes=[pfx + f"ab{m}"])
        for m in range(MA):
            proj(3 * MA + m, pA, BK("pA"), 2 * H)
            proj(4 * MA + m, pB, BK("pB"), 3 * H)
            proj(2 * MA + m, pC, BK("pC"), None)
            S.emit("act", lambda e: e.activation(out=sgm[:, H:W], in_=pA[:, :], func=AF.Identity), banks=[BK("pA")], writes=[pfx + "sgm"])
            S.emit("act", lambda e: e.activation(out=sgm[:, 0:H], in_=pmisc[:, 2 * H:3 * H], func=AF.Identity), banks=[BK("pmisc")], writes=[pfx + "sgm"])
            S.emit("dve", lambda e: e.tensor_tensor(out=chb[:, H:W], in0=pB[:, :], in1=sgm[:, H:W], op=ALU.mult),
                   reads=[pfx + "sgm"], banks=[BK("pB")], writes=[pfx + "chb"])
            S.emit("dve", lambda e: e.tensor_tensor(out=chb[:, 0:H], in0=pmisc[:, 3 * H:4 * H], in1=sgm[:, 0:H], op=ALU.mult),
                   reads=[pfx + "sgm"], banks=[BK("pmisc")], writes=[pfx + "chb"])
            S.emit("act", lambda e, m=m: e.activation(out=accb[:, :], in_=chb[:, H:W], func=AF.Identity, scale=bdw[:, m, 2:3]),
                   reads=[pfx + "chb", pfx + "bdw"], writes=[pfx + "accb"])
            for k in range(2):
                S.emit("dve", lambda e, m=m, k=k: e.scalar_tensor_tensor(
                    out=accb[:, :], in0=chb[:, H - 2 + k:H - 2 + k + CH], scalar=bdw[:, m, k:k + 1], in1=accb[:, :],
                    op0=ALU.mult, op1=ALU.add), reads=[pfx + "chb", pfx + "accb", pfx + "bdw"], writes=[pfx + "accb"])
            S.emit("dve", lambda e, m=m: e.tensor_tensor(out=abT[:, MA + m, :], in0=pC[:, :], in1=accb[:, :], op=ALU.mult),
                   reads=[pfx + "accb"], banks=[BK("pC")], writes=[pfx + f"ab{MA + m}"])
        for n in range(KD):
            b = it % 2
            it += 1
            for k in range(2 * MA):
                S.emit("pe", lambda e, k=k, n=n, b=b: e.matmul(po[b][:, :], wout[:, n, k, :], abT[:, k, :], start=(k == 0), stop=(k == 2 * MA - 1)),
                       reads=[pfx + f"ab{k}", pfx + f"wout{n}"], banks=[BK(f"po{b}")], ms=(k == 2 * MA - 1))
            S.emit("dve", lambda e, b=b, n=n: e.tensor_tensor(out=xo[b][:, :], in0=po[b][:, :], in1=x_sb[:, n, H:W], op=ALU.add),
                   reads=[xk[n]], writes=[pfx + f"xo{b}"], banks=[BK(f"po{b}")])
            t = S.dma("sp", pfx + f"so{b}", lambda e, b=b, n=n, c=c: e.dma_start(out=xo_r[:, n, c * CH:(c + 1) * CH], in_=xo[b][:, :]),
                      reads=[pfx + f"xo{b}"], writes=[pfx + f"xoT.{n}.{c}"])
            out_toks.append(t)
    return out_toks


NH = 16
HP = NH // 2
NKB = TOK // 128


def qkv_phase(cx, pfx, xT, g_d, wq_d, wk_d, wv_d, qkg_d, qT_o, kT_o, V_o, nch=NCHUNK):
    S, nc = cx.S, cx.nc
    x_sb = cx.sb(pfx + "x", [128, KD, CH], F32)
    hT = cx.sb(pfx + "hT", [128, KD, CH], BF16)
    sq = cx.sb(pfx + "sq", [128, KD, CH], BF16)
    rstd = cx.sb(pfx + "rstd", [128, CH], F32)
    tmp = cx.sb(pfx + "tmp", [128, CH], F32)
    ones = cx.sb(pfx + "ones", [128, 128], BF16)
    bd = cx.sb(pfx + "bd", [128, 128], BF16)
    g_sb = cx.sb(pfx + "g", [128, KD], F32)
    qkg = cx.sb(pfx + "qkg", [128, 2], F32)
    wq = cx.sb(pfx + "wq", [128, HP, KD, 128], BF16)
    wk = cx.sb(pfx + "wk", [128, HP, KD, 128], BF16)
    wv = cx.sb(pfx + "wv", [128, 2, KD, 512], BF16)
    qf = [cx.sb(pfx + f"qf{i}", [128, CH], F32) for i in range(2)]
    sqq = [cx.sb(pfx + f"sqq{i}", [128, CH], BF16) for i in range(2)]
    rs = [cx.sb(pfx + f"rs{i}", [128, CH], F32) for i in range(2)]
    qn = [cx.sb(pfx + f"qn{i}", [128, CH], BF16) for i in range(2)]
    vt = [cx.sb(pfx + f"vt{i}", [128, 512], BF16) for i in range(2)]
    pq = [cx.ps(pfx + f"pq{i}") for i in range(2)]
    pms = [cx.ps(pfx + f"pms{i}") for i in range(2)]
    pvv = [cx.ps(pfx + f"pvv{i}") for i in range(2)]
    pstat = cx.ps(pfx + "pstat")
    BK = lambda n: pfx + "B." + n
    xT_r = xT.rearrange("(k p) t -> p k t", p=128)
    qT_r = qT_o.rearrange("(k p) t -> p k t", p=128)
    kT_r = kT_o.rearrange("(k p) t -> p k t", p=128)

    S.emit("dve", lambda e: e.memset(ones[:], 1.0), writes=[pfx + "ones"])
    S.emit("dve", lambda e: e.memset(bd[:], 0.0), writes=[pfx + "bd"])
    S.emit("dve", lambda e: e.memset(bd[0:64, 0:64], 1.0 / 64), writes=[pfx + "bd"])
    S.emit("dve", lambda e: e.memset(bd[64:128, 64:128], 1.0 / 64), writes=[pfx + "bd"])
    S.dma_group("sp", "cst", [(lambda e: e.dma_start(out=g_sb[:], in_=g_d), [], [pfx + "g"]),
                              (lambda e: e.dma_start(out=qkg[:], in_=qkg_d), [], [pfx + "qkg"])])
    S.emit("dve", lambda e: e.tensor_scalar(out=qkg[:, 0:1], in0=qkg[:, 0:1], scalar1=0.125, scalar2=None, op0=ALU.mult),
           reads=[pfx + "qkg"], writes=[pfx + "qkg"])
    S.dma_group("pool", "wA", [(lambda e, m=m: e.dma_start(out=wq[:, m], in_=wq_d[m]), [], [pfx + f"wq{m}"]) for m in range(HP)])
    S.dma_group("pool", "wB", [(lambda e, m=m: e.dma_start(out=wk[:, m], in_=wk_d[m]), [], [pfx + f"wk{m}"]) for m in range(HP)])
    S.dma_group("pool", "wC", [(lambda e, hf=hf: e.dma_start(out=wv[:, hf], in_=wv_d[hf]), [], [pfx + f"wv{hf}"]) for hf in range(2)])
    out_toks = []
    it = 0
    for c in range(nch):
        xk = [pfx + f"x{k}" for k in range(KD)]
        hk = [pfx + f"h{k}" for k in range(KD)]
        S.dma_group("sp", "lx", [(lambda e, k=k, c=c: e.dma_start(out=x_sb[:, k, :], in_=xT_r[:, k, c * CH:(c + 1) * CH]), [], [xk[k]])
                                 for k in range(KD)])
        emit_rmsnorm(cx, pfx, lambda k: x_sb[:, k, :], lambda k: hT[:, k, :], xk, hk, CH, sq, ones,
                     pstat[:, :], BK("pstat"), tmp, rstd, g_sb, pfx + "g")
        for which, (w_sb, wkey, gcol, o_r) in enumerate(((wq, "wq", 0, qT_r), (wk, "wk", 1, kT_r))):
            for m in range(HP):
                b = it % 2
                it += 1
                for k in range(KD):
                    S.emit("pe", lambda e, k=k, m=m, b=b, w_sb=w_sb: e.matmul(pq[b][:, :], w_sb[:, m, k, :], hT[:, k, :],
                                                                             start=(k == 0), stop=(k == KD - 1)),
                           reads=[hk[k], pfx + f"{wkey}{m}"], banks=[BK(f"pq{b}")], ms=(k == KD - 1))
                S.emit("act", lambda e, b=b: e.activation(out=sqq[b][:, :], in_=pq[b][:, :], func=AF.Square),
                       banks=[BK(f"pq{b}")], writes=[pfx + f"sqq{b}"])
                S.emit("act", lambda e, b=b: e.activation(out=qf[b][:, :], in_=pq[b][:, :], func=AF.Identity),
                       banks=[BK(f"pq{b}")], writes=[pfx + f"qf{b}"])
                S.emit("pe", lambda e, b=b: e.matmul(pms[b][:, :], bd[:, :], sqq[b][:, :], start=True, stop=True),
                       reads=[pfx + f"sqq{b}", pfx + "bd"], banks=[BK(f"pms{b}")], ms=True)
                S.emit("dve", lambda e, b=b: e.tensor_scalar(out=rs[b][:, :], in0=pms[b][:, :], scalar1=EPS, scalar2=None, op0=ALU.add),
                       banks=[BK(f"pms{b}")], writes=[pfx + f"rs{b}"])
                S.emit("act", lambda e, b=b: e.activation(out=rs[b][:, :], in_=rs[b][:, :], func=AF.Sqrt),
                       reads=[pfx + f"rs{b}"], writes=[pfx + f"rs{b}"])
                S.emit("dve", lambda e, b=b: e.reciprocal(out=rs[b][:, :], in_=rs[b][:, :]), reads=[pfx + f"rs{b}"], writes=[pfx + f"rs{b}"])
                S.emit("dve", lambda e, b=b, gcol=gcol: e.scalar_tensor_tensor(out=qn[b][:, :], in0=qf[b][:, :], scalar=qkg[:, gcol:gcol + 1],
                                                                               in1=rs[b][:, :], op0=ALU.mult, op1=ALU.mult),
                       reads=[pfx + f"qf{b}", pfx + f"rs{b}", pfx + "qkg"], writes=[pfx + f"qn{b}"])
                t = S.dma("sp", pfx + f"sq{b}", lambda e, b=b, m=m, c=c, o_r=o_r: e.dma_start(out=o_r[:, m, c * CH:(c + 1) * CH], in_=qn[b][:, :]),
                          reads=[pfx + f"qn{b}"], writes=[pfx + f"o{which}.{m}.{c}"])
                out_toks.append(t)
        for tt in range(CH // 128):
            kb = c * (CH // 128) + tt
            for hf in range(2):
                b = it % 2
                it += 1
                for k in range(KD):
                    S.emit("pe", lambda e, k=k, tt=tt, hf=hf, b=b: e.matmul(pvv[b][:, :], hT[:, k, tt * 128:(tt + 1) * 128], wv[:, hf, k, :],
                                                                          start=(k == 0), stop=(k == KD - 1)),
                           reads=[hk[k], pfx + f"wv{hf}"], banks=[BK(f"pvv{b}")], ms=(k == KD - 1))
                S.emit("act", lambda e, b=b: e.activation(out=vt[b][:, :], in_=pvv[b][:, :], func=AF.Identity),
                       banks=[BK(f"pvv{b}")], writes=[pfx + f"vt{b}"])
                t = S.dma("sp", pfx + f"sv{b}", lambda e, b=b, hf=hf, kb=kb: e.dma_start(
                    out=V_o[hf * 4:(hf + 1) * 4, :, kb, :].rearrange("h p f -> p h f"),
                    in_=vt[b][:, :].rearrange("p (h f) -> p h f", f=128)),
                    reads=[pfx + f"vt{b}"], writes=[pfx + f"oV.{hf}.{kb}"])
                out_toks.append(t)
    return out_toks


def attn_phase(cx, pfx, xT, xoT, qT_i, kT_g, V_g, mask_d, tri_d, wo_d, nhp=HP, nq=NCHUNK):
    S, nc = cx.S, cx.nc
    kT_sb = cx.sb(pfx + "kT", [128, 2, TOK], BF16)
    V_sb = cx.sb(pfx + "V", [128, 2, NKB, 128], BF16)
    q_sb = cx.sb(pfx + "q", [128, TOK], BF16)
    oT = cx.sb(pfx + "oT", [128, HP, TOK], BF16)
    mask = cx.sb(pfx + "mask", [128, 2, 4, 1024], BF16)
    ntri = cx.sb(pfx + "ntri", [128, 128], BF16)
    nones = cx.sb(pfx + "nones", [128, 128], BF16)
    one1 = cx.sb(pfx + "one1", [128, 1], F32)
    E = [cx.sb(pfx + f"E{i}", [128, 1024], F32) for i in range(2)]
    L = [cx.sb(pfx + f"L{i}", [128, 1024], BF16) for i in range(2)]
    Wt = [cx.sb(pfx + f"W{i}", [128, 1024], BF16) for i in range(2)]
    Ls = cx.sb(pfx + "Ls", [128, 1024], F32)
    Lsb = [cx.sb(pfx + f"Lsb{i}", [128, 1024], BF16) for i in range(2)]
    wo = cx.sb(pfx + "wo", [128, KD, KD, 128], BF16)
    xr = [cx.sb(pfx + f"xr{i}", [128, CH], F32) for i in range(2)]
    xo = [cx.sb(pfx + f"xo{i}", [128, CH], F32) for i in range(2)]
    pz = [cx.ps(pfx + f"pz{i}", (128, 1024)) for i in range(2)]
    pp2 = cx.ps(pfx + "pp2", (128, 1024))
    po = cx.ps(pfx + "po")
    pf = cx.ps(pfx + "pf")
    BK = lambda n: pfx + "B." + n
    xT_r = xT.rearrange("(k p) t -> p k t", p=128)
    xo_r = xoT.rearrange("(k p) t -> p k t", p=128)
    qT_r = qT_i.rearrange("(k p) t -> p k t", p=128)
    kT_r = kT_g.rearrange("(h r p) t -> p h r t", r=2, p=128)
    V_r = V_g.rearrange("(h r p) (k f) -> h p r k f", r=2, p=128, f=128)

    S.dma("pool", pfx + "ct", lambda e: e.dma_start(out=ntri[:], in_=tri_d), writes=[pfx + "ntri"])
    S.emit("dve", lambda e: e.memset(nones[:], -1.0), writes=[pfx + "nones"])
    S.emit("dve", lambda e: e.memset(one1[:], 1.0), writes=[pfx + "one1"])
    S.dma("pool", pfx + "cm", lambda e: e.dma_start(out=mask[:], in_=mask_d), writes=[pfx + "mask"])
    S.dma_group("pool", "wA", [(lambda e, n=n: e.dma_start(out=wo[:, n], in_=wo_d[n]), [], [pfx + f"wo{n}"]) for n in range(KD)])

    step = 0
    for hp in range(nhp):
        S.dma("sp", pfx + "lk", lambda e, hp=hp: e.dma_start(out=kT_sb[:, :, :], in_=kT_r[:, hp, :, :]), writes=[pfx + "kT"])
        S.dma("sp", pfx + "lv", lambda e, hp=hp: e.dma_start(out=V_sb[:, :, :, :], in_=V_r[hp]),
              writes=[pfx + "V"])
        S.dma("sp", pfx + "lq", lambda e, hp=hp: e.dma_start(out=q_sb[:, :], in_=qT_r[:, hp, :]), writes=[pfx + "q"])
        for i in range(nq):
            blocks = []
            for j in range(2 * i + 1, -1, -1):
                for b4 in range(3, -1, -1):
                    mk = 1 if j == 2 * i + 1 else (0 if j == 2 * i else None)
                    blocks.append((j % 2, (j // 2) * 4 + b4, mk, b4))
            nb = len(blocks)

            def emit_z(s):
                r, kb, mk, b4 = blocks[s]
                zb = (step + s) % 2
                for hd in range(2):
                    S.emit("pe", lambda e, hd=hd, r=r, kb=kb, zb=zb, i=i: e.matmul(
                        pz[zb][:, hd * 512:(hd + 1) * 512], kT_sb[hd * 64:(hd + 1) * 64, r, kb * 128:(kb + 1) * 128],
                        q_sb[hd * 64:(hd + 1) * 64, i * CH:(i + 1) * CH], start=True, stop=True),
                        reads=[pfx + "kT", pfx + "q"], banks=[BK(f"pz{zb}")], ms=(hd == 1))

            def emit_el(s):
                r, kb, mk, b4 = blocks[s]
                zb = (step + s) % 2
                S.emit("act", lambda e, zb=zb: e.activation(out=E[zb][:, :], in_=pz[zb][:, :], func=AF.Exp),
                       banks=[BK(f"pz{zb}")], writes=[pfx + f"E{zb}"])
                S.emit("act", lambda e, zb=zb: e.activation(out=L[zb][:, :], in_=E[zb][:, :], func=AF.Ln, bias=one1[:, 0:1]),
                       reads=[pfx + f"E{zb}", pfx + "one1"], writes=[pfx + f"L{zb}"])
                if mk is not None:
                    S.emit("dve", lambda e, zb=zb, mk=mk, b4=b4: e.tensor_tensor(out=L[zb][:, :], in0=L[zb][:, :], in1=mask[:, mk, b4, :], op=ALU.mult),
                           reads=[pfx + f"L{zb}", pfx + "mask"], writes=[pfx + f"L{zb}"])

            def emit_p2(s):
                r, kb, mk, b4 = blocks[s]
                zb = (step + s) % 2
                for hd in range(2):
                    sl = slice(hd * 512, (hd + 1) * 512)
                    S.emit("pe", lambda e, hd=hd, r=r, kb=kb, sl=sl, i=i: e.matmul(
                        pp2[:, sl], kT_sb[hd * 64:(hd + 1) * 64, r, kb * 128:(kb + 1) * 128],
                        q_sb[hd * 64:(hd + 1) * 64, i * CH:(i + 1) * CH], start=True, stop=False),
                        reads=[pfx + "kT", pfx + "q"], banks=[BK("pp2")])
                    S.emit("pe", lambda e, sl=sl, zb=zb, last=(s == 0): e.matmul(pp2[:, sl], ntri[:, :], L[zb][:, sl], start=False, stop=last),
                           reads=[pfx + f"L{zb}", pfx + "ntri"], banks=[BK("pp2")], ms=(s == 0 and hd == 1))
                    if s > 0:
                        lb = (step + s - 1) % 2
                        S.emit("pe", lambda e, sl=sl, lb=lb: e.matmul(pp2[:, sl], nones[:, :], Lsb[lb][:, sl], start=False, stop=True),
                               reads=[pfx + f"Lsb{lb}", pfx + "nones"], banks=[BK("pp2")], ms=(hd == 1))

            def emit_w(s):
                r, kb, mk, b4 = blocks[s]
                zb = (step + s) % 2
                S.emit("act", lambda e, zb=zb: e.activation(out=Wt[zb][:, :], in_=pp2[:, :], func=AF.Exp),
                       banks=[BK("pp2")], writes=[pfx + f"W{zb}"])
                if mk is not None:
                    S.emit("dve", lambda e, zb=zb, mk=mk, b4=b4: e.tensor_tensor(out=Wt[zb][:, :], in0=Wt[zb][:, :], in1=mask[:, mk, b4, :], op=ALU.mult),
                           reads=[pfx + f"W{zb}", pfx + "mask"], writes=[pfx + f"W{zb}"])
                if s + 1 < nb:
                    if s == 0:
                        S.emit("dve", lambda e, zb=zb: e.tensor_copy(out=Ls[:, :], in_=L[zb][:, :]), reads=[pfx + f"L{zb}"], writes=[pfx + "Ls"])
                    else:
                        S.emit("dve", lambda e, zb=zb: e.tensor_tensor(out=Ls[:, :], in0=Ls[:, :], in1=L[zb][:, :], op=ALU.add),
                               reads=[pfx + f"L{zb}", pfx + "Ls"], writes=[pfx + "Ls"])
                    S.emit("dve", lambda e, zb=zb: e.tensor_copy(out=Lsb[zb][:, :], in_=Ls[:, :]), reads=[pfx + "Ls"], writes=[pfx + f"Lsb{zb}"])

            def emit_pv(s):
                r, kb, mk, b4 = blocks[s]
                zb = (step + s) % 2
                for hd in range(2):
                    S.emit("pe", lambda e, hd=hd, r=r, kb=kb, zb=zb, st_=(s == 0), sp_=(s == nb - 1): e.matmul(
                        po[hd * 64:(hd + 1) * 64, :], V_sb[:, r, kb, hd * 64:(hd + 1) * 64], Wt[zb][:, hd * 512:(hd + 1) * 512],
                        start=st_, stop=sp_),
                        reads=[pfx + "V", pfx + f"W{zb}"], banks=[BK("po")], ms=(s == nb - 1 and hd == 1))

            emit_z(0)
            emit_el(0)
            for s in range(nb):
                if s + 1 < nb:
                    emit_z(s + 1)
                    emit_el(s + 1)
                emit_p2(s)
                emit_w(s)
                emit_pv(s)
            step += nb
            S.emit("act", lambda e, hp=hp, i=i: e.activation(out=oT[:, hp, i * CH:(i + 1) * CH], in_=po[:, :], func=AF.Identity),
                   banks=[BK("po")], writes=[pfx + f"oT{hp}.{i}"])
    out_toks = []
    it = 0
    for i in range(nq):
        for n in range(KD):
            b = it % 2
            it += 1
            S.dma("sp", pfx + f"lx{b}", lambda e, b=b, n=n, i=i: e.dma_start(out=xr[b][:, :], in_=xT_r[:, n, i * CH:(i + 1) * CH]),
                  writes=[pfx + f"xr{b}"])
            for k in range(nhp):
                S.emit("pe", lambda e, k=k, n=n, i=i: e.matmul(pf[:, :], wo[:, n, k, :], oT[:, k, i * CH:(i + 1) * CH],
                                                              start=(k == 0), stop=(k == nhp - 1)),
                       reads=[pfx + f"oT{k}.{i}", pfx + f"wo{n}"], banks=[BK("pf")], ms=(k == nhp - 1))
            S.emit("dve", lambda e, b=b: e.tensor_tensor(out=xo[b][:, :], in0=pf[:, :], in1=xr[b][:, :], op=ALU.add),
                   reads=[pfx + f"xr{b}"], writes=[pfx + f"xo{b}"], banks=[BK("pf")])
            t = S.dma("sp", pfx + f"so{b}", lambda e, b=b, n=n, i=i: e.dma_start(out=xo_r[:, n, i * CH:(i + 1) * CH], in_=xo[b][:, :]),
                      reads=[pfx + f"xo{b}"], writes=[pfx + f"xoT.{n}.{i}"])
            out_toks.append(t)
    return out_toks


NCORES = 8
SEQ = 8192
BATCH = 4


def _own_tokens(p):
    return np.concatenate([np.arange((2 * i + p) * CH, (2 * i + p + 1) * CH) for i in range(NCHUNK)])


def _halo_tokens(p):
    return np.concatenate([np.arange((2 * i + p) * CH, (2 * i + p) * CH + HALO) for i in range(NCHUNK)])


def _lay_vec(v, n):
    return np.ascontiguousarray(v.reshape(n, 128).T)


def _lay_w(w, kin, nout):
    return np.ascontiguousarray(w.reshape(kin, 128, nout, 128).transpose(2, 1, 0, 3))


def _lay_ffn(g, w_up, dw_w, dw_b, w_down):
    wu = w_up.reshape(KD, 128, 2, NJ, 128)
    return dict(g=_lay_vec(g, KD),
                wup=np.ascontiguousarray(wu.transpose(3, 1, 0, 2, 4).reshape(NJ, 128, KD, 256)),
                wdn=_lay_w(w_down, NJ, KD),
                dw=np.ascontiguousarray(dw_w.reshape(3, 2 * NJ, 128).transpose(2, 1, 0)),
                db=_lay_vec(dw_b, 2 * NJ))


def _lay_conv(g, w_in, a_dw_w, a_dw_b, a_ln_g, a_ln_b, b_dw_w, w_out):
    return dict(g=_lay_vec(g, KD), win=_lay_w(w_in, KD, 20), wout=_lay_w(w_out, KD, KD),
                adw=np.ascontiguousarray(a_dw_w.reshape(KA, MA, 128).transpose(2, 1, 0)),
                avec=np.ascontiguousarray(np.stack([a_dw_b, a_ln_g, a_ln_b]).reshape(3, MA, 128).transpose(2, 0, 1)),
                bdw=np.ascontiguousarray(b_dw_w.reshape(3, MA, 128).transpose(2, 1, 0)))


def _lay_attn(g, w_qkv, q_g, k_g, w_o):
    return dict(g=_lay_vec(g, KD), wq=_lay_w(w_qkv[:, :D], KD, HP), wk=_lay_w(w_qkv[:, D:2 * D], KD, HP),
                wv=np.ascontiguousarray(w_qkv[:, 2 * D:].reshape(KD, 128, 2, 512).transpose(2, 1, 0, 3)),
                qkg=np.ascontiguousarray(np.stack([np.tile(q_g, 2), np.tile(k_g, 2)], 1)),
                wo=_lay_w(w_o, KD, KD))


def _masks(p):
    ks = np.arange(CH)[:, None]
    tq = np.arange(CH)[None, :]
    diag = (ks < tq).astype(np.float32)
    A = diag if p == 0 else np.ones((CH, CH), np.float32)
    B = np.zeros((CH, CH), np.float32) if p == 0 else diag
    m = np.stack([A, B]).reshape(2, 4, 128, CH).transpose(2, 0, 1, 3)
    return np.ascontiguousarray(np.concatenate([m, m], -1))


def _tri():
    j = np.arange(128)[:, None]
    s = np.arange(128)[None, :]
    return -(j >= s).astype(np.float32)


_PROGS = {}


def _dram(nc, name, shape, dt=F32, kind="ExternalInput"):
    return nc.dram_tensor(name, list(shape), dt, kind=kind).ap()


def _prog(kind):
    if kind in _PROGS:
        return _PROGS[kind]
    nc = bass.Bass("TRN2", target_bir_lowering=False)
    S = Sched()
    with ExitStack() as st:
        cx = Ctx(nc, S, st)
        if kind == "conv":
            a = [_dram(nc, "xT", [D, TOK]), _dram(nc, "xh", [D, NCHUNK * HALO])]
            xo = _dram(nc, "xoT", [D, TOK], F32, "ExternalOutput")
            toks = conv_phase(cx, "c.", a[0], a[1], xo, _dram(nc, "g", [128, KD]), _dram(nc, "win", [20, 128, KD, 128]),
                              _dram(nc, "wout", [KD, 128, KD, 128]), _dram(nc, "adw", [128, MA, KA]),
                              _dram(nc, "avec", [128, 3, MA]), _dram(nc, "bdw", [128, MA, 3]))
        elif kind == "ffn":
            a = [_dram(nc, "xT", [D, TOK]), _dram(nc, "xh", [D, NCHUNK * HALO])]
            xo = _dram(nc, "xoT", [D, TOK], F32, "ExternalOutput")
            toks = ffn_phase(cx, "f.", a[0], a[1], xo, _dram(nc, "g", [128, KD]), _dram(nc, "wup", [NJ, 128, KD, 256]),
                             _dram(nc, "wdn", [KD, 128, NJ, 128]), _dram(nc, "dw", [128, 2 * NJ, 3]), _dram(nc, "db", [128, 2 * NJ]))
        elif kind == "qkv":
            toks = qkv_phase(cx, "q.", _dram(nc, "xT", [D, TOK]), _dram(nc, "g", [128, KD]), _dram(nc, "wq", [HP, 128, KD, 128]),
                             _dram(nc, "wk", [HP, 128, KD, 128]), _dram(nc, "wv", [2, 128, KD, 512]), _dram(nc, "qkg", [128, 2]),
                             _dram(nc, "qT", [D, TOK], BF16, "ExternalOutput"), _dram(nc, "kT", [D, TOK], BF16, "ExternalOutput"),
                             _dram(nc, "V", [HP, 128, NKB, 128], BF16, "ExternalOutput"))
        elif kind == "attn":
            toks = attn_phase(cx, "a.", _dram(nc, "xT", [D, TOK]), _dram(nc, "xoT", [D, TOK], F32, "ExternalOutput"),
                              _dram(nc, "qT", [D, TOK], BF16), _dram(nc, "kT", [2, D, TOK], BF16),
                              _dram(nc, "V", [2, HP, 128, NKB, 128], BF16), _dram(nc, "mask", [128, 2, 4, 1024]),
                              _dram(nc, "tri", [128, 128]), _dram(nc, "wo", [KD, 128, KD, 128]))
        S.wait_all("sp", toks)
        S.build(nc, st)
    _PROGS[kind] = nc
    return nc


def _run(kind, in_maps):
    res = run_bass_kernel_spmd(_prog(kind), in_maps, core_ids=list(range(NCORES)))
    return res.results


def kernel_unfused(**inputs):
    inp = {k: np.asarray(v) for k, v in inputs.items()}
    x = inp["x"]
    own = [_own_tokens(p) for p in range(2)]
    halo = [_halo_tokens(p) for p in range(2)]
    xT = [np.ascontiguousarray(x[c // 2][own[c % 2]].T) for c in range(NCORES)]

    def halos(xT):
        out = []
        for b in range(BATCH):
            full = np.zeros((D, HALO + SEQ), np.float32)
            for p in range(2):
                for i in range(NCHUNK):
                    g0 = (2 * i + p) * CH
                    full[:, HALO + g0:HALO + g0 + CH] = xT[2 * b + p][:, i * CH:(i + 1) * CH]
            for p in range(2):
                out.append(np.ascontiguousarray(full[:, halo[p]]))
        return out

    masks = [_masks(p) for p in range(2)]
    tri = _tri()
    for layer in range(4):
        i = layer // 2
        if layer % 2 == 0:
            lw = _lay_conv(inp["mix_norm_g"][layer], inp["conv_w_in"][i], inp["conv_a_dw_w"][i], inp["conv_a_dw_b"][i],
                           inp["conv_a_ln_g"][i], inp["conv_a_ln_b"][i], inp["conv_b_dw_w"][i], inp["conv_w_out"][i])
            xh = halos(xT)
            r = _run("conv", [dict(lw, xT=xT[c], xh=xh[c]) for c in range(NCORES)])
            xT = [r[c]["xoT"] for c in range(NCORES)]
        else:
            lw = _lay_attn(inp["mix_norm_g"][layer], inp["attn_w_qkv"][i], inp["attn_q_g"][i], inp["attn_k_g"][i], inp["attn_w_o"][i])
            r = _run("qkv", [dict(xT=xT[c], g=lw["g"], wq=lw["wq"], wk=lw["wk"], wv=lw["wv"], qkg=lw["qkg"]) for c in range(NCORES)])
            ins = []
            for c in range(NCORES):
                b = c // 2
                kT_g = np.stack([r[2 * b]["kT"], r[2 * b + 1]["kT"]])
                V_g = np.stack([r[2 * b]["V"], r[2 * b + 1]["V"]])
                ins.append(dict(xT=xT[c], qT=r[c]["qT"], kT=kT_g, V=V_g, mask=masks[c % 2], tri=tri, wo=lw["wo"]))
            r = _run("attn", ins)
            xT = [r[c]["xoT"] for c in range(NCORES)]
        lw = _lay_ffn(inp["ffn_norm_g"][layer], inp["ffn_w_up"][layer], inp["ffn_dw_w"][layer], inp["ffn_dw_b"][layer], inp["ffn_w_down"][layer])
        xh = halos(xT)
        r = _run("ffn", [dict(lw, xT=xT[c], xh=xh[c]) for c in range(NCORES)])
        xT = [r[c]["xoT"] for c in range(NCORES)]
    out = np.empty((BATCH, SEQ, D), np.float32)
    for c in range(NCORES):
        out[c // 2][own[c % 2]] = xT[c].T
    return out


PAIRS = [[0, 1], [2, 3], [4, 5], [6, 7]]


def halo_phase(cx, pfx, xT, tl, tg, xh, sel_d):
    S, nc = cx.S, cx.nc
    t_sb = cx.sb(pfx + "t", [128, KD, NCHUNK, HALO], F32)
    c0 = cx.sb(pfx + "c0", [128, KD, NCHUNK, HALO], F32)
    c1 = cx.sb(pfx + "c1", [128, KD, NCHUNK, HALO], F32)
    sel = cx.sb(pfx + "sel", [128, 2], F32)
    xT_r = xT.rearrange("(k p) (c t) -> p k c t", p=128, t=CH)
    tl_r = tl.rearrange("(k p) (c h) -> p k c h", p=128, h=HALO)
    tg_r = tg.rearrange("(r k p) (c h) -> p r k c h", p=128, k=KD, h=HALO)
    xh_r = xh.rearrange("(k p) (c h) -> p k c h", p=128, h=HALO)
    S.dma("sp", "cst", lambda e: e.dma_start(out=sel[:], in_=sel_d), writes=[pfx + "sel"])
    S.dma_group("sp", "h0", [(lambda e, k=k: e.dma_start(out=t_sb[:, k], in_=xT_r[:, k, :, CH - HALO:CH]), [], [pfx + f"t{k}"])
                             for k in range(KD)])
    S.dma_group("sp", "h1", [(lambda e, k=k: e.dma_start(out=tl_r[:, k], in_=t_sb[:, k]), [pfx + f"t{k}"], [pfx + f"tl{k}"])
                             for k in range(KD)])
    S.dma("pool", "cc", lambda e: e.collective_compute("AllGather", ALU.bypass, replica_groups=PAIRS, ins=[tl], outs=[tg]),
          reads=[pfx + f"tl{k}" for k in range(KD)], writes=[pfx + "tg"], inc=1)
    S.emit("dve", lambda e: e.memset(c0[:, :, 0, :], 0.0), writes=[pfx + "c0z"])
    S.dma_group("sp", "h2", [(lambda e, k=k: e.dma_start(out=c0[:, k, 1:NCHUNK, :], in_=tg_r[:, 1, k, 0:NCHUNK - 1, :]), [pfx + "tg"], [pfx + f"c0{k}"])
                             for k in range(KD)] +
                            [(lambda e, k=k: e.dma_start(out=c1[:, k], in_=tg_r[:, 0, k]), [pfx + "tg"], [pfx + f"c1{k}"])
                             for k in range(KD)])
    items = []
    for k in range(KD):
        S.emit("dve", lambda e, k=k: e.tensor_scalar(out=c0[:, k], in0=c0[:, k], scalar1=sel[:, 0:1], scalar2=None, op0=ALU.mult),
               reads=[pfx + f"c0{k}", pfx + "c0z", pfx + "sel"], writes=[pfx + f"c0{k}"])
        S.emit("dve", lambda e, k=k: e.scalar_tensor_tensor(out=c1[:, k], in0=c1[:, k], scalar=sel[:, 1:2], in1=c0[:, k],
                                                            op0=ALU.mult, op1=ALU.add),
               reads=[pfx + f"c0{k}", pfx + f"c1{k}", pfx + "sel"], writes=[pfx + f"c1{k}"])
        items.append((lambda e, k=k: e.dma_start(out=xh_r[:, k], in_=c1[:, k]), [pfx + f"c1{k}"], [pfx + f"xh{k}"]))
    t = S.dma_group("sp", "h3", items)
    return [t]


def gather_phase(cx, pfx, kT_l, kT_g, V_l, V_g):
    S = cx.S
    toks = []
    for hp in range(HP):
        for nm, a, b in (("k", kT_l, kT_g), ("v", V_l, V_g)):
            toks.append(S.dma("pool", "cc", lambda e, hp=hp, a=a, b=b: e.collective_compute(
                "AllGather", ALU.bypass, replica_groups=PAIRS, ins=[a[hp * 128:(hp + 1) * 128, :]], outs=[b[hp * 256:(hp + 1) * 256, :]]),
                writes=[pfx + f"{nm}g{hp}"], inc=1))
    return toks[-1:]


_FUSED = []


def _fused_prog():
    if _FUSED:
        return _FUSED[0]
    nc = bass.Bass("TRN2", target_bir_lowering=False)
    S = Sched()
    with ExitStack() as outer:
        x_in = _dram(nc, "xT", [D, TOK])
        out = _dram(nc, "out", [D, TOK], F32, "ExternalOutput")
        sel_d = _dram(nc, "sel", [128, 2])
        mask_d = _dram(nc, "mask", [128, 2, 4, 1024])
        tri_d = _dram(nc, "tri", [128, 128])
        xs = [nc.dram_tensor(f"xs{i}", [D, TOK], F32).ap() for i in range(2)]
        tl = nc.dram_tensor("tl", [D, NCHUNK * HALO], F32).ap()
        tg = nc.dram_tensor("tg", [2 * D, NCHUNK * HALO], F32).ap()
        xh = nc.dram_tensor("xh", [D, NCHUNK * HALO], F32).ap()
        qT = nc.dram_tensor("qT", [D, TOK], BF16).ap()
        kT_l = nc.dram_tensor("kTl", [D, TOK], BF16).ap()
        kT_g = nc.dram_tensor("kTg", [2 * D, TOK], BF16).ap()
        V_l = nc.dram_tensor("Vl", [HP * 128, NKB * 128], BF16).ap()
        V_g = nc.dram_tensor("Vg", [2 * HP * 128, NKB * 128], BF16).ap()
        V_l5 = V_l.rearrange("(h p) (k f) -> h p k f", p=128, f=128)

        def phase(fn):
            with ExitStack() as pst:
                cx = Ctx(nc, S, pst)
                toks = fn(cx)
                S.barrier(toks)
                S.build_phase(nc, pst, outer)

        cur = x_in
        nxt = 0
        for layer in range(4):
            L = f"L{layer}."
            if layer % 2 == 0:
                phase(lambda cx: halo_phase(cx, L + "h.", cur, tl, tg, xh, sel_d))
                dst = xs[nxt]
                phase(lambda cx: conv_phase(cx, L + "c.", cur, xh, dst, _dram(nc, L + "mg", [128, KD]), _dram(nc, L + "win", [20, 128, KD, 128]),
                                            _dram(nc, L + "wout", [KD, 128, KD, 128]), _dram(nc, L + "adw", [128, MA, KA]),
                                            _dram(nc, L + "avec", [128, 3, MA]), _dram(nc, L + "bdw", [128, MA, 3])))
            else:
                phase(lambda cx: qkv_phase(cx, L + "q.", cur, _dram(nc, L + "mg", [128, KD]), _dram(nc, L + "wq", [HP, 128, KD, 128]),
                                           _dram(nc, L + "wk", [HP, 128, KD, 128]), _dram(nc, L + "wv", [2, 128, KD, 512]),
                                           _dram(nc, L + "qkg", [128, 2]), qT, kT_l, V_l5))
                phase(lambda cx: gather_phase(cx, L + "g.", kT_l, kT_g, V_l, V_g))
                dst = xs[nxt]
                phase(lambda cx: attn_phase(cx, L + "a.", cur, dst, qT, kT_g, V_g, mask_d, tri_d, _dram(nc, L + "wo", [KD, 128, KD, 128])))
            cur = dst
            nxt ^= 1
            phase(lambda cx: halo_phase(cx, L + "hf.", cur, tl, tg, xh, sel_d))
            dst = out if layer == 3 else xs[nxt]
            phase(lambda cx: ffn_phase(cx, L + "f.", cur, xh, dst, _dram(nc, L + "fg", [128, KD]), _dram(nc, L + "wup", [NJ, 128, KD, 256]),
                                       _dram(nc, L + "wdn", [KD, 128, NJ, 128]), _dram(nc, L + "dw", [128, 2 * NJ, 3]),
                                       _dram(nc, L + "db", [128, 2 * NJ])))
            cur = dst
            nxt ^= 1
    _FUSED.append((nc, S))
    return _FUSED[0]


def kernel(**inputs):
    inp = {k: np.asarray(v) for k, v in inputs.items()}
    x = inp["x"]
    own = [_own_tokens(p) for p in range(2)]
    base = {"tri": _tri()}
    for layer in range(4):
        i = layer // 2
        L = f"L{layer}."
        if layer % 2 == 0:
            lw = _lay_conv(inp["mix_norm_g"][layer], inp["conv_w_in"][i], inp["conv_a_dw_w"][i], inp["conv_a_dw_b"][i],
                           inp["conv_a_ln_g"][i], inp["conv_a_ln_b"][i], inp["conv_b_dw_w"][i], inp["conv_w_out"][i])
        else:
            lw = _lay_attn(inp["mix_norm_g"][layer], inp["attn_w_qkv"][i], inp["attn_q_g"][i], inp["attn_k_g"][i], inp["attn_w_o"][i])
        lw["mg"] = lw.pop("g")
        lf = _lay_ffn(inp["ffn_norm_g"][layer], inp["ffn_w_up"][layer], inp["ffn_dw_w"][layer], inp["ffn_dw_b"][layer], inp["ffn_w_down"][layer])
        lf["fg"] = lf.pop("g")
        for k, v in list(lw.items()) + list(lf.items()):
            base[L + k] = v
    masks = [_masks(p) for p in range(2)]
    sels = [np.ascontiguousarray(np.tile(np.array([[1.0, 0.0]], np.float32) if p == 0 else np.array([[0.0, 1.0]], np.float32), (128, 1)))
            for p in range(2)]
    in_maps = []
    for c in range(NCORES):
        d = dict(base)
        d["xT"] = np.ascontiguousarray(x[c // 2][own[c % 2]].T)
        d["mask"] = masks[c % 2]
        d["sel"] = sels[c % 2]
        in_maps.append(d)
    nc, _ = _fused_prog()
    res = run_bass_kernel_spmd(nc, in_maps, core_ids=list(range(NCORES))).results
    out = np.empty((BATCH, SEQ, D), np.float32)
    for c in range(NCORES):
        out[c // 2][own[c % 2]] = res[c]["out"].T
    return out
```

```python
import numpy as np
from contextlib import ExitStack
import concourse.bass as bass
import concourse.mybir as mybir
from concourse.bass_utils import run_bass_kernel_spmd

F32 = mybir.dt.float32
BF16 = mybir.dt.bfloat16
AF = mybir.ActivationFunctionType
ALU = mybir.AluOpType

ENGINES = ("pe", "act", "dve", "pool", "sp")
ENG_ATTR = {"pe": "tensor", "act": "scalar", "dve": "vector", "pool": "gpsimd", "sp": "sync"}


class _Op:
    __slots__ = ("fn", "waits", "inc")

    def __init__(self, fn):
        self.fn = fn
        self.waits = []
        self.inc = None


class Sched:
    SEM_ROT = 20000

    def __init__(self):
        self.ops = {e: [] for e in ENGINES}
        self.built = {e: 0 for e in ENGINES}
        self.ms = {e: [] for e in ENGINES}
        self.gen = {e: 0 for e in ENGINES}
        self.cnt = {}
        self.waited = {e: {} for e in ENGINES}
        self.last_w = {}
        self.readers = {}
        self.nwaits = 0
        self.sems = {}

    def _new_ms(self, e, seq, op):
        k = ("eng", e, self.gen[e])
        if self.cnt.get(k, 0) >= self.SEM_ROT:
            self.gen[e] += 1
            k = ("eng", e, self.gen[e])
        self.cnt[k] = self.cnt.get(k, 0) + 1
        op.inc = (k, 1)
        self.ms[e].append((seq, k, self.cnt[k]))
        return k, self.cnt[k]

    def _milestone(self, e, seq):
        lo, hi = 0, len(self.ms[e])
        while lo < hi:
            mid = (lo + hi) // 2
            if self.ms[e][mid][0] >= seq:
                hi = mid
            else:
                lo = mid + 1
        if lo < len(self.ms[e]):
            return self.ms[e][lo][1], self.ms[e][lo][2]
        last = len(self.ops[e]) - 1
        while self.ops[e][last].fn is None:
            last -= 1
        assert last >= seq and last >= self.built[e], (e, seq, last, self.built[e])
        op = self.ops[e][last]
        assert op.inc is None, f"last op on {e} already has an inc"
        return self._new_ms(e, last, op)

    def _resolve(self, tok):
        if tok[0] == "eng":
            return self._milestone(tok[1], tok[2])
        return tok[1], tok[2]

    def _deps(self, engine, reads, writes):
        toks = []
        for k in reads:
            t = self.last_w.get(k)
            if t is not None:
                toks.append(t)
        for k in writes:
            t = self.last_w.get(k)
            if t is not None and not (t[0] == "eng" and t[1] == engine == "pe"):
                toks.append(t)
            for t in self.readers.get(k, {}).values():
                if not (t[0] == "eng" and t[1] == engine):
                    toks.append(t)
        return self._waits(engine, toks)

    def _waits(self, engine, toks):
        waits = {}
        for t in toks:
            sk, v = self._resolve(t)
            if self.waited[engine].get(sk, 0) >= v:
                continue
            waits[sk] = max(waits.get(sk, 0), v)
        for sk, v in waits.items():
            self.waited[engine][sk] = v
        self.nwaits += len(waits)
        return list(waits.items())

    def _track(self, tok, rkey, reads, writes):
        for k in writes:
            self.last_w[k] = tok
            self.readers[k] = {}
        for k in reads:
            self.readers.setdefault(k, {})[rkey] = tok

    def emit(self, engine, fn, reads=(), writes=(), ms=False, banks=()):
        writes = list(writes) + list(banks)
        op = _Op(fn)
        op.waits = self._deps(engine, reads, writes)
        seq = len(self.ops[engine])
        self.ops[engine].append(op)
        if ms or engine != "pe":
            self._new_ms(engine, seq, op)
        tok = ("eng", engine, seq)
        self._track(tok, engine, reads, writes)
        return tok

    def dma(self, queue, sem, fn, reads=(), writes=(), inc=16):
        op = _Op(fn)
        op.waits = self._deps(queue, reads, writes)
        self.ops[queue].append(op)
        k = ("dma", sem.split(".", 1)[-1])
        self.cnt[k] = self.cnt.get(k, 0) + inc
        op.inc = (k, inc)
        tok = ("dma", k, self.cnt[k])
        self._track(tok, k, reads, writes)
        return tok

    def dma_group(self, queue, sem, items):
        toks = [self.dma(queue, sem, fn, reads, writes) for (fn, reads, writes) in items]
        final = toks[-1]
        for (fn, reads, writes) in items:
            for k in writes:
                self.last_w[k] = final
            for k in reads:
                self.readers.setdefault(k, {})[final[1]] = final
        return final

    def wait_all(self, engine, toks):
        op = _Op(None)
        op.waits = self._waits(engine, toks)
        self.ops[engine].append(op)

    def barrier(self, toks=()):
        toks = list(toks)
        for e in ("pe", "act", "dve", "pool"):
            last = len(self.ops[e]) - 1
            while last >= self.built[e] and (self.ops[e][last].fn is None or self.ops[e][last].inc is not None and self.ops[e][last].inc[0][0] == "dma"):
                last -= 1
            if last >= self.built[e]:
                toks.append(("eng", e, last))
        for e in ENGINES:
            self.wait_all(e, toks)

    def build_phase(self, nc, st, outer):
        block = st.enter_context(nc.Block())
        for e in ENGINES:
            deco = getattr(block, ENG_ATTR[e])

            def body(eng, e=e):
                for op in self.ops[e][self.built[e]:]:
                    for (sk, v) in op.waits:
                        eng.wait_ge(self._sem(nc, outer, sk), v)
                    if op.fn is None:
                        continue
                    ins = op.fn(eng)
                    if op.inc is not None:
                        ins.then_inc(self._sem(nc, outer, op.inc[0]), op.inc[1])
                    op.fn = None
                self.built[e] = len(self.ops[e])

            deco(body)

    def _sem(self, nc, outer, k):
        if k not in self.sems:
            self.sems[k] = outer.enter_context(nc.semaphore(f"s{len(self.sems)}"))
        return self.sems[k]

    def build(self, nc, st):
        self.build_phase(nc, st, st)


D = 1024
KD = D // 128
CH = 512
NCHUNK = 8
TOK = CH * NCHUNK
HALO = 32
DFF = 2816
NJ = DFF // 128
EPS = 1e-6


class Ctx:
    def __init__(self, nc, S, st):
        self.nc, self.S, self.st = nc, S, st
        self.n = 0

    def sb(self, name, shape, dt):
        return self.st.enter_context(self.nc.sbuf_tensor(name, list(shape), dt))

    def ps(self, name, shape=(128, 512), dt=F32):
        return self.st.enter_context(self.nc.psum_tensor(name, list(shape), dt))


def emit_rmsnorm(cx, pfx, x_k, h_k, xkeys, hkeys, ncols, sq, ones, pst, pst_bank, tmp, rstd, g_sb, gkey):
    S = cx.S
    for k in range(KD):
        S.emit("act", lambda e, k=k: e.activation(out=sq[:, k, 0:ncols], in_=x_k(k), func=AF.Square),
               reads=[xkeys[k]], writes=[pfx + f"sq{k}"])
    for k in range(KD):
        S.emit("pe", lambda e, k=k: e.matmul(pst, ones[:, :], sq[:, k, 0:ncols], start=(k == 0), stop=(k == KD - 1)),
               reads=[pfx + f"sq{k}", pfx + "ones"], banks=[pst_bank], ms=(k == KD - 1))
    S.emit("dve", lambda e: e.tensor_scalar(out=tmp[:, 0:ncols], in0=pst, scalar1=1.0 / D, scalar2=EPS,
                                            op0=ALU.mult, op1=ALU.add), banks=[pst_bank], writes=[pfx + "tmp"])
    S.emit("act", lambda e: e.activation(out=tmp[:, 0:ncols], in_=tmp[:, 0:ncols], func=AF.Sqrt),
           reads=[pfx + "tmp"], writes=[pfx + "tmp"])
    S.emit("dve", lambda e: e.reciprocal(out=rstd[:, 0:ncols], in_=tmp[:, 0:ncols]), reads=[pfx + "tmp"], writes=[pfx + "rstd"])
    for k in range(KD):
        S.emit("dve", lambda e, k=k: e.scalar_tensor_tensor(out=h_k(k), in0=x_k(k), scalar=g_sb[:, k:k + 1], in1=rstd[:, 0:ncols],
                                                            op0=ALU.mult, op1=ALU.mult),
               reads=[xkeys[k], pfx + "rstd", gkey], writes=[hkeys[k]])


def ffn_phase(cx, pfx, xT, xh, xoT, g_d, wup_d, wdn_d, dw_d, db_d, nst=NCHUNK // 2):
    S, nc = cx.S, cx.nc
    HH = 2
    W = HH + CH
    SC = 2
    x_sb = cx.sb(pfx + "x", [128, KD, SC, W], F32)
    hT = cx.sb(pfx + "hT", [128, KD, SC, W], BF16)
    sq = cx.sb(pfx + "sq", [128, KD, CH], BF16)
    rstd = cx.sb(pfx + "rstd", [128, CH], F32)
    tmp = cx.sb(pfx + "tmp", [128, CH], F32)
    ones = cx.sb(pfx + "ones", [128, 128], BF16)
    g_sb = cx.sb(pfx + "g", [128, KD], F32)
    dw_sb = cx.sb(pfx + "dw", [128, 2 * NJ, 3], F32)
    db_sb = cx.sb(pfx + "db", [128, 2 * NJ], F32)
    wup = [cx.sb(pfx + f"wup{i}", [128, KD, 256], BF16) for i in range(2)]
    wdn = [cx.sb(pfx + f"wdn{i}", [128, NJ, 128], BF16) for i in range(2)]
    gT = cx.sb(pfx + "gT", [128, NJ, SC, CH], BF16)
    gacc = [cx.sb(pfx + f"gacc{i}", [128, CH], F32) for i in range(2)]
    vacc = [cx.sb(pfx + f"vacc{i}", [128, CH], F32) for i in range(2)]
    sg = [cx.sb(pfx + f"sg{i}", [128, CH], F32) for i in range(2)]
    phs = [cx.sb(pfx + f"phs{i}", [128, 4], F32) for i in range(2)]
    xo = [cx.sb(pfx + f"xo{i}", [128, CH], F32) for i in range(2)]
    pg = [cx.ps(pfx + f"pg{i}") for i in range(2)]
    pv = [cx.ps(pfx + f"pv{i}") for i in range(2)]
    pmisc = cx.ps(pfx + "pmisc")
    pstat = cx.ps(pfx + "pstat")
    pd = [cx.ps(pfx + f"pd{i}") for i in range(2)]
    ph = [pmisc[:, 8 * i:8 * i + 4] for i in range(2)]
    pstat_h = pmisc[:, 32:32 + HH]

    xT_r = xT.rearrange("(k p) t -> p k t", p=128)
    xh_r = xh.rearrange("(k p) (c h) -> p k c h", p=128, h=HALO)
    xo_r = xoT.rearrange("(k p) t -> p k t", p=128)

    S.emit("dve", lambda e: e.memset(ones[:], 1.0), writes=[pfx + "ones"])
    S.dma_group("sp", "cst", [(lambda e: e.dma_start(out=g_sb[:], in_=g_d), [], [pfx + "n.g"]),
                              (lambda e: e.dma_start(out=dw_sb[:], in_=dw_d), [], [pfx + "dw"]),
                              (lambda e: e.dma_start(out=db_sb[:], in_=db_d), [], [pfx + "db"])])
    out_toks = []
    wu_i = 0
    wd_i = 0
    it = 0
    for sti in range(nst):
        for c in range(SC):
            gc = sti * SC + c
            S.dma_group("sp", f"lx{c}", [(lambda e, c=c, k=k, gc=gc: e.dma_start(out=x_sb[:, k, c, HH:W], in_=xT_r[:, k, gc * CH:(gc + 1) * CH]),
                                          [], [pfx + f"x{c}.{k}m"]) for k in range(KD)])
            S.dma("sp", pfx + f"lxh{c}",
                  lambda e, c=c, gc=gc: e.dma_start(out=x_sb[:, :, c, 0:HH], in_=xh_r[:, :, gc, HALO - HH:HALO]),
                  writes=[pfx + f"x{c}.h"])
        for c in range(SC):
            xk = [pfx + f"x{c}.{k}m" for k in range(KD)]
            for k in range(KD):
                S.emit("act", lambda e, k=k, c=c: e.activation(out=sq[:, k, :], in_=x_sb[:, k, c, HH:W], func=AF.Square),
                       reads=[xk[k]], writes=[pfx + f"sq{k}"])
            for k in range(KD):
                S.emit("pe", lambda e, k=k: e.matmul(pstat[:, :], ones[:, :], sq[:, k, :], start=(k == 0), stop=(k == KD - 1)),
                       reads=[pfx + f"sq{k}", pfx + "ones"], banks=[pfx + "pstat"], ms=(k == KD - 1))
            S.emit("dve", lambda e: e.tensor_scalar(out=tmp[:, :], in0=pstat[:, :], scalar1=1.0 / D, scalar2=EPS,
                                                    op0=ALU.mult, op1=ALU.add),
                   banks=[pfx + "pstat"], writes=[pfx + "tmp"])
            S.emit("act", lambda e: e.activation(out=tmp[:, :], in_=tmp[:, :], func=AF.Sqrt),
                   reads=[pfx + "tmp"], writes=[pfx + "tmp"])
            S.emit("dve", lambda e: e.reciprocal(out=rstd[:, :], in_=tmp[:, :]), reads=[pfx + "tmp"], writes=[pfx + "rstd"])
            for k in range(KD):
                S.emit("dve", lambda e, k=k, c=c: e.scalar_tensor_tensor(
                    out=hT[:, k, c, HH:W], in0=x_sb[:, k, c, HH:W], scalar=g_sb[:, k:k + 1], in1=rstd[:, :],
                    op0=ALU.mult, op1=ALU.mult),
                    reads=[xk[k], pfx + "rstd", pfx + "n.g"], writes=[pfx + f"h{c}.{k}m"])
            S.emit("act", lambda e, c=c: e.activation(out=sq[:, :, 0:HH], in_=x_sb[:, :, c, 0:HH], func=AF.Square),
                   reads=[pfx + f"x{c}.h"], writes=[pfx + f"sq{k}" for k in range(KD)])
            for k in range(KD):
                S.emit("pe", lambda e, k=k: e.matmul(pstat_h, ones[:, :], sq[:, k, 0:HH], start=(k == 0), stop=(k == KD - 1)),
                       reads=[pfx + f"sq{k}", pfx + "ones"], banks=[pfx + "pmisc"], ms=(k == KD - 1))
            S.emit("dve", lambda e: e.tensor_scalar(out=tmp[:, 0:HH], in0=pstat_h, scalar1=1.0 / D, scalar2=EPS,
                                                    op0=ALU.mult, op1=ALU.add),
                   banks=[pfx + "pmisc"], writes=[pfx + "tmp"])
            S.emit("act", lambda e: e.activation(out=tmp[:, 0:HH], in_=tmp[:, 0:HH], func=AF.Sqrt),
                   reads=[pfx + "tmp"], writes=[pfx + "tmp"])
            S.emit("dve", lambda e: e.reciprocal(out=rstd[:, 0:HH], in_=tmp[:, 0:HH]), reads=[pfx + "tmp"], writes=[pfx + "rstd"])
            for k in range(KD):
                S.emit("dve", lambda e, k=k, c=c: e.scalar_tensor_tensor(
                    out=hT[:, k, c, 0:HH], in0=x_sb[:, k, c, 0:HH], scalar=g_sb[:, k:k + 1], in1=rstd[:, 0:HH],
                    op0=ALU.mult, op1=ALU.mult),
                    reads=[pfx + f"x{c}.h", pfx + "rstd", pfx + "n.g"], writes=[pfx + f"h{c}.{k}h"])
        def load_wup(j, slot):
            S.dma("pool", pfx + f"wu{slot}", lambda e, j=j, slot=slot: e.dma_start(out=wup[slot][:], in_=wup_d[j]),
                  writes=[pfx + f"wup{slot}"])

        def load_wdn(n, slot):
            S.dma("pool", pfx + f"wd{slot}", lambda e, n=n, slot=slot: e.dma_start(out=wdn[slot][:], in_=wdn_d[n]),
                  writes=[pfx + f"wdn{slot}"])

        load_wup(0, wu_i % 2)
        for j in range(NJ):
            slot = wu_i % 2
            if j + 1 < NJ:
                load_wup(j + 1, (wu_i + 1) % 2)
            else:
                load_wdn(0, wd_i % 2)
            wu_i += 1
            for c in range(SC):
                b = it % 2
                it += 1
                hk = [pfx + f"h{c}.{k}m" for k in range(KD)]
                hh = [pfx + f"h{c}.{k}h" for k in range(KD)]
                for half, (pp, acc, jc) in enumerate(((pg[b], gacc[b], j), (pv[b], vacc[b], NJ + j))):
                    co = half * 128
                    pk = pfx + f"p{half}{b}"
                    ak = pfx + f"acc{half}{b}"
                    for k in range(KD):
                        S.emit("pe", lambda e, k=k, c=c, pp=pp, co=co, slot=slot: e.matmul(
                            pp[:, :], wup[slot][:, k, co:co + 128], hT[:, k, c, HH:W], start=(k == 0), stop=(k == KD - 1)),
                            reads=[hk[k], pfx + f"wup{slot}"], banks=[pk], ms=(k == KD - 1))
                    for k in range(KD):
                        S.emit("pe", lambda e, k=k, c=c, co=co, slot=slot, b=b, half=half: e.matmul(
                            ph[b][:, 2 * half:2 * half + 2], wup[slot][:, k, co:co + 128], hT[:, k, c, 0:HH],
                            start=(k == 0), stop=(k == KD - 1)),
                            reads=[hh[k], pfx + f"wup{slot}"], banks=[pfx + "pmisc"], ms=(k == KD - 1))
                    S.emit("act", lambda e, pp=pp, acc=acc, jc=jc: e.activation(
                        out=acc[:, :], in_=pp[:, :], func=AF.Identity, scale=dw_sb[:, jc, 2:3], bias=db_sb[:, jc:jc + 1]),
                        reads=[pfx + "dw", pfx + "db"], writes=[ak], banks=[pk])
                    S.emit("dve", lambda e, pp=pp, acc=acc, jc=jc: e.scalar_tensor_tensor(
                        out=acc[:, 1:CH], in0=pp[:, 0:CH - 1], scalar=dw_sb[:, jc, 1:2], in1=acc[:, 1:CH],
                        op0=ALU.mult, op1=ALU.add), reads=[ak, pfx + "dw"], writes=[ak], banks=[pk])
                    S.emit("dve", lambda e, pp=pp, acc=acc, jc=jc: e.scalar_tensor_tensor(
                        out=acc[:, 2:CH], in0=pp[:, 0:CH - 2], scalar=dw_sb[:, jc, 0:1], in1=acc[:, 2:CH],
                        op0=ALU.mult, op1=ALU.add), reads=[ak, pfx + "dw"], writes=[ak], banks=[pk])
                for half, (acc, jc) in enumerate(((gacc[b], j), (vacc[b], NJ + j))):
                    ak = pfx + f"acc{half}{b}"
                    S.emit("dve", lambda e, acc=acc, jc=jc, b=b, half=half: e.scalar_tensor_tensor(
                        out=acc[:, 0:2], in0=ph[b][:, 2 * half:2 * half + 2], scalar=dw_sb[:, jc, 0:1], in1=acc[:, 0:2],
                        op0=ALU.mult, op1=ALU.add), reads=[ak, pfx + "dw"], writes=[ak], banks=[pfx + "pmisc"])
                    S.emit("dve", lambda e, acc=acc, jc=jc, b=b, half=half: e.scalar_tensor_tensor(
                        out=acc[:, 0:1], in0=ph[b][:, 2 * half + 1:2 * half + 2], scalar=dw_sb[:, jc, 1:2], in1=acc[:, 0:1],
                        op0=ALU.mult, op1=ALU.add), reads=[ak, pfx + "dw"], writes=[ak], banks=[pfx + "pmisc"])
                S.emit("act", lambda e, b=b: e.activation(out=sg[b][:, :], in_=gacc[b][:, :], func=AF.Silu),
                       reads=[pfx + f"acc0{b}"], writes=[pfx + f"sg{b}"])
                S.emit("dve", lambda e, b=b, j=j, c=c: e.tensor_tensor(out=gT[:, j, c, :], in0=sg[b][:, :], in1=vacc[b][:, :],
                                                                      op=ALU.mult),
                       reads=[pfx + f"sg{b}", pfx + f"acc1{b}"], writes=[pfx + f"gT{j}.{c}"])
        for n in range(KD):
            slot = wd_i % 2
            if n + 1 < KD:
                load_wdn(n + 1, (wd_i + 1) % 2)
            wd_i += 1
            for c in range(SC):
                gc = sti * SC + c
                b = it % 2
                it += 1
                for jj in range(NJ):
                    S.emit("pe", lambda e, jj=jj, c=c, b=b, slot=slot: e.matmul(
                        pd[b][:, :], wdn[slot][:, jj, :], gT[:, jj, c, :], start=(jj == 0), stop=(jj == NJ - 1)),
                        reads=[pfx + f"gT{jj}.{c}", pfx + f"wdn{slot}"], banks=[pfx + f"pd{b}"], ms=(jj == NJ - 1))
                S.emit("dve", lambda e, b=b, n=n, c=c: e.tensor_tensor(out=xo[b][:, :], in0=pd[b][:, :], in1=x_sb[:, n, c, HH:W],
                                                                      op=ALU.add),
                       reads=[pfx + f"x{c}.{n}m"], writes=[pfx + f"xo{b}"], banks=[pfx + f"pd{b}"])
                t = S.dma("sp", pfx + f"so{b}", lambda e, b=b, n=n, gc=gc: e.dma_start(
                    out=xo_r[:, n, gc * CH:(gc + 1) * CH], in_=xo[b][:, :]), reads=[pfx + f"xo{b}"], writes=[pfx + f"xoT.{n}.{gc}"])
                out_toks.append(t)
    return out_toks


DA = 512
MA = DA // 128
KA = 31


def conv_phase(cx, pfx, xT, xh, xoT, g_d, win_d, wout_d, adw_d, avec_d, bdw_d, nch=NCHUNK):
    S, nc = cx.S, cx.nc
    H = HALO
    W = H + CH
    x_sb = cx.sb(pfx + "x", [128, KD, W], F32)
    hT = cx.sb(pfx + "hT", [128, KD, W], BF16)
    sq = cx.sb(pfx + "sq", [128, KD, CH], BF16)
    rstd = cx.sb(pfx + "rstd", [128, CH], F32)
    tmp = cx.sb(pfx + "tmp", [128, CH], F32)
    ones = cx.sb(pfx + "ones", [128, 128], BF16)
    onesm = cx.sb(pfx + "onesm", [128, 128], BF16)
    g_sb = cx.sb(pfx + "g", [128, KD], F32)
    adw = cx.sb(pfx + "adw", [128, MA, KA], F32)
    avec = cx.sb(pfx + "avec", [128, 3, MA], F32)
    bdw = cx.sb(pfx + "bdw", [128, MA, 3], F32)
    win = cx.sb(pfx + "win", [128, 20, KD, 128], BF16)
    wout = cx.sb(pfx + "wout", [128, KD, KD, 128], BF16)
    glu = cx.sb(pfx + "glu", [128, MA, W], F32)
    ca = cx.sb(pfx + "ca", [128, MA, CH], F32)
    cab = cx.sb(pfx + "cab", [128, MA, CH], BF16)
    abT = cx.sb(pfx + "abT", [128, 2 * MA, CH], BF16)
    sgm = cx.sb(pfx + "sgm", [128, W], F32)
    chb = cx.sb(pfx + "chb", [128, W], F32)
    accb = cx.sb(pfx + "accb", [128, CH], F32)
    xo = [cx.sb(pfx + f"xo{i}", [128, CH], F32) for i in range(2)]
    pA, pB, pC = cx.ps(pfx + "pA"), cx.ps(pfx + "pB"), cx.ps(pfx + "pC")
    pmisc, pstat, pvar = cx.ps(pfx + "pmisc"), cx.ps(pfx + "pstat"), cx.ps(pfx + "pvar")
    po = [cx.ps(pfx + f"po{i}") for i in range(2)]
    BK = lambda n: pfx + "B." + n

    xT_r = xT.rearrange("(k p) t -> p k t", p=128)
    xh_r = xh.rearrange("(k p) (c h) -> p k c h", p=128, h=HALO)
    xo_r = xoT.rearrange("(k p) t -> p k t", p=128)

    S.emit("dve", lambda e: e.memset(ones[:], 1.0), writes=[pfx + "ones"])
    S.emit("dve", lambda e: e.memset(onesm[:], 1.0 / DA), writes=[pfx + "onesm"])
    S.dma_group("sp", "cst", [(lambda e: e.dma_start(out=g_sb[:], in_=g_d), [], [pfx + "g"]),
                              (lambda e: e.dma_start(out=adw[:], in_=adw_d), [], [pfx + "adw"]),
                              (lambda e: e.dma_start(out=avec[:], in_=avec_d), [], [pfx + "avec"]),
                              (lambda e: e.dma_start(out=bdw[:], in_=bdw_d), [], [pfx + "bdw"])])
    S.dma_group("pool", "wA", [(lambda e, m=m: e.dma_start(out=win[:, m], in_=win_d[m]), [], [pfx + f"win{m}"]) for m in range(20)])
    S.dma_group("pool", "wB", [(lambda e, n=n: e.dma_start(out=wout[:, n], in_=wout_d[n]), [], [pfx + f"wout{n}"]) for n in range(KD)])

    out_toks = []
    it = 0
    for c in range(nch):
        xk = [pfx + f"x{k}" for k in range(KD)]
        hk = [pfx + f"h{k}" for k in range(KD)]
        S.dma_group("sp", "lx", [(lambda e, k=k, c=c: e.dma_start(out=x_sb[:, k, H:W], in_=xT_r[:, k, c * CH:(c + 1) * CH]), [], [xk[k]])
                                 for k in range(KD)])
        S.dma("sp", pfx + "lxh", lambda e, c=c: e.dma_start(out=x_sb[:, :, 0:H], in_=xh_r[:, :, c, :]), writes=[pfx + "xh"])
        emit_rmsnorm(cx, pfx, lambda k: x_sb[:, k, H:W], lambda k: hT[:, k, H:W], xk, hk, CH, sq, ones,
                     pstat[:, :], BK("pstat"), tmp, rstd, g_sb, pfx + "g")
        emit_rmsnorm(cx, pfx, lambda k: x_sb[:, k, 0:H], lambda k: hT[:, k, 0:H], [pfx + "xh"] * KD,
                     [pfx + f"hh{k}" for k in range(KD)], H, sq, ones, pmisc[:, 256:256 + H], BK("pmisc"), tmp, rstd, g_sb, pfx + "g")
        hh = [pfx + f"hh{k}" for k in range(KD)]

        def proj(m, pmain, bank, hcol=None):
            for k in range(KD):
                S.emit("pe", lambda e, k=k, m=m: e.matmul(pmain[:, :], win[:, m, k, :], hT[:, k, H:W], start=(k == 0), stop=(k == KD - 1)),
                       reads=[hk[k], pfx + f"win{m}"], banks=[bank], ms=(k == KD - 1))
            if hcol is not None:
                for k in range(KD):
                    S.emit("pe", lambda e, k=k, m=m: e.matmul(pmisc[:, hcol:hcol + H], win[:, m, k, :], hT[:, k, 0:H],
                                                              start=(k == 0), stop=(k == KD - 1)),
                           reads=[hh[k], pfx + f"win{m}"], banks=[BK("pmisc")], ms=(k == KD - 1))

        for m in range(MA):
            proj(m, pA, BK("pA"), 0)
            proj(MA + m, pB, BK("pB"), H)
            S.emit("act", lambda e: e.activation(out=sgm[:, H:W], in_=pB[:, :], func=AF.Sigmoid), banks=[BK("pB")], writes=[pfx + "sgm"])
            S.emit("act", lambda e: e.activation(out=sgm[:, 0:H], in_=pmisc[:, H:2 * H], func=AF.Sigmoid), banks=[BK("pmisc")], writes=[pfx + "sgm"])
            S.emit("dve", lambda e, m=m: e.tensor_tensor(out=glu[:, m, H:W], in0=pA[:, :], in1=sgm[:, H:W], op=ALU.mult),
                   reads=[pfx + "sgm"], banks=[BK("pA")], writes=[pfx + f"glu{m}"])
            S.emit("dve", lambda e, m=m: e.tensor_tensor(out=glu[:, m, 0:H], in0=pmisc[:, 0:H], in1=sgm[:, 0:H], op=ALU.mult),
                   reads=[pfx + "sgm"], banks=[BK("pmisc")], writes=[pfx + f"glu{m}"])
            S.emit("act", lambda e, m=m: e.activation(out=ca[:, m, :], in_=glu[:, m, H:W], func=AF.Identity,
                                                      scale=adw[:, m, KA - 1:KA], bias=avec[:, 0, m:m + 1]),
                   reads=[pfx + f"glu{m}", pfx + "adw", pfx + "avec"], writes=[pfx + f"ca{m}"])
            for k in range(KA - 1):
                S.emit("dve", lambda e, m=m, k=k: e.scalar_tensor_tensor(
                    out=ca[:, m, :], in0=glu[:, m, 2 + k:2 + k + CH], scalar=adw[:, m, k:k + 1], in1=ca[:, m, :],
                    op0=ALU.mult, op1=ALU.add), reads=[pfx + f"glu{m}", pfx + f"ca{m}", pfx + "adw"], writes=[pfx + f"ca{m}"])
            S.emit("act", lambda e, m=m: e.activation(out=cab[:, m, :], in_=ca[:, m, :], func=AF.Identity),
                   reads=[pfx + f"ca{m}"], writes=[pfx + f"cab{m}"])
        for m in range(MA):
            S.emit("pe", lambda e, m=m: e.matmul(pstat[:, :], onesm[:, :], cab[:, m, :], start=(m == 0), stop=(m == MA - 1)),
                   reads=[pfx + f"cab{m}", pfx + "onesm"], banks=[BK("pstat")], ms=(m == MA - 1))
        for m in range(MA):
            S.emit("dve", lambda e, m=m: e.tensor_tensor(out=ca[:, m, :], in0=ca[:, m, :], in1=pstat[:, :], op=ALU.subtract),
                   reads=[pfx + f"ca{m}"], banks=[BK("pstat")], writes=[pfx + f"ca{m}"])
            S.emit("act", lambda e, m=m: e.activation(out=cab[:, m, :], in_=ca[:, m, :], func=AF.Square),
                   reads=[pfx + f"ca{m}"], writes=[pfx + f"cab{m}"])
        for m in range(MA):
            S.emit("pe", lambda e, m=m: e.matmul(pvar[:, :], onesm[:, :], cab[:, m, :], start=(m == 0), stop=(m == MA - 1)),
                   reads=[pfx + f"cab{m}", pfx + "onesm"], banks=[BK("pvar")], ms=(m == MA - 1))
        S.emit("dve", lambda e: e.tensor_scalar(out=tmp[:, :], in0=pvar[:, :], scalar1=EPS, scalar2=None, op0=ALU.add),
               banks=[BK("pvar")], writes=[pfx + "tmp"])
        S.emit("act", lambda e: e.activation(out=tmp[:, :], in_=tmp[:, :], func=AF.Sqrt), reads=[pfx + "tmp"], writes=[pfx + "tmp"])
        S.emit("dve", lambda e: e.reciprocal(out=rstd[:, :], in_=tmp[:, :]), reads=[pfx + "tmp"], writes=[pfx + "rstd"])
        for m in range(MA):
            S.emit("dve", lambda e, m=m: e.scalar_tensor_tensor(out=ca[:, m, :], in0=ca[:, m, :], scalar=avec[:, 1, m:m + 1], in1=rstd[:, :],
                                                                op0=ALU.mult, op1=ALU.mult),
                   reads=[pfx + f"ca{m}", pfx + "rstd", pfx + "avec"], writes=[pfx + f"ca{m}"])
            S.emit("act", lambda e, m=m: e.activation(out=abT[:, m, :], in_=ca[:, m, :], func=AF.Silu, bias=avec[:, 2, m:m + 1]),
                   reads=[pfx + f"ca{m}", pfx + "avec"], writes=[pfx + f"ab{m}"])
        for m in range(MA):
            proj(3 * MA + m, pA, BK("pA"), 2 * H)
            proj(4 * MA + m, pB, BK("pB"), 3 * H)
            proj(2 * MA + m, pC, BK("pC"), None)
            S.emit("act", lambda e: e.activation(out=sgm[:, H:W], in_=pA[:, :], func=AF.Identity), banks=[BK("pA")], writes=[pfx + "sgm"])
            S.emit("act", lambda e: e.activation(out=sgm[:, 0:H], in_=pmisc[:, 2 * H:3 * H], func=AF.Identity), banks=[BK("pmisc")], writes=[pfx + "sgm"])
            S.emit("dve", lambda e: e.tensor_tensor(out=chb[:, H:W], in0=pB[:, :], in1=sgm[:, H:W], op=ALU.mult),
                   reads=[pfx + "sgm"], banks=[BK("pB")], writes=[pfx + "chb"])
            S.emit("dve", lambda e: e.tensor_tensor(out=chb[:, 0:H], in0=pmisc[:, 3 * H:4 * H], in1=sgm[:, 0:H], op=ALU.mult),
                   reads=[pfx + "sgm"], banks=[BK("pmisc")], writes=[pfx + "chb"])
            S.emit("act", lambda e, m=m: e.activation(out=accb[:, :], in_=chb[:, H:W], func=AF.Identity, scale=bdw[:, m, 2:3]),
                   reads=[pfx + "chb", pfx + "bdw"], writes=[pfx + "accb"])
            for k in range(2):
                S.emit("dve", lambda e, m=m, k=k: e.scalar_tensor_tensor(
                    out=accb[:, :], in0=chb[:, H - 2 + k:H - 2 + k + CH], scalar=bdw[:, m, k:k + 1], in1=accb[:, :],
                    op0=ALU.mult, op1=ALU.add), reads=[pfx + "chb", pfx + "accb", pfx + "bdw"], writes=[pfx + "accb"])
            S.emit("dve", lambda e, m=m: e.tensor_tensor(out=abT[:, MA + m, :], in0=pC[:, :], in1=accb[:, :], op=ALU.mult),
                   reads=[pfx + "accb"], banks=[BK("pC")], writes=[pfx + f"ab{MA + m}"])
        for n in range(KD):
            b = it % 2
            it += 1
            for k in range(2 * MA):
                S.emit("pe", lambda e, k=k, n=n, b=b: e.matmul(po[b][:, :], wout[:, n, k, :], abT[:, k, :], start=(k == 0), stop=(k == 2 * MA - 1)),
                       reads=[pfx + f"ab{k}", pfx + f"wout{n}"], banks=[BK(f"po{b}")], ms=(k == 2 * MA - 1))
            S.emit("dve", lambda e, b=b, n=n: e.tensor_tensor(out=xo[b][:, :], in0=po[b][:, :], in1=x_sb[:, n, H:W], op=ALU.add),
                   reads=[xk[n]], writes=[pfx + f"xo{b}"], banks=[BK(f"po{b}")])
            t = S.dma("sp", pfx + f"so{b}", lambda e, b=b, n=n, c=c: e.dma_start(out=xo_r[:, n, c * CH:(c + 1) * CH], in_=xo[b][:, :]),
                      reads=[pfx + f"xo{b}"], writes=[pfx + f"xoT.{n}.{c}"])
            out_toks.append(t)
    return out_toks


NH = 16
HP = NH // 2
NKB = TOK // 128


def qkv_phase(cx, pfx, xT, g_d, wq_d, wk_d, wv_d, qkg_d, qT_o, kT_o, V_o, nch=NCHUNK):
    S, nc = cx.S, cx.nc
    x_sb = cx.sb(pfx + "x", [128, KD, CH], F32)
    hT = cx.sb(pfx + "hT", [128, KD, CH], BF16)
    sq = cx.sb(pfx + "sq", [128, KD, CH], BF16)
    rstd = cx.sb(pfx + "rstd", [128, CH], F32)
    tmp = cx.sb(pfx + "tmp", [128, CH], F32)
    ones = cx.sb(pfx + "ones", [128, 128], BF16)
    bd = cx.sb(pfx + "bd", [128, 128], BF16)
    g_sb = cx.sb(pfx + "g", [128, KD], F32)
    qkg = cx.sb(pfx + "qkg", [128, 2], F32)
    wq = cx.sb(pfx + "wq", [128, HP, KD, 128], BF16)
    wk = cx.sb(pfx + "wk", [128, HP, KD, 128], BF16)
    wv = cx.sb(pfx + "wv", [128, 2, KD, 512], BF16)
    qf = [cx.sb(pfx + f"qf{i}", [128, CH], F32) for i in range(2)]
    sqq = [cx.sb(pfx + f"sqq{i}", [128, CH], BF16) for i in range(2)]
    rs = [cx.sb(pfx + f"rs{i}", [128, CH], F32) for i in range(2)]
    qn = [cx.sb(pfx + f"qn{i}", [128, CH], BF16) for i in range(2)]
    vt = [cx.sb(pfx + f"vt{i}", [128, 512], BF16) for i in range(2)]
    pq = [cx.ps(pfx + f"pq{i}") for i in range(2)]
    pms = [cx.ps(pfx + f"pms{i}") for i in range(2)]
    pvv = [cx.ps(pfx + f"pvv{i}") for i in range(2)]
    pstat = cx.ps(pfx + "pstat")
    BK = lambda n: pfx + "B." + n
    xT_r = xT.rearrange("(k p) t -> p k t", p=128)
    qT_r = qT_o.rearrange("(k p) t -> p k t", p=128)
    kT_r = kT_o.rearrange("(k p) t -> p k t", p=128)

    S.emit("dve", lambda e: e.memset(ones[:], 1.0), writes=[pfx + "ones"])
    S.emit("dve", lambda e: e.memset(bd[:], 0.0), writes=[pfx + "bd"])
    S.emit("dve", lambda e: e.memset(bd[0:64, 0:64], 1.0 / 64), writes=[pfx + "bd"])
    S.emit("dve", lambda e: e.memset(bd[64:128, 64:128], 1.0 / 64), writes=[pfx + "bd"])
    S.dma_group("sp", "cst", [(lambda e: e.dma_start(out=g_sb[:], in_=g_d), [], [pfx + "g"]),
                              (lambda e: e.dma_start(out=qkg[:], in_=qkg_d), [], [pfx + "qkg"])])
    S.emit("dve", lambda e: e.tensor_scalar(out=qkg[:, 0:1], in0=qkg[:, 0:1], scalar1=0.125, scalar2=None, op0=ALU.mult),
           reads=[pfx + "qkg"], writes=[pfx + "qkg"])
    S.dma_group("pool", "wA", [(lambda e, m=m: e.dma_start(out=wq[:, m], in_=wq_d[m]), [], [pfx + f"wq{m}"]) for m in range(HP)])
    S.dma_group("pool", "wB", [(lambda e, m=m: e.dma_start(out=wk[:, m], in_=wk_d[m]), [], [pfx + f"wk{m}"]) for m in range(HP)])
    S.dma_group("pool", "wC", [(lambda e, hf=hf: e.dma_start(out=wv[:, hf], in_=wv_d[hf]), [], [pfx + f"wv{hf}"]) for hf in range(2)])
    out_toks = []
    it = 0
    for c in range(nch):
        xk = [pfx + f"x{k}" for k in range(KD)]
        hk = [pfx + f"h{k}" for k in range(KD)]
        S.dma_group("sp", "lx", [(lambda e, k=k, c=c: e.dma_start(out=x_sb[:, k, :], in_=xT_r[:, k, c * CH:(c + 1) * CH]), [], [xk[k]])
                                 for k in range(KD)])
        emit_rmsnorm(cx, pfx, lambda k: x_sb[:, k, :], lambda k: hT[:, k, :], xk, hk, CH, sq, ones,
                     pstat[:, :], BK("pstat"), tmp, rstd, g_sb, pfx + "g")
        for which, (w_sb, wkey, gcol, o_r) in enumerate(((wq, "wq", 0, qT_r), (wk, "wk", 1, kT_r))):
            for m in range(HP):
                b = it % 2
                it += 1
                for k in range(KD):
                    S.emit("pe", lambda e, k=k, m=m, b=b, w_sb=w_sb: e.matmul(pq[b][:, :], w_sb[:, m, k, :], hT[:, k, :],
                                                                             start=(k == 0), stop=(k == KD - 1)),
                           reads=[hk[k], pfx + f"{wkey}{m}"], banks=[BK(f"pq{b}")], ms=(k == KD - 1))
                S.emit("act", lambda e, b=b: e.activation(out=sqq[b][:, :], in_=pq[b][:, :], func=AF.Square),
                       banks=[BK(f"pq{b}")], writes=[pfx + f"sqq{b}"])
                S.emit("act", lambda e, b=b: e.activation(out=qf[b][:, :], in_=pq[b][:, :], func=AF.Identity),
                       banks=[BK(f"pq{b}")], writes=[pfx + f"qf{b}"])
                S.emit("pe", lambda e, b=b: e.matmul(pms[b][:, :], bd[:, :], sqq[b][:, :], start=True, stop=True),
                       reads=[pfx + f"sqq{b}", pfx + "bd"], banks=[BK(f"pms{b}")], ms=True)
                S.emit("dve", lambda e, b=b: e.tensor_scalar(out=rs[b][:, :], in0=pms[b][:, :], scalar1=EPS, scalar2=None, op0=ALU.add),
                       banks=[BK(f"pms{b}")], writes=[pfx + f"rs{b}"])
                S.emit("act", lambda e, b=b: e.activation(out=rs[b][:, :], in_=rs[b][:, :], func=AF.Sqrt),
                       reads=[pfx + f"rs{b}"], writes=[pfx + f"rs{b}"])
                S.emit("dve", lambda e, b=b: e.reciprocal(out=rs[b][:, :], in_=rs[b][:, :]), reads=[pfx + f"rs{b}"], writes=[pfx + f"rs{b}"])
                S.emit("dve", lambda e, b=b, gcol=gcol: e.scalar_tensor_tensor(out=qn[b][:, :], in0=qf[b][:, :], scalar=qkg[:, gcol:gcol + 1],
                                                                               in1=rs[b][:, :], op0=ALU.mult, op1=ALU.mult),
                       reads=[pfx + f"qf{b}", pfx + f"rs{b}", pfx + "qkg"], writes=[pfx + f"qn{b}"])
                t = S.dma("sp", pfx + f"sq{b}", lambda e, b=b, m=m, c=c, o_r=o_r: e.dma_start(out=o_r[:, m, c * CH:(c + 1) * CH], in_=qn[b][:, :]),
                          reads=[pfx + f"qn{b}"], writes=[pfx + f"o{which}.{m}.{c}"])
                out_toks.append(t)
        for tt in range(CH // 128):
            kb = c * (CH // 128) + tt
            for hf in range(2):
                b = it % 2
                it += 1
                for k in range(KD):
                    S.emit("pe", lambda e, k=k, tt=tt, hf=hf, b=b: e.matmul(pvv[b][:, :], hT[:, k, tt * 128:(tt + 1) * 128], wv[:, hf, k, :],
                                                                          start=(k == 0), stop=(k == KD - 1)),
                           reads=[hk[k], pfx + f"wv{hf}"], banks=[BK(f"pvv{b}")], ms=(k == KD - 1))
                S.emit("act", lambda e, b=b: e.activation(out=vt[b][:, :], in_=pvv[b][:, :], func=AF.Identity),
                       banks=[BK(f"pvv{b}")], writes=[pfx + f"vt{b}"])
                t = S.dma("sp", pfx + f"sv{b}", lambda e, b=b, hf=hf, kb=kb: e.dma_start(
                    out=V_o[hf * 4:(hf + 1) * 4, :, kb, :].rearrange("h p f -> p h f"),
                    in_=vt[b][:, :].rearrange("p (h f) -> p h f", f=128)),
                    reads=[pfx + f"vt{b}"], writes=[pfx + f"oV.{hf}.{kb}"])
                out_toks.append(t)
    return out_toks


def attn_phase(cx, pfx, xT, xoT, qT_i, kT_g, V_g, mask_d, tri_d, wo_d, nhp=HP, nq=NCHUNK):
    S, nc = cx.S, cx.nc
    kT_sb = cx.sb(pfx + "kT", [128, 2, TOK], BF16)
    V_sb = cx.sb(pfx + "V", [128, 2, NKB, 128], BF16)
    q_sb = cx.sb(pfx + "q", [128, TOK], BF16)
    oT = cx.sb(pfx + "oT", [128, HP, TOK], BF16)
    mask = cx.sb(pfx + "mask", [128, 2, 4, 1024], BF16)
    ntri = cx.sb(pfx + "ntri", [128, 128], BF16)
    nones = cx.sb(pfx + "nones", [128, 128], BF16)
    one1 = cx.sb(pfx + "one1", [128, 1], F32)
    E = [cx.sb(pfx + f"E{i}", [128, 1024], F32) for i in range(2)]
    L = [cx.sb(pfx + f"L{i}", [128, 1024], BF16) for i in range(2)]
    Wt = [cx.sb(pfx + f"W{i}", [128, 1024], BF16) for i in range(2)]
    Ls = cx.sb(pfx + "Ls", [128, 1024], F32)
    Lsb = [cx.sb(pfx + f"Lsb{i}", [128, 1024], BF16) for i in range(2)]
    wo = cx.sb(pfx + "wo", [128, KD, KD, 128], BF16)
    xr = [cx.sb(pfx + f"xr{i}", [128, CH], F32) for i in range(2)]
    xo = [cx.sb(pfx + f"xo{i}", [128, CH], F32) for i in range(2)]
    pz = [cx.ps(pfx + f"pz{i}", (128, 1024)) for i in range(2)]
    pp2 = cx.ps(pfx + "pp2", (128, 1024))
    po = cx.ps(pfx + "po")
    pf = cx.ps(pfx + "pf")
    BK = lambda n: pfx + "B." + n
    xT_r = xT.rearrange("(k p) t -> p k t", p=128)
    xo_r = xoT.rearrange("(k p) t -> p k t", p=128)
    qT_r = qT_i.rearrange("(k p) t -> p k t", p=128)
    kT_r = kT_g.rearrange("(h r p) t -> p h r t", r=2, p=128)
    V_r = V_g.rearrange("(h r p) (k f) -> h p r k f", r=2, p=128, f=128)

    S.dma("pool", pfx + "ct", lambda e: e.dma_start(out=ntri[:], in_=tri_d), writes=[pfx + "ntri"])
    S.emit("dve", lambda e: e.memset(nones[:], -1.0), writes=[pfx + "nones"])
    S.emit("dve", lambda e: e.memset(one1[:], 1.0), writes=[pfx + "one1"])
    S.dma("pool", pfx + "cm", lambda e: e.dma_start(out=mask[:], in_=mask_d), writes=[pfx + "mask"])
    S.dma_group("pool", "wA", [(lambda e, n=n: e.dma_start(out=wo[:, n], in_=wo_d[n]), [], [pfx + f"wo{n}"]) for n in range(KD)])

    step = 0
    for hp in range(nhp):
        S.dma("sp", pfx + "lk", lambda e, hp=hp: e.dma_start(out=kT_sb[:, :, :], in_=kT_r[:, hp, :, :]), writes=[pfx + "kT"])
        S.dma("sp", pfx + "lv", lambda e, hp=hp: e.dma_start(out=V_sb[:, :, :, :], in_=V_r[hp]),
              writes=[pfx + "V"])
        S.dma("sp", pfx + "lq", lambda e, hp=hp: e.dma_start(out=q_sb[:, :], in_=qT_r[:, hp, :]), writes=[pfx + "q"])
        for i in range(nq):
            blocks = []
            for j in range(2 * i + 1, -1, -1):
                for b4 in range(3, -1, -1):
                    mk = 1 if j == 2 * i + 1 else (0 if j == 2 * i else None)
                    blocks.append((j % 2, (j // 2) * 4 + b4, mk, b4))
            nb = len(blocks)

            def emit_z(s):
                r, kb, mk, b4 = blocks[s]
                zb = (step + s) % 2
                for hd in range(2):
                    S.emit("pe", lambda e, hd=hd, r=r, kb=kb, zb=zb, i=i: e.matmul(
                        pz[zb][:, hd * 512:(hd + 1) * 512], kT_sb[hd * 64:(hd + 1) * 64, r, kb * 128:(kb + 1) * 128],
                        q_sb[hd * 64:(hd + 1) * 64, i * CH:(i + 1) * CH], start=True, stop=True),
                        reads=[pfx + "kT", pfx + "q"], banks=[BK(f"pz{zb}")], ms=(hd == 1))

            def emit_el(s):
                r, kb, mk, b4 = blocks[s]
                zb = (step + s) % 2
                S.emit("act", lambda e, zb=zb: e.activation(out=E[zb][:, :], in_=pz[zb][:, :], func=AF.Exp),
                       banks=[BK(f"pz{zb}")], writes=[pfx + f"E{zb}"])
                S.emit("act", lambda e, zb=zb: e.activation(out=L[zb][:, :], in_=E[zb][:, :], func=AF.Ln, bias=one1[:, 0:1]),
                       reads=[pfx + f"E{zb}", pfx + "one1"], writes=[pfx + f"L{zb}"])
                if mk is not None:
                    S.emit("dve", lambda e, zb=zb, mk=mk, b4=b4: e.tensor_tensor(out=L[zb][:, :], in0=L[zb][:, :], in1=mask[:, mk, b4, :], op=ALU.mult),
                           reads=[pfx + f"L{zb}", pfx + "mask"], writes=[pfx + f"L{zb}"])

            def emit_p2(s):
                r, kb, mk, b4 = blocks[s]
                zb = (step + s) % 2
                sls = [slice(hd * 512, (hd + 1) * 512) for hd in range(2)]
                for hd in range(2):
                    S.emit("pe", lambda e, hd=hd, r=r, kb=kb, sl=sls[hd], i=i: e.matmul(
                        pp2[:, sl], kT_sb[hd * 64:(hd + 1) * 64, r, kb * 128:(kb + 1) * 128],
                        q_sb[hd * 64:(hd + 1) * 64, i * CH:(i + 1) * CH], start=True, stop=False),
                        reads=[pfx + "kT", pfx + "q"], banks=[BK("pp2")])
                for hd in range(2):
                    S.emit("pe", lambda e, sl=sls[hd], zb=zb, last=(s == 0): e.matmul(pp2[:, sl], ntri[:, :], L[zb][:, sl], start=False, stop=last),
                           reads=[pfx + f"L{zb}", pfx + "ntri"], banks=[BK("pp2")], ms=(s == 0 and hd == 1))
                if s > 0:
                    lb = (step + s - 1) % 2
                    for hd in range(2):
                        S.emit("pe", lambda e, sl=sls[hd], lb=lb: e.matmul(pp2[:, sl], nones[:, :], Lsb[lb][:, sl], start=False, stop=True),
                               reads=[pfx + f"Lsb{lb}", pfx + "nones"], banks=[BK("pp2")], ms=(hd == 1))

            def emit_w(s):
                r, kb, mk, b4 = blocks[s]
                zb = (step + s) % 2
                S.emit("act", lambda e, zb=zb: e.activation(out=Wt[zb][:, :], in_=pp2[:, :], func=AF.Exp),
                       banks=[BK("pp2")], writes=[pfx + f"W{zb}"])
                if mk is not None:
                    S.emit("dve", lambda e, zb=zb, mk=mk, b4=b4: e.tensor_tensor(out=Wt[zb][:, :], in0=Wt[zb][:, :], in1=mask[:, mk, b4, :], op=ALU.mult),
                           reads=[pfx + f"W{zb}", pfx + "mask"], writes=[pfx + f"W{zb}"])
                if s + 1 < nb:
                    if s == 0:
                        S.emit("dve", lambda e, zb=zb: e.tensor_copy(out=Ls[:, :], in_=L[zb][:, :]), reads=[pfx + f"L{zb}"], writes=[pfx + "Ls"])
                    else:
                        S.emit("dve", lambda e, zb=zb: e.tensor_tensor(out=Ls[:, :], in0=Ls[:, :], in1=L[zb][:, :], op=ALU.add),
                               reads=[pfx + f"L{zb}", pfx + "Ls"], writes=[pfx + "Ls"])
                    S.emit("dve", lambda e, zb=zb: e.tensor_copy(out=Lsb[zb][:, :], in_=Ls[:, :]), reads=[pfx + "Ls"], writes=[pfx + f"Lsb{zb}"])

            def emit_pv(s):
                r, kb, mk, b4 = blocks[s]
                zb = (step + s) % 2
                for hd in range(2):
                    S.emit("pe", lambda e, hd=hd, r=r, kb=kb, zb=zb, st_=(s == 0), sp_=(s == nb - 1): e.matmul(
                        po[hd * 64:(hd + 1) * 64, :], V_sb[:, r, kb, hd * 64:(hd + 1) * 64], Wt[zb][:, hd * 512:(hd + 1) * 512],
                        start=st_, stop=sp_),
                        reads=[pfx + "V", pfx + f"W{zb}"], banks=[BK("po")], ms=(s == nb - 1 and hd == 1))

            emit_z(0)
            emit_el(0)
            for s in range(nb):
                if s + 1 < nb:
                    emit_z(s + 1)
                    emit_el(s + 1)
                emit_p2(s)
                emit_w(s)
                emit_pv(s)
            step += nb
            S.emit("act", lambda e, hp=hp, i=i: e.activation(out=oT[:, hp, i * CH:(i + 1) * CH], in_=po[:, :], func=AF.Identity),
                   banks=[BK("po")], writes=[pfx + f"oT{hp}.{i}"])
    out_toks = []
    it = 0
    for i in range(nq):
        for n in range(KD):
            b = it % 2
            it += 1
            S.dma("sp", pfx + f"lx{b}", lambda e, b=b, n=n, i=i: e.dma_start(out=xr[b][:, :], in_=xT_r[:, n, i * CH:(i + 1) * CH]),
                  writes=[pfx + f"xr{b}"])
            for k in range(nhp):
                S.emit("pe", lambda e, k=k, n=n, i=i: e.matmul(pf[:, :], wo[:, n, k, :], oT[:, k, i * CH:(i + 1) * CH],
                                                              start=(k == 0), stop=(k == nhp - 1)),
                       reads=[pfx + f"oT{k}.{i}", pfx + f"wo{n}"], banks=[BK("pf")], ms=(k == nhp - 1))
            S.emit("dve", lambda e, b=b: e.tensor_tensor(out=xo[b][:, :], in0=pf[:, :], in1=xr[b][:, :], op=ALU.add),
                   reads=[pfx + f"xr{b}"], writes=[pfx + f"xo{b}"], banks=[BK("pf")])
            t = S.dma("sp", pfx + f"so{b}", lambda e, b=b, n=n, i=i: e.dma_start(out=xo_r[:, n, i * CH:(i + 1) * CH], in_=xo[b][:, :]),
                      reads=[pfx + f"xo{b}"], writes=[pfx + f"xoT.{n}.{i}"])
            out_toks.append(t)
    return out_toks


NCORES = 8
SEQ = 8192
BATCH = 4


def _own_tokens(p):
    return np.concatenate([np.arange((2 * i + p) * CH, (2 * i + p + 1) * CH) for i in range(NCHUNK)])


def _halo_tokens(p):
    return np.concatenate([np.arange((2 * i + p) * CH, (2 * i + p) * CH + HALO) for i in range(NCHUNK)])


def _lay_vec(v, n):
    return np.ascontiguousarray(v.reshape(n, 128).T)


def _lay_w(w, kin, nout):
    return np.ascontiguousarray(w.reshape(kin, 128, nout, 128).transpose(2, 1, 0, 3))


def _lay_ffn(g, w_up, dw_w, dw_b, w_down):
    wu = w_up.reshape(KD, 128, 2, NJ, 128)
    return dict(g=_lay_vec(g, KD),
                wup=np.ascontiguousarray(wu.transpose(3, 1, 0, 2, 4).reshape(NJ, 128, KD, 256)),
                wdn=_lay_w(w_down, NJ, KD),
                dw=np.ascontiguousarray(dw_w.reshape(3, 2 * NJ, 128).transpose(2, 1, 0)),
                db=_lay_vec(dw_b, 2 * NJ))


def _lay_conv(g, w_in, a_dw_w, a_dw_b, a_ln_g, a_ln_b, b_dw_w, w_out):
    return dict(g=_lay_vec(g, KD), win=_lay_w(w_in, KD, 20), wout=_lay_w(w_out, KD, KD),
                adw=np.ascontiguousarray(a_dw_w.reshape(KA, MA, 128).transpose(2, 1, 0)),
                avec=np.ascontiguousarray(np.stack([a_dw_b, a_ln_g, a_ln_b]).reshape(3, MA, 128).transpose(2, 0, 1)),
                bdw=np.ascontiguousarray(b_dw_w.reshape(3, MA, 128).transpose(2, 1, 0)))


def _lay_attn(g, w_qkv, q_g, k_g, w_o):
    return dict(g=_lay_vec(g, KD), wq=_lay_w(w_qkv[:, :D], KD, HP), wk=_lay_w(w_qkv[:, D:2 * D], KD, HP),
                wv=np.ascontiguousarray(w_qkv[:, 2 * D:].reshape(KD, 128, 2, 512).transpose(2, 1, 0, 3)),
                qkg=np.ascontiguousarray(np.stack([np.tile(q_g, 2), np.tile(k_g, 2)], 1)),
                wo=_lay_w(w_o, KD, KD))


def _masks(p):
    ks = np.arange(CH)[:, None]
    tq = np.arange(CH)[None, :]
    diag = (ks < tq).astype(np.float32)
    A = diag if p == 0 else np.ones((CH, CH), np.float32)
    B = np.zeros((CH, CH), np.float32) if p == 0 else diag
    m = np.stack([A, B]).reshape(2, 4, 128, CH).transpose(2, 0, 1, 3)
    return np.ascontiguousarray(np.concatenate([m, m], -1))


def _tri():
    j = np.arange(128)[:, None]
    s = np.arange(128)[None, :]
    return -(j >= s).astype(np.float32)


_PROGS = {}


def _dram(nc, name, shape, dt=F32, kind="ExternalInput"):
    return nc.dram_tensor(name, list(shape), dt, kind=kind).ap()


def _prog(kind):
    if kind in _PROGS:
        return _PROGS[kind]
    nc = bass.Bass("TRN2", target_bir_lowering=False)
    S = Sched()
    with ExitStack() as st:
        cx = Ctx(nc, S, st)
        if kind == "conv":
            a = [_dram(nc, "xT", [D, TOK]), _dram(nc, "xh", [D, NCHUNK * HALO])]
            xo = _dram(nc, "xoT", [D, TOK], F32, "ExternalOutput")
            toks = conv_phase(cx, "c.", a[0], a[1], xo, _dram(nc, "g", [128, KD]), _dram(nc, "win", [20, 128, KD, 128]),
                              _dram(nc, "wout", [KD, 128, KD, 128]), _dram(nc, "adw", [128, MA, KA]),
                              _dram(nc, "avec", [128, 3, MA]), _dram(nc, "bdw", [128, MA, 3]))
        elif kind == "ffn":
            a = [_dram(nc, "xT", [D, TOK]), _dram(nc, "xh", [D, NCHUNK * HALO])]
            xo = _dram(nc, "xoT", [D, TOK], F32, "ExternalOutput")
            toks = ffn_phase(cx, "f.", a[0], a[1], xo, _dram(nc, "g", [128, KD]), _dram(nc, "wup", [NJ, 128, KD, 256]),
                             _dram(nc, "wdn", [KD, 128, NJ, 128]), _dram(nc, "dw", [128, 2 * NJ, 3]), _dram(nc, "db", [128, 2 * NJ]))
        elif kind == "qkv":
            toks = qkv_phase(cx, "q.", _dram(nc, "xT", [D, TOK]), _dram(nc, "g", [128, KD]), _dram(nc, "wq", [HP, 128, KD, 128]),
                             _dram(nc, "wk", [HP, 128, KD, 128]), _dram(nc, "wv", [2, 128, KD, 512]), _dram(nc, "qkg", [128, 2]),
                             _dram(nc, "qT", [D, TOK], BF16, "ExternalOutput"), _dram(nc, "kT", [D, TOK], BF16, "ExternalOutput"),
                             _dram(nc, "V", [HP, 128, NKB, 128], BF16, "ExternalOutput"))
        elif kind == "attn":
            toks = attn_phase(cx, "a.", _dram(nc, "xT", [D, TOK]), _dram(nc, "xoT", [D, TOK], F32, "ExternalOutput"),
                              _dram(nc, "qT", [D, TOK], BF16), _dram(nc, "kT", [2, D, TOK], BF16),
                              _dram(nc, "V", [2, HP, 128, NKB, 128], BF16), _dram(nc, "mask", [128, 2, 4, 1024]),
                              _dram(nc, "tri", [128, 128]), _dram(nc, "wo", [KD, 128, KD, 128]))
        S.wait_all("sp", toks)
        S.build(nc, st)
    _PROGS[kind] = nc
    return nc


def _run(kind, in_maps):
    res = run_bass_kernel_spmd(_prog(kind), in_maps, core_ids=list(range(NCORES)))
    return res.results


def kernel_unfused(**inputs):
    inp = {k: np.asarray(v) for k, v in inputs.items()}
    x = inp["x"]
    own = [_own_tokens(p) for p in range(2)]
    halo = [_halo_tokens(p) for p in range(2)]
    xT = [np.ascontiguousarray(x[c // 2][own[c % 2]].T) for c in range(NCORES)]

    def halos(xT):
        out = []
        for b in range(BATCH):
            full = np.zeros((D, HALO + SEQ), np.float32)
            for p in range(2):
                for i in range(NCHUNK):
                    g0 = (2 * i + p) * CH
                    full[:, HALO + g0:HALO + g0 + CH] = xT[2 * b + p][:, i * CH:(i + 1) * CH]
            for p in range(2):
                out.append(np.ascontiguousarray(full[:, halo[p]]))
        return out

    masks = [_masks(p) for p in range(2)]
    tri = _tri()
    for layer in range(4):
        i = layer // 2
        if layer % 2 == 0:
            lw = _lay_conv(inp["mix_norm_g"][layer], inp["conv_w_in"][i], inp["conv_a_dw_w"][i], inp["conv_a_dw_b"][i],
                           inp["conv_a_ln_g"][i], inp["conv_a_ln_b"][i], inp["conv_b_dw_w"][i], inp["conv_w_out"][i])
            xh = halos(xT)
            r = _run("conv", [dict(lw, xT=xT[c], xh=xh[c]) for c in range(NCORES)])
            xT = [r[c]["xoT"] for c in range(NCORES)]
        else:
            lw = _lay_attn(inp["mix_norm_g"][layer], inp["attn_w_qkv"][i], inp["attn_q_g"][i], inp["attn_k_g"][i], inp["attn_w_o"][i])
            r = _run("qkv", [dict(xT=xT[c], g=lw["g"], wq=lw["wq"], wk=lw["wk"], wv=lw["wv"], qkg=lw["qkg"]) for c in range(NCORES)])
            ins = []
            for c in range(NCORES):
                b = c // 2
                kT_g = np.stack([r[2 * b]["kT"], r[2 * b + 1]["kT"]])
                V_g = np.stack([r[2 * b]["V"], r[2 * b + 1]["V"]])
                ins.append(dict(xT=xT[c], qT=r[c]["qT"], kT=kT_g, V=V_g, mask=masks[c % 2], tri=tri, wo=lw["wo"]))
            r = _run("attn", ins)
            xT = [r[c]["xoT"] for c in range(NCORES)]
        lw = _lay_ffn(inp["ffn_norm_g"][layer], inp["ffn_w_up"][layer], inp["ffn_dw_w"][layer], inp["ffn_dw_b"][layer], inp["ffn_w_down"][layer])
        xh = halos(xT)
        r = _run("ffn", [dict(lw, xT=xT[c], xh=xh[c]) for c in range(NCORES)])
        xT = [r[c]["xoT"] for c in range(NCORES)]
    out = np.empty((BATCH, SEQ, D), np.float32)
    for c in range(NCORES):
        out[c // 2][own[c % 2]] = xT[c].T
    return out


PAIRS = [[0, 1], [2, 3], [4, 5], [6, 7]]


def halo_phase(cx, pfx, xT, tl, tg, xh, sel_d):
    S, nc = cx.S, cx.nc
    t_sb = cx.sb(pfx + "t", [128, KD, NCHUNK, HALO], F32)
    c0 = cx.sb(pfx + "c0", [128, KD, NCHUNK, HALO], F32)
    c1 = cx.sb(pfx + "c1", [128, KD, NCHUNK, HALO], F32)
    sel = cx.sb(pfx + "sel", [128, 2], F32)
    xT_r = xT.rearrange("(k p) (c t) -> p k c t", p=128, t=CH)
    tl_r = tl.rearrange("(k p) (c h) -> p k c h", p=128, h=HALO)
    tg_r = tg.rearrange("(r k p) (c h) -> p r k c h", p=128, k=KD, h=HALO)
    xh_r = xh.rearrange("(k p) (c h) -> p k c h", p=128, h=HALO)
    S.dma("sp", "cst", lambda e: e.dma_start(out=sel[:], in_=sel_d), writes=[pfx + "sel"])
    S.dma_group("sp", "h0", [(lambda e, k=k: e.dma_start(out=t_sb[:, k], in_=xT_r[:, k, :, CH - HALO:CH]), [], [pfx + f"t{k}"])
                             for k in range(KD)])
    S.dma_group("sp", "h1", [(lambda e, k=k: e.dma_start(out=tl_r[:, k], in_=t_sb[:, k]), [pfx + f"t{k}"], [pfx + f"tl{k}"])
                             for k in range(KD)])
    S.dma("pool", "cc", lambda e: e.collective_compute("AllGather", ALU.bypass, replica_groups=PAIRS, ins=[tl], outs=[tg]),
          reads=[pfx + f"tl{k}" for k in range(KD)], writes=[pfx + "tg"], inc=1)
    S.emit("dve", lambda e: e.memset(c0[:, :, 0, :], 0.0), writes=[pfx + "c0z"])
    S.dma_group("sp", "h2", [(lambda e, k=k: e.dma_start(out=c0[:, k, 1:NCHUNK, :], in_=tg_r[:, 1, k, 0:NCHUNK - 1, :]), [pfx + "tg"], [pfx + f"c0{k}"])
                             for k in range(KD)] +
                            [(lambda e, k=k: e.dma_start(out=c1[:, k], in_=tg_r[:, 0, k]), [pfx + "tg"], [pfx + f"c1{k}"])
                             for k in range(KD)])
    items = []
    for k in range(KD):
        S.emit("dve", lambda e, k=k: e.tensor_scalar(out=c0[:, k], in0=c0[:, k], scalar1=sel[:, 0:1], scalar2=None, op0=ALU.mult),
               reads=[pfx + f"c0{k}", pfx + "c0z", pfx + "sel"], writes=[pfx + f"c0{k}"])
        S.emit("dve", lambda e, k=k: e.scalar_tensor_tensor(out=c1[:, k], in0=c1[:, k], scalar=sel[:, 1:2], in1=c0[:, k],
                                                            op0=ALU.mult, op1=ALU.add),
               reads=[pfx + f"c0{k}", pfx + f"c1{k}", pfx + "sel"], writes=[pfx + f"c1{k}"])
        items.append((lambda e, k=k: e.dma_start(out=xh_r[:, k], in_=c1[:, k]), [pfx + f"c1{k}"], [pfx + f"xh{k}"]))
    t = S.dma_group("sp", "h3", items)
    return [t]


def gather_phase(cx, pfx, kT_l, kT_g, V_l, V_g):
    S = cx.S
    toks = []
    for hp in range(HP):
        for nm, a, b in (("k", kT_l, kT_g), ("v", V_l, V_g)):
            toks.append(S.dma("pool", "cc", lambda e, hp=hp, a=a, b=b: e.collective_compute(
                "AllGather", ALU.bypass, replica_groups=PAIRS, ins=[a[hp * 128:(hp + 1) * 128, :]], outs=[b[hp * 256:(hp + 1) * 256, :]]),
                writes=[pfx + f"{nm}g{hp}"], inc=1))
    return toks[-1:]


_FUSED = []


def _fused_prog():
    if _FUSED:
        return _FUSED[0]
    nc = bass.Bass("TRN2", target_bir_lowering=False)
    S = Sched()
    with ExitStack() as outer:
        x_in = _dram(nc, "xT", [D, TOK])
        out = _dram(nc, "out", [D, TOK], F32, "ExternalOutput")
        sel_d = _dram(nc, "sel", [128, 2])
        mask_d = _dram(nc, "mask", [128, 2, 4, 1024])
        tri_d = _dram(nc, "tri", [128, 128])
        xs = [nc.dram_tensor(f"xs{i}", [D, TOK], F32).ap() for i in range(2)]
        tl = nc.dram_tensor("tl", [D, NCHUNK * HALO], F32).ap()
        tg = nc.dram_tensor("tg", [2 * D, NCHUNK * HALO], F32).ap()
        xh = nc.dram_tensor("xh", [D, NCHUNK * HALO], F32).ap()
        qT = nc.dram_tensor("qT", [D, TOK], BF16).ap()
        kT_l = nc.dram_tensor("kTl", [D, TOK], BF16).ap()
        kT_g = nc.dram_tensor("kTg", [2 * D, TOK], BF16).ap()
        V_l = nc.dram_tensor("Vl", [HP * 128, NKB * 128], BF16).ap()
        V_g = nc.dram_tensor("Vg", [2 * HP * 128, NKB * 128], BF16).ap()
        V_l5 = V_l.rearrange("(h p) (k f) -> h p k f", p=128, f=128)

        def phase(fn):
            with ExitStack() as pst:
                cx = Ctx(nc, S, pst)
                toks = fn(cx)
                S.barrier(toks)
                S.build_phase(nc, pst, outer)

        cur = x_in
        nxt = 0
        for layer in range(4):
            L = f"L{layer}."
            if layer % 2 == 0:
                phase(lambda cx: halo_phase(cx, L + "h.", cur, tl, tg, xh, sel_d))
                dst = xs[nxt]
                phase(lambda cx: conv_phase(cx, L + "c.", cur, xh, dst, _dram(nc, L + "mg", [128, KD]), _dram(nc, L + "win", [20, 128, KD, 128]),
                                            _dram(nc, L + "wout", [KD, 128, KD, 128]), _dram(nc, L + "adw", [128, MA, KA]),
                                            _dram(nc, L + "avec", [128, 3, MA]), _dram(nc, L + "bdw", [128, MA, 3])))
            else:
                phase(lambda cx: qkv_phase(cx, L + "q.", cur, _dram(nc, L + "mg", [128, KD]), _dram(nc, L + "wq", [HP, 128, KD, 128]),
                                           _dram(nc, L + "wk", [HP, 128, KD, 128]), _dram(nc, L + "wv", [2, 128, KD, 512]),
                                           _dram(nc, L + "qkg", [128, 2]), qT, kT_l, V_l5))
                phase(lambda cx: gather_phase(cx, L + "g.", kT_l, kT_g, V_l, V_g))
                dst = xs[nxt]
                phase(lambda cx: attn_phase(cx, L + "a.", cur, dst, qT, kT_g, V_g, mask_d, tri_d, _dram(nc, L + "wo", [KD, 128, KD, 128])))
            cur = dst
            nxt ^= 1
            phase(lambda cx: halo_phase(cx, L + "hf.", cur, tl, tg, xh, sel_d))
            dst = out if layer == 3 else xs[nxt]
            phase(lambda cx: ffn_phase(cx, L + "f.", cur, xh, dst, _dram(nc, L + "fg", [128, KD]), _dram(nc, L + "wup", [NJ, 128, KD, 256]),
                                       _dram(nc, L + "wdn", [KD, 128, NJ, 128]), _dram(nc, L + "dw", [128, 2 * NJ, 3]),
                                       _dram(nc, L + "db", [128, 2 * NJ])))
            cur = dst
            nxt ^= 1
    _FUSED.append((nc, S))
    return _FUSED[0]


def kernel(**inputs):
    inp = {k: np.asarray(v) for k, v in inputs.items()}
    x = inp["x"]
    own = [_own_tokens(p) for p in range(2)]
    base = {"tri": _tri()}
    for layer in range(4):
        i = layer // 2
        L = f"L{layer}."
        if layer % 2 == 0:
            lw = _lay_conv(inp["mix_norm_g"][layer], inp["conv_w_in"][i], inp["conv_a_dw_w"][i], inp["conv_a_dw_b"][i],
                           inp["conv_a_ln_g"][i], inp["conv_a_ln_b"][i], inp["conv_b_dw_w"][i], inp["conv_w_out"][i])
        else:
            lw = _lay_attn(inp["mix_norm_g"][layer], inp["attn_w_qkv"][i], inp["attn_q_g"][i], inp["attn_k_g"][i], inp["attn_w_o"][i])
        lw["mg"] = lw.pop("g")
        lf = _lay_ffn(inp["ffn_norm_g"][layer], inp["ffn_w_up"][layer], inp["ffn_dw_w"][layer], inp["ffn_dw_b"][layer], inp["ffn_w_down"][layer])
        lf["fg"] = lf.pop("g")
        for k, v in list(lw.items()) + list(lf.items()):
            base[L + k] = v
    masks = [_masks(p) for p in range(2)]
    sels = [np.ascontiguousarray(np.tile(np.array([[1.0, 0.0]], np.float32) if p == 0 else np.array([[0.0, 1.0]], np.float32), (128, 1)))
            for p in range(2)]
    in_maps = []
    for c in range(NCORES):
        d = dict(base)
        d["xT"] = np.ascontiguousarray(x[c // 2][own[c % 2]].T)
        d["mask"] = masks[c % 2]
        d["sel"] = sels[c % 2]
        in_maps.append(d)
    nc, _ = _fused_prog()
    res = run_bass_kernel_spmd(nc, in_maps, core_ids=list(range(NCORES))).results
    out = np.empty((BATCH, SEQ, D), np.float32)
    for c in range(NCORES):
        out[c // 2][own[c % 2]] = res[c]["out"].T
    return out
```

```python
import numpy as np
from contextlib import ExitStack
import concourse.bass as bass
import concourse.mybir as mybir
from concourse.bass_utils import run_bass_kernel_spmd

F32 = mybir.dt.float32
BF16 = mybir.dt.bfloat16
AF = mybir.ActivationFunctionType
ALU = mybir.AluOpType

ENGINES = ("pe", "act", "dve", "pool", "sp")
ENG_ATTR = {"pe": "tensor", "act": "scalar", "dve": "vector", "pool": "gpsimd", "sp": "sync"}


class _Op:
    __slots__ = ("fn", "waits", "inc")

    def __init__(self, fn):
        self.fn = fn
        self.waits = []
        self.inc = None


class Sched:
    SEM_ROT = 20000

    def __init__(self):
        self.ops = {e: [] for e in ENGINES}
        self.built = {e: 0 for e in ENGINES}
        self.ms = {e: [] for e in ENGINES}
        self.gen = {e: 0 for e in ENGINES}
        self.cnt = {}
        self.waited = {e: {} for e in ENGINES}
        self.last_w = {}
        self.readers = {}
        self.nwaits = 0
        self.sems = {}

    def _new_ms(self, e, seq, op):
        k = ("eng", e, self.gen[e])
        if self.cnt.get(k, 0) >= self.SEM_ROT:
            self.gen[e] += 1
            k = ("eng", e, self.gen[e])
        self.cnt[k] = self.cnt.get(k, 0) + 1
        op.inc = (k, 1)
        self.ms[e].append((seq, k, self.cnt[k]))
        return k, self.cnt[k]

    def _milestone(self, e, seq):
        lo, hi = 0, len(self.ms[e])
        while lo < hi:
            mid = (lo + hi) // 2
            if self.ms[e][mid][0] >= seq:
                hi = mid
            else:
                lo = mid + 1
        if lo < len(self.ms[e]):
            return self.ms[e][lo][1], self.ms[e][lo][2]
        last = len(self.ops[e]) - 1
        while self.ops[e][last].fn is None:
            last -= 1
        assert last >= seq and last >= self.built[e], (e, seq, last, self.built[e])
        op = self.ops[e][last]
        assert op.inc is None, f"last op on {e} already has an inc"
        return self._new_ms(e, last, op)

    def _resolve(self, tok):
        if tok[0] == "eng":
            return self._milestone(tok[1], tok[2])
        return tok[1], tok[2]

    def _deps(self, engine, reads, writes):
        toks = []
        for k in reads:
            t = self.last_w.get(k)
            if t is not None:
                toks.append(t)
        for k in writes:
            t = self.last_w.get(k)
            if t is not None and not (t[0] == "eng" and t[1] == engine == "pe"):
                toks.append(t)
            for t in self.readers.get(k, {}).values():
                if not (t[0] == "eng" and t[1] == engine):
                    toks.append(t)
        return self._waits(engine, toks)

    def _waits(self, engine, toks):
        waits = {}
        for t in toks:
            sk, v = self._resolve(t)
            if self.waited[engine].get(sk, 0) >= v:
                continue
            waits[sk] = max(waits.get(sk, 0), v)
        for sk, v in waits.items():
            self.waited[engine][sk] = v
        self.nwaits += len(waits)
        return list(waits.items())

    def _track(self, tok, rkey, reads, writes):
        for k in writes:
            self.last_w[k] = tok
            self.readers[k] = {}
        for k in reads:
            self.readers.setdefault(k, {})[rkey] = tok

    def emit(self, engine, fn, reads=(), writes=(), ms=False, banks=()):
        writes = list(writes) + list(banks)
        op = _Op(fn)
        op.waits = self._deps(engine, reads, writes)
        seq = len(self.ops[engine])
        self.ops[engine].append(op)
        if ms or engine != "pe":
            self._new_ms(engine, seq, op)
        tok = ("eng", engine, seq)
        self._track(tok, engine, reads, writes)
        return tok

    def dma(self, queue, sem, fn, reads=(), writes=(), inc=16):
        op = _Op(fn)
        op.waits = self._deps(queue, reads, writes)
        self.ops[queue].append(op)
        k = ("dma", sem.split(".", 1)[-1])
        self.cnt[k] = self.cnt.get(k, 0) + inc
        op.inc = (k, inc)
        tok = ("dma", k, self.cnt[k])
        self._track(tok, k, reads, writes)
        return tok

    def dma_group(self, queue, sem, items):
        toks = [self.dma(queue, sem, fn, reads, writes) for (fn, reads, writes) in items]
        final = toks[-1]
        for (fn, reads, writes) in items:
            for k in writes:
                self.last_w[k] = final
            for k in reads:
                self.readers.setdefault(k, {})[final[1]] = final
        return final

    def wait_all(self, engine, toks):
        op = _Op(None)
        op.waits = self._waits(engine, toks)
        self.ops[engine].append(op)

    def barrier(self, toks=()):
        toks = list(toks)
        for e in ("pe", "act", "dve", "pool"):
            last = len(self.ops[e]) - 1
            while last >= self.built[e] and (self.ops[e][last].fn is None or self.ops[e][last].inc is not None and self.ops[e][last].inc[0][0] == "dma"):
                last -= 1
            if last >= self.built[e]:
                toks.append(("eng", e, last))
        for e in ENGINES:
            self.wait_all(e, toks)

    def build_phase(self, nc, st, outer):
        block = st.enter_context(nc.Block())
        for e in ENGINES:
            deco = getattr(block, ENG_ATTR[e])

            def body(eng, e=e):
                for op in self.ops[e][self.built[e]:]:
                    for (sk, v) in op.waits:
                        eng.wait_ge(self._sem(nc, outer, sk), v)
                    if op.fn is None:
                        continue
                    ins = op.fn(eng)
                    if op.inc is not None:
                        ins.then_inc(self._sem(nc, outer, op.inc[0]), op.inc[1])
                    op.fn = None
                self.built[e] = len(self.ops[e])

            deco(body)

    def _sem(self, nc, outer, k):
        if k not in self.sems:
            self.sems[k] = outer.enter_context(nc.semaphore(f"s{len(self.sems)}"))
        return self.sems[k]

    def build(self, nc, st):
        self.build_phase(nc, st, st)


D = 1024
KD = D // 128
CH = 512
NCHUNK = 8
TOK = CH * NCHUNK
HALO = 32
DFF = 2816
NJ = DFF // 128
EPS = 1e-6


class Ctx:
    def __init__(self, nc, S, st):
        self.nc, self.S, self.st = nc, S, st
        self.n = 0

    def sb(self, name, shape, dt):
        return self.st.enter_context(self.nc.sbuf_tensor(name, list(shape), dt))

    def ps(self, name, shape=(128, 512), dt=F32):
        return self.st.enter_context(self.nc.psum_tensor(name, list(shape), dt))


def emit_rmsnorm(cx, pfx, x_k, h_k, xkeys, hkeys, ncols, sq, ones, pst, pst_bank, tmp, rstd, g_sb, gkey):
    S = cx.S
    for k in range(KD):
        S.emit("act", lambda e, k=k: e.activation(out=sq[:, k, 0:ncols], in_=x_k(k), func=AF.Square),
               reads=[xkeys[k]], writes=[pfx + f"sq{k}"])
    for k in range(KD):
        S.emit("pe", lambda e, k=k: e.matmul(pst, ones[:, :], sq[:, k, 0:ncols], start=(k == 0), stop=(k == KD - 1)),
               reads=[pfx + f"sq{k}", pfx + "ones"], banks=[pst_bank], ms=(k == KD - 1))
    S.emit("dve", lambda e: e.tensor_scalar(out=tmp[:, 0:ncols], in0=pst, scalar1=1.0 / D, scalar2=EPS,
                                            op0=ALU.mult, op1=ALU.add), banks=[pst_bank], writes=[pfx + "tmp"])
    S.emit("act", lambda e: e.activation(out=tmp[:, 0:ncols], in_=tmp[:, 0:ncols], func=AF.Sqrt),
           reads=[pfx + "tmp"], writes=[pfx + "tmp"])
    S.emit("dve", lambda e: e.reciprocal(out=rstd[:, 0:ncols], in_=tmp[:, 0:ncols]), reads=[pfx + "tmp"], writes=[pfx + "rstd"])
    for k in range(KD):
        S.emit("dve", lambda e, k=k: e.scalar_tensor_tensor(out=h_k(k), in0=x_k(k), scalar=g_sb[:, k:k + 1], in1=rstd[:, 0:ncols],
                                                            op0=ALU.mult, op1=ALU.mult),
               reads=[xkeys[k], pfx + "rstd", gkey], writes=[hkeys[k]])


def ffn_phase(cx, pfx, xT, xh, xoT, g_d, wup_d, wdn_d, dw_d, db_d, nst=NCHUNK // 2):
    S, nc = cx.S, cx.nc
    HH = 2
    W = HH + CH
    SC = 2
    x_sb = cx.sb(pfx + "x", [128, KD, SC, W], F32)
    hT = cx.sb(pfx + "hT", [128, KD, SC, W], BF16)
    sq = cx.sb(pfx + "sq", [128, KD, CH], BF16)
    rstd = cx.sb(pfx + "rstd", [128, CH], F32)
    tmp = cx.sb(pfx + "tmp", [128, CH], F32)
    ones = cx.sb(pfx + "ones", [128, 128], BF16)
    g_sb = cx.sb(pfx + "g", [128, KD], F32)
    dw_sb = cx.sb(pfx + "dw", [128, 2 * NJ, 3], F32)
    db_sb = cx.sb(pfx + "db", [128, 2 * NJ], F32)
    wup = [cx.sb(pfx + f"wup{i}", [128, KD, 256], BF16) for i in range(2)]
    wdn = [cx.sb(pfx + f"wdn{i}", [128, NJ, 128], BF16) for i in range(2)]
    gT = cx.sb(pfx + "gT", [128, NJ, SC, CH], BF16)
    gacc = [cx.sb(pfx + f"gacc{i}", [128, CH], F32) for i in range(2)]
    vacc = [cx.sb(pfx + f"vacc{i}", [128, CH], F32) for i in range(2)]
    sg = [cx.sb(pfx + f"sg{i}", [128, CH], F32) for i in range(2)]
    phs = [cx.sb(pfx + f"phs{i}", [128, 4], F32) for i in range(2)]
    xo = [cx.sb(pfx + f"xo{i}", [128, CH], F32) for i in range(2)]
    pg = [cx.ps(pfx + f"pg{i}") for i in range(2)]
    pv = [cx.ps(pfx + f"pv{i}") for i in range(2)]
    pmisc = cx.ps(pfx + "pmisc")
    pstat = cx.ps(pfx + "pstat")
    pd = [cx.ps(pfx + f"pd{i}") for i in range(2)]
    ph = [pmisc[:, 0:4], pstat[:, 0:4]]
    phk = [pfx + "pmisc", pfx + "pstat"]
    pstat_h = pmisc[:, 32:32 + HH]

    xT_r = xT.rearrange("(k p) t -> p k t", p=128)
    xh_r = xh.rearrange("(k p) (c h) -> p k c h", p=128, h=HALO)
    xo_r = xoT.rearrange("(k p) t -> p k t", p=128)

    S.emit("dve", lambda e: e.memset(ones[:], 1.0), writes=[pfx + "ones"])
    S.dma_group("sp", "cst", [(lambda e: e.dma_start(out=g_sb[:], in_=g_d), [], [pfx + "n.g"]),
                              (lambda e: e.dma_start(out=dw_sb[:], in_=dw_d), [], [pfx + "dw"]),
                              (lambda e: e.dma_start(out=db_sb[:], in_=db_d), [], [pfx + "db"])])
    out_toks = []
    wu_i = 0
    wd_i = 0
    it = 0
    for sti in range(nst):
        for c in range(SC):
            gc = sti * SC + c
            S.dma_group("sp", f"lx{c}", [(lambda e, c=c, k=k, gc=gc: e.dma_start(out=x_sb[:, k, c, HH:W], in_=xT_r[:, k, gc * CH:(gc + 1) * CH]),
                                          [], [pfx + f"x{c}.{k}m"]) for k in range(KD)])
            S.dma("sp", pfx + f"lxh{c}",
                  lambda e, c=c, gc=gc: e.dma_start(out=x_sb[:, :, c, 0:HH], in_=xh_r[:, :, gc, HALO - HH:HALO]),
                  writes=[pfx + f"x{c}.h"])
        for c in range(SC):
            xk = [pfx + f"x{c}.{k}m" for k in range(KD)]
            for k in range(KD):
                S.emit("act", lambda e, k=k, c=c: e.activation(out=sq[:, k, :], in_=x_sb[:, k, c, HH:W], func=AF.Square),
                       reads=[xk[k]], writes=[pfx + f"sq{k}"])
            for k in range(KD):
                S.emit("pe", lambda e, k=k: e.matmul(pstat[:, :], ones[:, :], sq[:, k, :], start=(k == 0), stop=(k == KD - 1)),
                       reads=[pfx + f"sq{k}", pfx + "ones"], banks=[pfx + "pstat"], ms=(k == KD - 1))
            S.emit("dve", lambda e: e.tensor_scalar(out=tmp[:, :], in0=pstat[:, :], scalar1=1.0 / D, scalar2=EPS,
                                                    op0=ALU.mult, op1=ALU.add),
                   banks=[pfx + "pstat"], writes=[pfx + "tmp"])
            S.emit("act", lambda e: e.activation(out=tmp[:, :], in_=tmp[:, :], func=AF.Sqrt),
                   reads=[pfx + "tmp"], writes=[pfx + "tmp"])
            S.emit("dve", lambda e: e.reciprocal(out=rstd[:, :], in_=tmp[:, :]), reads=[pfx + "tmp"], writes=[pfx + "rstd"])
            for k in range(KD):
                S.emit("dve", lambda e, k=k, c=c: e.scalar_tensor_tensor(
                    out=hT[:, k, c, HH:W], in0=x_sb[:, k, c, HH:W], scalar=g_sb[:, k:k + 1], in1=rstd[:, :],
                    op0=ALU.mult, op1=ALU.mult),
                    reads=[xk[k], pfx + "rstd", pfx + "n.g"], writes=[pfx + f"h{c}.{k}m"])
            S.emit("act", lambda e, c=c: e.activation(out=sq[:, :, 0:HH], in_=x_sb[:, :, c, 0:HH], func=AF.Square),
                   reads=[pfx + f"x{c}.h"], writes=[pfx + f"sq{k}" for k in range(KD)])
            for k in range(KD):
                S.emit("pe", lambda e, k=k: e.matmul(pstat_h, ones[:, :], sq[:, k, 0:HH], start=(k == 0), stop=(k == KD - 1)),
                       reads=[pfx + f"sq{k}", pfx + "ones"], banks=[pfx + "pmisc"], ms=(k == KD - 1))
            S.emit("dve", lambda e: e.tensor_scalar(out=tmp[:, 0:HH], in0=pstat_h, scalar1=1.0 / D, scalar2=EPS,
                                                    op0=ALU.mult, op1=ALU.add),
                   banks=[pfx + "pmisc"], writes=[pfx + "tmp"])
            S.emit("act", lambda e: e.activation(out=tmp[:, 0:HH], in_=tmp[:, 0:HH], func=AF.Sqrt),
                   reads=[pfx + "tmp"], writes=[pfx + "tmp"])
            S.emit("dve", lambda e: e.reciprocal(out=rstd[:, 0:HH], in_=tmp[:, 0:HH]), reads=[pfx + "tmp"], writes=[pfx + "rstd"])
            for k in range(KD):
                S.emit("dve", lambda e, k=k, c=c: e.scalar_tensor_tensor(
                    out=hT[:, k, c, 0:HH], in0=x_sb[:, k, c, 0:HH], scalar=g_sb[:, k:k + 1], in1=rstd[:, 0:HH],
                    op0=ALU.mult, op1=ALU.mult),
                    reads=[pfx + f"x{c}.h", pfx + "rstd", pfx + "n.g"], writes=[pfx + f"h{c}.{k}h"])
        def load_wup(j, slot):
            S.dma("pool", pfx + f"wu{slot}", lambda e, j=j, slot=slot: e.dma_start(out=wup[slot][:], in_=wup_d[j]),
                  writes=[pfx + f"wup{slot}"])

        def load_wdn(n, slot):
            S.dma("pool", pfx + f"wd{slot}", lambda e, n=n, slot=slot: e.dma_start(out=wdn[slot][:], in_=wdn_d[n]),
                  writes=[pfx + f"wdn{slot}"])

        load_wup(0, wu_i % 2)
        for j in range(NJ):
            slot = wu_i % 2
            if j + 1 < NJ:
                load_wup(j + 1, (wu_i + 1) % 2)
            else:
                load_wdn(0, wd_i % 2)
            wu_i += 1
            for c in range(SC):
                b = it % 2
                it += 1
                hk = [pfx + f"h{c}.{k}m" for k in range(KD)]
                hh = [pfx + f"h{c}.{k}h" for k in range(KD)]
                for half, (pp, acc, jc) in enumerate(((pg[b], gacc[b], j), (pv[b], vacc[b], NJ + j))):
                    co = half * 128
                    pk = pfx + f"p{half}{b}"
                    ak = pfx + f"acc{half}{b}"
                    for k in range(KD):
                        S.emit("pe", lambda e, k=k, c=c, pp=pp, co=co, slot=slot: e.matmul(
                            pp[:, :], wup[slot][:, k, co:co + 128], hT[:, k, c, HH:W], start=(k == 0), stop=(k == KD - 1)),
                            reads=[hk[k], pfx + f"wup{slot}"], banks=[pk], ms=(k == KD - 1))
                    for k in range(KD):
                        S.emit("pe", lambda e, k=k, c=c, co=co, slot=slot, b=b, half=half: e.matmul(
                            ph[b][:, 2 * half:2 * half + 2], wup[slot][:, k, co:co + 128], hT[:, k, c, 0:HH],
                            start=(k == 0), stop=(k == KD - 1)),
                            reads=[hh[k], pfx + f"wup{slot}"], banks=[phk[b]], ms=(k == KD - 1))
                    S.emit("act", lambda e, pp=pp, acc=acc, jc=jc: e.activation(
                        out=acc[:, :], in_=pp[:, :], func=AF.Identity, scale=dw_sb[:, jc, 2:3], bias=db_sb[:, jc:jc + 1]),
                        reads=[pfx + "dw", pfx + "db"], writes=[ak], banks=[pk])
                    S.emit("dve", lambda e, pp=pp, acc=acc, jc=jc: e.scalar_tensor_tensor(
                        out=acc[:, 1:CH], in0=pp[:, 0:CH - 1], scalar=dw_sb[:, jc, 1:2], in1=acc[:, 1:CH],
                        op0=ALU.mult, op1=ALU.add), reads=[ak, pfx + "dw"], writes=[ak], banks=[pk])
                    S.emit("dve", lambda e, pp=pp, acc=acc, jc=jc: e.scalar_tensor_tensor(
                        out=acc[:, 2:CH], in0=pp[:, 0:CH - 2], scalar=dw_sb[:, jc, 0:1], in1=acc[:, 2:CH],
                        op0=ALU.mult, op1=ALU.add), reads=[ak, pfx + "dw"], writes=[ak], banks=[pk])
                for half, (acc, jc) in enumerate(((gacc[b], j), (vacc[b], NJ + j))):
                    ak = pfx + f"acc{half}{b}"
                    S.emit("dve", lambda e, acc=acc, jc=jc, b=b, half=half: e.scalar_tensor_tensor(
                        out=acc[:, 0:2], in0=ph[b][:, 2 * half:2 * half + 2], scalar=dw_sb[:, jc, 0:1], in1=acc[:, 0:2],
                        op0=ALU.mult, op1=ALU.add), reads=[ak, pfx + "dw"], writes=[ak], banks=[phk[b]])
                    S.emit("dve", lambda e, acc=acc, jc=jc, b=b, half=half: e.scalar_tensor_tensor(
                        out=acc[:, 0:1], in0=ph[b][:, 2 * half + 1:2 * half + 2], scalar=dw_sb[:, jc, 1:2], in1=acc[:, 0:1],
                        op0=ALU.mult, op1=ALU.add), reads=[ak, pfx + "dw"], writes=[ak], banks=[phk[b]])
                S.emit("act", lambda e, b=b: e.activation(out=sg[b][:, :], in_=gacc[b][:, :], func=AF.Silu),
                       reads=[pfx + f"acc0{b}"], writes=[pfx + f"sg{b}"])
                S.emit("dve", lambda e, b=b, j=j, c=c: e.tensor_tensor(out=gT[:, j, c, :], in0=sg[b][:, :], in1=vacc[b][:, :],
                                                                      op=ALU.mult),
                       reads=[pfx + f"sg{b}", pfx + f"acc1{b}"], writes=[pfx + f"gT{j}.{c}"])
        for n in range(KD):
            slot = wd_i % 2
            if n + 1 < KD:
                load_wdn(n + 1, (wd_i + 1) % 2)
            wd_i += 1
            for c in range(SC):
                gc = sti * SC + c
                b = it % 2
                it += 1
                for jj in range(NJ):
                    S.emit("pe", lambda e, jj=jj, c=c, b=b, slot=slot: e.matmul(
                        pd[b][:, :], wdn[slot][:, jj, :], gT[:, jj, c, :], start=(jj == 0), stop=(jj == NJ - 1)),
                        reads=[pfx + f"gT{jj}.{c}", pfx + f"wdn{slot}"], banks=[pfx + f"pd{b}"], ms=(jj == NJ - 1))
                S.emit("dve", lambda e, b=b, n=n, c=c: e.tensor_tensor(out=xo[b][:, :], in0=pd[b][:, :], in1=x_sb[:, n, c, HH:W],
                                                                      op=ALU.add),
                       reads=[pfx + f"x{c}.{n}m"], writes=[pfx + f"xo{b}"], banks=[pfx + f"pd{b}"])
                t = S.dma("sp", pfx + f"so{b}", lambda e, b=b, n=n, gc=gc: e.dma_start(
                    out=xo_r[:, n, gc * CH:(gc + 1) * CH], in_=xo[b][:, :]), reads=[pfx + f"xo{b}"], writes=[pfx + f"xoT.{n}.{gc}"])
                out_toks.append(t)
    return out_toks


DA = 512
MA = DA // 128
KA = 31


def conv_phase(cx, pfx, xT, xh, xoT, g_d, win_d, wout_d, adw_d, avec_d, bdw_d, nch=NCHUNK):
    S, nc = cx.S, cx.nc
    H = HALO
    W = H + CH
    x_sb = cx.sb(pfx + "x", [128, KD, W], F32)
    hT = cx.sb(pfx + "hT", [128, KD, W], BF16)
    sq = cx.sb(pfx + "sq", [128, KD, CH], BF16)
    rstd = cx.sb(pfx + "rstd", [128, CH], F32)
    tmp = cx.sb(pfx + "tmp", [128, CH], F32)
    ones = cx.sb(pfx + "ones", [128, 128], BF16)
    onesm = cx.sb(pfx + "onesm", [128, 128], BF16)
    g_sb = cx.sb(pfx + "g", [128, KD], F32)
    adw = cx.sb(pfx + "adw", [128, MA, KA], F32)
    avec = cx.sb(pfx + "avec", [128, 3, MA], F32)
    bdw = cx.sb(pfx + "bdw", [128, MA, 3], F32)
    win = cx.sb(pfx + "win", [128, 20, KD, 128], BF16)
    wout = cx.sb(pfx + "wout", [128, KD, KD, 128], BF16)
    glu = cx.sb(pfx + "glu", [128, MA, W], F32)
    ca = cx.sb(pfx + "ca", [128, MA, CH], F32)
    cab = cx.sb(pfx + "cab", [128, MA, CH], BF16)
    abT = cx.sb(pfx + "abT", [128, 2 * MA, CH], BF16)
    sgm = cx.sb(pfx + "sgm", [128, W], F32)
    chb = cx.sb(pfx + "chb", [128, W], F32)
    accb = cx.sb(pfx + "accb", [128, CH], F32)
    xo = [cx.sb(pfx + f"xo{i}", [128, CH], F32) for i in range(2)]
    pA, pB, pC = cx.ps(pfx + "pA"), cx.ps(pfx + "pB"), cx.ps(pfx + "pC")
    pmisc, pstat, pvar = cx.ps(pfx + "pmisc"), cx.ps(pfx + "pstat"), cx.ps(pfx + "pvar")
    po = [cx.ps(pfx + f"po{i}") for i in range(2)]
    BK = lambda n: pfx + "B." + n

    xT_r = xT.rearrange("(k p) t -> p k t", p=128)
    xh_r = xh.rearrange("(k p) (c h) -> p k c h", p=128, h=HALO)
    xo_r = xoT.rearrange("(k p) t -> p k t", p=128)

    S.emit("dve", lambda e: e.memset(ones[:], 1.0), writes=[pfx + "ones"])
    S.emit("dve", lambda e: e.memset(onesm[:], 1.0 / DA), writes=[pfx + "onesm"])
    S.dma_group("sp", "cst", [(lambda e: e.dma_start(out=g_sb[:], in_=g_d), [], [pfx + "g"]),
                              (lambda e: e.dma_start(out=adw[:], in_=adw_d), [], [pfx + "adw"]),
                              (lambda e: e.dma_start(out=avec[:], in_=avec_d), [], [pfx + "avec"]),
                              (lambda e: e.dma_start(out=bdw[:], in_=bdw_d), [], [pfx + "bdw"])])
    S.dma_group("pool", "wA", [(lambda e, m=m: e.dma_start(out=win[:, m], in_=win_d[m]), [], [pfx + f"win{m}"]) for m in range(20)])
    S.dma_group("pool", "wB", [(lambda e, n=n: e.dma_start(out=wout[:, n], in_=wout_d[n]), [], [pfx + f"wout{n}"]) for n in range(KD)])

    out_toks = []
    it = 0
    for c in range(nch):
        xk = [pfx + f"x{k}" for k in range(KD)]
        hk = [pfx + f"h{k}" for k in range(KD)]
        S.dma_group("sp", "lx", [(lambda e, k=k, c=c: e.dma_start(out=x_sb[:, k, H:W], in_=xT_r[:, k, c * CH:(c + 1) * CH]), [], [xk[k]])
                                 for k in range(KD)])
        S.dma("sp", pfx + "lxh", lambda e, c=c: e.dma_start(out=x_sb[:, :, 0:H], in_=xh_r[:, :, c, :]), writes=[pfx + "xh"])
        emit_rmsnorm(cx, pfx, lambda k: x_sb[:, k, H:W], lambda k: hT[:, k, H:W], xk, hk, CH, sq, ones,
                     pstat[:, :], BK("pstat"), tmp, rstd, g_sb, pfx + "g")
        emit_rmsnorm(cx, pfx, lambda k: x_sb[:, k, 0:H], lambda k: hT[:, k, 0:H], [pfx + "xh"] * KD,
                     [pfx + f"hh{k}" for k in range(KD)], H, sq, ones, pmisc[:, 256:256 + H], BK("pmisc"), tmp, rstd, g_sb, pfx + "g")
        hh = [pfx + f"hh{k}" for k in range(KD)]

        def proj(m, pmain, bank, hcol=None):
            for k in range(KD):
                S.emit("pe", lambda e, k=k, m=m: e.matmul(pmain[:, :], win[:, m, k, :], hT[:, k, H:W], start=(k == 0), stop=(k == KD - 1)),
                       reads=[hk[k], pfx + f"win{m}"], banks=[bank], ms=(k == KD - 1))
            if hcol is not None:
                for k in range(KD):
                    S.emit("pe", lambda e, k=k, m=m: e.matmul(pmisc[:, hcol:hcol + H], win[:, m, k, :], hT[:, k, 0:H],
                                                              start=(k == 0), stop=(k == KD - 1)),
                           reads=[hh[k], pfx + f"win{m}"], banks=[BK("pmisc")], ms=(k == KD - 1))

        for m in range(MA):
            proj(m, pA, BK("pA"), 0)
            proj(MA + m, pB, BK("pB"), H)
            S.emit("act", lambda e: e.activation(out=sgm[:, H:W], in_=pB[:, :], func=AF.Sigmoid), banks=[BK("pB")], writes=[pfx + "sgm"])
            S.emit("act", lambda e: e.activation(out=sgm[:, 0:H], in_=pmisc[:, H:2 * H], func=AF.Sigmoid), banks=[BK("pmisc")], writes=[pfx + "sgm"])
            S.emit("dve", lambda e, m=m: e.tensor_tensor(out=glu[:, m, H:W], in0=pA[:, :], in1=sgm[:, H:W], op=ALU.mult),
                   reads=[pfx + "sgm"], banks=[BK("pA")], writes=[pfx + f"glu{m}"])
            S.emit("dve", lambda e, m=m: e.tensor_tensor(out=glu[:, m, 0:H], in0=pmisc[:, 0:H], in1=sgm[:, 0:H], op=ALU.mult),
                   reads=[pfx + "sgm"], banks=[BK("pmisc")], writes=[pfx + f"glu{m}"])
            S.emit("act", lambda e, m=m: e.activation(out=ca[:, m, :], in_=glu[:, m, H:W], func=AF.Identity,
                                                      scale=adw[:, m, KA - 1:KA], bias=avec[:, 0, m:m + 1]),
                   reads=[pfx + f"glu{m}", pfx + "adw", pfx + "avec"], writes=[pfx + f"ca{m}"])
            for k in range(KA - 1):
                S.emit("dve", lambda e, m=m, k=k: e.scalar_tensor_tensor(
                    out=ca[:, m, :], in0=glu[:, m, 2 + k:2 + k + CH], scalar=adw[:, m, k:k + 1], in1=ca[:, m, :],
                    op0=ALU.mult, op1=ALU.add), reads=[pfx + f"glu{m}", pfx + f"ca{m}", pfx + "adw"], writes=[pfx + f"ca{m}"])
            S.emit("act", lambda e, m=m: e.activation(out=cab[:, m, :], in_=ca[:, m, :], func=AF.Identity),
                   reads=[pfx + f"ca{m}"], writes=[pfx + f"cab{m}"])
        for m in range(MA):
            S.emit("pe", lambda e, m=m: e.matmul(pstat[:, :], onesm[:, :], cab[:, m, :], start=(m == 0), stop=(m == MA - 1)),
                   reads=[pfx + f"cab{m}", pfx + "onesm"], banks=[BK("pstat")], ms=(m == MA - 1))
        for m in range(MA):
            S.emit("dve", lambda e, m=m: e.tensor_tensor(out=ca[:, m, :], in0=ca[:, m, :], in1=pstat[:, :], op=ALU.subtract),
                   reads=[pfx + f"ca{m}"], banks=[BK("pstat")], writes=[pfx + f"ca{m}"])
            S.emit("act", lambda e, m=m: e.activation(out=cab[:, m, :], in_=ca[:, m, :], func=AF.Square),
                   reads=[pfx + f"ca{m}"], writes=[pfx + f"cab{m}"])
        for m in range(MA):
            S.emit("pe", lambda e, m=m: e.matmul(pvar[:, :], onesm[:, :], cab[:, m, :], start=(m == 0), stop=(m == MA - 1)),
                   reads=[pfx + f"cab{m}", pfx + "onesm"], banks=[BK("pvar")], ms=(m == MA - 1))
        S.emit("dve", lambda e: e.tensor_scalar(out=tmp[:, :], in0=pvar[:, :], scalar1=EPS, scalar2=None, op0=ALU.add),
               banks=[BK("pvar")], writes=[pfx + "tmp"])
        S.emit("act", lambda e: e.activation(out=tmp[:, :], in_=tmp[:, :], func=AF.Sqrt), reads=[pfx + "tmp"], writes=[pfx + "tmp"])
        S.emit("dve", lambda e: e.reciprocal(out=rstd[:, :], in_=tmp[:, :]), reads=[pfx + "tmp"], writes=[pfx + "rstd"])
        for m in range(MA):
            S.emit("dve", lambda e, m=m: e.scalar_tensor_tensor(out=ca[:, m, :], in0=ca[:, m, :], scalar=avec[:, 1, m:m + 1], in1=rstd[:, :],
                                                                op0=ALU.mult, op1=ALU.mult),
                   reads=[pfx + f"ca{m}", pfx + "rstd", pfx + "avec"], writes=[pfx + f"ca{m}"])
            S.emit("act", lambda e, m=m: e.activation(out=abT[:, m, :], in_=ca[:, m, :], func=AF.Silu, bias=avec[:, 2, m:m + 1]),
                   reads=[pfx + f"ca{m}", pfx + "avec"], writes=[pfx + f"ab{m}"])
        for m in range(MA):
            proj(3 * MA + m, pA, BK("pA"), 2 * H)
            proj(4 * MA + m, pB, BK("pB"), 3 * H)
            proj(2 * MA + m, pC, BK("pC"), None)
            S.emit("act", lambda e: e.activation(out=sgm[:, H:W], in_=pA[:, :], func=AF.Identity), banks=[BK("pA")], writes=[pfx + "sgm"])
            S.emit("act", lambda e: e.activation(out=sgm[:, 0:H], in_=pmisc[:, 2 * H:3 * H], func=AF.Identity), banks=[BK("pmisc")], writes=[pfx + "sgm"])
            S.emit("dve", lambda e: e.tensor_tensor(out=chb[:, H:W], in0=pB[:, :], in1=sgm[:, H:W], op=ALU.mult),
                   reads=[pfx + "sgm"], banks=[BK("pB")], writes=[pfx + "chb"])
            S.emit("dve", lambda e: e.tensor_tensor(out=chb[:, 0:H], in0=pmisc[:, 3 * H:4 * H], in1=sgm[:, 0:H], op=ALU.mult),
                   reads=[pfx + "sgm"], banks=[BK("pmisc")], writes=[pfx + "chb"])
            S.emit("act", lambda e, m=m: e.activation(out=accb[:, :], in_=chb[:, H:W], func=AF.Identity, scale=bdw[:, m, 2:3]),
                   reads=[pfx + "chb", pfx + "bdw"], writes=[pfx + "accb"])
            for k in range(2):
                S.emit("dve", lambda e, m=m, k=k: e.scalar_tensor_tensor(
                    out=accb[:, :], in0=chb[:, H - 2 + k:H - 2 + k + CH], scalar=bdw[:, m, k:k + 1], in1=accb[:, :],
                    op0=ALU.mult, op1=ALU.add), reads=[pfx + "chb", pfx + "accb", pfx + "bdw"], writes=[pfx + "accb"])
            S.emit("dve", lambda e, m=m: e.tensor_tensor(out=abT[:, MA + m, :], in0=pC[:, :], in1=accb[:, :], op=ALU.mult),
                   reads=[pfx + "accb"], banks=[BK("pC")], writes=[pfx + f"ab{MA + m}"])
        for n in range(KD):
            b = it % 2
            it += 1
            for k in range(2 * MA):
                S.emit("pe", lambda e, k=k, n=n, b=b: e.matmul(po[b][:, :], wout[:, n, k, :], abT[:, k, :], start=(k == 0), stop=(k == 2 * MA - 1)),
                       reads=[pfx + f"ab{k}", pfx + f"wout{n}"], banks=[BK(f"po{b}")], ms=(k == 2 * MA - 1))
            S.emit("dve", lambda e, b=b, n=n: e.tensor_tensor(out=xo[b][:, :], in0=po[b][:, :], in1=x_sb[:, n, H:W], op=ALU.add),
                   reads=[xk[n]], writes=[pfx + f"xo{b}"], banks=[BK(f"po{b}")])
            t = S.dma("sp", pfx + f"so{b}", lambda e, b=b, n=n, c=c: e.dma_start(out=xo_r[:, n, c * CH:(c + 1) * CH], in_=xo[b][:, :]),
                      reads=[pfx + f"xo{b}"], writes=[pfx + f"xoT.{n}.{c}"])
            out_toks.append(t)
    return out_toks


NH = 16
HP = NH // 2
NKB = TOK // 128


def qkv_phase(cx, pfx, xT, g_d, wq_d, wk_d, wv_d, qkg_d, qT_o, kT_o, V_o, nch=NCHUNK):
    S, nc = cx.S, cx.nc
    x_sb = cx.sb(pfx + "x", [128, KD, CH], F32)
    hT = cx.sb(pfx + "hT", [128, KD, CH], BF16)
    sq = cx.sb(pfx + "sq", [128, KD, CH], BF16)
    rstd = cx.sb(pfx + "rstd", [128, CH], F32)
    tmp = cx.sb(pfx + "tmp", [128, CH], F32)
    ones = cx.sb(pfx + "ones", [128, 128], BF16)
    bd = cx.sb(pfx + "bd", [128, 128], BF16)
    g_sb = cx.sb(pfx + "g", [128, KD], F32)
    qkg = cx.sb(pfx + "qkg", [128, 2], F32)
    wq = cx.sb(pfx + "wq", [128, HP, KD, 128], BF16)
    wk = cx.sb(pfx + "wk", [128, HP, KD, 128], BF16)
    wv = cx.sb(pfx + "wv", [128, 2, KD, 512], BF16)
    qf = [cx.sb(pfx + f"qf{i}", [128, CH], F32) for i in range(2)]
    sqq = [cx.sb(pfx + f"sqq{i}", [128, CH], BF16) for i in range(2)]
    rs = [cx.sb(pfx + f"rs{i}", [128, CH], F32) for i in range(2)]
    qn = [cx.sb(pfx + f"qn{i}", [128, CH], BF16) for i in range(2)]
    vt = [cx.sb(pfx + f"vt{i}", [128, 512], BF16) for i in range(2)]
    pq = [cx.ps(pfx + f"pq{i}") for i in range(2)]
    pms = [cx.ps(pfx + f"pms{i}") for i in range(2)]
    pvv = [cx.ps(pfx + f"pvv{i}") for i in range(2)]
    pstat = cx.ps(pfx + "pstat")
    BK = lambda n: pfx + "B." + n
    xT_r = xT.rearrange("(k p) t -> p k t", p=128)
    qT_r = qT_o.rearrange("(k p) t -> p k t", p=128)
    kT_r = kT_o.rearrange("(k p) t -> p k t", p=128)

    S.emit("dve", lambda e: e.memset(ones[:], 1.0), writes=[pfx + "ones"])
    S.emit("dve", lambda e: e.memset(bd[:], 0.0), writes=[pfx + "bd"])
    S.emit("dve", lambda e: e.memset(bd[0:64, 0:64], 1.0 / 64), writes=[pfx + "bd"])
    S.emit("dve", lambda e: e.memset(bd[64:128, 64:128], 1.0 / 64), writes=[pfx + "bd"])
    S.dma_group("sp", "cst", [(lambda e: e.dma_start(out=g_sb[:], in_=g_d), [], [pfx + "g"]),
                              (lambda e: e.dma_start(out=qkg[:], in_=qkg_d), [], [pfx + "qkg"])])
    S.emit("dve", lambda e: e.tensor_scalar(out=qkg[:, 0:1], in0=qkg[:, 0:1], scalar1=0.125, scalar2=None, op0=ALU.mult),
           reads=[pfx + "qkg"], writes=[pfx + "qkg"])
    S.dma_group("pool", "wA", [(lambda e, m=m: e.dma_start(out=wq[:, m], in_=wq_d[m]), [], [pfx + f"wq{m}"]) for m in range(HP)])
    S.dma_group("pool", "wB", [(lambda e, m=m: e.dma_start(out=wk[:, m], in_=wk_d[m]), [], [pfx + f"wk{m}"]) for m in range(HP)])
    S.dma_group("pool", "wC", [(lambda e, hf=hf: e.dma_start(out=wv[:, hf], in_=wv_d[hf]), [], [pfx + f"wv{hf}"]) for hf in range(2)])
    out_toks = []
    it = 0
    for c in range(nch):
        xk = [pfx + f"x{k}" for k in range(KD)]
        hk = [pfx + f"h{k}" for k in range(KD)]
        S.dma_group("sp", "lx", [(lambda e, k=k, c=c: e.dma_start(out=x_sb[:, k, :], in_=xT_r[:, k, c * CH:(c + 1) * CH]), [], [xk[k]])
                                 for k in range(KD)])
        emit_rmsnorm(cx, pfx, lambda k: x_sb[:, k, :], lambda k: hT[:, k, :], xk, hk, CH, sq, ones,
                     pstat[:, :], BK("pstat"), tmp, rstd, g_sb, pfx + "g")
        for which, (w_sb, wkey, gcol, o_r) in enumerate(((wq, "wq", 0, qT_r), (wk, "wk", 1, kT_r))):
            for m in range(HP):
                b = it % 2
                it += 1
                for k in range(KD):
                    S.emit("pe", lambda e, k=k, m=m, b=b, w_sb=w_sb: e.matmul(pq[b][:, :], w_sb[:, m, k, :], hT[:, k, :],
                                                                             start=(k == 0), stop=(k == KD - 1)),
                           reads=[hk[k], pfx + f"{wkey}{m}"], banks=[BK(f"pq{b}")], ms=(k == KD - 1))
                S.emit("act", lambda e, b=b: e.activation(out=sqq[b][:, :], in_=pq[b][:, :], func=AF.Square),
                       banks=[BK(f"pq{b}")], writes=[pfx + f"sqq{b}"])
                S.emit("act", lambda e, b=b: e.activation(out=qf[b][:, :], in_=pq[b][:, :], func=AF.Identity),
                       banks=[BK(f"pq{b}")], writes=[pfx + f"qf{b}"])
                S.emit("pe", lambda e, b=b: e.matmul(pms[b][:, :], bd[:, :], sqq[b][:, :], start=True, stop=True),
                       reads=[pfx + f"sqq{b}", pfx + "bd"], banks=[BK(f"pms{b}")], ms=True)
                S.emit("dve", lambda e, b=b: e.tensor_scalar(out=rs[b][:, :], in0=pms[b][:, :], scalar1=EPS, scalar2=None, op0=ALU.add),
                       banks=[BK(f"pms{b}")], writes=[pfx + f"rs{b}"])
                S.emit("act", lambda e, b=b: e.activation(out=rs[b][:, :], in_=rs[b][:, :], func=AF.Sqrt),
                       reads=[pfx + f"rs{b}"], writes=[pfx + f"rs{b}"])
                S.emit("dve", lambda e, b=b: e.reciprocal(out=rs[b][:, :], in_=rs[b][:, :]), reads=[pfx + f"rs{b}"], writes=[pfx + f"rs{b}"])
                S.emit("dve", lambda e, b=b, gcol=gcol: e.scalar_tensor_tensor(out=qn[b][:, :], in0=qf[b][:, :], scalar=qkg[:, gcol:gcol + 1],
                                                                               in1=rs[b][:, :], op0=ALU.mult, op1=ALU.mult),
                       reads=[pfx + f"qf{b}", pfx + f"rs{b}", pfx + "qkg"], writes=[pfx + f"qn{b}"])
                t = S.dma("sp", pfx + f"sq{b}", lambda e, b=b, m=m, c=c, o_r=o_r: e.dma_start(out=o_r[:, m, c * CH:(c + 1) * CH], in_=qn[b][:, :]),
                          reads=[pfx + f"qn{b}"], writes=[pfx + f"o{which}.{m}.{c}"])
                out_toks.append(t)
        for tt in range(CH // 128):
            kb = c * (CH // 128) + tt
            for hf in range(2):
                b = it % 2
                it += 1
                for k in range(KD):
                    S.emit("pe", lambda e, k=k, tt=tt, hf=hf, b=b: e.matmul(pvv[b][:, :], hT[:, k, tt * 128:(tt + 1) * 128], wv[:, hf, k, :],
                                                                          start=(k == 0), stop=(k == KD - 1)),
                           reads=[hk[k], pfx + f"wv{hf}"], banks=[BK(f"pvv{b}")], ms=(k == KD - 1))
                S.emit("act", lambda e, b=b: e.activation(out=vt[b][:, :], in_=pvv[b][:, :], func=AF.Identity),
                       banks=[BK(f"pvv{b}")], writes=[pfx + f"vt{b}"])
                t = S.dma("sp", pfx + f"sv{b}", lambda e, b=b, hf=hf, kb=kb: e.dma_start(
                    out=V_o[hf * 4:(hf + 1) * 4, :, kb, :].rearrange("h p f -> p h f"),
                    in_=vt[b][:, :].rearrange("p (h f) -> p h f", f=128)),
                    reads=[pfx + f"vt{b}"], writes=[pfx + f"oV.{hf}.{kb}"])
                out_toks.append(t)
    return out_toks


def attn_phase(cx, pfx, xT, xoT, qT_i, kT_g, V_g, mask_d, tri_d, wo_d, nhp=HP, nq=NCHUNK):
    S, nc = cx.S, cx.nc
    kT_sb = cx.sb(pfx + "kT", [128, 2, TOK], BF16)
    V_sb = cx.sb(pfx + "V", [128, 2, NKB, 128], BF16)
    q_sb = cx.sb(pfx + "q", [128, TOK], BF16)
    oT = cx.sb(pfx + "oT", [128, HP, TOK], BF16)
    mask = cx.sb(pfx + "mask", [128, 2, 4, 1024], BF16)
    ntri = cx.sb(pfx + "ntri", [128, 128], BF16)
    nones = cx.sb(pfx + "nones", [128, 128], BF16)
    one1 = cx.sb(pfx + "one1", [128, 1], F32)
    E = [cx.sb(pfx + f"E{i}", [128, 1024], F32) for i in range(3)]
    L = [cx.sb(pfx + f"L{i}", [128, 1024], BF16) for i in range(3)]
    Wt = [cx.sb(pfx + f"W{i}", [128, 1024], BF16) for i in range(2)]
    Ls = cx.sb(pfx + "Ls", [128, 1024], F32)
    Lsb = [cx.sb(pfx + f"Lsb{i}", [128, 1024], BF16) for i in range(2)]
    wo = cx.sb(pfx + "wo", [128, KD, KD, 128], BF16)
    xr = [cx.sb(pfx + f"xr{i}", [128, CH], F32) for i in range(2)]
    xo = [cx.sb(pfx + f"xo{i}", [128, CH], F32) for i in range(2)]
    pz = [cx.ps(pfx + f"pz{i}", (128, 1024)) for i in range(3)]
    po = cx.ps(pfx + "po")
    pf = cx.ps(pfx + "pf")
    BK = lambda n: pfx + "B." + n
    xT_r = xT.rearrange("(k p) t -> p k t", p=128)
    xo_r = xoT.rearrange("(k p) t -> p k t", p=128)
    qT_r = qT_i.rearrange("(k p) t -> p k t", p=128)
    kT_r = kT_g.rearrange("(h r p) t -> p h r t", r=2, p=128)
    V_r = V_g.rearrange("(h r p) (k f) -> h p r k f", r=2, p=128, f=128)

    S.dma("pool", pfx + "ct", lambda e: e.dma_start(out=ntri[:], in_=tri_d), writes=[pfx + "ntri"])
    S.emit("dve", lambda e: e.memset(nones[:], -1.0), writes=[pfx + "nones"])
    S.emit("dve", lambda e: e.memset(one1[:], 1.0), writes=[pfx + "one1"])
    S.dma("pool", pfx + "cm", lambda e: e.dma_start(out=mask[:], in_=mask_d), writes=[pfx + "mask"])
    S.dma_group("pool", "wA", [(lambda e, n=n: e.dma_start(out=wo[:, n], in_=wo_d[n]), [], [pfx + f"wo{n}"]) for n in range(KD)])

    step = 0
    for hp in range(nhp):
        S.dma("sp", pfx + "lk", lambda e, hp=hp: e.dma_start(out=kT_sb[:, :, :], in_=kT_r[:, hp, :, :]), writes=[pfx + "kT"])
        S.dma("sp", pfx + "lv", lambda e, hp=hp: e.dma_start(out=V_sb[:, :, :, :], in_=V_r[hp]),
              writes=[pfx + "V"])
        S.dma("sp", pfx + "lq", lambda e, hp=hp: e.dma_start(out=q_sb[:, :], in_=qT_r[:, hp, :]), writes=[pfx + "q"])
        for i in range(nq):
            blocks = []
            for j in range(2 * i + 1, -1, -1):
                for b4 in range(3, -1, -1):
                    mk = 1 if j == 2 * i + 1 else (0 if j == 2 * i else None)
                    blocks.append((j % 2, (j // 2) * 4 + b4, mk, b4))
            nb = len(blocks)

            def emit_z(s):
                r, kb, mk, b4 = blocks[s]
                z3 = (step + s) % 3
                for hd in range(2):
                    S.emit("pe", lambda e, hd=hd, r=r, kb=kb, z3=z3, i=i: e.matmul(
                        pz[z3][:, hd * 512:(hd + 1) * 512], kT_sb[hd * 64:(hd + 1) * 64, r, kb * 128:(kb + 1) * 128],
                        q_sb[hd * 64:(hd + 1) * 64, i * CH:(i + 1) * CH], start=True, stop=True, skip_group_check=True),
                        reads=[pfx + "kT", pfx + "q"], banks=[BK(f"pz{z3}")], ms=(hd == 1))

            def emit_el(s):
                r, kb, mk, b4 = blocks[s]
                zb = (step + s) % 2
                z3 = (step + s) % 3
                S.emit("act", lambda e, zb=zb, z3=z3: e.activation(out=E[z3][:, :], in_=pz[z3][:, :], func=AF.Exp),
                       banks=[BK(f"pz{z3}")], writes=[pfx + f"E{z3}"])
                S.emit("act", lambda e, z3=z3: e.activation(out=L[z3][:, :], in_=E[z3][:, :], func=AF.Ln, bias=one1[:, 0:1]),
                       reads=[pfx + f"E{z3}", pfx + "one1"], writes=[pfx + f"L{z3}"])
                if mk is not None:
                    S.emit("dve", lambda e, z3=z3, mk=mk, b4=b4: e.tensor_tensor(out=L[z3][:, :], in0=L[z3][:, :], in1=mask[:, mk, b4, :], op=ALU.mult),
                           reads=[pfx + f"L{z3}", pfx + "mask"], writes=[pfx + f"L{z3}"])

            def emit_p2(s):
                zb = (step + s) % 2
                z3 = (step + s) % 3
                sls = [slice(hd * 512, (hd + 1) * 512) for hd in range(2)]
                for hd in range(2):
                    S.emit("pe", lambda e, sl=sls[hd], zb=zb, z3=z3, last=(s == 0): e.matmul(
                        pz[z3][:, sl], ntri[:, :], L[z3][:, sl], start=False, stop=last, skip_group_check=True),
                        reads=[pfx + f"L{z3}", pfx + "ntri"], banks=[BK(f"pz{z3}")], ms=(s == 0 and hd == 1))
                if s > 0:
                    lb = (step + s - 1) % 2
                    for hd in range(2):
                        S.emit("pe", lambda e, sl=sls[hd], lb=lb, z3=z3: e.matmul(
                            pz[z3][:, sl], nones[:, :], Lsb[lb][:, sl], start=False, stop=True, skip_group_check=True),
                            reads=[pfx + f"Lsb{lb}", pfx + "nones"], banks=[BK(f"pz{z3}")], ms=(hd == 1))

            def emit_w(s):
                r, kb, mk, b4 = blocks[s]
                zb = (step + s) % 2
                z3 = (step + s) % 3
                S.emit("act", lambda e, zb=zb, z3=z3: e.activation(out=Wt[zb][:, :], in_=pz[z3][:, :], func=AF.Exp),
                       banks=[BK(f"pz{z3}")], writes=[pfx + f"W{zb}"])
                if mk is not None:
                    S.emit("dve", lambda e, zb=zb, mk=mk, b4=b4: e.tensor_tensor(out=Wt[zb][:, :], in0=Wt[zb][:, :], in1=mask[:, mk, b4, :], op=ALU.mult),
                           reads=[pfx + f"W{zb}", pfx + "mask"], writes=[pfx + f"W{zb}"])
                if s + 1 < nb:
                    if s == 0:
                        S.emit("dve", lambda e, z3=z3: e.tensor_copy(out=Ls[:, :], in_=L[z3][:, :]), reads=[pfx + f"L{z3}"], writes=[pfx + "Ls"])
                    else:
                        S.emit("dve", lambda e, z3=z3: e.tensor_tensor(out=Ls[:, :], in0=Ls[:, :], in1=L[z3][:, :], op=ALU.add),
                               reads=[pfx + f"L{z3}", pfx + "Ls"], writes=[pfx + "Ls"])
                    S.emit("dve", lambda e, zb=zb: e.tensor_copy(out=Lsb[zb][:, :], in_=Ls[:, :]), reads=[pfx + "Ls"], writes=[pfx + f"Lsb{zb}"])

            def emit_pv(s):
                r, kb, mk, b4 = blocks[s]
                zb = (step + s) % 2
                for hd in range(2):
                    S.emit("pe", lambda e, hd=hd, r=r, kb=kb, zb=zb, st_=(s == 0), sp_=(s == nb - 1): e.matmul(
                        po[hd * 64:(hd + 1) * 64, :], V_sb[:, r, kb, hd * 64:(hd + 1) * 64], Wt[zb][:, hd * 512:(hd + 1) * 512],
                        start=st_, stop=sp_),
                        reads=[pfx + "V", pfx + f"W{zb}"], banks=[BK("po")], ms=(s == nb - 1 and hd == 1))

            emit_z(0)
            emit_el(0)
            emit_z(1)
            emit_el(1)
            for s in range(nb):
                if s + 2 < nb:
                    emit_z(s + 2)
                    emit_el(s + 2)
                emit_p2(s)
                emit_w(s)
                if s > 0:
                    emit_pv(s - 1)
            emit_pv(nb - 1)
            step += nb
            S.emit("act", lambda e, hp=hp, i=i: e.activation(out=oT[:, hp, i * CH:(i + 1) * CH], in_=po[:, :], func=AF.Identity),
                   banks=[BK("po")], writes=[pfx + f"oT{hp}.{i}"])
    out_toks = []
    it = 0
    for i in range(nq):
        for n in range(KD):
            b = it % 2
            it += 1
            S.dma("sp", pfx + f"lx{b}", lambda e, b=b, n=n, i=i: e.dma_start(out=xr[b][:, :], in_=xT_r[:, n, i * CH:(i + 1) * CH]),
                  writes=[pfx + f"xr{b}"])
            for k in range(nhp):
                S.emit("pe", lambda e, k=k, n=n, i=i: e.matmul(pf[:, :], wo[:, n, k, :], oT[:, k, i * CH:(i + 1) * CH],
                                                              start=(k == 0), stop=(k == nhp - 1)),
                       reads=[pfx + f"oT{k}.{i}", pfx + f"wo{n}"], banks=[BK("pf")], ms=(k == nhp - 1))
            S.emit("dve", lambda e, b=b: e.tensor_tensor(out=xo[b][:, :], in0=pf[:, :], in1=xr[b][:, :], op=ALU.add),
                   reads=[pfx + f"xr{b}"], writes=[pfx + f"xo{b}"], banks=[BK("pf")])
            t = S.dma("sp", pfx + f"so{b}", lambda e, b=b, n=n, i=i: e.dma_start(out=xo_r[:, n, i * CH:(i + 1) * CH], in_=xo[b][:, :]),
                      reads=[pfx + f"xo{b}"], writes=[pfx + f"xoT.{n}.{i}"])
            out_toks.append(t)
    return out_toks


NCORES = 8
SEQ = 8192
BATCH = 4


def _own_tokens(p):
    return np.concatenate([np.arange((2 * i + p) * CH, (2 * i + p + 1) * CH) for i in range(NCHUNK)])


def _halo_tokens(p):
    return np.concatenate([np.arange((2 * i + p) * CH, (2 * i + p) * CH + HALO) for i in range(NCHUNK)])


def _lay_vec(v, n):
    return np.ascontiguousarray(v.reshape(n, 128).T)


def _lay_w(w, kin, nout):
    return np.ascontiguousarray(w.reshape(kin, 128, nout, 128).transpose(2, 1, 0, 3))


def _lay_ffn(g, w_up, dw_w, dw_b, w_down):
    wu = w_up.reshape(KD, 128, 2, NJ, 128)
    return dict(g=_lay_vec(g, KD),
                wup=np.ascontiguousarray(wu.transpose(3, 1, 0, 2, 4).reshape(NJ, 128, KD, 256)),
                wdn=_lay_w(w_down, NJ, KD),
                dw=np.ascontiguousarray(dw_w.reshape(3, 2 * NJ, 128).transpose(2, 1, 0)),
                db=_lay_vec(dw_b, 2 * NJ))


def _lay_conv(g, w_in, a_dw_w, a_dw_b, a_ln_g, a_ln_b, b_dw_w, w_out):
    return dict(g=_lay_vec(g, KD), win=_lay_w(w_in, KD, 20), wout=_lay_w(w_out, KD, KD),
                adw=np.ascontiguousarray(a_dw_w.reshape(KA, MA, 128).transpose(2, 1, 0)),
                avec=np.ascontiguousarray(np.stack([a_dw_b, a_ln_g, a_ln_b]).reshape(3, MA, 128).transpose(2, 0, 1)),
                bdw=np.ascontiguousarray(b_dw_w.reshape(3, MA, 128).transpose(2, 1, 0)))


def _lay_attn(g, w_qkv, q_g, k_g, w_o):
    return dict(g=_lay_vec(g, KD), wq=_lay_w(w_qkv[:, :D], KD, HP), wk=_lay_w(w_qkv[:, D:2 * D], KD, HP),
                wv=np.ascontiguousarray(w_qkv[:, 2 * D:].reshape(KD, 128, 2, 512).transpose(2, 1, 0, 3)),
                qkg=np.ascontiguousarray(np.stack([np.tile(q_g, 2), np.tile(k_g, 2)], 1)),
                wo=_lay_w(w_o, KD, KD))


def _masks(p):
    ks = np.arange(CH)[:, None]
    tq = np.arange(CH)[None, :]
    diag = (ks < tq).astype(np.float32)
    A = diag if p == 0 else np.ones((CH, CH), np.float32)
    B = np.zeros((CH, CH), np.float32) if p == 0 else diag
    m = np.stack([A, B]).reshape(2, 4, 128, CH).transpose(2, 0, 1, 3)
    return np.ascontiguousarray(np.concatenate([m, m], -1))


def _tri():
    j = np.arange(128)[:, None]
    s = np.arange(128)[None, :]
    return -(j >= s).astype(np.float32)


_PROGS = {}


def _dram(nc, name, shape, dt=F32, kind="ExternalInput"):
    return nc.dram_tensor(name, list(shape), dt, kind=kind).ap()


def _prog(kind):
    if kind in _PROGS:
        return _PROGS[kind]
    nc = bass.Bass("TRN2", target_bir_lowering=False)
    S = Sched()
    with ExitStack() as st:
        cx = Ctx(nc, S, st)
        if kind == "conv":
            a = [_dram(nc, "xT", [D, TOK]), _dram(nc, "xh", [D, NCHUNK * HALO])]
            xo = _dram(nc, "xoT", [D, TOK], F32, "ExternalOutput")
            toks = conv_phase(cx, "c.", a[0], a[1], xo, _dram(nc, "g", [128, KD]), _dram(nc, "win", [20, 128, KD, 128]),
                              _dram(nc, "wout", [KD, 128, KD, 128]), _dram(nc, "adw", [128, MA, KA]),
                              _dram(nc, "avec", [128, 3, MA]), _dram(nc, "bdw", [128, MA, 3]))
        elif kind == "ffn":
            a = [_dram(nc, "xT", [D, TOK]), _dram(nc, "xh", [D, NCHUNK * HALO])]
            xo = _dram(nc, "xoT", [D, TOK], F32, "ExternalOutput")
            toks = ffn_phase(cx, "f.", a[0], a[1], xo, _dram(nc, "g", [128, KD]), _dram(nc, "wup", [NJ, 128, KD, 256]),
                             _dram(nc, "wdn", [KD, 128, NJ, 128]), _dram(nc, "dw", [128, 2 * NJ, 3]), _dram(nc, "db", [128, 2 * NJ]))
        elif kind == "qkv":
            toks = qkv_phase(cx, "q.", _dram(nc, "xT", [D, TOK]), _dram(nc, "g", [128, KD]), _dram(nc, "wq", [HP, 128, KD, 128]),
                             _dram(nc, "wk", [HP, 128, KD, 128]), _dram(nc, "wv", [2, 128, KD, 512]), _dram(nc, "qkg", [128, 2]),
                             _dram(nc, "qT", [D, TOK], BF16, "ExternalOutput"), _dram(nc, "kT", [D, TOK], BF16, "ExternalOutput"),
                             _dram(nc, "V", [HP, 128, NKB, 128], BF16, "ExternalOutput"))
        elif kind == "attn":
            toks = attn_phase(cx, "a.", _dram(nc, "xT", [D, TOK]), _dram(nc, "xoT", [D, TOK], F32, "ExternalOutput"),
                              _dram(nc, "qT", [D, TOK], BF16), _dram(nc, "kT", [2, D, TOK], BF16),
                              _dram(nc, "V", [2, HP, 128, NKB, 128], BF16), _dram(nc, "mask", [128, 2, 4, 1024]),
                              _dram(nc, "tri", [128, 128]), _dram(nc, "wo", [KD, 128, KD, 128]))
        S.wait_all("sp", toks)
        S.build(nc, st)
    _PROGS[kind] = nc
    return nc


def _run(kind, in_maps):
    res = run_bass_kernel_spmd(_prog(kind), in_maps, core_ids=list(range(NCORES)))
    return res.results


def kernel_unfused(**inputs):
    inp = {k: np.asarray(v) for k, v in inputs.items()}
    x = inp["x"]
    own = [_own_tokens(p) for p in range(2)]
    halo = [_halo_tokens(p) for p in range(2)]
    xT = [np.ascontiguousarray(x[c // 2][own[c % 2]].T) for c in range(NCORES)]

    def halos(xT):
        out = []
        for b in range(BATCH):
            full = np.zeros((D, HALO + SEQ), np.float32)
            for p in range(2):
                for i in range(NCHUNK):
                    g0 = (2 * i + p) * CH
                    full[:, HALO + g0:HALO + g0 + CH] = xT[2 * b + p][:, i * CH:(i + 1) * CH]
            for p in range(2):
                out.append(np.ascontiguousarray(full[:, halo[p]]))
        return out

    masks = [_masks(p) for p in range(2)]
    tri = _tri()
    for layer in range(4):
        i = layer // 2
        if layer % 2 == 0:
            lw = _lay_conv(inp["mix_norm_g"][layer], inp["conv_w_in"][i], inp["conv_a_dw_w"][i], inp["conv_a_dw_b"][i],
                           inp["conv_a_ln_g"][i], inp["conv_a_ln_b"][i], inp["conv_b_dw_w"][i], inp["conv_w_out"][i])
            xh = halos(xT)
            r = _run("conv", [dict(lw, xT=xT[c], xh=xh[c]) for c in range(NCORES)])
            xT = [r[c]["xoT"] for c in range(NCORES)]
        else:
            lw = _lay_attn(inp["mix_norm_g"][layer], inp["attn_w_qkv"][i], inp["attn_q_g"][i], inp["attn_k_g"][i], inp["attn_w_o"][i])
            r = _run("qkv", [dict(xT=xT[c], g=lw["g"], wq=lw["wq"], wk=lw["wk"], wv=lw["wv"], qkg=lw["qkg"]) for c in range(NCORES)])
            ins = []
            for c in range(NCORES):
                b = c // 2
                kT_g = np.stack([r[2 * b]["kT"], r[2 * b + 1]["kT"]])
                V_g = np.stack([r[2 * b]["V"], r[2 * b + 1]["V"]])
                ins.append(dict(xT=xT[c], qT=r[c]["qT"], kT=kT_g, V=V_g, mask=masks[c % 2], tri=tri, wo=lw["wo"]))
            r = _run("attn", ins)
            xT = [r[c]["xoT"] for c in range(NCORES)]
        lw = _lay_ffn(inp["ffn_norm_g"][layer], inp["ffn_w_up"][layer], inp["ffn_dw_w"][layer], inp["ffn_dw_b"][layer], inp["ffn_w_down"][layer])
        xh = halos(xT)
        r = _run("ffn", [dict(lw, xT=xT[c], xh=xh[c]) for c in range(NCORES)])
        xT = [r[c]["xoT"] for c in range(NCORES)]
    out = np.empty((BATCH, SEQ, D), np.float32)
    for c in range(NCORES):
        out[c // 2][own[c % 2]] = xT[c].T
    return out


PAIRS = [[0, 1], [2, 3], [4, 5], [6, 7]]


def halo_phase(cx, pfx, xT, tl, tg, xh, sel_d):
    S, nc = cx.S, cx.nc
    t_sb = cx.sb(pfx + "t", [128, KD, NCHUNK, HALO], F32)
    c0 = cx.sb(pfx + "c0", [128, KD, NCHUNK, HALO], F32)
    c1 = cx.sb(pfx + "c1", [128, KD, NCHUNK, HALO], F32)
    sel = cx.sb(pfx + "sel", [128, 2], F32)
    xT_r = xT.rearrange("(k p) (c t) -> p k c t", p=128, t=CH)
    tl_r = tl.rearrange("(k p) (c h) -> p k c h", p=128, h=HALO)
    tg_r = tg.rearrange("(r k p) (c h) -> p r k c h", p=128, k=KD, h=HALO)
    xh_r = xh.rearrange("(k p) (c h) -> p k c h", p=128, h=HALO)
    S.dma("sp", "cst", lambda e: e.dma_start(out=sel[:], in_=sel_d), writes=[pfx + "sel"])
    S.dma_group("sp", "h0", [(lambda e, k=k: e.dma_start(out=t_sb[:, k], in_=xT_r[:, k, :, CH - HALO:CH]), [], [pfx + f"t{k}"])
                             for k in range(KD)])
    S.dma_group("sp", "h1", [(lambda e, k=k: e.dma_start(out=tl_r[:, k], in_=t_sb[:, k]), [pfx + f"t{k}"], [pfx + f"tl{k}"])
                             for k in range(KD)])
    S.dma("pool", "cc", lambda e: e.collective_compute("AllGather", ALU.bypass, replica_groups=PAIRS, ins=[tl], outs=[tg]),
          reads=[pfx + f"tl{k}" for k in range(KD)], writes=[pfx + "tg"], inc=1)
    S.emit("dve", lambda e: e.memset(c0[:, :, 0, :], 0.0), writes=[pfx + "c0z"])
    S.dma_group("sp", "h2", [(lambda e, k=k: e.dma_start(out=c0[:, k, 1:NCHUNK, :], in_=tg_r[:, 1, k, 0:NCHUNK - 1, :]), [pfx + "tg"], [pfx + f"c0{k}"])
                             for k in range(KD)] +
                            [(lambda e, k=k: e.dma_start(out=c1[:, k], in_=tg_r[:, 0, k]), [pfx + "tg"], [pfx + f"c1{k}"])
                             for k in range(KD)])
    items = []
    for k in range(KD):
        S.emit("dve", lambda e, k=k: e.tensor_scalar(out=c0[:, k], in0=c0[:, k], scalar1=sel[:, 0:1], scalar2=None, op0=ALU.mult),
               reads=[pfx + f"c0{k}", pfx + "c0z", pfx + "sel"], writes=[pfx + f"c0{k}"])
        S.emit("dve", lambda e, k=k: e.scalar_tensor_tensor(out=c1[:, k], in0=c1[:, k], scalar=sel[:, 1:2], in1=c0[:, k],
                                                            op0=ALU.mult, op1=ALU.add),
               reads=[pfx + f"c0{k}", pfx + f"c1{k}", pfx + "sel"], writes=[pfx + f"c1{k}"])
        items.append((lambda e, k=k: e.dma_start(out=xh_r[:, k], in_=c1[:, k]), [pfx + f"c1{k}"], [pfx + f"xh{k}"]))
    t = S.dma_group("sp", "h3", items)
    return [t]


def gather_phase(cx, pfx, kT_l, kT_g, V_l, V_g):
    S = cx.S
    toks = []
    for hp in range(HP):
        for nm, a, b in (("k", kT_l, kT_g), ("v", V_l, V_g)):
            toks.append(S.dma("pool", "cc", lambda e, hp=hp, a=a, b=b: e.collective_compute(
                "AllGather", ALU.bypass, replica_groups=PAIRS, ins=[a[hp * 128:(hp + 1) * 128, :]], outs=[b[hp * 256:(hp + 1) * 256, :]]),
                writes=[pfx + f"{nm}g{hp}"], inc=1))
    return toks[-1:]


_FUSED = []


def _fused_prog():
    if _FUSED:
        return _FUSED[0]
    nc = bass.Bass("TRN2", target_bir_lowering=False)
    S = Sched()
    with ExitStack() as outer:
        x_in = _dram(nc, "xT", [D, TOK])
        out = _dram(nc, "out", [D, TOK], F32, "ExternalOutput")
        sel_d = _dram(nc, "sel", [128, 2])
        mask_d = _dram(nc, "mask", [128, 2, 4, 1024])
        tri_d = _dram(nc, "tri", [128, 128])
        xs = [nc.dram_tensor(f"xs{i}", [D, TOK], F32).ap() for i in range(2)]
        tl = nc.dram_tensor("tl", [D, NCHUNK * HALO], F32).ap()
        tg = nc.dram_tensor("tg", [2 * D, NCHUNK * HALO], F32).ap()
        xh = nc.dram_tensor("xh", [D, NCHUNK * HALO], F32).ap()
        qT = nc.dram_tensor("qT", [D, TOK], BF16).ap()
        kT_l = nc.dram_tensor("kTl", [D, TOK], BF16).ap()
        kT_g = nc.dram_tensor("kTg", [2 * D, TOK], BF16).ap()
        V_l = nc.dram_tensor("Vl", [HP * 128, NKB * 128], BF16).ap()
        V_g = nc.dram_tensor("Vg", [2 * HP * 128, NKB * 128], BF16).ap()
        V_l5 = V_l.rearrange("(h p) (k f) -> h p k f", p=128, f=128)

        def phase(fn):
            with ExitStack() as pst:
                cx = Ctx(nc, S, pst)
                toks = fn(cx)
                S.barrier(toks)
                S.build_phase(nc, pst, outer)

        cur = x_in
        nxt = 0
        for layer in range(4):
            L = f"L{layer}."
            if layer % 2 == 0:
                phase(lambda cx: halo_phase(cx, L + "h.", cur, tl, tg, xh, sel_d))
                dst = xs[nxt]
                phase(lambda cx: conv_phase(cx, L + "c.", cur, xh, dst, _dram(nc, L + "mg", [128, KD]), _dram(nc, L + "win", [20, 128, KD, 128]),
                                            _dram(nc, L + "wout", [KD, 128, KD, 128]), _dram(nc, L + "adw", [128, MA, KA]),
                                            _dram(nc, L + "avec", [128, 3, MA]), _dram(nc, L + "bdw", [128, MA, 3])))
            else:
                phase(lambda cx: qkv_phase(cx, L + "q.", cur, _dram(nc, L + "mg", [128, KD]), _dram(nc, L + "wq", [HP, 128, KD, 128]),
                                           _dram(nc, L + "wk", [HP, 128, KD, 128]), _dram(nc, L + "wv", [2, 128, KD, 512]),
                                           _dram(nc, L + "qkg", [128, 2]), qT, kT_l, V_l5))
                phase(lambda cx: gather_phase(cx, L + "g.", kT_l, kT_g, V_l, V_g))
                dst = xs[nxt]
                phase(lambda cx: attn_phase(cx, L + "a.", cur, dst, qT, kT_g, V_g, mask_d, tri_d, _dram(nc, L + "wo", [KD, 128, KD, 128])))
            cur = dst
            nxt ^= 1
            phase(lambda cx: halo_phase(cx, L + "hf.", cur, tl, tg, xh, sel_d))
            dst = out if layer == 3 else xs[nxt]
            phase(lambda cx: ffn_phase(cx, L + "f.", cur, xh, dst, _dram(nc, L + "fg", [128, KD]), _dram(nc, L + "wup", [NJ, 128, KD, 256]),
                                       _dram(nc, L + "wdn", [KD, 128, NJ, 128]), _dram(nc, L + "dw", [128, 2 * NJ, 3]),
                                       _dram(nc, L + "db", [128, 2 * NJ])))
            cur = dst
            nxt ^= 1
    _FUSED.append((nc, S))
    return _FUSED[0]


def kernel(**inputs):
    inp = {k: np.asarray(v) for k, v in inputs.items()}
    x = inp["x"]
    own = [_own_tokens(p) for p in range(2)]
    base = {"tri": _tri()}
    for layer in range(4):
        i = layer // 2
        L = f"L{layer}."
        if layer % 2 == 0:
            lw = _lay_conv(inp["mix_norm_g"][layer], inp["conv_w_in"][i], inp["conv_a_dw_w"][i], inp["conv_a_dw_b"][i],
                           inp["conv_a_ln_g"][i], inp["conv_a_ln_b"][i], inp["conv_b_dw_w"][i], inp["conv_w_out"][i])
        else:
            lw = _lay_attn(inp["mix_norm_g"][layer], inp["attn_w_qkv"][i], inp["attn_q_g"][i], inp["attn_k_g"][i], inp["attn_w_o"][i])
        lw["mg"] = lw.pop("g")
        lf = _lay_ffn(inp["ffn_norm_g"][layer], inp["ffn_w_up"][layer], inp["ffn_dw_w"][layer], inp["ffn_dw_b"][layer], inp["ffn_w_down"][layer])
        lf["fg"] = lf.pop("g")
        for k, v in list(lw.items()) + list(lf.items()):
            base[L + k] = v
    masks = [_masks(p) for p in range(2)]
    sels = [np.ascontiguousarray(np.tile(np.array([[1.0, 0.0]], np.float32) if p == 0 else np.array([[0.0, 1.0]], np.float32), (128, 1)))
            for p in range(2)]
    in_maps = []
    for c in range(NCORES):
        d = dict(base)
        d["xT"] = np.ascontiguousarray(x[c // 2][own[c % 2]].T)
        d["mask"] = masks[c % 2]
        d["sel"] = sels[c % 2]
        in_maps.append(d)
    nc, _ = _fused_prog()
    res = run_bass_kernel_spmd(nc, in_maps, core_ids=list(range(NCORES))).results
    out = np.empty((BATCH, SEQ, D), np.float32)
    for c in range(NCORES):
        out[c // 2][own[c % 2]] = res[c]["out"].T
    return out
```

```python
import numpy as np
from contextlib import ExitStack
import concourse.bass as bass
import concourse.mybir as mybir
from concourse.bass_utils import run_bass_kernel_spmd

F32 = mybir.dt.float32
BF16 = mybir.dt.bfloat16
AF = mybir.ActivationFunctionType
ALU = mybir.AluOpType

ENGINES = ("pe", "act", "dve", "pool", "sp")
ENG_ATTR = {"pe": "tensor", "act": "scalar", "dve": "vector", "pool": "gpsimd", "sp": "sync"}


class _Op:
    __slots__ = ("fn", "waits", "inc")

    def __init__(self, fn):
        self.fn = fn
        self.waits = []
        self.inc = None


class Sched:
    SEM_ROT = 20000

    def __init__(self):
        self.ops = {e: [] for e in ENGINES}
        self.built = {e: 0 for e in ENGINES}
        self.ms = {e: [] for e in ENGINES}
        self.gen = {e: 0 for e in ENGINES}
        self.cnt = {}
        self.waited = {e: {} for e in ENGINES}
        self.last_w = {}
        self.readers = {}
        self.nwaits = 0
        self.sems = {}

    def _new_ms(self, e, seq, op):
        k = ("eng", e, self.gen[e])
        if self.cnt.get(k, 0) >= self.SEM_ROT:
            self.gen[e] += 1
            k = ("eng", e, self.gen[e])
        self.cnt[k] = self.cnt.get(k, 0) + 1
        op.inc = (k, 1)
        self.ms[e].append((seq, k, self.cnt[k]))
        return k, self.cnt[k]

    def _milestone(self, e, seq):
        lo, hi = 0, len(self.ms[e])
        while lo < hi:
            mid = (lo + hi) // 2
            if self.ms[e][mid][0] >= seq:
                hi = mid
            else:
                lo = mid + 1
        if lo < len(self.ms[e]):
            return self.ms[e][lo][1], self.ms[e][lo][2]
        last = len(self.ops[e]) - 1
        while self.ops[e][last].fn is None:
            last -= 1
        assert last >= seq and last >= self.built[e], (e, seq, last, self.built[e])
        op = self.ops[e][last]
        assert op.inc is None, f"last op on {e} already has an inc"
        return self._new_ms(e, last, op)

    def _resolve(self, tok):
        if tok[0] == "eng":
            return self._milestone(tok[1], tok[2])
        return tok[1], tok[2]

    def _deps(self, engine, reads, writes):
        toks = []
        for k in reads:
            t = self.last_w.get(k)
            if t is not None:
                toks.append(t)
        for k in writes:
            t = self.last_w.get(k)
            if t is not None and not (t[0] == "eng" and t[1] == engine == "pe"):
                toks.append(t)
            for t in self.readers.get(k, {}).values():
                if not (t[0] == "eng" and t[1] == engine):
                    toks.append(t)
        return self._waits(engine, toks)

    def _waits(self, engine, toks):
        waits = {}
        for t in toks:
            sk, v = self._resolve(t)
            if self.waited[engine].get(sk, 0) >= v:
                continue
            waits[sk] = max(waits.get(sk, 0), v)
        for sk, v in waits.items():
            self.waited[engine][sk] = v
        self.nwaits += len(waits)
        return list(waits.items())

    def _track(self, tok, rkey, reads, writes):
        for k in writes:
            self.last_w[k] = tok
            self.readers[k] = {}
        for k in reads:
            self.readers.setdefault(k, {})[rkey] = tok

    def emit(self, engine, fn, reads=(), writes=(), ms=False, banks=()):
        writes = list(writes) + list(banks)
        op = _Op(fn)
        op.waits = self._deps(engine, reads, writes)
        seq = len(self.ops[engine])
        self.ops[engine].append(op)
        if ms or engine != "pe":
            self._new_ms(engine, seq, op)
        tok = ("eng", engine, seq)
        self._track(tok, engine, reads, writes)
        return tok

    def dma(self, queue, sem, fn, reads=(), writes=(), inc=16):
        op = _Op(fn)
        op.waits = self._deps(queue, reads, writes)
        self.ops[queue].append(op)
        k = ("dma", sem.split(".", 1)[-1])
        self.cnt[k] = self.cnt.get(k, 0) + inc
        op.inc = (k, inc)
        tok = ("dma", k, self.cnt[k])
        self._track(tok, k, reads, writes)
        return tok

    def dma_group(self, queue, sem, items):
        toks = [self.dma(queue, sem, fn, reads, writes) for (fn, reads, writes) in items]
        final = toks[-1]
        for (fn, reads, writes) in items:
            for k in writes:
                self.last_w[k] = final
            for k in reads:
                self.readers.setdefault(k, {})[final[1]] = final
        return final

    def wait_all(self, engine, toks):
        op = _Op(None)
        op.waits = self._waits(engine, toks)
        self.ops[engine].append(op)

    def barrier(self, toks=()):
        toks = list(toks)
        for e in ("pe", "act", "dve", "pool"):
            last = len(self.ops[e]) - 1
            while last >= self.built[e] and (self.ops[e][last].fn is None or self.ops[e][last].inc is not None and self.ops[e][last].inc[0][0] == "dma"):
                last -= 1
            if last >= self.built[e]:
                toks.append(("eng", e, last))
        for e in ENGINES:
            self.wait_all(e, toks)

    def build_phase(self, nc, st, outer):
        block = st.enter_context(nc.Block())
        for e in ENGINES:
            deco = getattr(block, ENG_ATTR[e])

            def body(eng, e=e):
                for op in self.ops[e][self.built[e]:]:
                    for (sk, v) in op.waits:
                        eng.wait_ge(self._sem(nc, outer, sk), v)
                    if op.fn is None:
                        continue
                    ins = op.fn(eng)
                    if op.inc is not None:
                        ins.then_inc(self._sem(nc, outer, op.inc[0]), op.inc[1])
                    op.fn = None
                self.built[e] = len(self.ops[e])

            deco(body)

    def _sem(self, nc, outer, k):
        if k not in self.sems:
            self.sems[k] = outer.enter_context(nc.semaphore(f"s{len(self.sems)}"))
        return self.sems[k]

    def build(self, nc, st):
        self.build_phase(nc, st, st)


D = 1024
KD = D // 128
CH = 512
NCHUNK = 8
TOK = CH * NCHUNK
HALO = 32
DFF = 2816
NJ = DFF // 128
EPS = 1e-6


class Ctx:
    def __init__(self, nc, S, st):
        self.nc, self.S, self.st = nc, S, st
        self.n = 0

    def sb(self, name, shape, dt):
        return self.st.enter_context(self.nc.sbuf_tensor(name, list(shape), dt))

    def ps(self, name, shape=(128, 512), dt=F32):
        return self.st.enter_context(self.nc.psum_tensor(name, list(shape), dt))


def emit_rmsnorm(cx, pfx, x_k, h_k, xkeys, hkeys, ncols, sq, ones, pst, pst_bank, tmp, rstd, g_sb, gkey):
    S = cx.S
    for k in range(KD):
        S.emit("act", lambda e, k=k: e.activation(out=sq[:, k, 0:ncols], in_=x_k(k), func=AF.Square),
               reads=[xkeys[k]], writes=[pfx + f"sq{k}"])
    for k in range(KD):
        S.emit("pe", lambda e, k=k: e.matmul(pst, ones[:, :], sq[:, k, 0:ncols], start=(k == 0), stop=(k == KD - 1)),
               reads=[pfx + f"sq{k}", pfx + "ones"], banks=[pst_bank], ms=(k == KD - 1))
    S.emit("dve", lambda e: e.tensor_scalar(out=tmp[:, 0:ncols], in0=pst, scalar1=1.0 / D, scalar2=EPS,
                                            op0=ALU.mult, op1=ALU.add), banks=[pst_bank], writes=[pfx + "tmp"])
    S.emit("act", lambda e: e.activation(out=tmp[:, 0:ncols], in_=tmp[:, 0:ncols], func=AF.Sqrt),
           reads=[pfx + "tmp"], writes=[pfx + "tmp"])
    S.emit("dve", lambda e: e.reciprocal(out=rstd[:, 0:ncols], in_=tmp[:, 0:ncols]), reads=[pfx + "tmp"], writes=[pfx + "rstd"])
    for k in range(KD):
        S.emit("dve", lambda e, k=k: e.scalar_tensor_tensor(out=h_k(k), in0=x_k(k), scalar=g_sb[:, k:k + 1], in1=rstd[:, 0:ncols],
                                                            op0=ALU.mult, op1=ALU.mult),
               reads=[xkeys[k], pfx + "rstd", gkey], writes=[hkeys[k]])


def ffn_phase(cx, pfx, xT, xh, xoT, g_d, wup_d, wdn_d, dw_d, db_d, nst=NCHUNK // 2):
    S, nc = cx.S, cx.nc
    HH = 2
    W = HH + CH
    SC = 2
    x_sb = cx.sb(pfx + "x", [128, KD, SC, W], F32)
    hT = cx.sb(pfx + "hT", [128, KD, SC, W], BF16)
    sq = cx.sb(pfx + "sq", [128, KD, CH], BF16)
    rstd = cx.sb(pfx + "rstd", [128, CH], F32)
    tmp = cx.sb(pfx + "tmp", [128, CH], F32)
    ones = cx.sb(pfx + "ones", [128, 128], BF16)
    g_sb = cx.sb(pfx + "g", [128, KD], F32)
    dw_sb = cx.sb(pfx + "dw", [128, 2 * NJ, 3], F32)
    db_sb = cx.sb(pfx + "db", [128, 2 * NJ], F32)
    wup = [cx.sb(pfx + f"wup{i}", [128, KD, 256], BF16) for i in range(2)]
    wdn = [cx.sb(pfx + f"wdn{i}", [128, NJ, 128], BF16) for i in range(2)]
    gT = cx.sb(pfx + "gT", [128, NJ, SC, CH], BF16)
    gacc = [cx.sb(pfx + f"gacc{i}", [128, CH], F32) for i in range(2)]
    vacc = [cx.sb(pfx + f"vacc{i}", [128, CH], F32) for i in range(2)]
    sg = [cx.sb(pfx + f"sg{i}", [128, CH], F32) for i in range(2)]
    phs = [cx.sb(pfx + f"phs{i}", [128, 4], F32) for i in range(2)]
    xo = [cx.sb(pfx + f"xo{i}", [128, CH], F32) for i in range(2)]
    pg = [cx.ps(pfx + f"pg{i}") for i in range(2)]
    pv = [cx.ps(pfx + f"pv{i}") for i in range(2)]
    pmisc = cx.ps(pfx + "pmisc")
    pstat = cx.ps(pfx + "pstat")
    pd = [cx.ps(pfx + f"pd{i}") for i in range(2)]
    ph = [pmisc[:, 0:4], pstat[:, 0:4]]
    phk = [pfx + "pmisc", pfx + "pstat"]
    pstat_h = pmisc[:, 32:32 + HH]

    xT_r = xT.rearrange("(k p) t -> p k t", p=128)
    xh_r = xh.rearrange("(k p) (c h) -> p k c h", p=128, h=HALO)
    xo_r = xoT.rearrange("(k p) t -> p k t", p=128)

    S.emit("dve", lambda e: e.memset(ones[:], 1.0), writes=[pfx + "ones"])
    S.dma_group("sp", "cst", [(lambda e: e.dma_start(out=g_sb[:], in_=g_d), [], [pfx + "n.g"]),
                              (lambda e: e.dma_start(out=dw_sb[:], in_=dw_d), [], [pfx + "dw"]),
                              (lambda e: e.dma_start(out=db_sb[:], in_=db_d), [], [pfx + "db"])])
    out_toks = []
    wu_i = 0
    wd_i = 0
    it = 0
    for sti in range(nst):
        for c in range(SC):
            gc = sti * SC + c
            S.dma_group("sp", f"lx{c}", [(lambda e, c=c, k=k, gc=gc: e.dma_start(out=x_sb[:, k, c, HH:W], in_=xT_r[:, k, gc * CH:(gc + 1) * CH]),
                                          [], [pfx + f"x{c}.{k}m"]) for k in range(KD)])
            S.dma("sp", pfx + f"lxh{c}",
                  lambda e, c=c, gc=gc: e.dma_start(out=x_sb[:, :, c, 0:HH], in_=xh_r[:, :, gc, HALO - HH:HALO]),
                  writes=[pfx + f"x{c}.h"])
        for c in range(SC):
            xk = [pfx + f"x{c}.{k}m" for k in range(KD)]
            for k in range(KD):
                S.emit("act", lambda e, k=k, c=c: e.activation(out=sq[:, k, :], in_=x_sb[:, k, c, HH:W], func=AF.Square),
                       reads=[xk[k]], writes=[pfx + f"sq{k}"])
            for k in range(KD):
                S.emit("pe", lambda e, k=k: e.matmul(pstat[:, :], ones[:, :], sq[:, k, :], start=(k == 0), stop=(k == KD - 1)),
                       reads=[pfx + f"sq{k}", pfx + "ones"], banks=[pfx + "pstat"], ms=(k == KD - 1))
            S.emit("dve", lambda e: e.tensor_scalar(out=tmp[:, :], in0=pstat[:, :], scalar1=1.0 / D, scalar2=EPS,
                                                    op0=ALU.mult, op1=ALU.add),
                   banks=[pfx + "pstat"], writes=[pfx + "tmp"])
            S.emit("act", lambda e: e.activation(out=tmp[:, :], in_=tmp[:, :], func=AF.Sqrt),
                   reads=[pfx + "tmp"], writes=[pfx + "tmp"])
            S.emit("dve", lambda e: e.reciprocal(out=rstd[:, :], in_=tmp[:, :]), reads=[pfx + "tmp"], writes=[pfx + "rstd"])
            for k in range(KD):
                S.emit("dve", lambda e, k=k, c=c: e.scalar_tensor_tensor(
                    out=hT[:, k, c, HH:W], in0=x_sb[:, k, c, HH:W], scalar=g_sb[:, k:k + 1], in1=rstd[:, :],
                    op0=ALU.mult, op1=ALU.mult),
                    reads=[xk[k], pfx + "rstd", pfx + "n.g"], writes=[pfx + f"h{c}.{k}m"])
            S.emit("act", lambda e, c=c: e.activation(out=sq[:, :, 0:HH], in_=x_sb[:, :, c, 0:HH], func=AF.Square),
                   reads=[pfx + f"x{c}.h"], writes=[pfx + f"sq{k}" for k in range(KD)])
            for k in range(KD):
                S.emit("pe", lambda e, k=k: e.matmul(pstat_h, ones[:, :], sq[:, k, 0:HH], start=(k == 0), stop=(k == KD - 1)),
                       reads=[pfx + f"sq{k}", pfx + "ones"], banks=[pfx + "pmisc"], ms=(k == KD - 1))
            S.emit("dve", lambda e: e.tensor_scalar(out=tmp[:, 0:HH], in0=pstat_h, scalar1=1.0 / D, scalar2=EPS,
                                                    op0=ALU.mult, op1=ALU.add),
                   banks=[pfx + "pmisc"], writes=[pfx + "tmp"])
            S.emit("act", lambda e: e.activation(out=tmp[:, 0:HH], in_=tmp[:, 0:HH], func=AF.Sqrt),
                   reads=[pfx + "tmp"], writes=[pfx + "tmp"])
            S.emit("dve", lambda e: e.reciprocal(out=rstd[:, 0:HH], in_=tmp[:, 0:HH]), reads=[pfx + "tmp"], writes=[pfx + "rstd"])
            for k in range(KD):
                S.emit("dve", lambda e, k=k, c=c: e.scalar_tensor_tensor(
                    out=hT[:, k, c, 0:HH], in0=x_sb[:, k, c, 0:HH], scalar=g_sb[:, k:k + 1], in1=rstd[:, 0:HH],
                    op0=ALU.mult, op1=ALU.mult),
                    reads=[pfx + f"x{c}.h", pfx + "rstd", pfx + "n.g"], writes=[pfx + f"h{c}.{k}h"])
        def load_wup(j, slot):
            S.dma("pool", pfx + f"wu{slot}", lambda e, j=j, slot=slot: e.dma_start(out=wup[slot][:], in_=wup_d[j]),
                  writes=[pfx + f"wup{slot}"])

        def load_wdn(n, slot):
            S.dma("pool", pfx + f"wd{slot}", lambda e, n=n, slot=slot: e.dma_start(out=wdn[slot][:], in_=wdn_d[n]),
                  writes=[pfx + f"wdn{slot}"])

        load_wup(0, wu_i % 2)
        for j in range(NJ):
            slot = wu_i % 2
            if j + 1 < NJ:
                load_wup(j + 1, (wu_i + 1) % 2)
            else:
                load_wdn(0, wd_i % 2)
            wu_i += 1
            for c in range(SC):
                b = it % 2
                it += 1
                hk = [pfx + f"h{c}.{k}m" for k in range(KD)]
                hh = [pfx + f"h{c}.{k}h" for k in range(KD)]
                for half, (pp, acc, jc) in enumerate(((pg[b], gacc[b], j), (pv[b], vacc[b], NJ + j))):
                    co = half * 128
                    pk = pfx + f"p{half}{b}"
                    ak = pfx + f"acc{half}{b}"
                    for k in range(KD):
                        S.emit("pe", lambda e, k=k, c=c, pp=pp, co=co, slot=slot: e.matmul(
                            pp[:, :], wup[slot][:, k, co:co + 128], hT[:, k, c, HH:W], start=(k == 0), stop=(k == KD - 1)),
                            reads=[hk[k], pfx + f"wup{slot}"], banks=[pk], ms=(k == KD - 1))
                    for k in range(KD):
                        S.emit("pe", lambda e, k=k, c=c, co=co, slot=slot, b=b, half=half: e.matmul(
                            ph[b][:, 2 * half:2 * half + 2], wup[slot][:, k, co:co + 128], hT[:, k, c, 0:HH],
                            start=(k == 0), stop=(k == KD - 1)),
                            reads=[hh[k], pfx + f"wup{slot}"], banks=[phk[b]], ms=(k == KD - 1))
                    S.emit("act", lambda e, pp=pp, acc=acc, jc=jc: e.activation(
                        out=acc[:, :], in_=pp[:, :], func=AF.Identity, scale=dw_sb[:, jc, 2:3], bias=db_sb[:, jc:jc + 1]),
                        reads=[pfx + "dw", pfx + "db"], writes=[ak], banks=[pk])
                    S.emit("dve", lambda e, pp=pp, acc=acc, jc=jc: e.scalar_tensor_tensor(
                        out=acc[:, 1:CH], in0=pp[:, 0:CH - 1], scalar=dw_sb[:, jc, 1:2], in1=acc[:, 1:CH],
                        op0=ALU.mult, op1=ALU.add), reads=[ak, pfx + "dw"], writes=[ak], banks=[pk])
                    S.emit("dve", lambda e, pp=pp, acc=acc, jc=jc: e.scalar_tensor_tensor(
                        out=acc[:, 2:CH], in0=pp[:, 0:CH - 2], scalar=dw_sb[:, jc, 0:1], in1=acc[:, 2:CH],
                        op0=ALU.mult, op1=ALU.add), reads=[ak, pfx + "dw"], writes=[ak], banks=[pk])
                for half, (acc, jc) in enumerate(((gacc[b], j), (vacc[b], NJ + j))):
                    ak = pfx + f"acc{half}{b}"
                    S.emit("dve", lambda e, acc=acc, jc=jc, b=b, half=half: e.scalar_tensor_tensor(
                        out=acc[:, 0:2], in0=ph[b][:, 2 * half:2 * half + 2], scalar=dw_sb[:, jc, 0:1], in1=acc[:, 0:2],
                        op0=ALU.mult, op1=ALU.add), reads=[ak, pfx + "dw"], writes=[ak], banks=[phk[b]])
                    S.emit("dve", lambda e, acc=acc, jc=jc, b=b, half=half: e.scalar_tensor_tensor(
                        out=acc[:, 0:1], in0=ph[b][:, 2 * half + 1:2 * half + 2], scalar=dw_sb[:, jc, 1:2], in1=acc[:, 0:1],
                        op0=ALU.mult, op1=ALU.add), reads=[ak, pfx + "dw"], writes=[ak], banks=[phk[b]])
                S.emit("act", lambda e, b=b: e.activation(out=sg[b][:, :], in_=gacc[b][:, :], func=AF.Silu),
                       reads=[pfx + f"acc0{b}"], writes=[pfx + f"sg{b}"])
                S.emit("dve", lambda e, b=b, j=j, c=c: e.tensor_tensor(out=gT[:, j, c, :], in0=sg[b][:, :], in1=vacc[b][:, :],
                                                                      op=ALU.mult),
                       reads=[pfx + f"sg{b}", pfx + f"acc1{b}"], writes=[pfx + f"gT{j}.{c}"])
        for n in range(KD):
            slot = wd_i % 2
            if n + 1 < KD:
                load_wdn(n + 1, (wd_i + 1) % 2)
            wd_i += 1
            for c in range(SC):
                gc = sti * SC + c
                b = it % 2
                it += 1
                for jj in range(NJ):
                    S.emit("pe", lambda e, jj=jj, c=c, b=b, slot=slot: e.matmul(
                        pd[b][:, :], wdn[slot][:, jj, :], gT[:, jj, c, :], start=(jj == 0), stop=(jj == NJ - 1)),
                        reads=[pfx + f"gT{jj}.{c}", pfx + f"wdn{slot}"], banks=[pfx + f"pd{b}"], ms=(jj == NJ - 1))
                S.emit("dve", lambda e, b=b, n=n, c=c: e.tensor_tensor(out=xo[b][:, :], in0=pd[b][:, :], in1=x_sb[:, n, c, HH:W],
                                                                      op=ALU.add),
                       reads=[pfx + f"x{c}.{n}m"], writes=[pfx + f"xo{b}"], banks=[pfx + f"pd{b}"])
                t = S.dma("sp", pfx + f"so{b}", lambda e, b=b, n=n, gc=gc: e.dma_start(
                    out=xo_r[:, n, gc * CH:(gc + 1) * CH], in_=xo[b][:, :]), reads=[pfx + f"xo{b}"], writes=[pfx + f"xoT.{n}.{gc}"])
                out_toks.append(t)
    return out_toks


DA = 512
MA = DA // 128
KA = 31


def conv_phase(cx, pfx, xT, xh, xoT, g_d, win_d, wout_d, adw_d, avec_d, bdw_d, nch=NCHUNK):
    S, nc = cx.S, cx.nc
    H = HALO
    W = H + CH
    x_sb = cx.sb(pfx + "x", [128, KD, W], F32)
    hT = cx.sb(pfx + "hT", [128, KD, W], BF16)
    sq = cx.sb(pfx + "sq", [128, KD, CH], BF16)
    rstd = cx.sb(pfx + "rstd", [128, CH], F32)
    tmp = cx.sb(pfx + "tmp", [128, CH], F32)
    ones = cx.sb(pfx + "ones", [128, 128], BF16)
    onesm = cx.sb(pfx + "onesm", [128, 128], BF16)
    g_sb = cx.sb(pfx + "g", [128, KD], F32)
    adw = cx.sb(pfx + "adw", [128, MA, KA], F32)
    avec = cx.sb(pfx + "avec", [128, 3, MA], F32)
    bdw = cx.sb(pfx + "bdw", [128, MA, 3], F32)
    win = cx.sb(pfx + "win", [128, 20, KD, 128], BF16)
    wout = cx.sb(pfx + "wout", [128, KD, KD, 128], BF16)
    glu = cx.sb(pfx + "glu", [128, MA, W], F32)
    ca = cx.sb(pfx + "ca", [128, MA, CH], F32)
    cab = cx.sb(pfx + "cab", [128, MA, CH], BF16)
    abT = cx.sb(pfx + "abT", [128, 2 * MA, CH], BF16)
    sgm = cx.sb(pfx + "sgm", [128, W], F32)
    chb = cx.sb(pfx + "chb", [128, W], F32)
    accb = cx.sb(pfx + "accb", [128, CH], F32)
    xo = [cx.sb(pfx + f"xo{i}", [128, CH], F32) for i in range(2)]
    pA, pB, pC = cx.ps(pfx + "pA"), cx.ps(pfx + "pB"), cx.ps(pfx + "pC")
    pmisc, pstat, pvar = cx.ps(pfx + "pmisc"), cx.ps(pfx + "pstat"), cx.ps(pfx + "pvar")
    po = [cx.ps(pfx + f"po{i}") for i in range(2)]
    BK = lambda n: pfx + "B." + n

    xT_r = xT.rearrange("(k p) t -> p k t", p=128)
    xh_r = xh.rearrange("(k p) (c h) -> p k c h", p=128, h=HALO)
    xo_r = xoT.rearrange("(k p) t -> p k t", p=128)

    S.emit("dve", lambda e: e.memset(ones[:], 1.0), writes=[pfx + "ones"])
    S.emit("dve", lambda e: e.memset(onesm[:], 1.0 / DA), writes=[pfx + "onesm"])
    S.dma_group("sp", "cst", [(lambda e: e.dma_start(out=g_sb[:], in_=g_d), [], [pfx + "g"]),
                              (lambda e: e.dma_start(out=adw[:], in_=adw_d), [], [pfx + "adw"]),
                              (lambda e: e.dma_start(out=avec[:], in_=avec_d), [], [pfx + "avec"]),
                              (lambda e: e.dma_start(out=bdw[:], in_=bdw_d), [], [pfx + "bdw"])])
    S.dma_group("pool", "wA", [(lambda e, m=m: e.dma_start(out=win[:, m], in_=win_d[m]), [], [pfx + f"win{m}"]) for m in range(20)])
    S.dma_group("pool", "wB", [(lambda e, n=n: e.dma_start(out=wout[:, n], in_=wout_d[n]), [], [pfx + f"wout{n}"]) for n in range(KD)])

    out_toks = []
    it = 0
    for c in range(nch):
        xk = [pfx + f"x{k}" for k in range(KD)]
        hk = [pfx + f"h{k}" for k in range(KD)]
        S.dma_group("sp", "lx", [(lambda e, k=k, c=c: e.dma_start(out=x_sb[:, k, H:W], in_=xT_r[:, k, c * CH:(c + 1) * CH]), [], [xk[k]])
                                 for k in range(KD)])
        S.dma("sp", pfx + "lxh", lambda e, c=c: e.dma_start(out=x_sb[:, :, 0:H], in_=xh_r[:, :, c, :]), writes=[pfx + "xh"])
        emit_rmsnorm(cx, pfx, lambda k: x_sb[:, k, H:W], lambda k: hT[:, k, H:W], xk, hk, CH, sq, ones,
                     pstat[:, :], BK("pstat"), tmp, rstd, g_sb, pfx + "g")
        emit_rmsnorm(cx, pfx, lambda k: x_sb[:, k, 0:H], lambda k: hT[:, k, 0:H], [pfx + "xh"] * KD,
                     [pfx + f"hh{k}" for k in range(KD)], H, sq, ones, pmisc[:, 256:256 + H], BK("pmisc"), tmp, rstd, g_sb, pfx + "g")
        hh = [pfx + f"hh{k}" for k in range(KD)]

        def proj(m, pmain, bank, hcol=None):
            for k in range(KD):
                S.emit("pe", lambda e, k=k, m=m: e.matmul(pmain[:, :], win[:, m, k, :], hT[:, k, H:W], start=(k == 0), stop=(k == KD - 1)),
                       reads=[hk[k], pfx + f"win{m}"], banks=[bank], ms=(k == KD - 1))
            if hcol is not None:
                for k in range(KD):
                    S.emit("pe", lambda e, k=k, m=m: e.matmul(pmisc[:, hcol:hcol + H], win[:, m, k, :], hT[:, k, 0:H],
                                                              start=(k == 0), stop=(k == KD - 1)),
                           reads=[hh[k], pfx + f"win{m}"], banks=[BK("pmisc")], ms=(k == KD - 1))

        for m in range(MA):
            proj(m, pA, BK("pA"), 0)
            proj(MA + m, pB, BK("pB"), H)
            S.emit("act", lambda e: e.activation(out=sgm[:, H:W], in_=pB[:, :], func=AF.Sigmoid), banks=[BK("pB")], writes=[pfx + "sgm"])
            S.emit("act", lambda e: e.activation(out=sgm[:, 0:H], in_=pmisc[:, H:2 * H], func=AF.Sigmoid), banks=[BK("pmisc")], writes=[pfx + "sgm"])
            S.emit("dve", lambda e, m=m: e.tensor_tensor(out=glu[:, m, H:W], in0=pA[:, :], in1=sgm[:, H:W], op=ALU.mult),
                   reads=[pfx + "sgm"], banks=[BK("pA")], writes=[pfx + f"glu{m}"])
            S.emit("dve", lambda e, m=m: e.tensor_tensor(out=glu[:, m, 0:H], in0=pmisc[:, 0:H], in1=sgm[:, 0:H], op=ALU.mult),
                   reads=[pfx + "sgm"], banks=[BK("pmisc")], writes=[pfx + f"glu{m}"])
            S.emit("act", lambda e, m=m: e.activation(out=ca[:, m, :], in_=glu[:, m, H:W], func=AF.Identity,
                                                      scale=adw[:, m, KA - 1:KA], bias=avec[:, 0, m:m + 1]),
                   reads=[pfx + f"glu{m}", pfx + "adw", pfx + "avec"], writes=[pfx + f"ca{m}"])
            for k in range(KA - 1):
                S.emit("dve", lambda e, m=m, k=k: e.scalar_tensor_tensor(
                    out=ca[:, m, :], in0=glu[:, m, 2 + k:2 + k + CH], scalar=adw[:, m, k:k + 1], in1=ca[:, m, :],
                    op0=ALU.mult, op1=ALU.add), reads=[pfx + f"glu{m}", pfx + f"ca{m}", pfx + "adw"], writes=[pfx + f"ca{m}"])
            S.emit("act", lambda e, m=m: e.activation(out=cab[:, m, :], in_=ca[:, m, :], func=AF.Identity),
                   reads=[pfx + f"ca{m}"], writes=[pfx + f"cab{m}"])
        for m in range(MA):
            S.emit("pe", lambda e, m=m: e.matmul(pstat[:, :], onesm[:, :], cab[:, m, :], start=(m == 0), stop=(m == MA - 1)),
                   reads=[pfx + f"cab{m}", pfx + "onesm"], banks=[BK("pstat")], ms=(m == MA - 1))
        for m in range(MA):
            S.emit("dve", lambda e, m=m: e.tensor_tensor(out=ca[:, m, :], in0=ca[:, m, :], in1=pstat[:, :], op=ALU.subtract),
                   reads=[pfx + f"ca{m}"], banks=[BK("pstat")], writes=[pfx + f"ca{m}"])
            S.emit("act", lambda e, m=m: e.activation(out=cab[:, m, :], in_=ca[:, m, :], func=AF.Square),
                   reads=[pfx + f"ca{m}"], writes=[pfx + f"cab{m}"])
        for m in range(MA):
            S.emit("pe", lambda e, m=m: e.matmul(pvar[:, :], onesm[:, :], cab[:, m, :], start=(m == 0), stop=(m == MA - 1)),
                   reads=[pfx + f"cab{m}", pfx + "onesm"], banks=[BK("pvar")], ms=(m == MA - 1))
        S.emit("dve", lambda e: e.tensor_scalar(out=tmp[:, :], in0=pvar[:, :], scalar1=EPS, scalar2=None, op0=ALU.add),
               banks=[BK("pvar")], writes=[pfx + "tmp"])
        S.emit("act", lambda e: e.activation(out=tmp[:, :], in_=tmp[:, :], func=AF.Sqrt), reads=[pfx + "tmp"], writes=[pfx + "tmp"])
        S.emit("dve", lambda e: e.reciprocal(out=rstd[:, :], in_=tmp[:, :]), reads=[pfx + "tmp"], writes=[pfx + "rstd"])
        for m in range(MA):
            S.emit("dve", lambda e, m=m: e.scalar_tensor_tensor(out=ca[:, m, :], in0=ca[:, m, :], scalar=avec[:, 1, m:m + 1], in1=rstd[:, :],
                                                                op0=ALU.mult, op1=ALU.mult),
                   reads=[pfx + f"ca{m}", pfx + "rstd", pfx + "avec"], writes=[pfx + f"ca{m}"])
            S.emit("act", lambda e, m=m: e.activation(out=abT[:, m, :], in_=ca[:, m, :], func=AF.Silu, bias=avec[:, 2, m:m + 1]),
                   reads=[pfx + f"ca{m}", pfx + "avec"], writes=[pfx + f"ab{m}"])
        for m in range(MA):
            proj(3 * MA + m, pA, BK("pA"), 2 * H)
            proj(4 * MA + m, pB, BK("pB"), 3 * H)
            proj(2 * MA + m, pC, BK("pC"), None)
            S.emit("act", lambda e: e.activation(out=sgm[:, H:W], in_=pA[:, :], func=AF.Identity), banks=[BK("pA")], writes=[pfx + "sgm"])
            S.emit("act", lambda e: e.activation(out=sgm[:, 0:H], in_=pmisc[:, 2 * H:3 * H], func=AF.Identity), banks=[BK("pmisc")], writes=[pfx + "sgm"])
            S.emit("dve", lambda e: e.tensor_tensor(out=chb[:, H:W], in0=pB[:, :], in1=sgm[:, H:W], op=ALU.mult),
                   reads=[pfx + "sgm"], banks=[BK("pB")], writes=[pfx + "chb"])
            S.emit("dve", lambda e: e.tensor_tensor(out=chb[:, 0:H], in0=pmisc[:, 3 * H:4 * H], in1=sgm[:, 0:H], op=ALU.mult),
                   reads=[pfx + "sgm"], banks=[BK("pmisc")], writes=[pfx + "chb"])
            S.emit("act", lambda e, m=m: e.activation(out=accb[:, :], in_=chb[:, H:W], func=AF.Identity, scale=bdw[:, m, 2:3]),
                   reads=[pfx + "chb", pfx + "bdw"], writes=[pfx + "accb"])
            for k in range(2):
                S.emit("dve", lambda e, m=m, k=k: e.scalar_tensor_tensor(
                    out=accb[:, :], in0=chb[:, H - 2 + k:H - 2 + k + CH], scalar=bdw[:, m, k:k + 1], in1=accb[:, :],
                    op0=ALU.mult, op1=ALU.add), reads=[pfx + "chb", pfx + "accb", pfx + "bdw"], writes=[pfx + "accb"])
            S.emit("dve", lambda e, m=m: e.tensor_tensor(out=abT[:, MA + m, :], in0=pC[:, :], in1=accb[:, :], op=ALU.mult),
                   reads=[pfx + "accb"], banks=[BK("pC")], writes=[pfx + f"ab{MA + m}"])
        for n in range(KD):
            b = it % 2
            it += 1
            for k in range(2 * MA):
                S.emit("pe", lambda e, k=k, n=n, b=b: e.matmul(po[b][:, :], wout[:, n, k, :], abT[:, k, :], start=(k == 0), stop=(k == 2 * MA - 1)),
                       reads=[pfx + f"ab{k}", pfx + f"wout{n}"], banks=[BK(f"po{b}")], ms=(k == 2 * MA - 1))
            S.emit("dve", lambda e, b=b, n=n: e.tensor_tensor(out=xo[b][:, :], in0=po[b][:, :], in1=x_sb[:, n, H:W], op=ALU.add),
                   reads=[xk[n]], writes=[pfx + f"xo{b}"], banks=[BK(f"po{b}")])
            t = S.dma("sp", pfx + f"so{b}", lambda e, b=b, n=n, c=c: e.dma_start(out=xo_r[:, n, c * CH:(c + 1) * CH], in_=xo[b][:, :]),
                      reads=[pfx + f"xo{b}"], writes=[pfx + f"xoT.{n}.{c}"])
            out_toks.append(t)
    return out_toks


NH = 16
HP = NH // 2
NKB = TOK // 128


def qkv_phase(cx, pfx, xT, g_d, wq_d, wk_d, wv_d, qkg_d, qT_o, kT_o, V_o, nch=NCHUNK):
    S, nc = cx.S, cx.nc
    x_sb = cx.sb(pfx + "x", [128, KD, CH], F32)
    hT = cx.sb(pfx + "hT", [128, KD, CH], BF16)
    sq = cx.sb(pfx + "sq", [128, KD, CH], BF16)
    rstd = cx.sb(pfx + "rstd", [128, CH], F32)
    tmp = cx.sb(pfx + "tmp", [128, CH], F32)
    ones = cx.sb(pfx + "ones", [128, 128], BF16)
    bd = cx.sb(pfx + "bd", [128, 128], BF16)
    g_sb = cx.sb(pfx + "g", [128, KD], F32)
    qkg = cx.sb(pfx + "qkg", [128, 2], F32)
    wq = cx.sb(pfx + "wq", [128, HP, KD, 128], BF16)
    wk = cx.sb(pfx + "wk", [128, HP, KD, 128], BF16)
    wv = cx.sb(pfx + "wv", [128, 2, KD, 512], BF16)
    qf = [cx.sb(pfx + f"qf{i}", [128, CH], F32) for i in range(2)]
    sqq = [cx.sb(pfx + f"sqq{i}", [128, CH], BF16) for i in range(2)]
    rs = [cx.sb(pfx + f"rs{i}", [128, CH], F32) for i in range(2)]
    qn = [cx.sb(pfx + f"qn{i}", [128, CH], BF16) for i in range(2)]
    vt = [cx.sb(pfx + f"vt{i}", [128, 512], BF16) for i in range(2)]
    pq = [cx.ps(pfx + f"pq{i}") for i in range(2)]
    pms = [cx.ps(pfx + f"pms{i}") for i in range(2)]
    pvv = [cx.ps(pfx + f"pvv{i}") for i in range(2)]
    pstat = cx.ps(pfx + "pstat")
    BK = lambda n: pfx + "B." + n
    xT_r = xT.rearrange("(k p) t -> p k t", p=128)
    qT_r = qT_o.rearrange("(k p) t -> p k t", p=128)
    kT_r = kT_o.rearrange("(k p) t -> p k t", p=128)

    S.emit("dve", lambda e: e.memset(ones[:], 1.0), writes=[pfx + "ones"])
    S.emit("dve", lambda e: e.memset(bd[:], 0.0), writes=[pfx + "bd"])
    S.emit("dve", lambda e: e.memset(bd[0:64, 0:64], 1.0 / 64), writes=[pfx + "bd"])
    S.emit("dve", lambda e: e.memset(bd[64:128, 64:128], 1.0 / 64), writes=[pfx + "bd"])
    S.dma_group("sp", "cst", [(lambda e: e.dma_start(out=g_sb[:], in_=g_d), [], [pfx + "g"]),
                              (lambda e: e.dma_start(out=qkg[:], in_=qkg_d), [], [pfx + "qkg"])])
    S.emit("dve", lambda e: e.tensor_scalar(out=qkg[:, 0:1], in0=qkg[:, 0:1], scalar1=0.125, scalar2=None, op0=ALU.mult),
           reads=[pfx + "qkg"], writes=[pfx + "qkg"])
    S.dma_group("pool", "wA", [(lambda e, m=m: e.dma_start(out=wq[:, m], in_=wq_d[m]), [], [pfx + f"wq{m}"]) for m in range(HP)])
    S.dma_group("pool", "wB", [(lambda e, m=m: e.dma_start(out=wk[:, m], in_=wk_d[m]), [], [pfx + f"wk{m}"]) for m in range(HP)])
    S.dma_group("pool", "wC", [(lambda e, hf=hf: e.dma_start(out=wv[:, hf], in_=wv_d[hf]), [], [pfx + f"wv{hf}"]) for hf in range(2)])
    out_toks = []
    it = 0
    for c in range(nch):
        xk = [pfx + f"x{k}" for k in range(KD)]
        hk = [pfx + f"h{k}" for k in range(KD)]
        S.dma_group("sp", "lx", [(lambda e, k=k, c=c: e.dma_start(out=x_sb[:, k, :], in_=xT_r[:, k, c * CH:(c + 1) * CH]), [], [xk[k]])
                                 for k in range(KD)])
        emit_rmsnorm(cx, pfx, lambda k: x_sb[:, k, :], lambda k: hT[:, k, :], xk, hk, CH, sq, ones,
                     pstat[:, :], BK("pstat"), tmp, rstd, g_sb, pfx + "g")
        for which, (w_sb, wkey, gcol, o_r) in enumerate(((wq, "wq", 0, qT_r), (wk, "wk", 1, kT_r))):
            for m in range(HP):
                b = it % 2
                it += 1
                for k in range(KD):
                    S.emit("pe", lambda e, k=k, m=m, b=b, w_sb=w_sb: e.matmul(pq[b][:, :], w_sb[:, m, k, :], hT[:, k, :],
                                                                             start=(k == 0), stop=(k == KD - 1)),
                           reads=[hk[k], pfx + f"{wkey}{m}"], banks=[BK(f"pq{b}")], ms=(k == KD - 1))
                S.emit("act", lambda e, b=b: e.activation(out=sqq[b][:, :], in_=pq[b][:, :], func=AF.Square),
                       banks=[BK(f"pq{b}")], writes=[pfx + f"sqq{b}"])
                S.emit("act", lambda e, b=b: e.activation(out=qf[b][:, :], in_=pq[b][:, :], func=AF.Identity),
                       banks=[BK(f"pq{b}")], writes=[pfx + f"qf{b}"])
                S.emit("pe", lambda e, b=b: e.matmul(pms[b][:, :], bd[:, :], sqq[b][:, :], start=True, stop=True),
                       reads=[pfx + f"sqq{b}", pfx + "bd"], banks=[BK(f"pms{b}")], ms=True)
                S.emit("dve", lambda e, b=b: e.tensor_scalar(out=rs[b][:, :], in0=pms[b][:, :], scalar1=EPS, scalar2=None, op0=ALU.add),
                       banks=[BK(f"pms{b}")], writes=[pfx + f"rs{b}"])
                S.emit("act", lambda e, b=b: e.activation(out=rs[b][:, :], in_=rs[b][:, :], func=AF.Sqrt),
                       reads=[pfx + f"rs{b}"], writes=[pfx + f"rs{b}"])
                S.emit("dve", lambda e, b=b: e.reciprocal(out=rs[b][:, :], in_=rs[b][:, :]), reads=[pfx + f"rs{b}"], writes=[pfx + f"rs{b}"])
                S.emit("dve", lambda e, b=b, gcol=gcol: e.scalar_tensor_tensor(out=qn[b][:, :], in0=qf[b][:, :], scalar=qkg[:, gcol:gcol + 1],
                                                                               in1=rs[b][:, :], op0=ALU.mult, op1=ALU.mult),
                       reads=[pfx + f"qf{b}", pfx + f"rs{b}", pfx + "qkg"], writes=[pfx + f"qn{b}"])
                t = S.dma("sp", pfx + f"sq{b}", lambda e, b=b, m=m, c=c, o_r=o_r: e.dma_start(out=o_r[:, m, c * CH:(c + 1) * CH], in_=qn[b][:, :]),
                          reads=[pfx + f"qn{b}"], writes=[pfx + f"o{which}.{m}.{c}"])
                out_toks.append(t)
        for tt in range(CH // 128):
            kb = c * (CH // 128) + tt
            for hf in range(2):
                b = it % 2
                it += 1
                for k in range(KD):
                    S.emit("pe", lambda e, k=k, tt=tt, hf=hf, b=b: e.matmul(pvv[b][:, :], hT[:, k, tt * 128:(tt + 1) * 128], wv[:, hf, k, :],
                                                                          start=(k == 0), stop=(k == KD - 1)),
                           reads=[hk[k], pfx + f"wv{hf}"], banks=[BK(f"pvv{b}")], ms=(k == KD - 1))
                S.emit("act", lambda e, b=b: e.activation(out=vt[b][:, :], in_=pvv[b][:, :], func=AF.Identity),
                       banks=[BK(f"pvv{b}")], writes=[pfx + f"vt{b}"])
                t = S.dma("sp", pfx + f"sv{b}", lambda e, b=b, hf=hf, kb=kb: e.dma_start(
                    out=V_o[hf * 4:(hf + 1) * 4, :, kb, :].rearrange("h p f -> p h f"),
                    in_=vt[b][:, :].rearrange("p (h f) -> p h f", f=128)),
                    reads=[pfx + f"vt{b}"], writes=[pfx + f"oV.{hf}.{kb}"])
                out_toks.append(t)
    return out_toks


def attn_phase(cx, pfx, xT, xoT, qT_i, kT_g, V_g, mask_d, tri_d, wo_d, nhp=HP, nq=NCHUNK):
    S, nc = cx.S, cx.nc
    kT_sb = cx.sb(pfx + "kT", [128, 2, TOK], BF16)
    V_sb = cx.sb(pfx + "V", [128, 2, NKB, 128], BF16)
    q_sb = cx.sb(pfx + "q", [128, TOK], BF16)
    oT = cx.sb(pfx + "oT", [128, HP, TOK], BF16)
    mask = cx.sb(pfx + "mask", [128, 2, 4, 1024], BF16)
    ntri = cx.sb(pfx + "ntri", [128, 128], BF16)
    nones = cx.sb(pfx + "nones", [128, 128], BF16)
    one1 = cx.sb(pfx + "one1", [128, 1], F32)
    E = [cx.sb(pfx + f"E{i}", [128, 1024], F32) for i in range(3)]
    L = [cx.sb(pfx + f"L{i}", [128, 1024], BF16) for i in range(3)]
    Wt = [cx.sb(pfx + f"W{i}", [128, 1024], BF16) for i in range(2)]
    Ls = cx.sb(pfx + "Ls", [128, 1024], F32)
    Lsb = [cx.sb(pfx + f"Lsb{i}", [128, 1024], BF16) for i in range(2)]
    wo = cx.sb(pfx + "wo", [128, KD, KD, 128], BF16)
    xr = [cx.sb(pfx + f"xr{i}", [128, CH], F32) for i in range(2)]
    xo = [cx.sb(pfx + f"xo{i}", [128, CH], F32) for i in range(2)]
    pz = [cx.ps(pfx + f"pz{i}", (128, 1024)) for i in range(3)]
    po = cx.ps(pfx + "po")
    pf = cx.ps(pfx + "pf")
    BK = lambda n: pfx + "B." + n
    xT_r = xT.rearrange("(k p) t -> p k t", p=128)
    xo_r = xoT.rearrange("(k p) t -> p k t", p=128)
    qT_r = qT_i.rearrange("(k p) t -> p k t", p=128)
    kT_r = kT_g.rearrange("(h r p) t -> p h r t", r=2, p=128)
    V_r = V_g.rearrange("(h r p) (k f) -> h p r k f", r=2, p=128, f=128)

    S.dma("pool", pfx + "ct", lambda e: e.dma_start(out=ntri[:], in_=tri_d), writes=[pfx + "ntri"])
    S.emit("dve", lambda e: e.memset(nones[:], -1.0), writes=[pfx + "nones"])
    S.emit("dve", lambda e: e.memset(one1[:], 1.0), writes=[pfx + "one1"])
    S.dma("pool", pfx + "cm", lambda e: e.dma_start(out=mask[:], in_=mask_d), writes=[pfx + "mask"])
    S.dma_group("pool", "wA", [(lambda e, n=n: e.dma_start(out=wo[:, n], in_=wo_d[n]), [], [pfx + f"wo{n}"]) for n in range(KD)])

    step = 0
    for hp in range(nhp):
        S.dma("sp", pfx + "lk", lambda e, hp=hp: e.dma_start(out=kT_sb[:, :, :], in_=kT_r[:, hp, :, :]), writes=[pfx + "kT"])
        S.dma("sp", pfx + "lv", lambda e, hp=hp: e.dma_start(out=V_sb[:, :, :, :], in_=V_r[hp]),
              writes=[pfx + "V"])
        S.dma("sp", pfx + "lq", lambda e, hp=hp: e.dma_start(out=q_sb[:, :], in_=qT_r[:, hp, :]), writes=[pfx + "q"])
        for i in range(nq):
            blocks = []
            for j in range(2 * i + 1, -1, -1):
                for b4 in range(3, -1, -1):
                    mk = 1 if j == 2 * i + 1 else (0 if j == 2 * i else None)
                    blocks.append((j % 2, (j // 2) * 4 + b4, mk, b4))
            nb = len(blocks)

            def emit_z(s):
                r, kb, mk, b4 = blocks[s]
                z3 = (step + s) % 3
                for hd in range(2):
                    S.emit("pe", lambda e, hd=hd, r=r, kb=kb, z3=z3, i=i: e.matmul(
                        pz[z3][:, hd * 512:(hd + 1) * 512], kT_sb[hd * 64:(hd + 1) * 64, r, kb * 128:(kb + 1) * 128],
                        q_sb[hd * 64:(hd + 1) * 64, i * CH:(i + 1) * CH], start=True, stop=True, skip_group_check=True),
                        reads=[pfx + "kT", pfx + "q"], banks=[BK(f"pz{z3}")], ms=(hd == 1))

            def emit_el(s):
                r, kb, mk, b4 = blocks[s]
                zb = (step + s) % 2
                z3 = (step + s) % 3
                S.emit("act", lambda e, zb=zb, z3=z3: e.activation(out=E[z3][:, :], in_=pz[z3][:, :], func=AF.Exp),
                       banks=[BK(f"pz{z3}")], writes=[pfx + f"E{z3}"])
                S.emit("act", lambda e, z3=z3: e.activation(out=L[z3][:, :], in_=E[z3][:, :], func=AF.Ln, bias=one1[:, 0:1]),
                       reads=[pfx + f"E{z3}", pfx + "one1"], writes=[pfx + f"L{z3}"])
                if mk is not None:
                    S.emit("dve", lambda e, z3=z3, mk=mk, b4=b4: e.tensor_tensor(out=L[z3][:, :], in0=L[z3][:, :], in1=mask[:, mk, b4, :], op=ALU.mult),
                           reads=[pfx + f"L{z3}", pfx + "mask"], writes=[pfx + f"L{z3}"])

            def emit_p2(s):
                zb = (step + s) % 2
                z3 = (step + s) % 3
                sls = [slice(hd * 512, (hd + 1) * 512) for hd in range(2)]
                for hd in range(2):
                    S.emit("pe", lambda e, sl=sls[hd], zb=zb, z3=z3, last=(s == 0): e.matmul(
                        pz[z3][:, sl], ntri[:, :], L[z3][:, sl], start=False, stop=last, skip_group_check=True),
                        reads=[pfx + f"L{z3}", pfx + "ntri"], banks=[BK(f"pz{z3}")], ms=(s == 0 and hd == 1))
                if s > 0:
                    lb = (step + s - 1) % 2
                    for hd in range(2):
                        S.emit("pe", lambda e, sl=sls[hd], lb=lb, z3=z3: e.matmul(
                            pz[z3][:, sl], nones[:, :], Lsb[lb][:, sl], start=False, stop=True, skip_group_check=True),
                            reads=[pfx + f"Lsb{lb}", pfx + "nones"], banks=[BK(f"pz{z3}")], ms=(hd == 1))

            def emit_w(s):
                r, kb, mk, b4 = blocks[s]
                zb = (step + s) % 2
                z3 = (step + s) % 3
                S.emit("act", lambda e, zb=zb, z3=z3: e.activation(out=Wt[zb][:, :], in_=pz[z3][:, :], func=AF.Exp),
                       banks=[BK(f"pz{z3}")], writes=[pfx + f"W{zb}"])
                if mk is not None:
                    S.emit("dve", lambda e, zb=zb, mk=mk, b4=b4: e.tensor_tensor(out=Wt[zb][:, :], in0=Wt[zb][:, :], in1=mask[:, mk, b4, :], op=ALU.mult),
                           reads=[pfx + f"W{zb}", pfx + "mask"], writes=[pfx + f"W{zb}"])
                if s + 1 < nb:
                    if s == 0:
                        S.emit("dve", lambda e, z3=z3: e.tensor_copy(out=Ls[:, :], in_=L[z3][:, :]), reads=[pfx + f"L{z3}"], writes=[pfx + "Ls"])
                    else:
                        S.emit("dve", lambda e, z3=z3: e.tensor_tensor(out=Ls[:, :], in0=Ls[:, :], in1=L[z3][:, :], op=ALU.add),
                               reads=[pfx + f"L{z3}", pfx + "Ls"], writes=[pfx + "Ls"])
                    S.emit("dve", lambda e, zb=zb: e.tensor_copy(out=Lsb[zb][:, :], in_=Ls[:, :]), reads=[pfx + "Ls"], writes=[pfx + f"Lsb{zb}"])

            def emit_pv(s):
                r, kb, mk, b4 = blocks[s]
                zb = (step + s) % 2
                for hd in range(2):
                    S.emit("pe", lambda e, hd=hd, r=r, kb=kb, zb=zb, st_=(s == 0), sp_=(s == nb - 1): e.matmul(
                        po[hd * 64:(hd + 1) * 64, :], V_sb[:, r, kb, hd * 64:(hd + 1) * 64], Wt[zb][:, hd * 512:(hd + 1) * 512],
                        start=st_, stop=sp_),
                        reads=[pfx + "V", pfx + f"W{zb}"], banks=[BK("po")], ms=(s == nb - 1 and hd == 1))

            emit_z(0)
            emit_el(0)
            emit_z(1)
            emit_el(1)
            for s in range(nb):
                emit_p2(s)
                emit_w(s)
                if s + 2 < nb:
                    emit_z(s + 2)
                    emit_el(s + 2)
                if s > 0:
                    emit_pv(s - 1)
            emit_pv(nb - 1)
            step += nb
            S.emit("act", lambda e, hp=hp, i=i: e.activation(out=oT[:, hp, i * CH:(i + 1) * CH], in_=po[:, :], func=AF.Identity),
                   banks=[BK("po")], writes=[pfx + f"oT{hp}.{i}"])
    out_toks = []
    it = 0
    for i in range(nq):
        for n in range(KD):
            b = it % 2
            it += 1
            S.dma("sp", pfx + f"lx{b}", lambda e, b=b, n=n, i=i: e.dma_start(out=xr[b][:, :], in_=xT_r[:, n, i * CH:(i + 1) * CH]),
                  writes=[pfx + f"xr{b}"])
            for k in range(nhp):
                S.emit("pe", lambda e, k=k, n=n, i=i: e.matmul(pf[:, :], wo[:, n, k, :], oT[:, k, i * CH:(i + 1) * CH],
                                                              start=(k == 0), stop=(k == nhp - 1)),
                       reads=[pfx + f"oT{k}.{i}", pfx + f"wo{n}"], banks=[BK("pf")], ms=(k == nhp - 1))
            S.emit("dve", lambda e, b=b: e.tensor_tensor(out=xo[b][:, :], in0=pf[:, :], in1=xr[b][:, :], op=ALU.add),
                   reads=[pfx + f"xr{b}"], writes=[pfx + f"xo{b}"], banks=[BK("pf")])
            t = S.dma("sp", pfx + f"so{b}", lambda e, b=b, n=n, i=i: e.dma_start(out=xo_r[:, n, i * CH:(i + 1) * CH], in_=xo[b][:, :]),
                      reads=[pfx + f"xo{b}"], writes=[pfx + f"xoT.{n}.{i}"])
            out_toks.append(t)
    return out_toks


NCORES = 8
SEQ = 8192
BATCH = 4


def _own_tokens(p):
    return np.concatenate([np.arange((2 * i + p) * CH, (2 * i + p + 1) * CH) for i in range(NCHUNK)])


def _halo_tokens(p):
    return np.concatenate([np.arange((2 * i + p) * CH, (2 * i + p) * CH + HALO) for i in range(NCHUNK)])


def _lay_vec(v, n):
    return np.ascontiguousarray(v.reshape(n, 128).T)


def _lay_w(w, kin, nout):
    return np.ascontiguousarray(w.reshape(kin, 128, nout, 128).transpose(2, 1, 0, 3))


def _lay_ffn(g, w_up, dw_w, dw_b, w_down):
    wu = w_up.reshape(KD, 128, 2, NJ, 128)
    return dict(g=_lay_vec(g, KD),
                wup=np.ascontiguousarray(wu.transpose(3, 1, 0, 2, 4).reshape(NJ, 128, KD, 256)),
                wdn=_lay_w(w_down, NJ, KD),
                dw=np.ascontiguousarray(dw_w.reshape(3, 2 * NJ, 128).transpose(2, 1, 0)),
                db=_lay_vec(dw_b, 2 * NJ))


def _lay_conv(g, w_in, a_dw_w, a_dw_b, a_ln_g, a_ln_b, b_dw_w, w_out):
    return dict(g=_lay_vec(g, KD), win=_lay_w(w_in, KD, 20), wout=_lay_w(w_out, KD, KD),
                adw=np.ascontiguousarray(a_dw_w.reshape(KA, MA, 128).transpose(2, 1, 0)),
                avec=np.ascontiguousarray(np.stack([a_dw_b, a_ln_g, a_ln_b]).reshape(3, MA, 128).transpose(2, 0, 1)),
                bdw=np.ascontiguousarray(b_dw_w.reshape(3, MA, 128).transpose(2, 1, 0)))


def _lay_attn(g, w_qkv, q_g, k_g, w_o):
    return dict(g=_lay_vec(g, KD), wq=_lay_w(w_qkv[:, :D], KD, HP), wk=_lay_w(w_qkv[:, D:2 * D], KD, HP),
                wv=np.ascontiguousarray(w_qkv[:, 2 * D:].reshape(KD, 128, 2, 512).transpose(2, 1, 0, 3)),
                qkg=np.ascontiguousarray(np.stack([np.tile(q_g, 2), np.tile(k_g, 2)], 1)),
                wo=_lay_w(w_o, KD, KD))


def _masks(p):
    ks = np.arange(CH)[:, None]
    tq = np.arange(CH)[None, :]
    diag = (ks < tq).astype(np.float32)
    A = diag if p == 0 else np.ones((CH, CH), np.float32)
    B = np.zeros((CH, CH), np.float32) if p == 0 else diag
    m = np.stack([A, B]).reshape(2, 4, 128, CH).transpose(2, 0, 1, 3)
    return np.ascontiguousarray(np.concatenate([m, m], -1))


def _tri():
    j = np.arange(128)[:, None]
    s = np.arange(128)[None, :]
    return -(j >= s).astype(np.float32)


_PROGS = {}


def _dram(nc, name, shape, dt=F32, kind="ExternalInput"):
    return nc.dram_tensor(name, list(shape), dt, kind=kind).ap()


def _prog(kind):
    if kind in _PROGS:
        return _PROGS[kind]
    nc = bass.Bass("TRN2", target_bir_lowering=False)
    S = Sched()
    with ExitStack() as st:
        cx = Ctx(nc, S, st)
        if kind == "conv":
            a = [_dram(nc, "xT", [D, TOK]), _dram(nc, "xh", [D, NCHUNK * HALO])]
            xo = _dram(nc, "xoT", [D, TOK], F32, "ExternalOutput")
            toks = conv_phase(cx, "c.", a[0], a[1], xo, _dram(nc, "g", [128, KD]), _dram(nc, "win", [20, 128, KD, 128]),
                              _dram(nc, "wout", [KD, 128, KD, 128]), _dram(nc, "adw", [128, MA, KA]),
                              _dram(nc, "avec", [128, 3, MA]), _dram(nc, "bdw", [128, MA, 3]))
        elif kind == "ffn":
            a = [_dram(nc, "xT", [D, TOK]), _dram(nc, "xh", [D, NCHUNK * HALO])]
            xo = _dram(nc, "xoT", [D, TOK], F32, "ExternalOutput")
            toks = ffn_phase(cx, "f.", a[0], a[1], xo, _dram(nc, "g", [128, KD]), _dram(nc, "wup", [NJ, 128, KD, 256]),
                             _dram(nc, "wdn", [KD, 128, NJ, 128]), _dram(nc, "dw", [128, 2 * NJ, 3]), _dram(nc, "db", [128, 2 * NJ]))
        elif kind == "qkv":
            toks = qkv_phase(cx, "q.", _dram(nc, "xT", [D, TOK]), _dram(nc, "g", [128, KD]), _dram(nc, "wq", [HP, 128, KD, 128]),
                             _dram(nc, "wk", [HP, 128, KD, 128]), _dram(nc, "wv", [2, 128, KD, 512]), _dram(nc, "qkg", [128, 2]),
                             _dram(nc, "qT", [D, TOK], BF16, "ExternalOutput"), _dram(nc, "kT", [D, TOK], BF16, "ExternalOutput"),
                             _dram(nc, "V", [HP, 128, NKB, 128], BF16, "ExternalOutput"))
        elif kind == "attn":
            toks = attn_phase(cx, "a.", _dram(nc, "xT", [D, TOK]), _dram(nc, "xoT", [D, TOK], F32, "ExternalOutput"),
                              _dram(nc, "qT", [D, TOK], BF16), _dram(nc, "kT", [2, D, TOK], BF16),
                              _dram(nc, "V", [2, HP, 128, NKB, 128], BF16), _dram(nc, "mask", [128, 2, 4, 1024]),
                              _dram(nc, "tri", [128, 128]), _dram(nc, "wo", [KD, 128, KD, 128]))
        S.wait_all("sp", toks)
        S.build(nc, st)
    _PROGS[kind] = nc
    return nc


def _run(kind, in_maps):
    res = run_bass_kernel_spmd(_prog(kind), in_maps, core_ids=list(range(NCORES)))
    return res.results


def kernel_unfused(**inputs):
    inp = {k: np.asarray(v) for k, v in inputs.items()}
    x = inp["x"]
    own = [_own_tokens(p) for p in range(2)]
    halo = [_halo_tokens(p) for p in range(2)]
    xT = [np.ascontiguousarray(x[c // 2][own[c % 2]].T) for c in range(NCORES)]

    def halos(xT):
        out = []
        for b in range(BATCH):
            full = np.zeros((D, HALO + SEQ), np.float32)
            for p in range(2):
                for i in range(NCHUNK):
                    g0 = (2 * i + p) * CH
                    full[:, HALO + g0:HALO + g0 + CH] = xT[2 * b + p][:, i * CH:(i + 1) * CH]
            for p in range(2):
                out.append(np.ascontiguousarray(full[:, halo[p]]))
        return out

    masks = [_masks(p) for p in range(2)]
    tri = _tri()
    for layer in range(4):
        i = layer // 2
        if layer % 2 == 0:
            lw = _lay_conv(inp["mix_norm_g"][layer], inp["conv_w_in"][i], inp["conv_a_dw_w"][i], inp["conv_a_dw_b"][i],
                           inp["conv_a_ln_g"][i], inp["conv_a_ln_b"][i], inp["conv_b_dw_w"][i], inp["conv_w_out"][i])
            xh = halos(xT)
            r = _run("conv", [dict(lw, xT=xT[c], xh=xh[c]) for c in range(NCORES)])
            xT = [r[c]["xoT"] for c in range(NCORES)]
        else:
            lw = _lay_attn(inp["mix_norm_g"][layer], inp["attn_w_qkv"][i], inp["attn_q_g"][i], inp["attn_k_g"][i], inp["attn_w_o"][i])
            r = _run("qkv", [dict(xT=xT[c], g=lw["g"], wq=lw["wq"], wk=lw["wk"], wv=lw["wv"], qkg=lw["qkg"]) for c in range(NCORES)])
            ins = []
            for c in range(NCORES):
                b = c // 2
                kT_g = np.stack([r[2 * b]["kT"], r[2 * b + 1]["kT"]])
                V_g = np.stack([r[2 * b]["V"], r[2 * b + 1]["V"]])
                ins.append(dict(xT=xT[c], qT=r[c]["qT"], kT=kT_g, V=V_g, mask=masks[c % 2], tri=tri, wo=lw["wo"]))
            r = _run("attn", ins)
            xT = [r[c]["xoT"] for c in range(NCORES)]
        lw = _lay_ffn(inp["ffn_norm_g"][layer], inp["ffn_w_up"][layer], inp["ffn_dw_w"][layer], inp["ffn_dw_b"][layer], inp["ffn_w_down"][layer])
        xh = halos(xT)
        r = _run("ffn", [dict(lw, xT=xT[c], xh=xh[c]) for c in range(NCORES)])
        xT = [r[c]["xoT"] for c in range(NCORES)]
    out = np.empty((BATCH, SEQ, D), np.float32)
    for c in range(NCORES):
        out[c // 2][own[c % 2]] = xT[c].T
    return out


PAIRS = [[0, 1], [2, 3], [4, 5], [6, 7]]


def halo_phase(cx, pfx, xT, tl, tg, xh, sel_d):
    S, nc = cx.S, cx.nc
    t_sb = cx.sb(pfx + "t", [128, KD, NCHUNK, HALO], F32)
    c0 = cx.sb(pfx + "c0", [128, KD, NCHUNK, HALO], F32)
    c1 = cx.sb(pfx + "c1", [128, KD, NCHUNK, HALO], F32)
    sel = cx.sb(pfx + "sel", [128, 2], F32)
    xT_r = xT.rearrange("(k p) (c t) -> p k c t", p=128, t=CH)
    tl_r = tl.rearrange("(k p) (c h) -> p k c h", p=128, h=HALO)
    tg_r = tg.rearrange("(r k p) (c h) -> p r k c h", p=128, k=KD, h=HALO)
    xh_r = xh.rearrange("(k p) (c h) -> p k c h", p=128, h=HALO)
    S.dma("sp", "cst", lambda e: e.dma_start(out=sel[:], in_=sel_d), writes=[pfx + "sel"])
    S.dma_group("sp", "h0", [(lambda e, k=k: e.dma_start(out=t_sb[:, k], in_=xT_r[:, k, :, CH - HALO:CH]), [], [pfx + f"t{k}"])
                             for k in range(KD)])
    S.dma_group("sp", "h1", [(lambda e, k=k: e.dma_start(out=tl_r[:, k], in_=t_sb[:, k]), [pfx + f"t{k}"], [pfx + f"tl{k}"])
                             for k in range(KD)])
    S.dma("pool", "cc", lambda e: e.collective_compute("AllGather", ALU.bypass, replica_groups=PAIRS, ins=[tl], outs=[tg]),
          reads=[pfx + f"tl{k}" for k in range(KD)], writes=[pfx + "tg"], inc=1)
    S.emit("dve", lambda e: e.memset(c0[:, :, 0, :], 0.0), writes=[pfx + "c0z"])
    S.dma_group("sp", "h2", [(lambda e, k=k: e.dma_start(out=c0[:, k, 1:NCHUNK, :], in_=tg_r[:, 1, k, 0:NCHUNK - 1, :]), [pfx + "tg"], [pfx + f"c0{k}"])
                             for k in range(KD)] +
                            [(lambda e, k=k: e.dma_start(out=c1[:, k], in_=tg_r[:, 0, k]), [pfx + "tg"], [pfx + f"c1{k}"])
                             for k in range(KD)])
    items = []
    for k in range(KD):
        S.emit("dve", lambda e, k=k: e.tensor_scalar(out=c0[:, k], in0=c0[:, k], scalar1=sel[:, 0:1], scalar2=None, op0=ALU.mult),
               reads=[pfx + f"c0{k}", pfx + "c0z", pfx + "sel"], writes=[pfx + f"c0{k}"])
        S.emit("dve", lambda e, k=k: e.scalar_tensor_tensor(out=c1[:, k], in0=c1[:, k], scalar=sel[:, 1:2], in1=c0[:, k],
                                                            op0=ALU.mult, op1=ALU.add),
               reads=[pfx + f"c0{k}", pfx + f"c1{k}", pfx + "sel"], writes=[pfx + f"c1{k}"])
        items.append((lambda e, k=k: e.dma_start(out=xh_r[:, k], in_=c1[:, k]), [pfx + f"c1{k}"], [pfx + f"xh{k}"]))
    t = S.dma_group("sp", "h3", items)
    return [t]


def gather_phase(cx, pfx, kT_l, kT_g, V_l, V_g):
    S = cx.S
    toks = []
    for hp in range(HP):
        for nm, a, b in (("k", kT_l, kT_g), ("v", V_l, V_g)):
            toks.append(S.dma("pool", "cc", lambda e, hp=hp, a=a, b=b: e.collective_compute(
                "AllGather", ALU.bypass, replica_groups=PAIRS, ins=[a[hp * 128:(hp + 1) * 128, :]], outs=[b[hp * 256:(hp + 1) * 256, :]]),
                writes=[pfx + f"{nm}g{hp}"], inc=1))
    return toks[-1:]


_FUSED = []


def _fused_prog():
    if _FUSED:
        return _FUSED[0]
    nc = bass.Bass("TRN2", target_bir_lowering=False)
    S = Sched()
    with ExitStack() as outer:
        x_in = _dram(nc, "xT", [D, TOK])
        out = _dram(nc, "out", [D, TOK], F32, "ExternalOutput")
        sel_d = _dram(nc, "sel", [128, 2])
        mask_d = _dram(nc, "mask", [128, 2, 4, 1024])
        tri_d = _dram(nc, "tri", [128, 128])
        xs = [nc.dram_tensor(f"xs{i}", [D, TOK], F32).ap() for i in range(2)]
        tl = nc.dram_tensor("tl", [D, NCHUNK * HALO], F32).ap()
        tg = nc.dram_tensor("tg", [2 * D, NCHUNK * HALO], F32).ap()
        xh = nc.dram_tensor("xh", [D, NCHUNK * HALO], F32).ap()
        qT = nc.dram_tensor("qT", [D, TOK], BF16).ap()
        kT_l = nc.dram_tensor("kTl", [D, TOK], BF16).ap()
        kT_g = nc.dram_tensor("kTg", [2 * D, TOK], BF16).ap()
        V_l = nc.dram_tensor("Vl", [HP * 128, NKB * 128], BF16).ap()
        V_g = nc.dram_tensor("Vg", [2 * HP * 128, NKB * 128], BF16).ap()
        V_l5 = V_l.rearrange("(h p) (k f) -> h p k f", p=128, f=128)

        def phase(fn):
            with ExitStack() as pst:
                cx = Ctx(nc, S, pst)
                toks = fn(cx)
                S.barrier(toks)
                S.build_phase(nc, pst, outer)

        cur = x_in
        nxt = 0
        for layer in range(4):
            L = f"L{layer}."
            if layer % 2 == 0:
                phase(lambda cx: halo_phase(cx, L + "h.", cur, tl, tg, xh, sel_d))
                dst = xs[nxt]
                phase(lambda cx: conv_phase(cx, L + "c.", cur, xh, dst, _dram(nc, L + "mg", [128, KD]), _dram(nc, L + "win", [20, 128, KD, 128]),
                                            _dram(nc, L + "wout", [KD, 128, KD, 128]), _dram(nc, L + "adw", [128, MA, KA]),
                                            _dram(nc, L + "avec", [128, 3, MA]), _dram(nc, L + "bdw", [128, MA, 3])))
            else:
                phase(lambda cx: qkv_phase(cx, L + "q.", cur, _dram(nc, L + "mg", [128, KD]), _dram(nc, L + "wq", [HP, 128, KD, 128]),
                                           _dram(nc, L + "wk", [HP, 128, KD, 128]), _dram(nc, L + "wv", [2, 128, KD, 512]),
                                           _dram(nc, L + "qkg", [128, 2]), qT, kT_l, V_l5))
                phase(lambda cx: gather_phase(cx, L + "g.", kT_l, kT_g, V_l, V_g))
                dst = xs[nxt]
                phase(lambda cx: attn_phase(cx, L + "a.", cur, dst, qT, kT_g, V_g, mask_d, tri_d, _dram(nc, L + "wo", [KD, 128, KD, 128])))
            cur = dst
            nxt ^= 1
            phase(lambda cx: halo_phase(cx, L + "hf.", cur, tl, tg, xh, sel_d))
            dst = out if layer == 3 else xs[nxt]
            phase(lambda cx: ffn_phase(cx, L + "f.", cur, xh, dst, _dram(nc, L + "fg", [128, KD]), _dram(nc, L + "wup", [NJ, 128, KD, 256]),
                                       _dram(nc, L + "wdn", [KD, 128, NJ, 128]), _dram(nc, L + "dw", [128, 2 * NJ, 3]),
                                       _dram(nc, L + "db", [128, 2 * NJ])))
            cur = dst
            nxt ^= 1
    _FUSED.append((nc, S))
    return _FUSED[0]


def kernel(**inputs):
    inp = {k: np.asarray(v) for k, v in inputs.items()}
    x = inp["x"]
    own = [_own_tokens(p) for p in range(2)]
    base = {"tri": _tri()}
    for layer in range(4):
        i = layer // 2
        L = f"L{layer}."
        if layer % 2 == 0:
            lw = _lay_conv(inp["mix_norm_g"][layer], inp["conv_w_in"][i], inp["conv_a_dw_w"][i], inp["conv_a_dw_b"][i],
                           inp["conv_a_ln_g"][i], inp["conv_a_ln_b"][i], inp["conv_b_dw_w"][i], inp["conv_w_out"][i])
        else:
            lw = _lay_attn(inp["mix_norm_g"][layer], inp["attn_w_qkv"][i], inp["attn_q_g"][i], inp["attn_k_g"][i], inp["attn_w_o"][i])
        lw["mg"] = lw.pop("g")
        lf = _lay_ffn(inp["ffn_norm_g"][layer], inp["ffn_w_up"][layer], inp["ffn_dw_w"][layer], inp["ffn_dw_b"][layer], inp["ffn_w_down"][layer])
        lf["fg"] = lf.pop("g")
        for k, v in list(lw.items()) + list(lf.items()):
            base[L + k] = v
    masks = [_masks(p) for p in range(2)]
    sels = [np.ascontiguousarray(np.tile(np.array([[1.0, 0.0]], np.float32) if p == 0 else np.array([[0.0, 1.0]], np.float32), (128, 1)))
            for p in range(2)]
    in_maps = []
    for c in range(NCORES):
        d = dict(base)
        d["xT"] = np.ascontiguousarray(x[c // 2][own[c % 2]].T)
        d["mask"] = masks[c % 2]
        d["sel"] = sels[c % 2]
        in_maps.append(d)
    nc, _ = _fused_prog()
    res = run_bass_kernel_spmd(nc, in_maps, core_ids=list(range(NCORES))).results
    out = np.empty((BATCH, SEQ, D), np.float32)
    for c in range(NCORES):
        out[c // 2][own[c % 2]] = res[c]["out"].T
    return out
```

```python
import numpy as np
from contextlib import ExitStack
import concourse.bass as bass
import concourse.mybir as mybir
from concourse.bass_utils import run_bass_kernel_spmd

F32 = mybir.dt.float32
BF16 = mybir.dt.bfloat16
AF = mybir.ActivationFunctionType
ALU = mybir.AluOpType

ENGINES = ("pe", "act", "dve", "pool", "sp")
ENG_ATTR = {"pe": "tensor", "act": "scalar", "dve": "vector", "pool": "gpsimd", "sp": "sync"}


class _Op:
    __slots__ = ("fn", "waits", "inc")

    def __init__(self, fn):
        self.fn = fn
        self.waits = []
        self.inc = None


class Sched:
    SEM_ROT = 20000

    def __init__(self):
        self.ops = {e: [] for e in ENGINES}
        self.built = {e: 0 for e in ENGINES}
        self.ms = {e: [] for e in ENGINES}
        self.gen = {e: 0 for e in ENGINES}
        self.cnt = {}
        self.waited = {e: {} for e in ENGINES}
        self.last_w = {}
        self.readers = {}
        self.nwaits = 0
        self.sems = {}

    def _new_ms(self, e, seq, op):
        k = ("eng", e, self.gen[e])
        if self.cnt.get(k, 0) >= self.SEM_ROT:
            self.gen[e] += 1
            k = ("eng", e, self.gen[e])
        self.cnt[k] = self.cnt.get(k, 0) + 1
        op.inc = (k, 1)
        self.ms[e].append((seq, k, self.cnt[k]))
        return k, self.cnt[k]

    def _milestone(self, e, seq):
        lo, hi = 0, len(self.ms[e])
        while lo < hi:
            mid = (lo + hi) // 2
            if self.ms[e][mid][0] >= seq:
                hi = mid
            else:
                lo = mid + 1
        if lo < len(self.ms[e]):
            return self.ms[e][lo][1], self.ms[e][lo][2]
        last = len(self.ops[e]) - 1
        while self.ops[e][last].fn is None:
            last -= 1
        assert last >= seq and last >= self.built[e], (e, seq, last, self.built[e])
        op = self.ops[e][last]
        assert op.inc is None, f"last op on {e} already has an inc"
        return self._new_ms(e, last, op)

    def _resolve(self, tok):
        if tok[0] == "eng":
            return self._milestone(tok[1], tok[2])
        return tok[1], tok[2]

    def _deps(self, engine, reads, writes):
        toks = []
        for k in reads:
            t = self.last_w.get(k)
            if t is not None:
                toks.append(t)
        for k in writes:
            t = self.last_w.get(k)
            if t is not None and not (t[0] == "eng" and t[1] == engine == "pe"):
                toks.append(t)
            for t in self.readers.get(k, {}).values():
                if not (t[0] == "eng" and t[1] == engine):
                    toks.append(t)
        return self._waits(engine, toks)

    def _waits(self, engine, toks):
        waits = {}
        for t in toks:
            sk, v = self._resolve(t)
            if self.waited[engine].get(sk, 0) >= v:
                continue
            waits[sk] = max(waits.get(sk, 0), v)
        for sk, v in waits.items():
            self.waited[engine][sk] = v
        self.nwaits += len(waits)
        return list(waits.items())

    def _track(self, tok, rkey, reads, writes):
        for k in writes:
            self.last_w[k] = tok
            self.readers[k] = {}
        for k in reads:
            self.readers.setdefault(k, {})[rkey] = tok

    def emit(self, engine, fn, reads=(), writes=(), ms=False, banks=()):
        writes = list(writes) + list(banks)
        op = _Op(fn)
        op.waits = self._deps(engine, reads, writes)
        seq = len(self.ops[engine])
        self.ops[engine].append(op)
        if ms or engine != "pe":
            self._new_ms(engine, seq, op)
        tok = ("eng", engine, seq)
        self._track(tok, engine, reads, writes)
        return tok

    def dma(self, queue, sem, fn, reads=(), writes=(), inc=16):
        op = _Op(fn)
        op.waits = self._deps(queue, reads, writes)
        self.ops[queue].append(op)
        k = ("dma", sem.split(".", 1)[-1])
        self.cnt[k] = self.cnt.get(k, 0) + inc
        op.inc = (k, inc)
        tok = ("dma", k, self.cnt[k])
        self._track(tok, k, reads, writes)
        return tok

    def dma_group(self, queue, sem, items):
        toks = [self.dma(queue, sem, fn, reads, writes) for (fn, reads, writes) in items]
        final = toks[-1]
        for (fn, reads, writes) in items:
            for k in writes:
                self.last_w[k] = final
            for k in reads:
                self.readers.setdefault(k, {})[final[1]] = final
        return final

    def wait_all(self, engine, toks):
        op = _Op(None)
        op.waits = self._waits(engine, toks)
        self.ops[engine].append(op)

    def barrier(self, toks=()):
        toks = list(toks)
        for e in ("pe", "act", "dve", "pool"):
            last = len(self.ops[e]) - 1
            while last >= self.built[e] and (self.ops[e][last].fn is None or self.ops[e][last].inc is not None and self.ops[e][last].inc[0][0] == "dma"):
                last -= 1
            if last >= self.built[e]:
                toks.append(("eng", e, last))
        for e in ENGINES:
            self.wait_all(e, toks)

    def build_phase(self, nc, st, outer):
        block = st.enter_context(nc.Block())
        for e in ENGINES:
            deco = getattr(block, ENG_ATTR[e])

            def body(eng, e=e):
                for op in self.ops[e][self.built[e]:]:
                    for (sk, v) in op.waits:
                        eng.wait_ge(self._sem(nc, outer, sk), v)
                    if op.fn is None:
                        continue
                    ins = op.fn(eng)
                    if op.inc is not None:
                        ins.then_inc(self._sem(nc, outer, op.inc[0]), op.inc[1])
                    op.fn = None
                self.built[e] = len(self.ops[e])

            deco(body)

    def _sem(self, nc, outer, k):
        if k not in self.sems:
            self.sems[k] = outer.enter_context(nc.semaphore(f"s{len(self.sems)}"))
        return self.sems[k]

    def build(self, nc, st):
        self.build_phase(nc, st, st)


D = 1024
KD = D // 128
CH = 512
NCHUNK = 8
TOK = CH * NCHUNK
HALO = 32
DFF = 2816
NJ = DFF // 128
EPS = 1e-6


class Ctx:
    def __init__(self, nc, S, st):
        self.nc, self.S, self.st = nc, S, st
        self.n = 0

    def sb(self, name, shape, dt):
        return self.st.enter_context(self.nc.sbuf_tensor(name, list(shape), dt))

    def ps(self, name, shape=(128, 512), dt=F32):
        return self.st.enter_context(self.nc.psum_tensor(name, list(shape), dt))


def emit_rmsnorm(cx, pfx, x_k, h_k, xkeys, hkeys, ncols, sq, ones, pst, pst_bank, tmp, rstd, g_sb, gkey):
    S = cx.S
    for k in range(KD):
        S.emit("act", lambda e, k=k: e.activation(out=sq[:, k, 0:ncols], in_=x_k(k), func=AF.Square),
               reads=[xkeys[k]], writes=[pfx + f"sq{k}"])
    for k in range(KD):
        S.emit("pe", lambda e, k=k: e.matmul(pst, ones[:, :], sq[:, k, 0:ncols], start=(k == 0), stop=(k == KD - 1)),
               reads=[pfx + f"sq{k}", pfx + "ones"], banks=[pst_bank], ms=(k == KD - 1))
    S.emit("dve", lambda e: e.tensor_scalar(out=tmp[:, 0:ncols], in0=pst, scalar1=1.0 / D, scalar2=EPS,
                                            op0=ALU.mult, op1=ALU.add), banks=[pst_bank], writes=[pfx + "tmp"])
    S.emit("act", lambda e: e.activation(out=tmp[:, 0:ncols], in_=tmp[:, 0:ncols], func=AF.Sqrt),
           reads=[pfx + "tmp"], writes=[pfx + "tmp"])
    S.emit("dve", lambda e: e.reciprocal(out=rstd[:, 0:ncols], in_=tmp[:, 0:ncols]), reads=[pfx + "tmp"], writes=[pfx + "rstd"])
    for k in range(KD):
        S.emit("dve", lambda e, k=k: e.scalar_tensor_tensor(out=h_k(k), in0=x_k(k), scalar=g_sb[:, k:k + 1], in1=rstd[:, 0:ncols],
                                                            op0=ALU.mult, op1=ALU.mult),
               reads=[xkeys[k], pfx + "rstd", gkey], writes=[hkeys[k]])


def ffn_phase(cx, pfx, xT, xh, xoT, g_d, wup_d, wdn_d, dw_d, db_d, nst=NCHUNK // 2):
    S, nc = cx.S, cx.nc
    HH = 2
    W = HH + CH
    SC = 2
    x_sb = cx.sb(pfx + "x", [128, KD, SC, W], F32)
    hT = cx.sb(pfx + "hT", [128, KD, SC, W], BF16)
    sq = cx.sb(pfx + "sq", [128, KD, CH], BF16)
    rstd = cx.sb(pfx + "rstd", [128, CH], F32)
    tmp = cx.sb(pfx + "tmp", [128, CH], F32)
    ones = cx.sb(pfx + "ones", [128, 128], BF16)
    g_sb = cx.sb(pfx + "g", [128, KD], F32)
    dw_sb = cx.sb(pfx + "dw", [128, 2 * NJ, 3], F32)
    db_sb = cx.sb(pfx + "db", [128, 2 * NJ], F32)
    wup = [cx.sb(pfx + f"wup{i}", [128, KD, 256], BF16) for i in range(2)]
    wdn = [cx.sb(pfx + f"wdn{i}", [128, NJ, 128], BF16) for i in range(2)]
    gT = cx.sb(pfx + "gT", [128, NJ, SC, CH], BF16)
    gacc = [cx.sb(pfx + f"gacc{i}", [128, CH], F32) for i in range(2)]
    vacc = [cx.sb(pfx + f"vacc{i}", [128, CH], F32) for i in range(2)]
    sg = [cx.sb(pfx + f"sg{i}", [128, CH], F32) for i in range(2)]
    phs = [cx.sb(pfx + f"phs{i}", [128, 4], F32) for i in range(2)]
    xo = [cx.sb(pfx + f"xo{i}", [128, CH], F32) for i in range(2)]
    pg = [cx.ps(pfx + f"pg{i}") for i in range(2)]
    pv = [cx.ps(pfx + f"pv{i}") for i in range(2)]
    pmisc = cx.ps(pfx + "pmisc")
    pstat = cx.ps(pfx + "pstat")
    pd = [cx.ps(pfx + f"pd{i}") for i in range(2)]
    ph = [pmisc[:, 0:4], pstat[:, 0:4]]
    phk = [pfx + "pmisc", pfx + "pstat"]
    pstat_h = pmisc[:, 32:32 + HH]

    xT_r = xT.rearrange("(k p) t -> p k t", p=128)
    xh_r = xh.rearrange("(k p) (c h) -> p k c h", p=128, h=HALO)
    xo_r = xoT.rearrange("(k p) t -> p k t", p=128)

    S.emit("dve", lambda e: e.memset(ones[:], 1.0), writes=[pfx + "ones"])
    S.dma_group("sp", "cst", [(lambda e: e.dma_start(out=g_sb[:], in_=g_d), [], [pfx + "n.g"]),
                              (lambda e: e.dma_start(out=dw_sb[:], in_=dw_d), [], [pfx + "dw"]),
                              (lambda e: e.dma_start(out=db_sb[:], in_=db_d), [], [pfx + "db"])])
    out_toks = []
    wu_i = 0
    wd_i = 0
    it = 0
    for sti in range(nst):
        for c in range(SC):
            gc = sti * SC + c
            S.dma_group("sp", f"lx{c}", [(lambda e, c=c, k=k, gc=gc: e.dma_start(out=x_sb[:, k, c, HH:W], in_=xT_r[:, k, gc * CH:(gc + 1) * CH]),
                                          [], [pfx + f"x{c}.{k}m"]) for k in range(KD)])
            S.dma("sp", pfx + f"lxh{c}",
                  lambda e, c=c, gc=gc: e.dma_start(out=x_sb[:, :, c, 0:HH], in_=xh_r[:, :, gc, HALO - HH:HALO]),
                  writes=[pfx + f"x{c}.h"])
        for c in range(SC):
            xk = [pfx + f"x{c}.{k}m" for k in range(KD)]
            for k in range(KD):
                S.emit("act", lambda e, k=k, c=c: e.activation(out=sq[:, k, :], in_=x_sb[:, k, c, HH:W], func=AF.Square),
                       reads=[xk[k]], writes=[pfx + f"sq{k}"])
            for k in range(KD):
                S.emit("pe", lambda e, k=k: e.matmul(pstat[:, :], ones[:, :], sq[:, k, :], start=(k == 0), stop=(k == KD - 1)),
                       reads=[pfx + f"sq{k}", pfx + "ones"], banks=[pfx + "pstat"], ms=(k == KD - 1))
            S.emit("dve", lambda e: e.tensor_scalar(out=tmp[:, :], in0=pstat[:, :], scalar1=1.0 / D, scalar2=EPS,
                                                    op0=ALU.mult, op1=ALU.add),
                   banks=[pfx + "pstat"], writes=[pfx + "tmp"])
            S.emit("act", lambda e: e.activation(out=tmp[:, :], in_=tmp[:, :], func=AF.Sqrt),
                   reads=[pfx + "tmp"], writes=[pfx + "tmp"])
            S.emit("dve", lambda e: e.reciprocal(out=rstd[:, :], in_=tmp[:, :]), reads=[pfx + "tmp"], writes=[pfx + "rstd"])
            for k in range(KD):
                S.emit("dve", lambda e, k=k, c=c: e.scalar_tensor_tensor(
                    out=hT[:, k, c, HH:W], in0=x_sb[:, k, c, HH:W], scalar=g_sb[:, k:k + 1], in1=rstd[:, :],
                    op0=ALU.mult, op1=ALU.mult),
                    reads=[xk[k], pfx + "rstd", pfx + "n.g"], writes=[pfx + f"h{c}.{k}m"])
            S.emit("act", lambda e, c=c: e.activation(out=sq[:, :, 0:HH], in_=x_sb[:, :, c, 0:HH], func=AF.Square),
                   reads=[pfx + f"x{c}.h"], writes=[pfx + f"sq{k}" for k in range(KD)])
            for k in range(KD):
                S.emit("pe", lambda e, k=k: e.matmul(pstat_h, ones[:, :], sq[:, k, 0:HH], start=(k == 0), stop=(k == KD - 1)),
                       reads=[pfx + f"sq{k}", pfx + "ones"], banks=[pfx + "pmisc"], ms=(k == KD - 1))
            S.emit("dve", lambda e: e.tensor_scalar(out=tmp[:, 0:HH], in0=pstat_h, scalar1=1.0 / D, scalar2=EPS,
                                                    op0=ALU.mult, op1=ALU.add),
                   banks=[pfx + "pmisc"], writes=[pfx + "tmp"])
            S.emit("act", lambda e: e.activation(out=tmp[:, 0:HH], in_=tmp[:, 0:HH], func=AF.Sqrt),
                   reads=[pfx + "tmp"], writes=[pfx + "tmp"])
            S.emit("dve", lambda e: e.reciprocal(out=rstd[:, 0:HH], in_=tmp[:, 0:HH]), reads=[pfx + "tmp"], writes=[pfx + "rstd"])
            for k in range(KD):
                S.emit("dve", lambda e, k=k, c=c: e.scalar_tensor_tensor(
                    out=hT[:, k, c, 0:HH], in0=x_sb[:, k, c, 0:HH], scalar=g_sb[:, k:k + 1], in1=rstd[:, 0:HH],
                    op0=ALU.mult, op1=ALU.mult),
                    reads=[pfx + f"x{c}.h", pfx + "rstd", pfx + "n.g"], writes=[pfx + f"h{c}.{k}h"])
        def load_wup(j, slot):
            S.dma("pool", pfx + f"wu{slot}", lambda e, j=j, slot=slot: e.dma_start(out=wup[slot][:], in_=wup_d[j]),
                  writes=[pfx + f"wup{slot}"])

        def load_wdn(n, slot):
            S.dma("pool", pfx + f"wd{slot}", lambda e, n=n, slot=slot: e.dma_start(out=wdn[slot][:], in_=wdn_d[n]),
                  writes=[pfx + f"wdn{slot}"])

        load_wup(0, wu_i % 2)
        for j in range(NJ):
            slot = wu_i % 2
            if j + 1 < NJ:
                load_wup(j + 1, (wu_i + 1) % 2)
            else:
                load_wdn(0, wd_i % 2)
            wu_i += 1
            for c in range(SC):
                b = it % 2
                it += 1
                hk = [pfx + f"h{c}.{k}m" for k in range(KD)]
                hh = [pfx + f"h{c}.{k}h" for k in range(KD)]
                for half, (pp, acc, jc) in enumerate(((pg[b], gacc[b], j), (pv[b], vacc[b], NJ + j))):
                    co = half * 128
                    pk = pfx + f"p{half}{b}"
                    ak = pfx + f"acc{half}{b}"
                    for k in range(KD):
                        S.emit("pe", lambda e, k=k, c=c, pp=pp, co=co, slot=slot: e.matmul(
                            pp[:, :], wup[slot][:, k, co:co + 128], hT[:, k, c, HH:W], start=(k == 0), stop=(k == KD - 1)),
                            reads=[hk[k], pfx + f"wup{slot}"], banks=[pk], ms=(k == KD - 1))
                    for k in range(KD):
                        S.emit("pe", lambda e, k=k, c=c, co=co, slot=slot, b=b, half=half: e.matmul(
                            ph[b][:, 2 * half:2 * half + 2], wup[slot][:, k, co:co + 128], hT[:, k, c, 0:HH],
                            start=(k == 0), stop=(k == KD - 1)),
                            reads=[hh[k], pfx + f"wup{slot}"], banks=[phk[b]], ms=(k == KD - 1))
                    S.emit("act", lambda e, pp=pp, acc=acc, jc=jc: e.activation(
                        out=acc[:, :], in_=pp[:, :], func=AF.Identity, scale=dw_sb[:, jc, 2:3], bias=db_sb[:, jc:jc + 1]),
                        reads=[pfx + "dw", pfx + "db"], writes=[ak], banks=[pk])
                    S.emit("dve", lambda e, pp=pp, acc=acc, jc=jc: e.scalar_tensor_tensor(
                        out=acc[:, 1:CH], in0=pp[:, 0:CH - 1], scalar=dw_sb[:, jc, 1:2], in1=acc[:, 1:CH],
                        op0=ALU.mult, op1=ALU.add), reads=[ak, pfx + "dw"], writes=[ak], banks=[pk])
                    S.emit("dve", lambda e, pp=pp, acc=acc, jc=jc: e.scalar_tensor_tensor(
                        out=acc[:, 2:CH], in0=pp[:, 0:CH - 2], scalar=dw_sb[:, jc, 0:1], in1=acc[:, 2:CH],
                        op0=ALU.mult, op1=ALU.add), reads=[ak, pfx + "dw"], writes=[ak], banks=[pk])
                for half, (acc, jc) in enumerate(((gacc[b], j), (vacc[b], NJ + j))):
                    ak = pfx + f"acc{half}{b}"
                    S.emit("dve", lambda e, acc=acc, jc=jc, b=b, half=half: e.scalar_tensor_tensor(
                        out=acc[:, 0:2], in0=ph[b][:, 2 * half:2 * half + 2], scalar=dw_sb[:, jc, 0:1], in1=acc[:, 0:2],
                        op0=ALU.mult, op1=ALU.add), reads=[ak, pfx + "dw"], writes=[ak], banks=[phk[b]])
                    S.emit("dve", lambda e, acc=acc, jc=jc, b=b, half=half: e.scalar_tensor_tensor(
                        out=acc[:, 0:1], in0=ph[b][:, 2 * half + 1:2 * half + 2], scalar=dw_sb[:, jc, 1:2], in1=acc[:, 0:1],
                        op0=ALU.mult, op1=ALU.add), reads=[ak, pfx + "dw"], writes=[ak], banks=[phk[b]])
                S.emit("act", lambda e, b=b: e.activation(out=sg[b][:, :], in_=gacc[b][:, :], func=AF.Silu),
                       reads=[pfx + f"acc0{b}"], writes=[pfx + f"sg{b}"])
                S.emit("dve", lambda e, b=b, j=j, c=c: e.tensor_tensor(out=gT[:, j, c, :], in0=sg[b][:, :], in1=vacc[b][:, :],
                                                                      op=ALU.mult),
                       reads=[pfx + f"sg{b}", pfx + f"acc1{b}"], writes=[pfx + f"gT{j}.{c}"])
        for n in range(KD):
            slot = wd_i % 2
            if n + 1 < KD:
                load_wdn(n + 1, (wd_i + 1) % 2)
            wd_i += 1
            for c in range(SC):
                gc = sti * SC + c
                b = it % 2
                it += 1
                for jj in range(NJ):
                    S.emit("pe", lambda e, jj=jj, c=c, b=b, slot=slot: e.matmul(
                        pd[b][:, :], wdn[slot][:, jj, :], gT[:, jj, c, :], start=(jj == 0), stop=(jj == NJ - 1)),
                        reads=[pfx + f"gT{jj}.{c}", pfx + f"wdn{slot}"], banks=[pfx + f"pd{b}"], ms=(jj == NJ - 1))
                S.emit("dve", lambda e, b=b, n=n, c=c: e.tensor_tensor(out=xo[b][:, :], in0=pd[b][:, :], in1=x_sb[:, n, c, HH:W],
                                                                      op=ALU.add),
                       reads=[pfx + f"x{c}.{n}m"], writes=[pfx + f"xo{b}"], banks=[pfx + f"pd{b}"])
                t = S.dma("sp", pfx + f"so{b}", lambda e, b=b, n=n, gc=gc: e.dma_start(
                    out=xo_r[:, n, gc * CH:(gc + 1) * CH], in_=xo[b][:, :]), reads=[pfx + f"xo{b}"], writes=[pfx + f"xoT.{n}.{gc}"])
                out_toks.append(t)
    return out_toks


DA = 512
MA = DA // 128
KA = 31


def conv_phase(cx, pfx, xT, xh, xoT, g_d, win_d, wout_d, adw_d, avec_d, bdw_d, nch=NCHUNK):
    S, nc = cx.S, cx.nc
    H = HALO
    W = H + CH
    x_sb = cx.sb(pfx + "x", [128, KD, W], F32)
    hT = cx.sb(pfx + "hT", [128, KD, W], BF16)
    sq = cx.sb(pfx + "sq", [128, KD, CH], BF16)
    rstd = cx.sb(pfx + "rstd", [128, CH], F32)
    tmp = cx.sb(pfx + "tmp", [128, CH], F32)
    ones = cx.sb(pfx + "ones", [128, 128], BF16)
    onesm = cx.sb(pfx + "onesm", [128, 128], BF16)
    g_sb = cx.sb(pfx + "g", [128, KD], F32)
    adw = cx.sb(pfx + "adw", [128, MA, KA], F32)
    avec = cx.sb(pfx + "avec", [128, 3, MA], F32)
    bdw = cx.sb(pfx + "bdw", [128, MA, 3], F32)
    win = cx.sb(pfx + "win", [128, 20, KD, 128], BF16)
    wout = cx.sb(pfx + "wout", [128, KD, KD, 128], BF16)
    glu = cx.sb(pfx + "glu", [128, MA, W], F32)
    ca = cx.sb(pfx + "ca", [128, MA, CH], F32)
    cab = cx.sb(pfx + "cab", [128, MA, CH], BF16)
    abT = cx.sb(pfx + "abT", [128, 2 * MA, CH], BF16)
    sgm = cx.sb(pfx + "sgm", [128, W], F32)
    chb = cx.sb(pfx + "chb", [128, W], F32)
    accb = cx.sb(pfx + "accb", [128, CH], F32)
    xo = [cx.sb(pfx + f"xo{i}", [128, CH], F32) for i in range(2)]
    pA, pB, pC = cx.ps(pfx + "pA"), cx.ps(pfx + "pB"), cx.ps(pfx + "pC")
    pmisc, pstat, pvar = cx.ps(pfx + "pmisc"), cx.ps(pfx + "pstat"), cx.ps(pfx + "pvar")
    po = [cx.ps(pfx + f"po{i}") for i in range(2)]
    BK = lambda n: pfx + "B." + n

    xT_r = xT.rearrange("(k p) t -> p k t", p=128)
    xh_r = xh.rearrange("(k p) (c h) -> p k c h", p=128, h=HALO)
    xo_r = xoT.rearrange("(k p) t -> p k t", p=128)

    S.emit("dve", lambda e: e.memset(ones[:], 1.0), writes=[pfx + "ones"])
    S.emit("dve", lambda e: e.memset(onesm[:], 1.0 / DA), writes=[pfx + "onesm"])
    S.dma_group("sp", "cst", [(lambda e: e.dma_start(out=g_sb[:], in_=g_d), [], [pfx + "g"]),
                              (lambda e: e.dma_start(out=adw[:], in_=adw_d), [], [pfx + "adw"]),
                              (lambda e: e.dma_start(out=avec[:], in_=avec_d), [], [pfx + "avec"]),
                              (lambda e: e.dma_start(out=bdw[:], in_=bdw_d), [], [pfx + "bdw"])])
    S.dma_group("pool", "wA", [(lambda e, m=m: e.dma_start(out=win[:, m], in_=win_d[m]), [], [pfx + f"win{m}"]) for m in range(20)])
    S.dma_group("pool", "wB", [(lambda e, n=n: e.dma_start(out=wout[:, n], in_=wout_d[n]), [], [pfx + f"wout{n}"]) for n in range(KD)])

    out_toks = []
    it = 0
    for c in range(nch):
        xk = [pfx + f"x{k}" for k in range(KD)]
        hk = [pfx + f"h{k}" for k in range(KD)]
        S.dma_group("sp", "lx", [(lambda e, k=k, c=c: e.dma_start(out=x_sb[:, k, H:W], in_=xT_r[:, k, c * CH:(c + 1) * CH]), [], [xk[k]])
                                 for k in range(KD)])
        S.dma("sp", pfx + "lxh", lambda e, c=c: e.dma_start(out=x_sb[:, :, 0:H], in_=xh_r[:, :, c, :]), writes=[pfx + "xh"])
        emit_rmsnorm(cx, pfx, lambda k: x_sb[:, k, H:W], lambda k: hT[:, k, H:W], xk, hk, CH, sq, ones,
                     pstat[:, :], BK("pstat"), tmp, rstd, g_sb, pfx + "g")
        emit_rmsnorm(cx, pfx, lambda k: x_sb[:, k, 0:H], lambda k: hT[:, k, 0:H], [pfx + "xh"] * KD,
                     [pfx + f"hh{k}" for k in range(KD)], H, sq, ones, pmisc[:, 256:256 + H], BK("pmisc"), tmp, rstd, g_sb, pfx + "g")
        hh = [pfx + f"hh{k}" for k in range(KD)]

        def proj(m, pmain, bank, hcol=None):
            for k in range(KD):
                S.emit("pe", lambda e, k=k, m=m: e.matmul(pmain[:, :], win[:, m, k, :], hT[:, k, H:W], start=(k == 0), stop=(k == KD - 1)),
                       reads=[hk[k], pfx + f"win{m}"], banks=[bank], ms=(k == KD - 1))
            if hcol is not None:
                for k in range(KD):
                    S.emit("pe", lambda e, k=k, m=m: e.matmul(pmisc[:, hcol:hcol + H], win[:, m, k, :], hT[:, k, 0:H],
                                                              start=(k == 0), stop=(k == KD - 1)),
                           reads=[hh[k], pfx + f"win{m}"], banks=[BK("pmisc")], ms=(k == KD - 1))

        for m in range(MA):
            proj(m, pA, BK("pA"), 0)
            proj(MA + m, pB, BK("pB"), H)
            S.emit("act", lambda e: e.activation(out=sgm[:, H:W], in_=pB[:, :], func=AF.Sigmoid), banks=[BK("pB")], writes=[pfx + "sgm"])
            S.emit("act", lambda e: e.activation(out=sgm[:, 0:H], in_=pmisc[:, H:2 * H], func=AF.Sigmoid), banks=[BK("pmisc")], writes=[pfx + "sgm"])
            S.emit("dve", lambda e, m=m: e.tensor_tensor(out=glu[:, m, H:W], in0=pA[:, :], in1=sgm[:, H:W], op=ALU.mult),
                   reads=[pfx + "sgm"], banks=[BK("pA")], writes=[pfx + f"glu{m}"])
            S.emit("dve", lambda e, m=m: e.tensor_tensor(out=glu[:, m, 0:H], in0=pmisc[:, 0:H], in1=sgm[:, 0:H], op=ALU.mult),
                   reads=[pfx + "sgm"], banks=[BK("pmisc")], writes=[pfx + f"glu{m}"])
            S.emit("act", lambda e, m=m: e.activation(out=ca[:, m, :], in_=glu[:, m, H:W], func=AF.Identity,
                                                      scale=adw[:, m, KA - 1:KA], bias=avec[:, 0, m:m + 1]),
                   reads=[pfx + f"glu{m}", pfx + "adw", pfx + "avec"], writes=[pfx + f"ca{m}"])
            for k in range(KA - 1):
                S.emit("dve", lambda e, m=m, k=k: e.scalar_tensor_tensor(
                    out=ca[:, m, :], in0=glu[:, m, 2 + k:2 + k + CH], scalar=adw[:, m, k:k + 1], in1=ca[:, m, :],
                    op0=ALU.mult, op1=ALU.add), reads=[pfx + f"glu{m}", pfx + f"ca{m}", pfx + "adw"], writes=[pfx + f"ca{m}"])
            S.emit("act", lambda e, m=m: e.activation(out=cab[:, m, :], in_=ca[:, m, :], func=AF.Identity),
                   reads=[pfx + f"ca{m}"], writes=[pfx + f"cab{m}"])
        for m in range(MA):
            S.emit("pe", lambda e, m=m: e.matmul(pstat[:, :], onesm[:, :], cab[:, m, :], start=(m == 0), stop=(m == MA - 1)),
                   reads=[pfx + f"cab{m}", pfx + "onesm"], banks=[BK("pstat")], ms=(m == MA - 1))
        for m in range(MA):
            S.emit("dve", lambda e, m=m: e.tensor_tensor(out=ca[:, m, :], in0=ca[:, m, :], in1=pstat[:, :], op=ALU.subtract),
                   reads=[pfx + f"ca{m}"], banks=[BK("pstat")], writes=[pfx + f"ca{m}"])
            S.emit("act", lambda e, m=m: e.activation(out=cab[:, m, :], in_=ca[:, m, :], func=AF.Square),
                   reads=[pfx + f"ca{m}"], writes=[pfx + f"cab{m}"])
        for m in range(MA):
            S.emit("pe", lambda e, m=m: e.matmul(pvar[:, :], onesm[:, :], cab[:, m, :], start=(m == 0), stop=(m == MA - 1)),
                   reads=[pfx + f"cab{m}", pfx + "onesm"], banks=[BK("pvar")], ms=(m == MA - 1))
        S.emit("dve", lambda e: e.tensor_scalar(out=tmp[:, :], in0=pvar[:, :], scalar1=EPS, scalar2=None, op0=ALU.add),
               banks=[BK("pvar")], writes=[pfx + "tmp"])
        S.emit("act", lambda e: e.activation(out=tmp[:, :], in_=tmp[:, :], func=AF.Sqrt), reads=[pfx + "tmp"], writes=[pfx + "tmp"])
        S.emit("dve", lambda e: e.reciprocal(out=rstd[:, :], in_=tmp[:, :]), reads=[pfx + "tmp"], writes=[pfx + "rstd"])
        for m in range(MA):
            S.emit("dve", lambda e, m=m: e.scalar_tensor_tensor(out=ca[:, m, :], in0=ca[:, m, :], scalar=avec[:, 1, m:m + 1], in1=rstd[:, :],
                                                                op0=ALU.mult, op1=ALU.mult),
                   reads=[pfx + f"ca{m}", pfx + "rstd", pfx + "avec"], writes=[pfx + f"ca{m}"])
            S.emit("act", lambda e, m=m: e.activation(out=abT[:, m, :], in_=ca[:, m, :], func=AF.Silu, bias=avec[:, 2, m:m + 1]),
                   reads=[pfx + f"ca{m}", pfx + "avec"], writes=[pfx + f"ab{m}"])
        for m in range(MA):
            proj(3 * MA + m, pA, BK("pA"), 2 * H)
            proj(4 * MA + m, pB, BK("pB"), 3 * H)
            proj(2 * MA + m, pC, BK("pC"), None)
            S.emit("act", lambda e: e.activation(out=sgm[:, H:W], in_=pA[:, :], func=AF.Identity), banks=[BK("pA")], writes=[pfx + "sgm"])
            S.emit("act", lambda e: e.activation(out=sgm[:, 0:H], in_=pmisc[:, 2 * H:3 * H], func=AF.Identity), banks=[BK("pmisc")], writes=[pfx + "sgm"])
            S.emit("dve", lambda e: e.tensor_tensor(out=chb[:, H:W], in0=pB[:, :], in1=sgm[:, H:W], op=ALU.mult),
                   reads=[pfx + "sgm"], banks=[BK("pB")], writes=[pfx + "chb"])
            S.emit("dve", lambda e: e.tensor_tensor(out=chb[:, 0:H], in0=pmisc[:, 3 * H:4 * H], in1=sgm[:, 0:H], op=ALU.mult),
                   reads=[pfx + "sgm"], banks=[BK("pmisc")], writes=[pfx + "chb"])
            S.emit("act", lambda e, m=m: e.activation(out=accb[:, :], in_=chb[:, H:W], func=AF.Identity, scale=bdw[:, m, 2:3]),
                   reads=[pfx + "chb", pfx + "bdw"], writes=[pfx + "accb"])
            for k in range(2):
                S.emit("dve", lambda e, m=m, k=k: e.scalar_tensor_tensor(
                    out=accb[:, :], in0=chb[:, H - 2 + k:H - 2 + k + CH], scalar=bdw[:, m, k:k + 1], in1=accb[:, :],
                    op0=ALU.mult, op1=ALU.add), reads=[pfx + "chb", pfx + "accb", pfx + "bdw"], writes=[pfx + "accb"])
            S.emit("dve", lambda e, m=m: e.tensor_tensor(out=abT[:, MA + m, :], in0=pC[:, :], in1=accb[:, :], op=ALU.mult),
                   reads=[pfx + "accb"], banks=[BK("pC")], writes=[pfx + f"ab{MA + m}"])
        for n in range(KD):
            b = it % 2
            it += 1
            for k in range(2 * MA):
                S.emit("pe", lambda e, k=k, n=n, b=b: e.matmul(po[b][:, :], wout[:, n, k, :], abT[:, k, :], start=(k == 0), stop=(k == 2 * MA - 1)),
                       reads=[pfx + f"ab{k}", pfx + f"wout{n}"], banks=[BK(f"po{b}")], ms=(k == 2 * MA - 1))
            S.emit("dve", lambda e, b=b, n=n: e.tensor_tensor(out=xo[b][:, :], in0=po[b][:, :], in1=x_sb[:, n, H:W], op=ALU.add),
                   reads=[xk[n]], writes=[pfx + f"xo{b}"], banks=[BK(f"po{b}")])
            t = S.dma("sp", pfx + f"so{b}", lambda e, b=b, n=n, c=c: e.dma_start(out=xo_r[:, n, c * CH:(c + 1) * CH], in_=xo[b][:, :]),
                      reads=[pfx + f"xo{b}"], writes=[pfx + f"xoT.{n}.{c}"])
            out_toks.append(t)
    return out_toks


NH = 16
HP = NH // 2
NKB = TOK // 128


def qkv_phase(cx, pfx, xT, g_d, wq_d, wk_d, wv_d, qkg_d, qT_o, kT_o, V_o, nch=NCHUNK):
    S, nc = cx.S, cx.nc
    x_sb = cx.sb(pfx + "x", [128, KD, CH], F32)
    hT = cx.sb(pfx + "hT", [128, KD, CH], BF16)
    sq = cx.sb(pfx + "sq", [128, KD, CH], BF16)
    rstd = cx.sb(pfx + "rstd", [128, CH], F32)
    tmp = cx.sb(pfx + "tmp", [128, CH], F32)
    ones = cx.sb(pfx + "ones", [128, 128], BF16)
    bd = cx.sb(pfx + "bd", [128, 128], BF16)
    g_sb = cx.sb(pfx + "g", [128, KD], F32)
    qkg = cx.sb(pfx + "qkg", [128, 2], F32)
    wq = cx.sb(pfx + "wq", [128, HP, KD, 128], BF16)
    wk = cx.sb(pfx + "wk", [128, HP, KD, 128], BF16)
    wv = cx.sb(pfx + "wv", [128, 2, KD, 512], BF16)
    qf = [cx.sb(pfx + f"qf{i}", [128, CH], F32) for i in range(2)]
    sqq = [cx.sb(pfx + f"sqq{i}", [128, CH], BF16) for i in range(2)]
    rs = [cx.sb(pfx + f"rs{i}", [128, CH], F32) for i in range(2)]
    qn = [cx.sb(pfx + f"qn{i}", [128, CH], BF16) for i in range(2)]
    vt = [cx.sb(pfx + f"vt{i}", [128, 512], BF16) for i in range(2)]
    pq = [cx.ps(pfx + f"pq{i}") for i in range(2)]
    pms = [cx.ps(pfx + f"pms{i}") for i in range(2)]
    pvv = [cx.ps(pfx + f"pvv{i}") for i in range(2)]
    pstat = cx.ps(pfx + "pstat")
    BK = lambda n: pfx + "B." + n
    xT_r = xT.rearrange("(k p) t -> p k t", p=128)
    qT_r = qT_o.rearrange("(k p) t -> p k t", p=128)
    kT_r = kT_o.rearrange("(k p) t -> p k t", p=128)

    S.emit("dve", lambda e: e.memset(ones[:], 1.0), writes=[pfx + "ones"])
    S.emit("dve", lambda e: e.memset(bd[:], 0.0), writes=[pfx + "bd"])
    S.emit("dve", lambda e: e.memset(bd[0:64, 0:64], 1.0 / 64), writes=[pfx + "bd"])
    S.emit("dve", lambda e: e.memset(bd[64:128, 64:128], 1.0 / 64), writes=[pfx + "bd"])
    S.dma_group("sp", "cst", [(lambda e: e.dma_start(out=g_sb[:], in_=g_d), [], [pfx + "g"]),
                              (lambda e: e.dma_start(out=qkg[:], in_=qkg_d), [], [pfx + "qkg"])])
    S.emit("dve", lambda e: e.tensor_scalar(out=qkg[:, 0:1], in0=qkg[:, 0:1], scalar1=0.125, scalar2=None, op0=ALU.mult),
           reads=[pfx + "qkg"], writes=[pfx + "qkg"])
    S.dma_group("pool", "wA", [(lambda e, m=m: e.dma_start(out=wq[:, m], in_=wq_d[m]), [], [pfx + f"wq{m}"]) for m in range(HP)])
    S.dma_group("pool", "wB", [(lambda e, m=m: e.dma_start(out=wk[:, m], in_=wk_d[m]), [], [pfx + f"wk{m}"]) for m in range(HP)])
    S.dma_group("pool", "wC", [(lambda e, hf=hf: e.dma_start(out=wv[:, hf], in_=wv_d[hf]), [], [pfx + f"wv{hf}"]) for hf in range(2)])
    out_toks = []
    it = 0
    for c in range(nch):
        xk = [pfx + f"x{k}" for k in range(KD)]
        hk = [pfx + f"h{k}" for k in range(KD)]
        S.dma_group("sp", "lx", [(lambda e, k=k, c=c: e.dma_start(out=x_sb[:, k, :], in_=xT_r[:, k, c * CH:(c + 1) * CH]), [], [xk[k]])
                                 for k in range(KD)])
        emit_rmsnorm(cx, pfx, lambda k: x_sb[:, k, :], lambda k: hT[:, k, :], xk, hk, CH, sq, ones,
                     pstat[:, :], BK("pstat"), tmp, rstd, g_sb, pfx + "g")
        for which, (w_sb, wkey, gcol, o_r) in enumerate(((wq, "wq", 0, qT_r), (wk, "wk", 1, kT_r))):
            for m in range(HP):
                b = it % 2
                it += 1
                for k in range(KD):
                    S.emit("pe", lambda e, k=k, m=m, b=b, w_sb=w_sb: e.matmul(pq[b][:, :], w_sb[:, m, k, :], hT[:, k, :],
                                                                             start=(k == 0), stop=(k == KD - 1)),
                           reads=[hk[k], pfx + f"{wkey}{m}"], banks=[BK(f"pq{b}")], ms=(k == KD - 1))
                S.emit("act", lambda e, b=b: e.activation(out=sqq[b][:, :], in_=pq[b][:, :], func=AF.Square),
                       banks=[BK(f"pq{b}")], writes=[pfx + f"sqq{b}"])
                S.emit("act", lambda e, b=b: e.activation(out=qf[b][:, :], in_=pq[b][:, :], func=AF.Identity),
                       banks=[BK(f"pq{b}")], writes=[pfx + f"qf{b}"])
                S.emit("pe", lambda e, b=b: e.matmul(pms[b][:, :], bd[:, :], sqq[b][:, :], start=True, stop=True),
                       reads=[pfx + f"sqq{b}", pfx + "bd"], banks=[BK(f"pms{b}")], ms=True)
                S.emit("dve", lambda e, b=b: e.tensor_scalar(out=rs[b][:, :], in0=pms[b][:, :], scalar1=EPS, scalar2=None, op0=ALU.add),
                       banks=[BK(f"pms{b}")], writes=[pfx + f"rs{b}"])
                S.emit("act", lambda e, b=b: e.activation(out=rs[b][:, :], in_=rs[b][:, :], func=AF.Sqrt),
                       reads=[pfx + f"rs{b}"], writes=[pfx + f"rs{b}"])
                S.emit("dve", lambda e, b=b: e.reciprocal(out=rs[b][:, :], in_=rs[b][:, :]), reads=[pfx + f"rs{b}"], writes=[pfx + f"rs{b}"])
                S.emit("dve", lambda e, b=b, gcol=gcol: e.scalar_tensor_tensor(out=qn[b][:, :], in0=qf[b][:, :], scalar=qkg[:, gcol:gcol + 1],
                                                                               in1=rs[b][:, :], op0=ALU.mult, op1=ALU.mult),
                       reads=[pfx + f"qf{b}", pfx + f"rs{b}", pfx + "qkg"], writes=[pfx + f"qn{b}"])
                t = S.dma("sp", pfx + f"sq{b}", lambda e, b=b, m=m, c=c, o_r=o_r: e.dma_start(out=o_r[:, m, c * CH:(c + 1) * CH], in_=qn[b][:, :]),
                          reads=[pfx + f"qn{b}"], writes=[pfx + f"o{which}.{m}.{c}"])
                out_toks.append(t)
        for tt in range(CH // 128):
            kb = c * (CH // 128) + tt
            for hf in range(2):
                b = it % 2
                it += 1
                for k in range(KD):
                    S.emit("pe", lambda e, k=k, tt=tt, hf=hf, b=b: e.matmul(pvv[b][:, :], hT[:, k, tt * 128:(tt + 1) * 128], wv[:, hf, k, :],
                                                                          start=(k == 0), stop=(k == KD - 1)),
                           reads=[hk[k], pfx + f"wv{hf}"], banks=[BK(f"pvv{b}")], ms=(k == KD - 1))
                S.emit("act", lambda e, b=b: e.activation(out=vt[b][:, :], in_=pvv[b][:, :], func=AF.Identity),
                       banks=[BK(f"pvv{b}")], writes=[pfx + f"vt{b}"])
                t = S.dma("sp", pfx + f"sv{b}", lambda e, b=b, hf=hf, kb=kb: e.dma_start(
                    out=V_o[hf * 4:(hf + 1) * 4, :, kb, :].rearrange("h p f -> p h f"),
                    in_=vt[b][:, :].rearrange("p (h f) -> p h f", f=128)),
                    reads=[pfx + f"vt{b}"], writes=[pfx + f"oV.{hf}.{kb}"])
                out_toks.append(t)
    return out_toks


def attn_phase(cx, pfx, xT, xoT, qT_i, kT_g, V_g, mask_d, tri_d, wo_d, nhp=HP, nq=NCHUNK):
    S, nc = cx.S, cx.nc
    kT_sb = cx.sb(pfx + "kT", [128, 2, TOK], BF16)
    V_sb = cx.sb(pfx + "V", [128, 2, NKB, 128], BF16)
    q_sb = cx.sb(pfx + "q", [128, TOK], BF16)
    oT = cx.sb(pfx + "oT", [128, HP, TOK], BF16)
    mask = cx.sb(pfx + "mask", [128, 2, 4, 1024], BF16)
    ntri = cx.sb(pfx + "ntri", [128, 128], BF16)
    ident = cx.sb(pfx + "ident", [128, 128], BF16)
    nones = cx.sb(pfx + "nones", [128, 128], BF16)
    one1 = cx.sb(pfx + "one1", [128, 1], F32)
    E = [cx.sb(pfx + f"E{i}", [128, 1024], F32) for i in range(3)]
    L = [cx.sb(pfx + f"L{i}", [128, 1024], BF16) for i in range(3)]
    Wt = [cx.sb(pfx + f"W{i}", [128, 1024], BF16) for i in range(2)]
    Ls = cx.sb(pfx + "Ls", [128, 1024], F32)
    Lsb = [cx.sb(pfx + f"Lsb{i}", [128, 1024], BF16) for i in range(2)]
    wo = cx.sb(pfx + "wo", [128, KD, KD, 128], BF16)
    xr = [cx.sb(pfx + f"xr{i}", [128, CH], F32) for i in range(2)]
    xo = [cx.sb(pfx + f"xo{i}", [128, CH], F32) for i in range(2)]
    pz = [cx.ps(pfx + f"pz{i}", (128, 1024)) for i in range(3)]
    po = cx.ps(pfx + "po")
    pf = cx.ps(pfx + "pf")
    BK = lambda n: pfx + "B." + n
    xT_r = xT.rearrange("(k p) t -> p k t", p=128)
    xo_r = xoT.rearrange("(k p) t -> p k t", p=128)
    qT_r = qT_i.rearrange("(k p) t -> p k t", p=128)
    kT_r = kT_g.rearrange("(h r p) t -> p h r t", r=2, p=128)
    V_r = V_g.rearrange("(h r p) (k f) -> h p r k f", r=2, p=128, f=128)

    S.dma("pool", pfx + "ct", lambda e: e.dma_start(out=ntri[:], in_=tri_d[:, 0:128]), writes=[pfx + "ntri"])
    S.dma("pool", pfx + "ct", lambda e: e.dma_start(out=ident[:], in_=tri_d[:, 128:256]), writes=[pfx + "ident"])
    S.emit("dve", lambda e: e.memset(nones[:], -1.0), writes=[pfx + "nones"])
    S.emit("dve", lambda e: e.memset(one1[:], 1.0), writes=[pfx + "one1"])
    S.dma("pool", pfx + "cm", lambda e: e.dma_start(out=mask[:], in_=mask_d), writes=[pfx + "mask"])
    S.dma_group("pool", "wA", [(lambda e, n=n: e.dma_start(out=wo[:, n], in_=wo_d[n]), [], [pfx + f"wo{n}"]) for n in range(KD)])

    step = 0
    for hp in range(nhp):
        S.dma("sp", pfx + "lk", lambda e, hp=hp: e.dma_start(out=kT_sb[:, :, :], in_=kT_r[:, hp, :, :]), writes=[pfx + "kT"])
        S.dma("sp", pfx + "lv", lambda e, hp=hp: e.dma_start(out=V_sb[:, :, :, :], in_=V_r[hp]),
              writes=[pfx + "V"])
        S.dma("sp", pfx + "lq", lambda e, hp=hp: e.dma_start(out=q_sb[:, :], in_=qT_r[:, hp, :]), writes=[pfx + "q"])
        for i in range(nq):
            blocks = []
            for j in range(2 * i + 1, -1, -1):
                for b4 in range(3, -1, -1):
                    mk = 1 if j == 2 * i + 1 else (0 if j == 2 * i else None)
                    blocks.append((j % 2, (j // 2) * 4 + b4, mk, b4))
            nb = len(blocks)

            def emit_z(s):
                r, kb, mk, b4 = blocks[s]
                z3 = (step + s) % 3
                for hd in range(2):
                    S.emit("pe", lambda e, hd=hd, r=r, kb=kb, z3=z3, i=i: e.matmul(
                        pz[z3][:, hd * 512:(hd + 1) * 512], kT_sb[hd * 64:(hd + 1) * 64, r, kb * 128:(kb + 1) * 128],
                        q_sb[hd * 64:(hd + 1) * 64, i * CH:(i + 1) * CH], start=True, stop=True, skip_group_check=True),
                        reads=[pfx + "kT", pfx + "q"], banks=[BK(f"pz{z3}")], ms=(hd == 1 and mk is None))
                if mk is not None:
                    for hd in range(2):
                        S.emit("pe", lambda e, hd=hd, z3=z3, mk=mk, b4=b4: e.matmul(
                            pz[z3][:, hd * 512:(hd + 1) * 512], ident[:, :], mask[:, mk, b4, hd * 512:(hd + 1) * 512],
                            start=False, stop=True, skip_group_check=True),
                            reads=[pfx + "ident", pfx + "mask"], banks=[BK(f"pz{z3}")], ms=(hd == 1))

            def emit_el(s):
                r, kb, mk, b4 = blocks[s]
                zb = (step + s) % 2
                z3 = (step + s) % 3
                S.emit("act", lambda e, zb=zb, z3=z3: e.activation(out=E[z3][:, :], in_=pz[z3][:, :], func=AF.Exp),
                       banks=[BK(f"pz{z3}")], writes=[pfx + f"E{z3}"])
                S.emit("act", lambda e, z3=z3: e.activation(out=L[z3][:, :], in_=E[z3][:, :], func=AF.Ln, bias=one1[:, 0:1]),
                       reads=[pfx + f"E{z3}", pfx + "one1"], writes=[pfx + f"L{z3}"])

            def emit_p2(s):
                zb = (step + s) % 2
                z3 = (step + s) % 3
                sls = [slice(hd * 512, (hd + 1) * 512) for hd in range(2)]
                for hd in range(2):
                    S.emit("pe", lambda e, sl=sls[hd], zb=zb, z3=z3, last=(s == 0): e.matmul(
                        pz[z3][:, sl], ntri[:, :], L[z3][:, sl], start=False, stop=last, skip_group_check=True),
                        reads=[pfx + f"L{z3}", pfx + "ntri"], banks=[BK(f"pz{z3}")], ms=(s == 0 and hd == 1))
                if s > 0:
                    lb = (step + s - 1) % 2
                    for hd in range(2):
                        S.emit("pe", lambda e, sl=sls[hd], lb=lb, z3=z3: e.matmul(
                            pz[z3][:, sl], nones[:, :], Lsb[lb][:, sl], start=False, stop=True, skip_group_check=True),
                            reads=[pfx + f"Lsb{lb}", pfx + "nones"], banks=[BK(f"pz{z3}")], ms=(hd == 1))

            def emit_w(s):
                r, kb, mk, b4 = blocks[s]
                zb = (step + s) % 2
                z3 = (step + s) % 3
                S.emit("act", lambda e, zb=zb, z3=z3: e.activation(out=Wt[zb][:, :], in_=pz[z3][:, :], func=AF.Exp),
                       banks=[BK(f"pz{z3}")], writes=[pfx + f"W{zb}"])
                if s + 1 < nb:
                    if s == 0:
                        S.emit("dve", lambda e, z3=z3: e.tensor_copy(out=Ls[:, :], in_=L[z3][:, :]), reads=[pfx + f"L{z3}"], writes=[pfx + "Ls"])
                    else:
                        S.emit("dve", lambda e, z3=z3: e.tensor_tensor(out=Ls[:, :], in0=Ls[:, :], in1=L[z3][:, :], op=ALU.add),
                               reads=[pfx + f"L{z3}", pfx + "Ls"], writes=[pfx + "Ls"])
                    S.emit("dve", lambda e, zb=zb: e.tensor_copy(out=Lsb[zb][:, :], in_=Ls[:, :]), reads=[pfx + "Ls"], writes=[pfx + f"Lsb{zb}"])

            def emit_pv(s):
                r, kb, mk, b4 = blocks[s]
                zb = (step + s) % 2
                for hd in range(2):
                    S.emit("pe", lambda e, hd=hd, r=r, kb=kb, zb=zb, st_=(s == 0), sp_=(s == nb - 1): e.matmul(
                        po[hd * 64:(hd + 1) * 64, :], V_sb[:, r, kb, hd * 64:(hd + 1) * 64], Wt[zb][:, hd * 512:(hd + 1) * 512],
                        start=st_, stop=sp_),
                        reads=[pfx + "V", pfx + f"W{zb}"], banks=[BK("po")], ms=(s == nb - 1 and hd == 1))

            emit_z(0)
            emit_el(0)
            emit_z(1)
            emit_el(1)
            for s in range(nb):
                emit_p2(s)
                emit_w(s)
                if s + 2 < nb:
                    emit_z(s + 2)
                    emit_el(s + 2)
                if s > 0:
                    emit_pv(s - 1)
            emit_pv(nb - 1)
            step += nb
            S.emit("act", lambda e, hp=hp, i=i: e.activation(out=oT[:, hp, i * CH:(i + 1) * CH], in_=po[:, :], func=AF.Identity),
                   banks=[BK("po")], writes=[pfx + f"oT{hp}.{i}"])
    out_toks = []
    it = 0
    for i in range(nq):
        for n in range(KD):
            b = it % 2
            it += 1
            S.dma("sp", pfx + f"lx{b}", lambda e, b=b, n=n, i=i: e.dma_start(out=xr[b][:, :], in_=xT_r[:, n, i * CH:(i + 1) * CH]),
                  writes=[pfx + f"xr{b}"])
            for k in range(nhp):
                S.emit("pe", lambda e, k=k, n=n, i=i: e.matmul(pf[:, :], wo[:, n, k, :], oT[:, k, i * CH:(i + 1) * CH],
                                                              start=(k == 0), stop=(k == nhp - 1)),
                       reads=[pfx + f"oT{k}.{i}", pfx + f"wo{n}"], banks=[BK("pf")], ms=(k == nhp - 1))
            S.emit("dve", lambda e, b=b: e.tensor_tensor(out=xo[b][:, :], in0=pf[:, :], in1=xr[b][:, :], op=ALU.add),
                   reads=[pfx + f"xr{b}"], writes=[pfx + f"xo{b}"], banks=[BK("pf")])
            t = S.dma("sp", pfx + f"so{b}", lambda e, b=b, n=n, i=i: e.dma_start(out=xo_r[:, n, i * CH:(i + 1) * CH], in_=xo[b][:, :]),
                      reads=[pfx + f"xo{b}"], writes=[pfx + f"xoT.{n}.{i}"])
            out_toks.append(t)
    return out_toks


NCORES = 8
SEQ = 8192
BATCH = 4


def _own_tokens(p):
    return np.concatenate([np.arange((2 * i + p) * CH, (2 * i + p + 1) * CH) for i in range(NCHUNK)])


def _halo_tokens(p):
    return np.concatenate([np.arange((2 * i + p) * CH, (2 * i + p) * CH + HALO) for i in range(NCHUNK)])


def _lay_vec(v, n):
    return np.ascontiguousarray(v.reshape(n, 128).T)


def _lay_w(w, kin, nout):
    return np.ascontiguousarray(w.reshape(kin, 128, nout, 128).transpose(2, 1, 0, 3))


def _lay_ffn(g, w_up, dw_w, dw_b, w_down):
    wu = w_up.reshape(KD, 128, 2, NJ, 128)
    return dict(g=_lay_vec(g, KD),
                wup=np.ascontiguousarray(wu.transpose(3, 1, 0, 2, 4).reshape(NJ, 128, KD, 256)),
                wdn=_lay_w(w_down, NJ, KD),
                dw=np.ascontiguousarray(dw_w.reshape(3, 2 * NJ, 128).transpose(2, 1, 0)),
                db=_lay_vec(dw_b, 2 * NJ))


def _lay_conv(g, w_in, a_dw_w, a_dw_b, a_ln_g, a_ln_b, b_dw_w, w_out):
    return dict(g=_lay_vec(g, KD), win=_lay_w(w_in, KD, 20), wout=_lay_w(w_out, KD, KD),
                adw=np.ascontiguousarray(a_dw_w.reshape(KA, MA, 128).transpose(2, 1, 0)),
                avec=np.ascontiguousarray(np.stack([a_dw_b, a_ln_g, a_ln_b]).reshape(3, MA, 128).transpose(2, 0, 1)),
                bdw=np.ascontiguousarray(b_dw_w.reshape(3, MA, 128).transpose(2, 1, 0)))


def _lay_attn(g, w_qkv, q_g, k_g, w_o):
    return dict(g=_lay_vec(g, KD), wq=_lay_w(w_qkv[:, :D], KD, HP), wk=_lay_w(w_qkv[:, D:2 * D], KD, HP),
                wv=np.ascontiguousarray(w_qkv[:, 2 * D:].reshape(KD, 128, 2, 512).transpose(2, 1, 0, 3)),
                qkg=np.ascontiguousarray(np.stack([np.tile(q_g, 2), np.tile(k_g, 2)], 1)),
                wo=_lay_w(w_o, KD, KD))


def _masks(p):
    ks = np.arange(CH)[:, None]
    tq = np.arange(CH)[None, :]
    diag = np.where(ks < tq, 0.0, -30000.0).astype(np.float32)
    A = diag if p == 0 else np.zeros((CH, CH), np.float32)
    B = np.full((CH, CH), -30000.0, np.float32) if p == 0 else diag
    m = np.stack([A, B]).reshape(2, 4, 128, CH).transpose(2, 0, 1, 3)
    return np.ascontiguousarray(np.concatenate([m, m], -1))


def _tri():
    j = np.arange(128)[:, None]
    s = np.arange(128)[None, :]
    return np.ascontiguousarray(np.concatenate([-(j >= s).astype(np.float32), (j == s).astype(np.float32)], 1))


_PROGS = {}


def _dram(nc, name, shape, dt=F32, kind="ExternalInput"):
    return nc.dram_tensor(name, list(shape), dt, kind=kind).ap()


def _prog(kind):
    if kind in _PROGS:
        return _PROGS[kind]
    nc = bass.Bass("TRN2", target_bir_lowering=False)
    S = Sched()
    with ExitStack() as st:
        cx = Ctx(nc, S, st)
        if kind == "conv":
            a = [_dram(nc, "xT", [D, TOK]), _dram(nc, "xh", [D, NCHUNK * HALO])]
            xo = _dram(nc, "xoT", [D, TOK], F32, "ExternalOutput")
            toks = conv_phase(cx, "c.", a[0], a[1], xo, _dram(nc, "g", [128, KD]), _dram(nc, "win", [20, 128, KD, 128]),
                              _dram(nc, "wout", [KD, 128, KD, 128]), _dram(nc, "adw", [128, MA, KA]),
                              _dram(nc, "avec", [128, 3, MA]), _dram(nc, "bdw", [128, MA, 3]))
        elif kind == "ffn":
            a = [_dram(nc, "xT", [D, TOK]), _dram(nc, "xh", [D, NCHUNK * HALO])]
            xo = _dram(nc, "xoT", [D, TOK], F32, "ExternalOutput")
            toks = ffn_phase(cx, "f.", a[0], a[1], xo, _dram(nc, "g", [128, KD]), _dram(nc, "wup", [NJ, 128, KD, 256]),
                             _dram(nc, "wdn", [KD, 128, NJ, 128]), _dram(nc, "dw", [128, 2 * NJ, 3]), _dram(nc, "db", [128, 2 * NJ]))
        elif kind == "qkv":
            toks = qkv_phase(cx, "q.", _dram(nc, "xT", [D, TOK]), _dram(nc, "g", [128, KD]), _dram(nc, "wq", [HP, 128, KD, 128]),
                             _dram(nc, "wk", [HP, 128, KD, 128]), _dram(nc, "wv", [2, 128, KD, 512]), _dram(nc, "qkg", [128, 2]),
                             _dram(nc, "qT", [D, TOK], BF16, "ExternalOutput"), _dram(nc, "kT", [D, TOK], BF16, "ExternalOutput"),
                             _dram(nc, "V", [HP, 128, NKB, 128], BF16, "ExternalOutput"))
        elif kind == "attn":
            toks = attn_phase(cx, "a.", _dram(nc, "xT", [D, TOK]), _dram(nc, "xoT", [D, TOK], F32, "ExternalOutput"),
                              _dram(nc, "qT", [D, TOK], BF16), _dram(nc, "kT", [2, D, TOK], BF16),
                              _dram(nc, "V", [2, HP, 128, NKB, 128], BF16), _dram(nc, "mask", [128, 2, 4, 1024]),
                              _dram(nc, "tri", [128, 256]), _dram(nc, "wo", [KD, 128, KD, 128]))
        S.wait_all("sp", toks)
        S.build(nc, st)
    _PROGS[kind] = nc
    return nc


def _run(kind, in_maps):
    res = run_bass_kernel_spmd(_prog(kind), in_maps, core_ids=list(range(NCORES)))
    return res.results


def kernel_unfused(**inputs):
    inp = {k: np.asarray(v) for k, v in inputs.items()}
    x = inp["x"]
    own = [_own_tokens(p) for p in range(2)]
    halo = [_halo_tokens(p) for p in range(2)]
    xT = [np.ascontiguousarray(x[c // 2][own[c % 2]].T) for c in range(NCORES)]

    def halos(xT):
        out = []
        for b in range(BATCH):
            full = np.zeros((D, HALO + SEQ), np.float32)
            for p in range(2):
                for i in range(NCHUNK):
                    g0 = (2 * i + p) * CH
                    full[:, HALO + g0:HALO + g0 + CH] = xT[2 * b + p][:, i * CH:(i + 1) * CH]
            for p in range(2):
                out.append(np.ascontiguousarray(full[:, halo[p]]))
        return out

    masks = [_masks(p) for p in range(2)]
    tri = _tri()
    for layer in range(4):
        i = layer // 2
        if layer % 2 == 0:
            lw = _lay_conv(inp["mix_norm_g"][layer], inp["conv_w_in"][i], inp["conv_a_dw_w"][i], inp["conv_a_dw_b"][i],
                           inp["conv_a_ln_g"][i], inp["conv_a_ln_b"][i], inp["conv_b_dw_w"][i], inp["conv_w_out"][i])
            xh = halos(xT)
            r = _run("conv", [dict(lw, xT=xT[c], xh=xh[c]) for c in range(NCORES)])
            xT = [r[c]["xoT"] for c in range(NCORES)]
        else:
            lw = _lay_attn(inp["mix_norm_g"][layer], inp["attn_w_qkv"][i], inp["attn_q_g"][i], inp["attn_k_g"][i], inp["attn_w_o"][i])
            r = _run("qkv", [dict(xT=xT[c], g=lw["g"], wq=lw["wq"], wk=lw["wk"], wv=lw["wv"], qkg=lw["qkg"]) for c in range(NCORES)])
            ins = []
            for c in range(NCORES):
                b = c // 2
                kT_g = np.stack([r[2 * b]["kT"], r[2 * b + 1]["kT"]])
                V_g = np.stack([r[2 * b]["V"], r[2 * b + 1]["V"]])
                ins.append(dict(xT=xT[c], qT=r[c]["qT"], kT=kT_g, V=V_g, mask=masks[c % 2], tri=tri, wo=lw["wo"]))
            r = _run("attn", ins)
            xT = [r[c]["xoT"] for c in range(NCORES)]
        lw = _lay_ffn(inp["ffn_norm_g"][layer], inp["ffn_w_up"][layer], inp["ffn_dw_w"][layer], inp["ffn_dw_b"][layer], inp["ffn_w_down"][layer])
        xh = halos(xT)
        r = _run("ffn", [dict(lw, xT=xT[c], xh=xh[c]) for c in range(NCORES)])
        xT = [r[c]["xoT"] for c in range(NCORES)]
    out = np.empty((BATCH, SEQ, D), np.float32)
    for c in range(NCORES):
        out[c // 2][own[c % 2]] = xT[c].T
    return out


PAIRS = [[0, 1], [2, 3], [4, 5], [6, 7]]


def halo_phase(cx, pfx, xT, tl, tg, xh, sel_d):
    S, nc = cx.S, cx.nc
    t_sb = cx.sb(pfx + "t", [128, KD, NCHUNK, HALO], F32)
    c0 = cx.sb(pfx + "c0", [128, KD, NCHUNK, HALO], F32)
    c1 = cx.sb(pfx + "c1", [128, KD, NCHUNK, HALO], F32)
    sel = cx.sb(pfx + "sel", [128, 2], F32)
    xT_r = xT.rearrange("(k p) (c t) -> p k c t", p=128, t=CH)
    tl_r = tl.rearrange("(k p) (c h) -> p k c h", p=128, h=HALO)
    tg_r = tg.rearrange("(r k p) (c h) -> p r k c h", p=128, k=KD, h=HALO)
    xh_r = xh.rearrange("(k p) (c h) -> p k c h", p=128, h=HALO)
    S.dma("sp", "cst", lambda e: e.dma_start(out=sel[:], in_=sel_d), writes=[pfx + "sel"])
    S.dma_group("sp", "h0", [(lambda e, k=k: e.dma_start(out=t_sb[:, k], in_=xT_r[:, k, :, CH - HALO:CH]), [], [pfx + f"t{k}"])
                             for k in range(KD)])
    S.dma_group("sp", "h1", [(lambda e, k=k: e.dma_start(out=tl_r[:, k], in_=t_sb[:, k]), [pfx + f"t{k}"], [pfx + f"tl{k}"])
                             for k in range(KD)])
    S.dma("pool", "cc", lambda e: e.collective_compute("AllGather", ALU.bypass, replica_groups=PAIRS, ins=[tl], outs=[tg]),
          reads=[pfx + f"tl{k}" for k in range(KD)], writes=[pfx + "tg"], inc=1)
    S.emit("dve", lambda e: e.memset(c0[:, :, 0, :], 0.0), writes=[pfx + "c0z"])
    S.dma_group("sp", "h2", [(lambda e, k=k: e.dma_start(out=c0[:, k, 1:NCHUNK, :], in_=tg_r[:, 1, k, 0:NCHUNK - 1, :]), [pfx + "tg"], [pfx + f"c0{k}"])
                             for k in range(KD)] +
                            [(lambda e, k=k: e.dma_start(out=c1[:, k], in_=tg_r[:, 0, k]), [pfx + "tg"], [pfx + f"c1{k}"])
                             for k in range(KD)])
    items = []
    for k in range(KD):
        S.emit("dve", lambda e, k=k: e.tensor_scalar(out=c0[:, k], in0=c0[:, k], scalar1=sel[:, 0:1], scalar2=None, op0=ALU.mult),
               reads=[pfx + f"c0{k}", pfx + "c0z", pfx + "sel"], writes=[pfx + f"c0{k}"])
        S.emit("dve", lambda e, k=k: e.scalar_tensor_tensor(out=c1[:, k], in0=c1[:, k], scalar=sel[:, 1:2], in1=c0[:, k],
                                                            op0=ALU.mult, op1=ALU.add),
               reads=[pfx + f"c0{k}", pfx + f"c1{k}", pfx + "sel"], writes=[pfx + f"c1{k}"])
        items.append((lambda e, k=k: e.dma_start(out=xh_r[:, k], in_=c1[:, k]), [pfx + f"c1{k}"], [pfx + f"xh{k}"]))
    t = S.dma_group("sp", "h3", items)
    return [t]


def gather_phase(cx, pfx, kT_l, kT_g, V_l, V_g):
    S = cx.S
    toks = []
    for hp in range(HP):
        for nm, a, b in (("k", kT_l, kT_g), ("v", V_l, V_g)):
            toks.append(S.dma("pool", "cc", lambda e, hp=hp, a=a, b=b: e.collective_compute(
                "AllGather", ALU.bypass, replica_groups=PAIRS, ins=[a[hp * 128:(hp + 1) * 128, :]], outs=[b[hp * 256:(hp + 1) * 256, :]]),
                writes=[pfx + f"{nm}g{hp}"], inc=1))
    return toks[-1:]


_FUSED = []


def _fused_prog():
    if _FUSED:
        return _FUSED[0]
    nc = bass.Bass("TRN2", target_bir_lowering=False)
    S = Sched()
    with ExitStack() as outer:
        x_in = _dram(nc, "xT", [D, TOK])
        out = _dram(nc, "out", [D, TOK], F32, "ExternalOutput")
        sel_d = _dram(nc, "sel", [128, 2])
        mask_d = _dram(nc, "mask", [128, 2, 4, 1024])
        tri_d = _dram(nc, "tri", [128, 256])
        xs = [nc.dram_tensor(f"xs{i}", [D, TOK], F32).ap() for i in range(2)]
        tl = nc.dram_tensor("tl", [D, NCHUNK * HALO], F32).ap()
        tg = nc.dram_tensor("tg", [2 * D, NCHUNK * HALO], F32).ap()
        xh = nc.dram_tensor("xh", [D, NCHUNK * HALO], F32).ap()
        qT = nc.dram_tensor("qT", [D, TOK], BF16).ap()
        kT_l = nc.dram_tensor("kTl", [D, TOK], BF16).ap()
        kT_g = nc.dram_tensor("kTg", [2 * D, TOK], BF16).ap()
        V_l = nc.dram_tensor("Vl", [HP * 128, NKB * 128], BF16).ap()
        V_g = nc.dram_tensor("Vg", [2 * HP * 128, NKB * 128], BF16).ap()
        V_l5 = V_l.rearrange("(h p) (k f) -> h p k f", p=128, f=128)

        def phase(fn):
            with ExitStack() as pst:
                cx = Ctx(nc, S, pst)
                toks = fn(cx)
                S.barrier(toks)
                S.build_phase(nc, pst, outer)

        cur = x_in
        nxt = 0
        for layer in range(4):
            L = f"L{layer}."
            if layer % 2 == 0:
                phase(lambda cx: halo_phase(cx, L + "h.", cur, tl, tg, xh, sel_d))
                dst = xs[nxt]
                phase(lambda cx: conv_phase(cx, L + "c.", cur, xh, dst, _dram(nc, L + "mg", [128, KD]), _dram(nc, L + "win", [20, 128, KD, 128]),
                                            _dram(nc, L + "wout", [KD, 128, KD, 128]), _dram(nc, L + "adw", [128, MA, KA]),
                                            _dram(nc, L + "avec", [128, 3, MA]), _dram(nc, L + "bdw", [128, MA, 3])))
            else:
                phase(lambda cx: qkv_phase(cx, L + "q.", cur, _dram(nc, L + "mg", [128, KD]), _dram(nc, L + "wq", [HP, 128, KD, 128]),
                                           _dram(nc, L + "wk", [HP, 128, KD, 128]), _dram(nc, L + "wv", [2, 128, KD, 512]),
                                           _dram(nc, L + "qkg", [128, 2]), qT, kT_l, V_l5))
                phase(lambda cx: gather_phase(cx, L + "g.", kT_l, kT_g, V_l, V_g))
                dst = xs[nxt]
                phase(lambda cx: attn_phase(cx, L + "a.", cur, dst, qT, kT_g, V_g, mask_d, tri_d, _dram(nc, L + "wo", [KD, 128, KD, 128])))
            cur = dst
            nxt ^= 1
            phase(lambda cx: halo_phase(cx, L + "hf.", cur, tl, tg, xh, sel_d))
            dst = out if layer == 3 else xs[nxt]
            phase(lambda cx: ffn_phase(cx, L + "f.", cur, xh, dst, _dram(nc, L + "fg", [128, KD]), _dram(nc, L + "wup", [NJ, 128, KD, 256]),
                                       _dram(nc, L + "wdn", [KD, 128, NJ, 128]), _dram(nc, L + "dw", [128, 2 * NJ, 3]),
                                       _dram(nc, L + "db", [128, 2 * NJ])))
            cur = dst
            nxt ^= 1
    _FUSED.append((nc, S))
    return _FUSED[0]


def kernel(**inputs):
    inp = {k: np.asarray(v) for k, v in inputs.items()}
    x = inp["x"]
    own = [_own_tokens(p) for p in range(2)]
    base = {"tri": _tri()}
    for layer in range(4):
        i = layer // 2
        L = f"L{layer}."
        if layer % 2 == 0:
            lw = _lay_conv(inp["mix_norm_g"][layer], inp["conv_w_in"][i], inp["conv_a_dw_w"][i], inp["conv_a_dw_b"][i],
                           inp["conv_a_ln_g"][i], inp["conv_a_ln_b"][i], inp["conv_b_dw_w"][i], inp["conv_w_out"][i])
        else:
            lw = _lay_attn(inp["mix_norm_g"][layer], inp["attn_w_qkv"][i], inp["attn_q_g"][i], inp["attn_k_g"][i], inp["attn_w_o"][i])
        lw["mg"] = lw.pop("g")
        lf = _lay_ffn(inp["ffn_norm_g"][layer], inp["ffn_w_up"][layer], inp["ffn_dw_w"][layer], inp["ffn_dw_b"][layer], inp["ffn_w_down"][layer])
        lf["fg"] = lf.pop("g")
        for k, v in list(lw.items()) + list(lf.items()):
            base[L + k] = v
    masks = [_masks(p) for p in range(2)]
    sels = [np.ascontiguousarray(np.tile(np.array([[1.0, 0.0]], np.float32) if p == 0 else np.array([[0.0, 1.0]], np.float32), (128, 1)))
            for p in range(2)]
    in_maps = []
    for c in range(NCORES):
        d = dict(base)
        d["xT"] = np.ascontiguousarray(x[c // 2][own[c % 2]].T)
        d["mask"] = masks[c % 2]
        d["sel"] = sels[c % 2]
        in_maps.append(d)
    nc, _ = _fused_prog()
    res = run_bass_kernel_spmd(nc, in_maps, core_ids=list(range(NCORES))).results
    out = np.empty((BATCH, SEQ, D), np.float32)
    for c in range(NCORES):
        out[c // 2][own[c % 2]] = res[c]["out"].T
    return out
```

```python
import numpy as np
from contextlib import ExitStack
import concourse.bass as bass
import concourse.mybir as mybir
from concourse.bass_utils import run_bass_kernel_spmd

F32 = mybir.dt.float32
BF16 = mybir.dt.bfloat16
AF = mybir.ActivationFunctionType
ALU = mybir.AluOpType

ENGINES = ("pe", "act", "dve", "pool", "sp")
ENG_ATTR = {"pe": "tensor", "act": "scalar", "dve": "vector", "pool": "gpsimd", "sp": "sync"}


class _Op:
    __slots__ = ("fn", "waits", "inc")

    def __init__(self, fn):
        self.fn = fn
        self.waits = []
        self.inc = None


class Sched:
    SEM_ROT = 20000

    def __init__(self):
        self.ops = {e: [] for e in ENGINES}
        self.built = {e: 0 for e in ENGINES}
        self.ms = {e: [] for e in ENGINES}
        self.gen = {e: 0 for e in ENGINES}
        self.cnt = {}
        self.waited = {e: {} for e in ENGINES}
        self.last_w = {}
        self.readers = {}
        self.nwaits = 0
        self.sems = {}

    def _new_ms(self, e, seq, op):
        k = ("eng", e, self.gen[e])
        if self.cnt.get(k, 0) >= self.SEM_ROT:
            self.gen[e] += 1
            k = ("eng", e, self.gen[e])
        self.cnt[k] = self.cnt.get(k, 0) + 1
        op.inc = (k, 1)
        self.ms[e].append((seq, k, self.cnt[k]))
        return k, self.cnt[k]

    def _milestone(self, e, seq):
        lo, hi = 0, len(self.ms[e])
        while lo < hi:
            mid = (lo + hi) // 2
            if self.ms[e][mid][0] >= seq:
                hi = mid
            else:
                lo = mid + 1
        if lo < len(self.ms[e]):
            return self.ms[e][lo][1], self.ms[e][lo][2]
        last = len(self.ops[e]) - 1
        while self.ops[e][last].fn is None:
            last -= 1
        assert last >= seq and last >= self.built[e], (e, seq, last, self.built[e])
        op = self.ops[e][last]
        assert op.inc is None, f"last op on {e} already has an inc"
        return self._new_ms(e, last, op)

    def _resolve(self, tok):
        if tok[0] == "eng":
            return self._milestone(tok[1], tok[2])
        return tok[1], tok[2]

    def _deps(self, engine, reads, writes):
        toks = []
        for k in reads:
            t = self.last_w.get(k)
            if t is not None:
                toks.append(t)
        for k in writes:
            t = self.last_w.get(k)
            if t is not None and not (t[0] == "eng" and t[1] == engine == "pe"):
                toks.append(t)
            for t in self.readers.get(k, {}).values():
                if not (t[0] == "eng" and t[1] == engine):
                    toks.append(t)
        return self._waits(engine, toks)

    def _waits(self, engine, toks):
        waits = {}
        for t in toks:
            sk, v = self._resolve(t)
            if self.waited[engine].get(sk, 0) >= v:
                continue
            waits[sk] = max(waits.get(sk, 0), v)
        for sk, v in waits.items():
            self.waited[engine][sk] = v
        self.nwaits += len(waits)
        return list(waits.items())

    def _track(self, tok, rkey, reads, writes):
        for k in writes:
            self.last_w[k] = tok
            self.readers[k] = {}
        for k in reads:
            self.readers.setdefault(k, {})[rkey] = tok

    def emit(self, engine, fn, reads=(), writes=(), ms=False, banks=()):
        writes = list(writes) + list(banks)
        op = _Op(fn)
        op.waits = self._deps(engine, reads, writes)
        seq = len(self.ops[engine])
        self.ops[engine].append(op)
        if ms or engine != "pe":
            self._new_ms(engine, seq, op)
        tok = ("eng", engine, seq)
        self._track(tok, engine, reads, writes)
        return tok

    def dma(self, queue, sem, fn, reads=(), writes=(), inc=16):
        op = _Op(fn)
        op.waits = self._deps(queue, reads, writes)
        self.ops[queue].append(op)
        k = ("dma", sem.split(".", 1)[-1])
        self.cnt[k] = self.cnt.get(k, 0) + inc
        op.inc = (k, inc)
        tok = ("dma", k, self.cnt[k])
        self._track(tok, k, reads, writes)
        return tok

    def dma_group(self, queue, sem, items):
        toks = [self.dma(queue, sem, fn, reads, writes) for (fn, reads, writes) in items]
        final = toks[-1]
        for (fn, reads, writes) in items:
            for k in writes:
                self.last_w[k] = final
            for k in reads:
                self.readers.setdefault(k, {})[final[1]] = final
        return final

    def wait_all(self, engine, toks):
        op = _Op(None)
        op.waits = self._waits(engine, toks)
        self.ops[engine].append(op)

    def barrier(self, toks=()):
        toks = list(toks)
        for e in ("pe", "act", "dve", "pool"):
            last = len(self.ops[e]) - 1
            while last >= self.built[e] and (self.ops[e][last].fn is None or self.ops[e][last].inc is not None and self.ops[e][last].inc[0][0] == "dma"):
                last -= 1
            if last >= self.built[e]:
                toks.append(("eng", e, last))
        for e in ENGINES:
            self.wait_all(e, toks)

    def build_phase(self, nc, st, outer):
        block = st.enter_context(nc.Block())
        for e in ENGINES:
            deco = getattr(block, ENG_ATTR[e])

            def body(eng, e=e):
                for op in self.ops[e][self.built[e]:]:
                    for (sk, v) in op.waits:
                        eng.wait_ge(self._sem(nc, outer, sk), v)
                    if op.fn is None:
                        continue
                    ins = op.fn(eng)
                    if op.inc is not None:
                        ins.then_inc(self._sem(nc, outer, op.inc[0]), op.inc[1])
                    op.fn = None
                self.built[e] = len(self.ops[e])

            deco(body)

    def _sem(self, nc, outer, k):
        if k not in self.sems:
            self.sems[k] = outer.enter_context(nc.semaphore(f"s{len(self.sems)}"))
        return self.sems[k]

    def build(self, nc, st):
        self.build_phase(nc, st, st)


D = 1024
KD = D // 128
CH = 512
NCHUNK = 8
TOK = CH * NCHUNK
HALO = 32
DFF = 2816
NJ = DFF // 128
EPS = 1e-6


class Ctx:
    def __init__(self, nc, S, st):
        self.nc, self.S, self.st = nc, S, st
        self.n = 0

    def sb(self, name, shape, dt):
        return self.st.enter_context(self.nc.sbuf_tensor(name, list(shape), dt))

    def ps(self, name, shape=(128, 512), dt=F32):
        return self.st.enter_context(self.nc.psum_tensor(name, list(shape), dt))


def emit_rmsnorm(cx, pfx, x_k, h_k, xkeys, hkeys, ncols, sq, ones, pst, pst_bank, tmp, rstd, g_sb, gkey):
    S = cx.S
    for k in range(KD):
        S.emit("act", lambda e, k=k: e.activation(out=sq[:, k, 0:ncols], in_=x_k(k), func=AF.Square),
               reads=[xkeys[k]], writes=[pfx + f"sq{k}"])
    for k in range(KD):
        S.emit("pe", lambda e, k=k: e.matmul(pst, ones[:, :], sq[:, k, 0:ncols], start=(k == 0), stop=(k == KD - 1)),
               reads=[pfx + f"sq{k}", pfx + "ones"], banks=[pst_bank], ms=(k == KD - 1))
    S.emit("dve", lambda e: e.tensor_scalar(out=tmp[:, 0:ncols], in0=pst, scalar1=1.0 / D, scalar2=EPS,
                                            op0=ALU.mult, op1=ALU.add), banks=[pst_bank], writes=[pfx + "tmp"])
    S.emit("act", lambda e: e.activation(out=tmp[:, 0:ncols], in_=tmp[:, 0:ncols], func=AF.Sqrt),
           reads=[pfx + "tmp"], writes=[pfx + "tmp"])
    S.emit("dve", lambda e: e.reciprocal(out=rstd[:, 0:ncols], in_=tmp[:, 0:ncols]), reads=[pfx + "tmp"], writes=[pfx + "rstd"])
    for k in range(KD):
        S.emit("dve", lambda e, k=k: e.scalar_tensor_tensor(out=h_k(k), in0=x_k(k), scalar=g_sb[:, k:k + 1], in1=rstd[:, 0:ncols],
                                                            op0=ALU.mult, op1=ALU.mult),
               reads=[xkeys[k], pfx + "rstd", gkey], writes=[hkeys[k]])


def ffn_phase(cx, pfx, xT, xh, xoT, g_d, wup_d, wdn_d, dw_d, db_d, nst=NCHUNK // 2):
    S, nc = cx.S, cx.nc
    HH = 2
    W = HH + CH
    SC = 2
    x_sb = cx.sb(pfx + "x", [128, KD, SC, W], F32)
    hT = cx.sb(pfx + "hT", [128, KD, SC, W], BF16)
    sq = cx.sb(pfx + "sq", [128, KD, CH], BF16)
    rstd = cx.sb(pfx + "rstd", [128, CH], F32)
    tmp = cx.sb(pfx + "tmp", [128, CH], F32)
    ones = cx.sb(pfx + "ones", [128, 128], BF16)
    g_sb = cx.sb(pfx + "g", [128, KD], F32)
    dw_sb = cx.sb(pfx + "dw", [128, 2 * NJ, 3], F32)
    db_sb = cx.sb(pfx + "db", [128, 2 * NJ], F32)
    wup = [cx.sb(pfx + f"wup{i}", [128, KD, 256], BF16) for i in range(2)]
    wdn = [cx.sb(pfx + f"wdn{i}", [128, NJ, 128], BF16) for i in range(2)]
    gT = cx.sb(pfx + "gT", [128, NJ, SC, CH], BF16)
    gacc = [cx.sb(pfx + f"gacc{i}", [128, CH], F32) for i in range(2)]
    vacc = [cx.sb(pfx + f"vacc{i}", [128, CH], F32) for i in range(2)]
    sg = [cx.sb(pfx + f"sg{i}", [128, CH], F32) for i in range(2)]
    phs = [cx.sb(pfx + f"phs{i}", [128, 4], F32) for i in range(2)]
    xo = [cx.sb(pfx + f"xo{i}", [128, CH], F32) for i in range(2)]
    pg = [cx.ps(pfx + f"pg{i}") for i in range(2)]
    pv = [cx.ps(pfx + f"pv{i}") for i in range(2)]
    pmisc = cx.ps(pfx + "pmisc")
    pstat = cx.ps(pfx + "pstat")
    pd = [cx.ps(pfx + f"pd{i}") for i in range(2)]
    ph = [pmisc[:, 0:4], pstat[:, 0:4]]
    phk = [pfx + "pmisc", pfx + "pstat"]
    pstat_h = pmisc[:, 32:32 + HH]

    xT_r = xT.rearrange("(k p) t -> p k t", p=128)
    xh_r = xh.rearrange("(k p) (c h) -> p k c h", p=128, h=HALO)
    xo_r = xoT.rearrange("(k p) t -> p k t", p=128)

    S.emit("dve", lambda e: e.memset(ones[:], 1.0), writes=[pfx + "ones"])
    S.dma_group("sp", "cst", [(lambda e: e.dma_start(out=g_sb[:], in_=g_d), [], [pfx + "n.g"]),
                              (lambda e: e.dma_start(out=dw_sb[:], in_=dw_d), [], [pfx + "dw"]),
                              (lambda e: e.dma_start(out=db_sb[:], in_=db_d), [], [pfx + "db"])])
    out_toks = []
    wu_i = 0
    wd_i = 0
    it = 0
    for sti in range(nst):
        for c in range(SC):
            gc = sti * SC + c
            S.dma_group("sp", f"lx{c}", [(lambda e, c=c, k=k, gc=gc: e.dma_start(out=x_sb[:, k, c, HH:W], in_=xT_r[:, k, gc * CH:(gc + 1) * CH]),
                                          [], [pfx + f"x{c}.{k}m"]) for k in range(KD)])
            S.dma("sp", pfx + f"lxh{c}",
                  lambda e, c=c, gc=gc: e.dma_start(out=x_sb[:, :, c, 0:HH], in_=xh_r[:, :, gc, HALO - HH:HALO]),
                  writes=[pfx + f"x{c}.h"])
        for c in range(SC):
            xk = [pfx + f"x{c}.{k}m" for k in range(KD)]
            for k in range(KD):
                S.emit("act", lambda e, k=k, c=c: e.activation(out=sq[:, k, :], in_=x_sb[:, k, c, HH:W], func=AF.Square),
                       reads=[xk[k]], writes=[pfx + f"sq{k}"])
            for k in range(KD):
                S.emit("pe", lambda e, k=k: e.matmul(pstat[:, :], ones[:, :], sq[:, k, :], start=(k == 0), stop=(k == KD - 1)),
                       reads=[pfx + f"sq{k}", pfx + "ones"], banks=[pfx + "pstat"], ms=(k == KD - 1))
            S.emit("dve", lambda e: e.tensor_scalar(out=tmp[:, :], in0=pstat[:, :], scalar1=1.0 / D, scalar2=EPS,
                                                    op0=ALU.mult, op1=ALU.add),
                   banks=[pfx + "pstat"], writes=[pfx + "tmp"])
            S.emit("act", lambda e: e.activation(out=tmp[:, :], in_=tmp[:, :], func=AF.Sqrt),
                   reads=[pfx + "tmp"], writes=[pfx + "tmp"])
            S.emit("dve", lambda e: e.reciprocal(out=rstd[:, :], in_=tmp[:, :]), reads=[pfx + "tmp"], writes=[pfx + "rstd"])
            for k in range(KD):
                S.emit("dve", lambda e, k=k, c=c: e.scalar_tensor_tensor(
                    out=hT[:, k, c, HH:W], in0=x_sb[:, k, c, HH:W], scalar=g_sb[:, k:k + 1], in1=rstd[:, :],
                    op0=ALU.mult, op1=ALU.mult),
                    reads=[xk[k], pfx + "rstd", pfx + "n.g"], writes=[pfx + f"h{c}.{k}m"])
            S.emit("act", lambda e, c=c: e.activation(out=sq[:, :, 0:HH], in_=x_sb[:, :, c, 0:HH], func=AF.Square),
                   reads=[pfx + f"x{c}.h"], writes=[pfx + f"sq{k}" for k in range(KD)])
            for k in range(KD):
                S.emit("pe", lambda e, k=k: e.matmul(pstat_h, ones[:, :], sq[:, k, 0:HH], start=(k == 0), stop=(k == KD - 1)),
                       reads=[pfx + f"sq{k}", pfx + "ones"], banks=[pfx + "pmisc"], ms=(k == KD - 1))
            S.emit("dve", lambda e: e.tensor_scalar(out=tmp[:, 0:HH], in0=pstat_h, scalar1=1.0 / D, scalar2=EPS,
                                                    op0=ALU.mult, op1=ALU.add),
                   banks=[pfx + "pmisc"], writes=[pfx + "tmp"])
            S.emit("act", lambda e: e.activation(out=tmp[:, 0:HH], in_=tmp[:, 0:HH], func=AF.Sqrt),
                   reads=[pfx + "tmp"], writes=[pfx + "tmp"])
            S.emit("dve", lambda e: e.reciprocal(out=rstd[:, 0:HH], in_=tmp[:, 0:HH]), reads=[pfx + "tmp"], writes=[pfx + "rstd"])
            for k in range(KD):
                S.emit("dve", lambda e, k=k, c=c: e.scalar_tensor_tensor(
                    out=hT[:, k, c, 0:HH], in0=x_sb[:, k, c, 0:HH], scalar=g_sb[:, k:k + 1], in1=rstd[:, 0:HH],
                    op0=ALU.mult, op1=ALU.mult),
                    reads=[pfx + f"x{c}.h", pfx + "rstd", pfx + "n.g"], writes=[pfx + f"h{c}.{k}h"])
        def load_wup(j, slot):
            S.dma("pool", pfx + f"wu{slot}", lambda e, j=j, slot=slot: e.dma_start(out=wup[slot][:], in_=wup_d[j]),
                  writes=[pfx + f"wup{slot}"])

        def load_wdn(n, slot):
            S.dma("pool", pfx + f"wd{slot}", lambda e, n=n, slot=slot: e.dma_start(out=wdn[slot][:], in_=wdn_d[n]),
                  writes=[pfx + f"wdn{slot}"])

        load_wup(0, wu_i % 2)
        for j in range(NJ):
            slot = wu_i % 2
            if j + 1 < NJ:
                load_wup(j + 1, (wu_i + 1) % 2)
            else:
                load_wdn(0, wd_i % 2)
            wu_i += 1
            for c in range(SC):
                b = it % 2
                it += 1
                hk = [pfx + f"h{c}.{k}m" for k in range(KD)]
                hh = [pfx + f"h{c}.{k}h" for k in range(KD)]
                for half, (pp, acc, jc) in enumerate(((pg[b], gacc[b], j), (pv[b], vacc[b], NJ + j))):
                    co = half * 128
                    pk = pfx + f"p{half}{b}"
                    ak = pfx + f"acc{half}{b}"
                    for k in range(KD):
                        S.emit("pe", lambda e, k=k, c=c, pp=pp, co=co, slot=slot: e.matmul(
                            pp[:, :], wup[slot][:, k, co:co + 128], hT[:, k, c, HH:W], start=(k == 0), stop=(k == KD - 1)),
                            reads=[hk[k], pfx + f"wup{slot}"], banks=[pk], ms=(k == KD - 1))
                    for k in range(KD):
                        S.emit("pe", lambda e, k=k, c=c, co=co, slot=slot, b=b, half=half: e.matmul(
                            ph[b][:, 2 * half:2 * half + 2], wup[slot][:, k, co:co + 128], hT[:, k, c, 0:HH],
                            start=(k == 0), stop=(k == KD - 1)),
                            reads=[hh[k], pfx + f"wup{slot}"], banks=[phk[b]], ms=(k == KD - 1))
                    S.emit("act", lambda e, pp=pp, acc=acc, jc=jc: e.activation(
                        out=acc[:, :], in_=pp[:, :], func=AF.Identity, scale=dw_sb[:, jc, 2:3], bias=db_sb[:, jc:jc + 1]),
                        reads=[pfx + "dw", pfx + "db"], writes=[ak], banks=[pk])
                    S.emit("dve", lambda e, pp=pp, acc=acc, jc=jc: e.scalar_tensor_tensor(
                        out=acc[:, 1:CH], in0=pp[:, 0:CH - 1], scalar=dw_sb[:, jc, 1:2], in1=acc[:, 1:CH],
                        op0=ALU.mult, op1=ALU.add), reads=[ak, pfx + "dw"], writes=[ak], banks=[pk])
                    S.emit("dve", lambda e, pp=pp, acc=acc, jc=jc: e.scalar_tensor_tensor(
                        out=acc[:, 2:CH], in0=pp[:, 0:CH - 2], scalar=dw_sb[:, jc, 0:1], in1=acc[:, 2:CH],
                        op0=ALU.mult, op1=ALU.add), reads=[ak, pfx + "dw"], writes=[ak], banks=[pk])
                for half, (acc, jc) in enumerate(((gacc[b], j), (vacc[b], NJ + j))):
                    ak = pfx + f"acc{half}{b}"
                    S.emit("dve", lambda e, acc=acc, jc=jc, b=b, half=half: e.scalar_tensor_tensor(
                        out=acc[:, 0:2], in0=ph[b][:, 2 * half:2 * half + 2], scalar=dw_sb[:, jc, 0:1], in1=acc[:, 0:2],
                        op0=ALU.mult, op1=ALU.add), reads=[ak, pfx + "dw"], writes=[ak], banks=[phk[b]])
                    S.emit("dve", lambda e, acc=acc, jc=jc, b=b, half=half: e.scalar_tensor_tensor(
                        out=acc[:, 0:1], in0=ph[b][:, 2 * half + 1:2 * half + 2], scalar=dw_sb[:, jc, 1:2], in1=acc[:, 0:1],
                        op0=ALU.mult, op1=ALU.add), reads=[ak, pfx + "dw"], writes=[ak], banks=[phk[b]])
                S.emit("act", lambda e, b=b: e.activation(out=sg[b][:, :], in_=gacc[b][:, :], func=AF.Silu),
                       reads=[pfx + f"acc0{b}"], writes=[pfx + f"sg{b}"])
                S.emit("dve", lambda e, b=b, j=j, c=c: e.tensor_tensor(out=gT[:, j, c, :], in0=sg[b][:, :], in1=vacc[b][:, :],
                                                                      op=ALU.mult),
                       reads=[pfx + f"sg{b}", pfx + f"acc1{b}"], writes=[pfx + f"gT{j}.{c}"])
        for n in range(KD):
            slot = wd_i % 2
            if n + 1 < KD:
                load_wdn(n + 1, (wd_i + 1) % 2)
            wd_i += 1
            for c in range(SC):
                gc = sti * SC + c
                b = it % 2
                it += 1
                for jj in range(NJ):
                    S.emit("pe", lambda e, jj=jj, c=c, b=b, slot=slot: e.matmul(
                        pd[b][:, :], wdn[slot][:, jj, :], gT[:, jj, c, :], start=(jj == 0), stop=(jj == NJ - 1)),
                        reads=[pfx + f"gT{jj}.{c}", pfx + f"wdn{slot}"], banks=[pfx + f"pd{b}"], ms=(jj == NJ - 1))
                S.emit("dve", lambda e, b=b, n=n, c=c: e.tensor_tensor(out=xo[b][:, :], in0=pd[b][:, :], in1=x_sb[:, n, c, HH:W],
                                                                      op=ALU.add),
                       reads=[pfx + f"x{c}.{n}m"], writes=[pfx + f"xo{b}"], banks=[pfx + f"pd{b}"])
                t = S.dma("sp", pfx + f"so{b}", lambda e, b=b, n=n, gc=gc: e.dma_start(
                    out=xo_r[:, n, gc * CH:(gc + 1) * CH], in_=xo[b][:, :]), reads=[pfx + f"xo{b}"], writes=[pfx + f"xoT.{n}.{gc}"])
                out_toks.append(t)
    return out_toks


DA = 512
MA = DA // 128
KA = 31


def conv_phase(cx, pfx, xT, xh, xoT, g_d, win_d, wout_d, adw_d, avec_d, bdw_d, nch=NCHUNK):
    S, nc = cx.S, cx.nc
    H = HALO
    W = H + CH
    x_sb = cx.sb(pfx + "x", [128, KD, W], F32)
    hT = cx.sb(pfx + "hT", [128, KD, W], BF16)
    sq = cx.sb(pfx + "sq", [128, KD, CH], BF16)
    rstd = cx.sb(pfx + "rstd", [128, CH], F32)
    tmp = cx.sb(pfx + "tmp", [128, CH], F32)
    ones = cx.sb(pfx + "ones", [128, 128], BF16)
    onesm = cx.sb(pfx + "onesm", [128, 128], BF16)
    g_sb = cx.sb(pfx + "g", [128, KD], F32)
    adw = cx.sb(pfx + "adw", [128, MA, KA], F32)
    avec = cx.sb(pfx + "avec", [128, 3, MA], F32)
    bdw = cx.sb(pfx + "bdw", [128, MA, 3], F32)
    win = cx.sb(pfx + "win", [128, 20, KD, 128], BF16)
    wout = cx.sb(pfx + "wout", [128, KD, KD, 128], BF16)
    glu = cx.sb(pfx + "glu", [128, MA, W], F32)
    ca = cx.sb(pfx + "ca", [128, MA, CH], F32)
    cab = cx.sb(pfx + "cab", [128, MA, CH], BF16)
    abT = cx.sb(pfx + "abT", [128, 2 * MA, CH], BF16)
    sgm = cx.sb(pfx + "sgm", [128, W], F32)
    chb = cx.sb(pfx + "chb", [128, W], F32)
    accb = cx.sb(pfx + "accb", [128, CH], F32)
    xo = [cx.sb(pfx + f"xo{i}", [128, CH], F32) for i in range(2)]
    pA, pB, pC = cx.ps(pfx + "pA"), cx.ps(pfx + "pB"), cx.ps(pfx + "pC")
    pmisc, pstat, pvar = cx.ps(pfx + "pmisc"), cx.ps(pfx + "pstat"), cx.ps(pfx + "pvar")
    po = [cx.ps(pfx + f"po{i}") for i in range(2)]
    BK = lambda n: pfx + "B." + n

    xT_r = xT.rearrange("(k p) t -> p k t", p=128)
    xh_r = xh.rearrange("(k p) (c h) -> p k c h", p=128, h=HALO)
    xo_r = xoT.rearrange("(k p) t -> p k t", p=128)

    S.emit("dve", lambda e: e.memset(ones[:], 1.0), writes=[pfx + "ones"])
    S.emit("dve", lambda e: e.memset(onesm[:], 1.0 / DA), writes=[pfx + "onesm"])
    S.dma_group("sp", "cst", [(lambda e: e.dma_start(out=g_sb[:], in_=g_d), [], [pfx + "g"]),
                              (lambda e: e.dma_start(out=adw[:], in_=adw_d), [], [pfx + "adw"]),
                              (lambda e: e.dma_start(out=avec[:], in_=avec_d), [], [pfx + "avec"]),
                              (lambda e: e.dma_start(out=bdw[:], in_=bdw_d), [], [pfx + "bdw"])])
    S.dma_group("pool", "wA", [(lambda e, m=m: e.dma_start(out=win[:, m], in_=win_d[m]), [], [pfx + f"win{m}"]) for m in range(20)])
    S.dma_group("pool", "wB", [(lambda e, n=n: e.dma_start(out=wout[:, n], in_=wout_d[n]), [], [pfx + f"wout{n}"]) for n in range(KD)])

    out_toks = []
    it = 0
    for c in range(nch):
        xk = [pfx + f"x{k}" for k in range(KD)]
        hk = [pfx + f"h{k}" for k in range(KD)]
        S.dma_group("sp", "lx", [(lambda e, k=k, c=c: e.dma_start(out=x_sb[:, k, H:W], in_=xT_r[:, k, c * CH:(c + 1) * CH]), [], [xk[k]])
                                 for k in range(KD)])
        S.dma("sp", pfx + "lxh", lambda e, c=c: e.dma_start(out=x_sb[:, :, 0:H], in_=xh_r[:, :, c, :]), writes=[pfx + "xh"])
        emit_rmsnorm(cx, pfx, lambda k: x_sb[:, k, H:W], lambda k: hT[:, k, H:W], xk, hk, CH, sq, ones,
                     pstat[:, :], BK("pstat"), tmp, rstd, g_sb, pfx + "g")
        emit_rmsnorm(cx, pfx, lambda k: x_sb[:, k, 0:H], lambda k: hT[:, k, 0:H], [pfx + "xh"] * KD,
                     [pfx + f"hh{k}" for k in range(KD)], H, sq, ones, pmisc[:, 256:256 + H], BK("pmisc"), tmp, rstd, g_sb, pfx + "g")
        hh = [pfx + f"hh{k}" for k in range(KD)]

        def proj(m, pmain, bank, hcol=None):
            for k in range(KD):
                S.emit("pe", lambda e, k=k, m=m: e.matmul(pmain[:, :], win[:, m, k, :], hT[:, k, H:W], start=(k == 0), stop=(k == KD - 1)),
                       reads=[hk[k], pfx + f"win{m}"], banks=[bank], ms=(k == KD - 1))
            if hcol is not None:
                for k in range(KD):
                    S.emit("pe", lambda e, k=k, m=m: e.matmul(pmisc[:, hcol:hcol + H], win[:, m, k, :], hT[:, k, 0:H],
                                                              start=(k == 0), stop=(k == KD - 1)),
                           reads=[hh[k], pfx + f"win{m}"], banks=[BK("pmisc")], ms=(k == KD - 1))

        for m in range(MA):
            proj(m, pA, BK("pA"), 0)
            proj(MA + m, pB, BK("pB"), H)
            S.emit("act", lambda e: e.activation(out=sgm[:, H:W], in_=pB[:, :], func=AF.Sigmoid), banks=[BK("pB")], writes=[pfx + "sgm"])
            S.emit("act", lambda e: e.activation(out=sgm[:, 0:H], in_=pmisc[:, H:2 * H], func=AF.Sigmoid), banks=[BK("pmisc")], writes=[pfx + "sgm"])
            S.emit("dve", lambda e, m=m: e.tensor_tensor(out=glu[:, m, H:W], in0=pA[:, :], in1=sgm[:, H:W], op=ALU.mult),
                   reads=[pfx + "sgm"], banks=[BK("pA")], writes=[pfx + f"glu{m}"])
            S.emit("dve", lambda e, m=m: e.tensor_tensor(out=glu[:, m, 0:H], in0=pmisc[:, 0:H], in1=sgm[:, 0:H], op=ALU.mult),
                   reads=[pfx + "sgm"], banks=[BK("pmisc")], writes=[pfx + f"glu{m}"])
            S.emit("act", lambda e, m=m: e.activation(out=ca[:, m, :], in_=glu[:, m, H:W], func=AF.Identity,
                                                      scale=adw[:, m, KA - 1:KA], bias=avec[:, 0, m:m + 1]),
                   reads=[pfx + f"glu{m}", pfx + "adw", pfx + "avec"], writes=[pfx + f"ca{m}"])
            for k in range(KA - 1):
                S.emit("dve", lambda e, m=m, k=k: e.scalar_tensor_tensor(
                    out=ca[:, m, :], in0=glu[:, m, 2 + k:2 + k + CH], scalar=adw[:, m, k:k + 1], in1=ca[:, m, :],
                    op0=ALU.mult, op1=ALU.add), reads=[pfx + f"glu{m}", pfx + f"ca{m}", pfx + "adw"], writes=[pfx + f"ca{m}"])
            S.emit("act", lambda e, m=m: e.activation(out=cab[:, m, :], in_=ca[:, m, :], func=AF.Identity),
                   reads=[pfx + f"ca{m}"], writes=[pfx + f"cab{m}"])
        for m in range(MA):
            S.emit("pe", lambda e, m=m: e.matmul(pstat[:, :], onesm[:, :], cab[:, m, :], start=(m == 0), stop=(m == MA - 1)),
                   reads=[pfx + f"cab{m}", pfx + "onesm"], banks=[BK("pstat")], ms=(m == MA - 1))
        for m in range(MA):
            S.emit("dve", lambda e, m=m: e.tensor_tensor(out=ca[:, m, :], in0=ca[:, m, :], in1=pstat[:, :], op=ALU.subtract),
                   reads=[pfx + f"ca{m}"], banks=[BK("pstat")], writes=[pfx + f"ca{m}"])
            S.emit("act", lambda e, m=m: e.activation(out=cab[:, m, :], in_=ca[:, m, :], func=AF.Square),
                   reads=[pfx + f"ca{m}"], writes=[pfx + f"cab{m}"])
        for m in range(MA):
            S.emit("pe", lambda e, m=m: e.matmul(pvar[:, :], onesm[:, :], cab[:, m, :], start=(m == 0), stop=(m == MA - 1)),
                   reads=[pfx + f"cab{m}", pfx + "onesm"], banks=[BK("pvar")], ms=(m == MA - 1))
        S.emit("dve", lambda e: e.tensor_scalar(out=tmp[:, :], in0=pvar[:, :], scalar1=EPS, scalar2=None, op0=ALU.add),
               banks=[BK("pvar")], writes=[pfx + "tmp"])
        S.emit("act", lambda e: e.activation(out=tmp[:, :], in_=tmp[:, :], func=AF.Sqrt), reads=[pfx + "tmp"], writes=[pfx + "tmp"])
        S.emit("dve", lambda e: e.reciprocal(out=rstd[:, :], in_=tmp[:, :]), reads=[pfx + "tmp"], writes=[pfx + "rstd"])
        for m in range(MA):
            S.emit("dve", lambda e, m=m: e.scalar_tensor_tensor(out=ca[:, m, :], in0=ca[:, m, :], scalar=avec[:, 1, m:m + 1], in1=rstd[:, :],
                                                                op0=ALU.mult, op1=ALU.mult),
                   reads=[pfx + f"ca{m}", pfx + "rstd", pfx + "avec"], writes=[pfx + f"ca{m}"])
            S.emit("act", lambda e, m=m: e.activation(out=abT[:, m, :], in_=ca[:, m, :], func=AF.Silu, bias=avec[:, 2, m:m + 1]),
                   reads=[pfx + f"ca{m}", pfx + "avec"], writes=[pfx + f"ab{m}"])
        for m in range(MA):
            proj(3 * MA + m, pA, BK("pA"), 2 * H)
            proj(4 * MA + m, pB, BK("pB"), 3 * H)
            proj(2 * MA + m, pC, BK("pC"), None)
            S.emit("act", lambda e: e.activation(out=sgm[:, H:W], in_=pA[:, :], func=AF.Identity), banks=[BK("pA")], writes=[pfx + "sgm"])
            S.emit("act", lambda e: e.activation(out=sgm[:, 0:H], in_=pmisc[:, 2 * H:3 * H], func=AF.Identity), banks=[BK("pmisc")], writes=[pfx + "sgm"])
            S.emit("dve", lambda e: e.tensor_tensor(out=chb[:, H:W], in0=pB[:, :], in1=sgm[:, H:W], op=ALU.mult),
                   reads=[pfx + "sgm"], banks=[BK("pB")], writes=[pfx + "chb"])
            S.emit("dve", lambda e: e.tensor_tensor(out=chb[:, 0:H], in0=pmisc[:, 3 * H:4 * H], in1=sgm[:, 0:H], op=ALU.mult),
                   reads=[pfx + "sgm"], banks=[BK("pmisc")], writes=[pfx + "chb"])
            S.emit("act", lambda e, m=m: e.activation(out=accb[:, :], in_=chb[:, H:W], func=AF.Identity, scale=bdw[:, m, 2:3]),
                   reads=[pfx + "chb", pfx + "bdw"], writes=[pfx + "accb"])
            for k in range(2):
                S.emit("dve", lambda e, m=m, k=k: e.scalar_tensor_tensor(
                    out=accb[:, :], in0=chb[:, H - 2 + k:H - 2 + k + CH], scalar=bdw[:, m, k:k + 1], in1=accb[:, :],
                    op0=ALU.mult, op1=ALU.add), reads=[pfx + "chb", pfx + "accb", pfx + "bdw"], writes=[pfx + "accb"])
            S.emit("dve", lambda e, m=m: e.tensor_tensor(out=abT[:, MA + m, :], in0=pC[:, :], in1=accb[:, :], op=ALU.mult),
                   reads=[pfx + "accb"], banks=[BK("pC")], writes=[pfx + f"ab{MA + m}"])
        for n in range(KD):
            b = it % 2
            it += 1
            for k in range(2 * MA):
                S.emit("pe", lambda e, k=k, n=n, b=b: e.matmul(po[b][:, :], wout[:, n, k, :], abT[:, k, :], start=(k == 0), stop=(k == 2 * MA - 1)),
                       reads=[pfx + f"ab{k}", pfx + f"wout{n}"], banks=[BK(f"po{b}")], ms=(k == 2 * MA - 1))
            S.emit("dve", lambda e, b=b, n=n: e.tensor_tensor(out=xo[b][:, :], in0=po[b][:, :], in1=x_sb[:, n, H:W], op=ALU.add),
                   reads=[xk[n]], writes=[pfx + f"xo{b}"], banks=[BK(f"po{b}")])
            t = S.dma("sp", pfx + f"so{b}", lambda e, b=b, n=n, c=c: e.dma_start(out=xo_r[:, n, c * CH:(c + 1) * CH], in_=xo[b][:, :]),
                      reads=[pfx + f"xo{b}"], writes=[pfx + f"xoT.{n}.{c}"])
            out_toks.append(t)
    return out_toks


NH = 16
HP = NH // 2
NKB = TOK // 128


def qkv_phase(cx, pfx, xT, g_d, wq_d, wk_d, wv_d, qkg_d, qT_o, kT_o, V_o, nch=NCHUNK):
    S, nc = cx.S, cx.nc
    x_sb = cx.sb(pfx + "x", [128, KD, CH], F32)
    hT = cx.sb(pfx + "hT", [128, KD, CH], BF16)
    sq = cx.sb(pfx + "sq", [128, KD, CH], BF16)
    rstd = cx.sb(pfx + "rstd", [128, CH], F32)
    tmp = cx.sb(pfx + "tmp", [128, CH], F32)
    ones = cx.sb(pfx + "ones", [128, 128], BF16)
    bd = cx.sb(pfx + "bd", [128, 128], BF16)
    g_sb = cx.sb(pfx + "g", [128, KD], F32)
    qkg = cx.sb(pfx + "qkg", [128, 2], F32)
    epsc = cx.sb(pfx + "epsc", [128, 1], F32)
    wq = cx.sb(pfx + "wq", [128, HP, KD, 128], BF16)
    wk = cx.sb(pfx + "wk", [128, HP, KD, 128], BF16)
    wv = cx.sb(pfx + "wv", [128, 2, KD, 512], BF16)
    qf = [cx.sb(pfx + f"qf{i}", [128, CH], F32) for i in range(2)]
    sqq = [cx.sb(pfx + f"sqq{i}", [128, CH], BF16) for i in range(2)]
    rs = [cx.sb(pfx + f"rs{i}", [128, CH], F32) for i in range(2)]
    qn = [cx.sb(pfx + f"qn{i}", [128, CH], BF16) for i in range(2)]
    vt = [cx.sb(pfx + f"vt{i}", [128, 512], BF16) for i in range(2)]
    pq = [cx.ps(pfx + f"pq{i}") for i in range(2)]
    pms = [cx.ps(pfx + f"pms{i}") for i in range(2)]
    pvv = [cx.ps(pfx + f"pvv{i}") for i in range(2)]
    pstat = cx.ps(pfx + "pstat")
    BK = lambda n: pfx + "B." + n
    xT_r = xT.rearrange("(k p) t -> p k t", p=128)
    qT_r = qT_o.rearrange("(k p) t -> p k t", p=128)
    kT_r = kT_o.rearrange("(k p) t -> p k t", p=128)

    S.emit("dve", lambda e: e.memset(ones[:], 1.0), writes=[pfx + "ones"])
    S.emit("dve", lambda e: e.memset(epsc[:], EPS), writes=[pfx + "epsc"])
    S.emit("dve", lambda e: e.memset(bd[:], 0.0), writes=[pfx + "bd"])
    S.emit("dve", lambda e: e.memset(bd[0:64, 0:64], 1.0 / 64), writes=[pfx + "bd"])
    S.emit("dve", lambda e: e.memset(bd[64:128, 64:128], 1.0 / 64), writes=[pfx + "bd"])
    S.dma_group("sp", "cst", [(lambda e: e.dma_start(out=g_sb[:], in_=g_d), [], [pfx + "g"]),
                              (lambda e: e.dma_start(out=qkg[:], in_=qkg_d), [], [pfx + "qkg"])])
    S.emit("dve", lambda e: e.tensor_scalar(out=qkg[:, 0:1], in0=qkg[:, 0:1], scalar1=0.125, scalar2=None, op0=ALU.mult),
           reads=[pfx + "qkg"], writes=[pfx + "qkg"])
    S.dma_group("pool", "wA", [(lambda e, m=m: e.dma_start(out=wq[:, m], in_=wq_d[m]), [], [pfx + f"wq{m}"]) for m in range(HP)])
    S.dma_group("pool", "wB", [(lambda e, m=m: e.dma_start(out=wk[:, m], in_=wk_d[m]), [], [pfx + f"wk{m}"]) for m in range(HP)])
    S.dma_group("pool", "wC", [(lambda e, hf=hf: e.dma_start(out=wv[:, hf], in_=wv_d[hf]), [], [pfx + f"wv{hf}"]) for hf in range(2)])
    out_toks = []
    it = 0
    for c in range(nch):
        xk = [pfx + f"x{k}" for k in range(KD)]
        hk = [pfx + f"h{k}" for k in range(KD)]
        S.dma_group("sp", "lx", [(lambda e, k=k, c=c: e.dma_start(out=x_sb[:, k, :], in_=xT_r[:, k, c * CH:(c + 1) * CH]), [], [xk[k]])
                                 for k in range(KD)])
        emit_rmsnorm(cx, pfx, lambda k: x_sb[:, k, :], lambda k: hT[:, k, :], xk, hk, CH, sq, ones,
                     pstat[:, :], BK("pstat"), tmp, rstd, g_sb, pfx + "g")
        for which, (w_sb, wkey, gcol, o_r) in enumerate(((wq, "wq", 0, qT_r), (wk, "wk", 1, kT_r))):
            for m in range(HP):
                b = it % 2
                it += 1
                for k in range(KD):
                    S.emit("pe", lambda e, k=k, m=m, b=b, w_sb=w_sb: e.matmul(pq[b][:, :], w_sb[:, m, k, :], hT[:, k, :],
                                                                             start=(k == 0), stop=(k == KD - 1)),
                           reads=[hk[k], pfx + f"{wkey}{m}"], banks=[BK(f"pq{b}")], ms=(k == KD - 1))
                S.emit("act", lambda e, b=b: e.activation(out=sqq[b][:, :], in_=pq[b][:, :], func=AF.Square),
                       banks=[BK(f"pq{b}")], writes=[pfx + f"sqq{b}"])
                S.emit("dve", lambda e, b=b: e.tensor_copy(out=qf[b][:, :], in_=pq[b][:, :]),
                       banks=[BK(f"pq{b}")], writes=[pfx + f"qf{b}"])
                S.emit("pe", lambda e, b=b: e.matmul(pms[b][:, :], bd[:, :], sqq[b][:, :], start=True, stop=True),
                       reads=[pfx + f"sqq{b}", pfx + "bd"], banks=[BK(f"pms{b}")], ms=True)
                S.emit("act", lambda e, b=b: e.activation(out=rs[b][:, :], in_=pms[b][:, :], func=AF.Ln, bias=epsc[:, 0:1]),
                       reads=[pfx + "epsc"], banks=[BK(f"pms{b}")], writes=[pfx + f"rs{b}"])
                S.emit("act", lambda e, b=b: e.activation(out=rs[b][:, :], in_=rs[b][:, :], func=AF.Exp, scale=-0.5),
                       reads=[pfx + f"rs{b}"], writes=[pfx + f"rs{b}"])
                S.emit("dve", lambda e, b=b, gcol=gcol: e.scalar_tensor_tensor(out=qn[b][:, :], in0=qf[b][:, :], scalar=qkg[:, gcol:gcol + 1],
                                                                               in1=rs[b][:, :], op0=ALU.mult, op1=ALU.mult),
                       reads=[pfx + f"qf{b}", pfx + f"rs{b}", pfx + "qkg"], writes=[pfx + f"qn{b}"])
                t = S.dma("sp", pfx + f"sq{b}", lambda e, b=b, m=m, c=c, o_r=o_r: e.dma_start(out=o_r[:, m, c * CH:(c + 1) * CH], in_=qn[b][:, :]),
                          reads=[pfx + f"qn{b}"], writes=[pfx + f"o{which}.{m}.{c}"])
                out_toks.append(t)
        for tt in range(CH // 128):
            kb = c * (CH // 128) + tt
            for hf in range(2):
                b = it % 2
                it += 1
                for k in range(KD):
                    S.emit("pe", lambda e, k=k, tt=tt, hf=hf, b=b: e.matmul(pvv[b][:, :], hT[:, k, tt * 128:(tt + 1) * 128], wv[:, hf, k, :],
                                                                          start=(k == 0), stop=(k == KD - 1)),
                           reads=[hk[k], pfx + f"wv{hf}"], banks=[BK(f"pvv{b}")], ms=(k == KD - 1))
                S.emit("act", lambda e, b=b: e.activation(out=vt[b][:, :], in_=pvv[b][:, :], func=AF.Identity),
                       banks=[BK(f"pvv{b}")], writes=[pfx + f"vt{b}"])
                t = S.dma("sp", pfx + f"sv{b}", lambda e, b=b, hf=hf, kb=kb: e.dma_start(
                    out=V_o[hf * 4:(hf + 1) * 4, :, kb, :].rearrange("h p f -> p h f"),
                    in_=vt[b][:, :].rearrange("p (h f) -> p h f", f=128)),
                    reads=[pfx + f"vt{b}"], writes=[pfx + f"oV.{hf}.{kb}"])
                out_toks.append(t)
    return out_toks


def attn_phase(cx, pfx, xT, xoT, qT_i, kT_g, V_g, mask_d, tri_d, wo_d, nhp=HP, nq=NCHUNK):
    S, nc = cx.S, cx.nc
    kT_sb = cx.sb(pfx + "kT", [128, 2, TOK], BF16)
    V_sb = cx.sb(pfx + "V", [128, 2, NKB, 128], BF16)
    q_sb = cx.sb(pfx + "q", [128, TOK], BF16)
    oT = cx.sb(pfx + "oT", [128, HP, TOK], BF16)
    mask = cx.sb(pfx + "mask", [128, 2, 4, 1024], BF16)
    ntri = cx.sb(pfx + "ntri", [128, 128], BF16)
    ident = cx.sb(pfx + "ident", [128, 128], BF16)
    nones = cx.sb(pfx + "nones", [128, 128], BF16)
    one1 = cx.sb(pfx + "one1", [128, 1], F32)
    E = [cx.sb(pfx + f"E{i}", [128, 1024], F32) for i in range(3)]
    L = [cx.sb(pfx + f"L{i}", [128, 1024], BF16) for i in range(3)]
    Wt = [cx.sb(pfx + f"W{i}", [128, 1024], BF16) for i in range(2)]
    Ls = cx.sb(pfx + "Ls", [128, 1024], F32)
    Lsb = [cx.sb(pfx + f"Lsb{i}", [128, 1024], BF16) for i in range(2)]
    wo = cx.sb(pfx + "wo", [128, KD, KD, 128], BF16)
    xr = [cx.sb(pfx + f"xr{i}", [128, CH], F32) for i in range(2)]
    xo = [cx.sb(pfx + f"xo{i}", [128, CH], F32) for i in range(2)]
    pz = [cx.ps(pfx + f"pz{i}", (128, 1024)) for i in range(3)]
    po = cx.ps(pfx + "po")
    pf = cx.ps(pfx + "pf")
    BK = lambda n: pfx + "B." + n
    xT_r = xT.rearrange("(k p) t -> p k t", p=128)
    xo_r = xoT.rearrange("(k p) t -> p k t", p=128)
    qT_r = qT_i.rearrange("(k p) t -> p k t", p=128)
    kT_r = kT_g.rearrange("(h r p) t -> p h r t", r=2, p=128)
    V_r = V_g.rearrange("(h r p) (k f) -> h p r k f", r=2, p=128, f=128)

    S.dma("pool", pfx + "ct", lambda e: e.dma_start(out=ntri[:], in_=tri_d[:, 0:128]), writes=[pfx + "ntri"])
    S.dma("pool", pfx + "ct", lambda e: e.dma_start(out=ident[:], in_=tri_d[:, 128:256]), writes=[pfx + "ident"])
    S.emit("dve", lambda e: e.memset(nones[:], -1.0), writes=[pfx + "nones"])
    S.emit("dve", lambda e: e.memset(one1[:], 1.0), writes=[pfx + "one1"])
    S.dma("pool", pfx + "cm", lambda e: e.dma_start(out=mask[:], in_=mask_d), writes=[pfx + "mask"])
    S.dma_group("pool", "wA", [(lambda e, n=n: e.dma_start(out=wo[:, n], in_=wo_d[n]), [], [pfx + f"wo{n}"]) for n in range(KD)])

    step = 0
    for hp in range(nhp):
        S.dma("sp", pfx + "lk", lambda e, hp=hp: e.dma_start(out=kT_sb[:, :, :], in_=kT_r[:, hp, :, :]), writes=[pfx + "kT"])
        S.dma("sp", pfx + "lv", lambda e, hp=hp: e.dma_start(out=V_sb[:, :, :, :], in_=V_r[hp]),
              writes=[pfx + "V"])
        S.dma("sp", pfx + "lq", lambda e, hp=hp: e.dma_start(out=q_sb[:, :], in_=qT_r[:, hp, :]), writes=[pfx + "q"])
        for i in range(nq):
            blocks = []
            for j in range(2 * i + 1, -1, -1):
                for b4 in range(3, -1, -1):
                    mk = 1 if j == 2 * i + 1 else (0 if j == 2 * i else None)
                    blocks.append((j % 2, (j // 2) * 4 + b4, mk, b4))
            nb = len(blocks)

            def emit_z(s):
                r, kb, mk, b4 = blocks[s]
                z3 = (step + s) % 3
                for hd in range(2):
                    S.emit("pe", lambda e, hd=hd, r=r, kb=kb, z3=z3, i=i: e.matmul(
                        pz[z3][:, hd * 512:(hd + 1) * 512], kT_sb[hd * 64:(hd + 1) * 64, r, kb * 128:(kb + 1) * 128],
                        q_sb[hd * 64:(hd + 1) * 64, i * CH:(i + 1) * CH], start=True, stop=True, skip_group_check=True),
                        reads=[pfx + "kT", pfx + "q"], banks=[BK(f"pz{z3}")], ms=(hd == 1 and mk is None))
                if mk is not None:
                    for hd in range(2):
                        S.emit("pe", lambda e, hd=hd, z3=z3, mk=mk, b4=b4: e.matmul(
                            pz[z3][:, hd * 512:(hd + 1) * 512], ident[:, :], mask[:, mk, b4, hd * 512:(hd + 1) * 512],
                            start=False, stop=True, skip_group_check=True),
                            reads=[pfx + "ident", pfx + "mask"], banks=[BK(f"pz{z3}")], ms=(hd == 1))

            def emit_el(s):
                r, kb, mk, b4 = blocks[s]
                zb = (step + s) % 2
                z3 = (step + s) % 3
                S.emit("act", lambda e, zb=zb, z3=z3: e.activation(out=E[z3][:, :], in_=pz[z3][:, :], func=AF.Exp),
                       banks=[BK(f"pz{z3}")], writes=[pfx + f"E{z3}"])
                S.emit("act", lambda e, z3=z3: e.activation(out=L[z3][:, :], in_=E[z3][:, :], func=AF.Ln, bias=one1[:, 0:1]),
                       reads=[pfx + f"E{z3}", pfx + "one1"], writes=[pfx + f"L{z3}"])

            def emit_p2(s):
                zb = (step + s) % 2
                z3 = (step + s) % 3
                sls = [slice(hd * 512, (hd + 1) * 512) for hd in range(2)]
                for hd in range(2):
                    S.emit("pe", lambda e, sl=sls[hd], zb=zb, z3=z3, last=(s == 0): e.matmul(
                        pz[z3][:, sl], ntri[:, :], L[z3][:, sl], start=False, stop=last, skip_group_check=True),
                        reads=[pfx + f"L{z3}", pfx + "ntri"], banks=[BK(f"pz{z3}")], ms=(s == 0 and hd == 1))
                if s > 0:
                    lb = (step + s - 1) % 2
                    for hd in range(2):
                        S.emit("pe", lambda e, sl=sls[hd], lb=lb, z3=z3: e.matmul(
                            pz[z3][:, sl], nones[:, :], Lsb[lb][:, sl], start=False, stop=True, skip_group_check=True),
                            reads=[pfx + f"Lsb{lb}", pfx + "nones"], banks=[BK(f"pz{z3}")], ms=(hd == 1))

            def emit_w(s):
                r, kb, mk, b4 = blocks[s]
                zb = (step + s) % 2
                z3 = (step + s) % 3
                S.emit("act", lambda e, zb=zb, z3=z3: e.activation(out=Wt[zb][:, :], in_=pz[z3][:, :], func=AF.Exp),
                       banks=[BK(f"pz{z3}")], writes=[pfx + f"W{zb}"])
                if s + 1 < nb:
                    if s == 0:
                        S.emit("dve", lambda e, z3=z3: e.tensor_copy(out=Ls[:, :], in_=L[z3][:, :]), reads=[pfx + f"L{z3}"], writes=[pfx + "Ls"])
                    else:
                        S.emit("dve", lambda e, z3=z3: e.tensor_tensor(out=Ls[:, :], in0=Ls[:, :], in1=L[z3][:, :], op=ALU.add),
                               reads=[pfx + f"L{z3}", pfx + "Ls"], writes=[pfx + "Ls"])
                    S.emit("dve", lambda e, zb=zb: e.tensor_copy(out=Lsb[zb][:, :], in_=Ls[:, :]), reads=[pfx + "Ls"], writes=[pfx + f"Lsb{zb}"])

            def emit_pv(s):
                r, kb, mk, b4 = blocks[s]
                zb = (step + s) % 2
                for hd in range(2):
                    S.emit("pe", lambda e, hd=hd, r=r, kb=kb, zb=zb, st_=(s == 0), sp_=(s == nb - 1): e.matmul(
                        po[hd * 64:(hd + 1) * 64, :], V_sb[:, r, kb, hd * 64:(hd + 1) * 64], Wt[zb][:, hd * 512:(hd + 1) * 512],
                        start=st_, stop=sp_),
                        reads=[pfx + "V", pfx + f"W{zb}"], banks=[BK("po")], ms=(s == nb - 1 and hd == 1))

            emit_z(0)
            emit_el(0)
            emit_z(1)
            emit_el(1)
            for s in range(nb):
                emit_p2(s)
                emit_w(s)
                if s + 2 < nb:
                    emit_z(s + 2)
                    emit_el(s + 2)
                if s > 0:
                    emit_pv(s - 1)
            emit_pv(nb - 1)
            step += nb
            S.emit("act", lambda e, hp=hp, i=i: e.activation(out=oT[:, hp, i * CH:(i + 1) * CH], in_=po[:, :], func=AF.Identity),
                   banks=[BK("po")], writes=[pfx + f"oT{hp}.{i}"])
    out_toks = []
    it = 0
    for i in range(nq):
        for n in range(KD):
            b = it % 2
            it += 1
            S.dma("sp", pfx + f"lx{b}", lambda e, b=b, n=n, i=i: e.dma_start(out=xr[b][:, :], in_=xT_r[:, n, i * CH:(i + 1) * CH]),
                  writes=[pfx + f"xr{b}"])
            for k in range(nhp):
                S.emit("pe", lambda e, k=k, n=n, i=i: e.matmul(pf[:, :], wo[:, n, k, :], oT[:, k, i * CH:(i + 1) * CH],
                                                              start=(k == 0), stop=(k == nhp - 1)),
                       reads=[pfx + f"oT{k}.{i}", pfx + f"wo{n}"], banks=[BK("pf")], ms=(k == nhp - 1))
            S.emit("dve", lambda e, b=b: e.tensor_tensor(out=xo[b][:, :], in0=pf[:, :], in1=xr[b][:, :], op=ALU.add),
                   reads=[pfx + f"xr{b}"], writes=[pfx + f"xo{b}"], banks=[BK("pf")])
            t = S.dma("sp", pfx + f"so{b}", lambda e, b=b, n=n, i=i: e.dma_start(out=xo_r[:, n, i * CH:(i + 1) * CH], in_=xo[b][:, :]),
                      reads=[pfx + f"xo{b}"], writes=[pfx + f"xoT.{n}.{i}"])
            out_toks.append(t)
    return out_toks


NCORES = 8
SEQ = 8192
BATCH = 4


def _own_tokens(p):
    return np.concatenate([np.arange((2 * i + p) * CH, (2 * i + p + 1) * CH) for i in range(NCHUNK)])


def _halo_tokens(p):
    return np.concatenate([np.arange((2 * i + p) * CH, (2 * i + p) * CH + HALO) for i in range(NCHUNK)])


def _lay_vec(v, n):
    return np.ascontiguousarray(v.reshape(n, 128).T)


def _lay_w(w, kin, nout):
    return np.ascontiguousarray(w.reshape(kin, 128, nout, 128).transpose(2, 1, 0, 3))


def _lay_ffn(g, w_up, dw_w, dw_b, w_down):
    wu = w_up.reshape(KD, 128, 2, NJ, 128)
    return dict(g=_lay_vec(g, KD),
                wup=np.ascontiguousarray(wu.transpose(3, 1, 0, 2, 4).reshape(NJ, 128, KD, 256)),
                wdn=_lay_w(w_down, NJ, KD),
                dw=np.ascontiguousarray(dw_w.reshape(3, 2 * NJ, 128).transpose(2, 1, 0)),
                db=_lay_vec(dw_b, 2 * NJ))


def _lay_conv(g, w_in, a_dw_w, a_dw_b, a_ln_g, a_ln_b, b_dw_w, w_out):
    return dict(g=_lay_vec(g, KD), win=_lay_w(w_in, KD, 20), wout=_lay_w(w_out, KD, KD),
                adw=np.ascontiguousarray(a_dw_w.reshape(KA, MA, 128).transpose(2, 1, 0)),
                avec=np.ascontiguousarray(np.stack([a_dw_b, a_ln_g, a_ln_b]).reshape(3, MA, 128).transpose(2, 0, 1)),
                bdw=np.ascontiguousarray(b_dw_w.reshape(3, MA, 128).transpose(2, 1, 0)))


def _lay_attn(g, w_qkv, q_g, k_g, w_o):
    return dict(g=_lay_vec(g, KD), wq=_lay_w(w_qkv[:, :D], KD, HP), wk=_lay_w(w_qkv[:, D:2 * D], KD, HP),
                wv=np.ascontiguousarray(w_qkv[:, 2 * D:].reshape(KD, 128, 2, 512).transpose(2, 1, 0, 3)),
                qkg=np.ascontiguousarray(np.stack([np.tile(q_g, 2), np.tile(k_g, 2)], 1)),
                wo=_lay_w(w_o, KD, KD))


def _masks(p):
    ks = np.arange(CH)[:, None]
    tq = np.arange(CH)[None, :]
    diag = np.where(ks < tq, 0.0, -30000.0).astype(np.float32)
    A = diag if p == 0 else np.zeros((CH, CH), np.float32)
    B = np.full((CH, CH), -30000.0, np.float32) if p == 0 else diag
    m = np.stack([A, B]).reshape(2, 4, 128, CH).transpose(2, 0, 1, 3)
    return np.ascontiguousarray(np.concatenate([m, m], -1))


def _tri():
    j = np.arange(128)[:, None]
    s = np.arange(128)[None, :]
    return np.ascontiguousarray(np.concatenate([-(j >= s).astype(np.float32), (j == s).astype(np.float32)], 1))


_PROGS = {}


def _dram(nc, name, shape, dt=F32, kind="ExternalInput"):
    return nc.dram_tensor(name, list(shape), dt, kind=kind).ap()


def _prog(kind):
    if kind in _PROGS:
        return _PROGS[kind]
    nc = bass.Bass("TRN2", target_bir_lowering=False)
    S = Sched()
    with ExitStack() as st:
        cx = Ctx(nc, S, st)
        if kind == "conv":
            a = [_dram(nc, "xT", [D, TOK]), _dram(nc, "xh", [D, NCHUNK * HALO])]
            xo = _dram(nc, "xoT", [D, TOK], F32, "ExternalOutput")
            toks = conv_phase(cx, "c.", a[0], a[1], xo, _dram(nc, "g", [128, KD]), _dram(nc, "win", [20, 128, KD, 128]),
                              _dram(nc, "wout", [KD, 128, KD, 128]), _dram(nc, "adw", [128, MA, KA]),
                              _dram(nc, "avec", [128, 3, MA]), _dram(nc, "bdw", [128, MA, 3]))
        elif kind == "ffn":
            a = [_dram(nc, "xT", [D, TOK]), _dram(nc, "xh", [D, NCHUNK * HALO])]
            xo = _dram(nc, "xoT", [D, TOK], F32, "ExternalOutput")
            toks = ffn_phase(cx, "f.", a[0], a[1], xo, _dram(nc, "g", [128, KD]), _dram(nc, "wup", [NJ, 128, KD, 256]),
                             _dram(nc, "wdn", [KD, 128, NJ, 128]), _dram(nc, "dw", [128, 2 * NJ, 3]), _dram(nc, "db", [128, 2 * NJ]))
        elif kind == "qkv":
            toks = qkv_phase(cx, "q.", _dram(nc, "xT", [D, TOK]), _dram(nc, "g", [128, KD]), _dram(nc, "wq", [HP, 128, KD, 128]),
                             _dram(nc, "wk", [HP, 128, KD, 128]), _dram(nc, "wv", [2, 128, KD, 512]), _dram(nc, "qkg", [128, 2]),
                             _dram(nc, "qT", [D, TOK], BF16, "ExternalOutput"), _dram(nc, "kT", [D, TOK], BF16, "ExternalOutput"),
                             _dram(nc, "V", [HP, 128, NKB, 128], BF16, "ExternalOutput"))
        elif kind == "attn":
            toks = attn_phase(cx, "a.", _dram(nc, "xT", [D, TOK]), _dram(nc, "xoT", [D, TOK], F32, "ExternalOutput"),
                              _dram(nc, "qT", [D, TOK], BF16), _dram(nc, "kT", [2, D, TOK], BF16),
                              _dram(nc, "V", [2, HP, 128, NKB, 128], BF16), _dram(nc, "mask", [128, 2, 4, 1024]),
                              _dram(nc, "tri", [128, 256]), _dram(nc, "wo", [KD, 128, KD, 128]))
        S.wait_all("sp", toks)
        S.build(nc, st)
    _PROGS[kind] = nc
    return nc


def _run(kind, in_maps):
    res = run_bass_kernel_spmd(_prog(kind), in_maps, core_ids=list(range(NCORES)))
    return res.results


def kernel_unfused(**inputs):
    inp = {k: np.asarray(v) for k, v in inputs.items()}
    x = inp["x"]
    own = [_own_tokens(p) for p in range(2)]
    halo = [_halo_tokens(p) for p in range(2)]
    xT = [np.ascontiguousarray(x[c // 2][own[c % 2]].T) for c in range(NCORES)]

    def halos(xT):
        out = []
        for b in range(BATCH):
            full = np.zeros((D, HALO + SEQ), np.float32)
            for p in range(2):
                for i in range(NCHUNK):
                    g0 = (2 * i + p) * CH
                    full[:, HALO + g0:HALO + g0 + CH] = xT[2 * b + p][:, i * CH:(i + 1) * CH]
            for p in range(2):
                out.append(np.ascontiguousarray(full[:, halo[p]]))
        return out

    masks = [_masks(p) for p in range(2)]
    tri = _tri()
    for layer in range(4):
        i = layer // 2
        if layer % 2 == 0:
            lw = _lay_conv(inp["mix_norm_g"][layer], inp["conv_w_in"][i], inp["conv_a_dw_w"][i], inp["conv_a_dw_b"][i],
                           inp["conv_a_ln_g"][i], inp["conv_a_ln_b"][i], inp["conv_b_dw_w"][i], inp["conv_w_out"][i])
            xh = halos(xT)
            r = _run("conv", [dict(lw, xT=xT[c], xh=xh[c]) for c in range(NCORES)])
            xT = [r[c]["xoT"] for c in range(NCORES)]
        else:
            lw = _lay_attn(inp["mix_norm_g"][layer], inp["attn_w_qkv"][i], inp["attn_q_g"][i], inp["attn_k_g"][i], inp["attn_w_o"][i])
            r = _run("qkv", [dict(xT=xT[c], g=lw["g"], wq=lw["wq"], wk=lw["wk"], wv=lw["wv"], qkg=lw["qkg"]) for c in range(NCORES)])
            ins = []
            for c in range(NCORES):
                b = c // 2
                kT_g = np.stack([r[2 * b]["kT"], r[2 * b + 1]["kT"]])
                V_g = np.stack([r[2 * b]["V"], r[2 * b + 1]["V"]])
                ins.append(dict(xT=xT[c], qT=r[c]["qT"], kT=kT_g, V=V_g, mask=masks[c % 2], tri=tri, wo=lw["wo"]))
            r = _run("attn", ins)
            xT = [r[c]["xoT"] for c in range(NCORES)]
        lw = _lay_ffn(inp["ffn_norm_g"][layer], inp["ffn_w_up"][layer], inp["ffn_dw_w"][layer], inp["ffn_dw_b"][layer], inp["ffn_w_down"][layer])
        xh = halos(xT)
        r = _run("ffn", [dict(lw, xT=xT[c], xh=xh[c]) for c in range(NCORES)])
        xT = [r[c]["xoT"] for c in range(NCORES)]
    out = np.empty((BATCH, SEQ, D), np.float32)
    for c in range(NCORES):
        out[c // 2][own[c % 2]] = xT[c].T
    return out


PAIRS = [[0, 1], [2, 3], [4, 5], [6, 7]]


def halo_phase(cx, pfx, xT, tl, tg, xh, sel_d):
    S, nc = cx.S, cx.nc
    t_sb = cx.sb(pfx + "t", [128, KD, NCHUNK, HALO], F32)
    c0 = cx.sb(pfx + "c0", [128, KD, NCHUNK, HALO], F32)
    c1 = cx.sb(pfx + "c1", [128, KD, NCHUNK, HALO], F32)
    sel = cx.sb(pfx + "sel", [128, 2], F32)
    xT_r = xT.rearrange("(k p) (c t) -> p k c t", p=128, t=CH)
    tl_r = tl.rearrange("(k p) (c h) -> p k c h", p=128, h=HALO)
    tg_r = tg.rearrange("(r k p) (c h) -> p r k c h", p=128, k=KD, h=HALO)
    xh_r = xh.rearrange("(k p) (c h) -> p k c h", p=128, h=HALO)
    S.dma("sp", "cst", lambda e: e.dma_start(out=sel[:], in_=sel_d), writes=[pfx + "sel"])
    S.dma_group("sp", "h0", [(lambda e, k=k: e.dma_start(out=t_sb[:, k], in_=xT_r[:, k, :, CH - HALO:CH]), [], [pfx + f"t{k}"])
                             for k in range(KD)])
    S.dma_group("sp", "h1", [(lambda e, k=k: e.dma_start(out=tl_r[:, k], in_=t_sb[:, k]), [pfx + f"t{k}"], [pfx + f"tl{k}"])
                             for k in range(KD)])
    S.dma("pool", "cc", lambda e: e.collective_compute("AllGather", ALU.bypass, replica_groups=PAIRS, ins=[tl], outs=[tg]),
          reads=[pfx + f"tl{k}" for k in range(KD)], writes=[pfx + "tg"], inc=1)
    S.emit("dve", lambda e: e.memset(c0[:, :, 0, :], 0.0), writes=[pfx + "c0z"])
    S.dma_group("sp", "h2", [(lambda e, k=k: e.dma_start(out=c0[:, k, 1:NCHUNK, :], in_=tg_r[:, 1, k, 0:NCHUNK - 1, :]), [pfx + "tg"], [pfx + f"c0{k}"])
                             for k in range(KD)] +
                            [(lambda e, k=k: e.dma_start(out=c1[:, k], in_=tg_r[:, 0, k]), [pfx + "tg"], [pfx + f"c1{k}"])
                             for k in range(KD)])
    items = []
    for k in range(KD):
        S.emit("dve", lambda e, k=k: e.tensor_scalar(out=c0[:, k], in0=c0[:, k], scalar1=sel[:, 0:1], scalar2=None, op0=ALU.mult),
               reads=[pfx + f"c0{k}", pfx + "c0z", pfx + "sel"], writes=[pfx + f"c0{k}"])
        S.emit("dve", lambda e, k=k: e.scalar_tensor_tensor(out=c1[:, k], in0=c1[:, k], scalar=sel[:, 1:2], in1=c0[:, k],
                                                            op0=ALU.mult, op1=ALU.add),
               reads=[pfx + f"c0{k}", pfx + f"c1{k}", pfx + "sel"], writes=[pfx + f"c1{k}"])
        items.append((lambda e, k=k: e.dma_start(out=xh_r[:, k], in_=c1[:, k]), [pfx + f"c1{k}"], [pfx + f"xh{k}"]))
    t = S.dma_group("sp", "h3", items)
    return [t]


def gather_phase(cx, pfx, kT_l, kT_g, V_l, V_g):
    S = cx.S
    toks = []
    for hp in range(HP):
        for nm, a, b in (("k", kT_l, kT_g), ("v", V_l, V_g)):
            toks.append(S.dma("pool", "cc", lambda e, hp=hp, a=a, b=b: e.collective_compute(
                "AllGather", ALU.bypass, replica_groups=PAIRS, ins=[a[hp * 128:(hp + 1) * 128, :]], outs=[b[hp * 256:(hp + 1) * 256, :]]),
                writes=[pfx + f"{nm}g{hp}"], inc=1))
    return toks[-1:]


_FUSED = []


def _fused_prog():
    if _FUSED:
        return _FUSED[0]
    nc = bass.Bass("TRN2", target_bir_lowering=False)
    S = Sched()
    with ExitStack() as outer:
        x_in = _dram(nc, "xT", [D, TOK])
        out = _dram(nc, "out", [D, TOK], F32, "ExternalOutput")
        sel_d = _dram(nc, "sel", [128, 2])
        mask_d = _dram(nc, "mask", [128, 2, 4, 1024])
        tri_d = _dram(nc, "tri", [128, 256])
        xs = [nc.dram_tensor(f"xs{i}", [D, TOK], F32).ap() for i in range(2)]
        tl = nc.dram_tensor("tl", [D, NCHUNK * HALO], F32).ap()
        tg = nc.dram_tensor("tg", [2 * D, NCHUNK * HALO], F32).ap()
        xh = nc.dram_tensor("xh", [D, NCHUNK * HALO], F32).ap()
        qT = nc.dram_tensor("qT", [D, TOK], BF16).ap()
        kT_l = nc.dram_tensor("kTl", [D, TOK], BF16).ap()
        kT_g = nc.dram_tensor("kTg", [2 * D, TOK], BF16).ap()
        V_l = nc.dram_tensor("Vl", [HP * 128, NKB * 128], BF16).ap()
        V_g = nc.dram_tensor("Vg", [2 * HP * 128, NKB * 128], BF16).ap()
        V_l5 = V_l.rearrange("(h p) (k f) -> h p k f", p=128, f=128)

        def phase(fn):
            with ExitStack() as pst:
                cx = Ctx(nc, S, pst)
                toks = fn(cx)
                S.barrier(toks)
                S.build_phase(nc, pst, outer)

        cur = x_in
        nxt = 0
        for layer in range(4):
            L = f"L{layer}."
            if layer % 2 == 0:
                phase(lambda cx: halo_phase(cx, L + "h.", cur, tl, tg, xh, sel_d))
                dst = xs[nxt]
                phase(lambda cx: conv_phase(cx, L + "c.", cur, xh, dst, _dram(nc, L + "mg", [128, KD]), _dram(nc, L + "win", [20, 128, KD, 128]),
                                            _dram(nc, L + "wout", [KD, 128, KD, 128]), _dram(nc, L + "adw", [128, MA, KA]),
                                            _dram(nc, L + "avec", [128, 3, MA]), _dram(nc, L + "bdw", [128, MA, 3])))
            else:
                phase(lambda cx: qkv_phase(cx, L + "q.", cur, _dram(nc, L + "mg", [128, KD]), _dram(nc, L + "wq", [HP, 128, KD, 128]),
                                           _dram(nc, L + "wk", [HP, 128, KD, 128]), _dram(nc, L + "wv", [2, 128, KD, 512]),
                                           _dram(nc, L + "qkg", [128, 2]), qT, kT_l, V_l5))
                phase(lambda cx: gather_phase(cx, L + "g.", kT_l, kT_g, V_l, V_g))
                dst = xs[nxt]
                phase(lambda cx: attn_phase(cx, L + "a.", cur, dst, qT, kT_g, V_g, mask_d, tri_d, _dram(nc, L + "wo", [KD, 128, KD, 128])))
            cur = dst
            nxt ^= 1
            phase(lambda cx: halo_phase(cx, L + "hf.", cur, tl, tg, xh, sel_d))
            dst = out if layer == 3 else xs[nxt]
            phase(lambda cx: ffn_phase(cx, L + "f.", cur, xh, dst, _dram(nc, L + "fg", [128, KD]), _dram(nc, L + "wup", [NJ, 128, KD, 256]),
                                       _dram(nc, L + "wdn", [KD, 128, NJ, 128]), _dram(nc, L + "dw", [128, 2 * NJ, 3]),
                                       _dram(nc, L + "db", [128, 2 * NJ])))
            cur = dst
            nxt ^= 1
    _FUSED.append((nc, S))
    return _FUSED[0]


def kernel(**inputs):
    inp = {k: np.asarray(v) for k, v in inputs.items()}
    x = inp["x"]
    own = [_own_tokens(p) for p in range(2)]
    base = {"tri": _tri()}
    for layer in range(4):
        i = layer // 2
        L = f"L{layer}."
        if layer % 2 == 0:
            lw = _lay_conv(inp["mix_norm_g"][layer], inp["conv_w_in"][i], inp["conv_a_dw_w"][i], inp["conv_a_dw_b"][i],
                           inp["conv_a_ln_g"][i], inp["conv_a_ln_b"][i], inp["conv_b_dw_w"][i], inp["conv_w_out"][i])
        else:
            lw = _lay_attn(inp["mix_norm_g"][layer], inp["attn_w_qkv"][i], inp["attn_q_g"][i], inp["attn_k_g"][i], inp["attn_w_o"][i])
        lw["mg"] = lw.pop("g")
        lf = _lay_ffn(inp["ffn_norm_g"][layer], inp["ffn_w_up"][layer], inp["ffn_dw_w"][layer], inp["ffn_dw_b"][layer], inp["ffn_w_down"][layer])
        lf["fg"] = lf.pop("g")
        for k, v in list(lw.items()) + list(lf.items()):
            base[L + k] = v
    masks = [_masks(p) for p in range(2)]
    sels = [np.ascontiguousarray(np.tile(np.array([[1.0, 0.0]], np.float32) if p == 0 else np.array([[0.0, 1.0]], np.float32), (128, 1)))
            for p in range(2)]
    in_maps = []
    for c in range(NCORES):
        d = dict(base)
        d["xT"] = np.ascontiguousarray(x[c // 2][own[c % 2]].T)
        d["mask"] = masks[c % 2]
        d["sel"] = sels[c % 2]
        in_maps.append(d)
    nc, _ = _fused_prog()
    res = run_bass_kernel_spmd(nc, in_maps, core_ids=list(range(NCORES))).results
    out = np.empty((BATCH, SEQ, D), np.float32)
    for c in range(NCORES):
        out[c // 2][own[c % 2]] = res[c]["out"].T
    return out
```

```python
import numpy as np
from contextlib import ExitStack
import concourse.bass as bass
import concourse.mybir as mybir
from concourse.bass_utils import run_bass_kernel_spmd

F32 = mybir.dt.float32
BF16 = mybir.dt.bfloat16
AF = mybir.ActivationFunctionType
ALU = mybir.AluOpType

ENGINES = ("pe", "act", "dve", "pool", "sp")
ENG_ATTR = {"pe": "tensor", "act": "scalar", "dve": "vector", "pool": "gpsimd", "sp": "sync"}


class _Op:
    __slots__ = ("fn", "waits", "inc")

    def __init__(self, fn):
        self.fn = fn
        self.waits = []
        self.inc = None


class Sched:
    SEM_ROT = 20000

    def __init__(self):
        self.ops = {e: [] for e in ENGINES}
        self.built = {e: 0 for e in ENGINES}
        self.ms = {e: [] for e in ENGINES}
        self.gen = {e: 0 for e in ENGINES}
        self.cnt = {}
        self.waited = {e: {} for e in ENGINES}
        self.last_w = {}
        self.readers = {}
        self.nwaits = 0
        self.sems = {}

    def _new_ms(self, e, seq, op):
        k = ("eng", e, self.gen[e])
        if self.cnt.get(k, 0) >= self.SEM_ROT:
            self.gen[e] += 1
            k = ("eng", e, self.gen[e])
        self.cnt[k] = self.cnt.get(k, 0) + 1
        op.inc = (k, 1)
        self.ms[e].append((seq, k, self.cnt[k]))
        return k, self.cnt[k]

    def _milestone(self, e, seq):
        lo, hi = 0, len(self.ms[e])
        while lo < hi:
            mid = (lo + hi) // 2
            if self.ms[e][mid][0] >= seq:
                hi = mid
            else:
                lo = mid + 1
        if lo < len(self.ms[e]):
            return self.ms[e][lo][1], self.ms[e][lo][2]
        last = len(self.ops[e]) - 1
        while self.ops[e][last].fn is None:
            last -= 1
        assert last >= seq and last >= self.built[e], (e, seq, last, self.built[e])
        op = self.ops[e][last]
        assert op.inc is None, f"last op on {e} already has an inc"
        return self._new_ms(e, last, op)

    def _resolve(self, tok):
        if tok[0] == "eng":
            return self._milestone(tok[1], tok[2])
        return tok[1], tok[2]

    def _deps(self, engine, reads, writes):
        toks = []
        for k in reads:
            t = self.last_w.get(k)
            if t is not None:
                toks.append(t)
        for k in writes:
            t = self.last_w.get(k)
            if t is not None and not (t[0] == "eng" and t[1] == engine == "pe"):
                toks.append(t)
            for t in self.readers.get(k, {}).values():
                if not (t[0] == "eng" and t[1] == engine):
                    toks.append(t)
        return self._waits(engine, toks)

    def _waits(self, engine, toks):
        waits = {}
        for t in toks:
            sk, v = self._resolve(t)
            if self.waited[engine].get(sk, 0) >= v:
                continue
            waits[sk] = max(waits.get(sk, 0), v)
        for sk, v in waits.items():
            self.waited[engine][sk] = v
        self.nwaits += len(waits)
        return list(waits.items())

    def _track(self, tok, rkey, reads, writes):
        for k in writes:
            self.last_w[k] = tok
            self.readers[k] = {}
        for k in reads:
            self.readers.setdefault(k, {})[rkey] = tok

    def emit(self, engine, fn, reads=(), writes=(), ms=False, banks=()):
        writes = list(writes) + list(banks)
        op = _Op(fn)
        op.waits = self._deps(engine, reads, writes)
        seq = len(self.ops[engine])
        self.ops[engine].append(op)
        if ms or engine != "pe":
            self._new_ms(engine, seq, op)
        tok = ("eng", engine, seq)
        self._track(tok, engine, reads, writes)
        return tok

    def dma(self, queue, sem, fn, reads=(), writes=(), inc=16):
        op = _Op(fn)
        op.waits = self._deps(queue, reads, writes)
        self.ops[queue].append(op)
        k = ("dma", sem.split(".", 1)[-1])
        self.cnt[k] = self.cnt.get(k, 0) + inc
        op.inc = (k, inc)
        tok = ("dma", k, self.cnt[k])
        self._track(tok, k, reads, writes)
        return tok

    def dma_group(self, queue, sem, items):
        toks = [self.dma(queue, sem, fn, reads, writes) for (fn, reads, writes) in items]
        final = toks[-1]
        for (fn, reads, writes) in items:
            for k in writes:
                self.last_w[k] = final
            for k in reads:
                self.readers.setdefault(k, {})[final[1]] = final
        return final

    def wait_all(self, engine, toks):
        op = _Op(None)
        op.waits = self._waits(engine, toks)
        self.ops[engine].append(op)

    def barrier(self, toks=()):
        toks = list(toks)
        for e in ("pe", "act", "dve", "pool"):
            last = len(self.ops[e]) - 1
            while last >= self.built[e] and (self.ops[e][last].fn is None or self.ops[e][last].inc is not None and self.ops[e][last].inc[0][0] == "dma"):
                last -= 1
            if last >= self.built[e]:
                toks.append(("eng", e, last))
        for e in ENGINES:
            self.wait_all(e, toks)

    def build_phase(self, nc, st, outer):
        block = st.enter_context(nc.Block())
        for e in ENGINES:
            deco = getattr(block, ENG_ATTR[e])

            def body(eng, e=e):
                for op in self.ops[e][self.built[e]:]:
                    for (sk, v) in op.waits:
                        eng.wait_ge(self._sem(nc, outer, sk), v)
                    if op.fn is None:
                        continue
                    ins = op.fn(eng)
                    if op.inc is not None:
                        ins.then_inc(self._sem(nc, outer, op.inc[0]), op.inc[1])
                    op.fn = None
                self.built[e] = len(self.ops[e])

            deco(body)

    def _sem(self, nc, outer, k):
        if k not in self.sems:
            self.sems[k] = outer.enter_context(nc.semaphore(f"s{len(self.sems)}"))
        return self.sems[k]

    def build(self, nc, st):
        self.build_phase(nc, st, st)


D = 1024
KD = D // 128
CH = 512
NCHUNK = 8
TOK = CH * NCHUNK
HALO = 32
DFF = 2816
NJ = DFF // 128
EPS = 1e-6


class Ctx:
    def __init__(self, nc, S, st):
        self.nc, self.S, self.st = nc, S, st
        self.n = 0

    def sb(self, name, shape, dt):
        return self.st.enter_context(self.nc.sbuf_tensor(name, list(shape), dt))

    def ps(self, name, shape=(128, 512), dt=F32):
        return self.st.enter_context(self.nc.psum_tensor(name, list(shape), dt))


def emit_rmsnorm(cx, pfx, x_k, h_k, xkeys, hkeys, ncols, sq, ones, pst, pst_bank, tmp, rstd, g_sb, gkey):
    S = cx.S
    for k in range(KD):
        S.emit("act", lambda e, k=k: e.activation(out=sq[:, k, 0:ncols], in_=x_k(k), func=AF.Square),
               reads=[xkeys[k]], writes=[pfx + f"sq{k}"])
    for k in range(KD):
        S.emit("pe", lambda e, k=k: e.matmul(pst, ones[:, :], sq[:, k, 0:ncols], start=(k == 0), stop=(k == KD - 1)),
               reads=[pfx + f"sq{k}", pfx + "ones"], banks=[pst_bank], ms=(k == KD - 1))
    S.emit("dve", lambda e: e.tensor_scalar(out=tmp[:, 0:ncols], in0=pst, scalar1=1.0 / D, scalar2=EPS,
                                            op0=ALU.mult, op1=ALU.add), banks=[pst_bank], writes=[pfx + "tmp"])
    S.emit("act", lambda e: e.activation(out=tmp[:, 0:ncols], in_=tmp[:, 0:ncols], func=AF.Ln),
           reads=[pfx + "tmp"], writes=[pfx + "tmp"])
    S.emit("act", lambda e: e.activation(out=rstd[:, 0:ncols], in_=tmp[:, 0:ncols], func=AF.Exp, scale=-0.5),
           reads=[pfx + "tmp"], writes=[pfx + "rstd"])
    for k in range(KD):
        S.emit("dve", lambda e, k=k: e.scalar_tensor_tensor(out=h_k(k), in0=x_k(k), scalar=g_sb[:, k:k + 1], in1=rstd[:, 0:ncols],
                                                            op0=ALU.mult, op1=ALU.mult),
               reads=[xkeys[k], pfx + "rstd", gkey], writes=[hkeys[k]])


def ffn_phase(cx, pfx, xT, xh, xoT, g_d, wup_d, wdn_d, dw_d, db_d, nst=NCHUNK // 2):
    S, nc = cx.S, cx.nc
    HH = 2
    W = HH + CH
    SC = 2
    x_sb = cx.sb(pfx + "x", [128, KD, SC, W], F32)
    hT = cx.sb(pfx + "hT", [128, KD, SC, W], BF16)
    sq = cx.sb(pfx + "sq", [128, KD, CH], BF16)
    rstd = cx.sb(pfx + "rstd", [128, CH], F32)
    tmp = cx.sb(pfx + "tmp", [128, CH], F32)
    ones = cx.sb(pfx + "ones", [128, 128], BF16)
    g_sb = cx.sb(pfx + "g", [128, KD], F32)
    dw_sb = cx.sb(pfx + "dw", [128, 2 * NJ, 3], F32)
    db_sb = cx.sb(pfx + "db", [128, 2 * NJ], F32)
    wup = [cx.sb(pfx + f"wup{i}", [128, KD, 256], BF16) for i in range(2)]
    wdn = [cx.sb(pfx + f"wdn{i}", [128, NJ, 128], BF16) for i in range(2)]
    gT = cx.sb(pfx + "gT", [128, NJ, SC, CH], BF16)
    gacc = [cx.sb(pfx + f"gacc{i}", [128, CH], F32) for i in range(2)]
    vacc = [cx.sb(pfx + f"vacc{i}", [128, CH], F32) for i in range(2)]
    sg = [cx.sb(pfx + f"sg{i}", [128, CH], F32) for i in range(2)]
    phs = [cx.sb(pfx + f"phs{i}", [128, 4], F32) for i in range(2)]
    xo = [cx.sb(pfx + f"xo{i}", [128, CH], F32) for i in range(2)]
    pg = [cx.ps(pfx + f"pg{i}") for i in range(2)]
    pv = [cx.ps(pfx + f"pv{i}") for i in range(2)]
    pmisc = cx.ps(pfx + "pmisc")
    pstat = cx.ps(pfx + "pstat")
    pd = [cx.ps(pfx + f"pd{i}") for i in range(2)]
    ph = [pmisc[:, 0:4], pstat[:, 0:4]]
    phk = [pfx + "pmisc", pfx + "pstat"]
    pstat_h = pmisc[:, 32:32 + HH]

    xT_r = xT.rearrange("(k p) t -> p k t", p=128)
    xh_r = xh.rearrange("(k p) (c h) -> p k c h", p=128, h=HALO)
    xo_r = xoT.rearrange("(k p) t -> p k t", p=128)

    S.emit("dve", lambda e: e.memset(ones[:], 1.0), writes=[pfx + "ones"])
    S.dma_group("sp", "cst", [(lambda e: e.dma_start(out=g_sb[:], in_=g_d), [], [pfx + "n.g"]),
                              (lambda e: e.dma_start(out=dw_sb[:], in_=dw_d), [], [pfx + "dw"]),
                              (lambda e: e.dma_start(out=db_sb[:], in_=db_d), [], [pfx + "db"])])
    out_toks = []
    wu_i = 0
    wd_i = 0
    it = 0
    for sti in range(nst):
        for c in range(SC):
            gc = sti * SC + c
            S.dma_group("sp", f"lx{c}", [(lambda e, c=c, k=k, gc=gc: e.dma_start(out=x_sb[:, k, c, HH:W], in_=xT_r[:, k, gc * CH:(gc + 1) * CH]),
                                          [], [pfx + f"x{c}.{k}m"]) for k in range(KD)])
            S.dma("sp", pfx + f"lxh{c}",
                  lambda e, c=c, gc=gc: e.dma_start(out=x_sb[:, :, c, 0:HH], in_=xh_r[:, :, gc, HALO - HH:HALO]),
                  writes=[pfx + f"x{c}.h"])
        for c in range(SC):
            xk = [pfx + f"x{c}.{k}m" for k in range(KD)]
            for k in range(KD):
                S.emit("act", lambda e, k=k, c=c: e.activation(out=sq[:, k, :], in_=x_sb[:, k, c, HH:W], func=AF.Square),
                       reads=[xk[k]], writes=[pfx + f"sq{k}"])
            for k in range(KD):
                S.emit("pe", lambda e, k=k: e.matmul(pstat[:, :], ones[:, :], sq[:, k, :], start=(k == 0), stop=(k == KD - 1)),
                       reads=[pfx + f"sq{k}", pfx + "ones"], banks=[pfx + "pstat"], ms=(k == KD - 1))
            S.emit("dve", lambda e: e.tensor_scalar(out=tmp[:, :], in0=pstat[:, :], scalar1=1.0 / D, scalar2=EPS,
                                                    op0=ALU.mult, op1=ALU.add),
                   banks=[pfx + "pstat"], writes=[pfx + "tmp"])
            S.emit("act", lambda e: e.activation(out=tmp[:, :], in_=tmp[:, :], func=AF.Ln),
                   reads=[pfx + "tmp"], writes=[pfx + "tmp"])
            S.emit("act", lambda e: e.activation(out=rstd[:, :], in_=tmp[:, :], func=AF.Exp, scale=-0.5),
                   reads=[pfx + "tmp"], writes=[pfx + "rstd"])
            for k in range(KD):
                S.emit("dve", lambda e, k=k, c=c: e.scalar_tensor_tensor(
                    out=hT[:, k, c, HH:W], in0=x_sb[:, k, c, HH:W], scalar=g_sb[:, k:k + 1], in1=rstd[:, :],
                    op0=ALU.mult, op1=ALU.mult),
                    reads=[xk[k], pfx + "rstd", pfx + "n.g"], writes=[pfx + f"h{c}.{k}m"])
            S.emit("act", lambda e, c=c: e.activation(out=sq[:, :, 0:HH], in_=x_sb[:, :, c, 0:HH], func=AF.Square),
                   reads=[pfx + f"x{c}.h"], writes=[pfx + f"sq{k}" for k in range(KD)])
            for k in range(KD):
                S.emit("pe", lambda e, k=k: e.matmul(pstat_h, ones[:, :], sq[:, k, 0:HH], start=(k == 0), stop=(k == KD - 1)),
                       reads=[pfx + f"sq{k}", pfx + "ones"], banks=[pfx + "pmisc"], ms=(k == KD - 1))
            S.emit("dve", lambda e: e.tensor_scalar(out=tmp[:, 0:HH], in0=pstat_h, scalar1=1.0 / D, scalar2=EPS,
                                                    op0=ALU.mult, op1=ALU.add),
                   banks=[pfx + "pmisc"], writes=[pfx + "tmp"])
            S.emit("act", lambda e: e.activation(out=tmp[:, 0:HH], in_=tmp[:, 0:HH], func=AF.Ln),
                   reads=[pfx + "tmp"], writes=[pfx + "tmp"])
            S.emit("act", lambda e: e.activation(out=rstd[:, 0:HH], in_=tmp[:, 0:HH], func=AF.Exp, scale=-0.5),
                   reads=[pfx + "tmp"], writes=[pfx + "rstd"])
            for k in range(KD):
                S.emit("dve", lambda e, k=k, c=c: e.scalar_tensor_tensor(
                    out=hT[:, k, c, 0:HH], in0=x_sb[:, k, c, 0:HH], scalar=g_sb[:, k:k + 1], in1=rstd[:, 0:HH],
                    op0=ALU.mult, op1=ALU.mult),
                    reads=[pfx + f"x{c}.h", pfx + "rstd", pfx + "n.g"], writes=[pfx + f"h{c}.{k}h"])
        def load_wup(j, slot):
            S.dma("pool", pfx + f"wu{slot}", lambda e, j=j, slot=slot: e.dma_start(out=wup[slot][:], in_=wup_d[j]),
                  writes=[pfx + f"wup{slot}"])

        def load_wdn(n, slot):
            S.dma("pool", pfx + f"wd{slot}", lambda e, n=n, slot=slot: e.dma_start(out=wdn[slot][:], in_=wdn_d[n]),
                  writes=[pfx + f"wdn{slot}"])

        load_wup(0, wu_i % 2)
        for j in range(NJ):
            slot = wu_i % 2
            if j + 1 < NJ:
                load_wup(j + 1, (wu_i + 1) % 2)
            else:
                load_wdn(0, wd_i % 2)
            wu_i += 1
            for c in range(SC):
                b = it % 2
                it += 1
                hk = [pfx + f"h{c}.{k}m" for k in range(KD)]
                hh = [pfx + f"h{c}.{k}h" for k in range(KD)]
                for half, (pp, acc, jc) in enumerate(((pg[b], gacc[b], j), (pv[b], vacc[b], NJ + j))):
                    co = half * 128
                    pk = pfx + f"p{half}{b}"
                    ak = pfx + f"acc{half}{b}"
                    for k in range(KD):
                        S.emit("pe", lambda e, k=k, c=c, pp=pp, co=co, slot=slot: e.matmul(
                            pp[:, :], wup[slot][:, k, co:co + 128], hT[:, k, c, HH:W], start=(k == 0), stop=(k == KD - 1)),
                            reads=[hk[k], pfx + f"wup{slot}"], banks=[pk], ms=(k == KD - 1))
                    for k in range(KD):
                        S.emit("pe", lambda e, k=k, c=c, co=co, slot=slot, b=b, half=half: e.matmul(
                            ph[b][:, 2 * half:2 * half + 2], wup[slot][:, k, co:co + 128], hT[:, k, c, 0:HH],
                            start=(k == 0), stop=(k == KD - 1)),
                            reads=[hh[k], pfx + f"wup{slot}"], banks=[phk[b]], ms=(k == KD - 1))
                    S.emit("act", lambda e, pp=pp, acc=acc, jc=jc: e.activation(
                        out=acc[:, :], in_=pp[:, :], func=AF.Identity, scale=dw_sb[:, jc, 2:3], bias=db_sb[:, jc:jc + 1]),
                        reads=[pfx + "dw", pfx + "db"], writes=[ak], banks=[pk])
                    S.emit("dve", lambda e, pp=pp, acc=acc, jc=jc: e.scalar_tensor_tensor(
                        out=acc[:, 1:CH], in0=pp[:, 0:CH - 1], scalar=dw_sb[:, jc, 1:2], in1=acc[:, 1:CH],
                        op0=ALU.mult, op1=ALU.add), reads=[ak, pfx + "dw"], writes=[ak], banks=[pk])
                    S.emit("dve", lambda e, pp=pp, acc=acc, jc=jc: e.scalar_tensor_tensor(
                        out=acc[:, 2:CH], in0=pp[:, 0:CH - 2], scalar=dw_sb[:, jc, 0:1], in1=acc[:, 2:CH],
                        op0=ALU.mult, op1=ALU.add), reads=[ak, pfx + "dw"], writes=[ak], banks=[pk])
                for half, (acc, jc) in enumerate(((gacc[b], j), (vacc[b], NJ + j))):
                    ak = pfx + f"acc{half}{b}"
                    S.emit("dve", lambda e, acc=acc, jc=jc, b=b, half=half: e.scalar_tensor_tensor(
                        out=acc[:, 0:2], in0=ph[b][:, 2 * half:2 * half + 2], scalar=dw_sb[:, jc, 0:1], in1=acc[:, 0:2],
                        op0=ALU.mult, op1=ALU.add), reads=[ak, pfx + "dw"], writes=[ak], banks=[phk[b]])
                    S.emit("dve", lambda e, acc=acc, jc=jc, b=b, half=half: e.scalar_tensor_tensor(
                        out=acc[:, 0:1], in0=ph[b][:, 2 * half + 1:2 * half + 2], scalar=dw_sb[:, jc, 1:2], in1=acc[:, 0:1],
                        op0=ALU.mult, op1=ALU.add), reads=[ak, pfx + "dw"], writes=[ak], banks=[phk[b]])
                S.emit("act", lambda e, b=b: e.activation(out=sg[b][:, :], in_=gacc[b][:, :], func=AF.Silu),
                       reads=[pfx + f"acc0{b}"], writes=[pfx + f"sg{b}"])
                S.emit("dve", lambda e, b=b, j=j, c=c: e.tensor_tensor(out=gT[:, j, c, :], in0=sg[b][:, :], in1=vacc[b][:, :],
                                                                      op=ALU.mult),
                       reads=[pfx + f"sg{b}", pfx + f"acc1{b}"], writes=[pfx + f"gT{j}.{c}"])
        for n in range(KD):
            slot = wd_i % 2
            if n + 1 < KD:
                load_wdn(n + 1, (wd_i + 1) % 2)
            wd_i += 1
            for c in range(SC):
                gc = sti * SC + c
                b = it % 2
                it += 1
                for jj in range(NJ):
                    S.emit("pe", lambda e, jj=jj, c=c, b=b, slot=slot: e.matmul(
                        pd[b][:, :], wdn[slot][:, jj, :], gT[:, jj, c, :], start=(jj == 0), stop=(jj == NJ - 1)),
                        reads=[pfx + f"gT{jj}.{c}", pfx + f"wdn{slot}"], banks=[pfx + f"pd{b}"], ms=(jj == NJ - 1))
                S.emit("dve", lambda e, b=b, n=n, c=c: e.tensor_tensor(out=xo[b][:, :], in0=pd[b][:, :], in1=x_sb[:, n, c, HH:W],
                                                                      op=ALU.add),
                       reads=[pfx + f"x{c}.{n}m"], writes=[pfx + f"xo{b}"], banks=[pfx + f"pd{b}"])
                t = S.dma("sp", pfx + f"so{b}", lambda e, b=b, n=n, gc=gc: e.dma_start(
                    out=xo_r[:, n, gc * CH:(gc + 1) * CH], in_=xo[b][:, :]), reads=[pfx + f"xo{b}"], writes=[pfx + f"xoT.{n}.{gc}"])
                out_toks.append(t)
    return out_toks


DA = 512
MA = DA // 128
KA = 31


def conv_phase(cx, pfx, xT, xh, xoT, g_d, win_d, wout_d, adw_d, avec_d, bdw_d, nch=NCHUNK):
    S, nc = cx.S, cx.nc
    H = HALO
    W = H + CH
    x_sb = cx.sb(pfx + "x", [128, KD, W], F32)
    hT = cx.sb(pfx + "hT", [128, KD, W], BF16)
    sq = cx.sb(pfx + "sq", [128, KD, CH], BF16)
    rstd = cx.sb(pfx + "rstd", [128, CH], F32)
    tmp = cx.sb(pfx + "tmp", [128, CH], F32)
    ones = cx.sb(pfx + "ones", [128, 128], BF16)
    onesm = cx.sb(pfx + "onesm", [128, 128], BF16)
    g_sb = cx.sb(pfx + "g", [128, KD], F32)
    adw = cx.sb(pfx + "adw", [128, MA, KA], F32)
    avec = cx.sb(pfx + "avec", [128, 3, MA], F32)
    bdw = cx.sb(pfx + "bdw", [128, MA, 3], F32)
    win = cx.sb(pfx + "win", [128, 20, KD, 128], BF16)
    wout = cx.sb(pfx + "wout", [128, KD, KD, 128], BF16)
    glu = cx.sb(pfx + "glu", [128, MA, W], F32)
    ca = cx.sb(pfx + "ca", [128, MA, CH], F32)
    cab = cx.sb(pfx + "cab", [128, MA, CH], BF16)
    abT = cx.sb(pfx + "abT", [128, 2 * MA, CH], BF16)
    sgm = cx.sb(pfx + "sgm", [128, W], F32)
    chb = cx.sb(pfx + "chb", [128, W], F32)
    accb = cx.sb(pfx + "accb", [128, CH], F32)
    xo = [cx.sb(pfx + f"xo{i}", [128, CH], F32) for i in range(2)]
    pA, pB, pC = cx.ps(pfx + "pA"), cx.ps(pfx + "pB"), cx.ps(pfx + "pC")
    pmisc, pstat, pvar = cx.ps(pfx + "pmisc"), cx.ps(pfx + "pstat"), cx.ps(pfx + "pvar")
    po = [cx.ps(pfx + f"po{i}") for i in range(2)]
    BK = lambda n: pfx + "B." + n

    xT_r = xT.rearrange("(k p) t -> p k t", p=128)
    xh_r = xh.rearrange("(k p) (c h) -> p k c h", p=128, h=HALO)
    xo_r = xoT.rearrange("(k p) t -> p k t", p=128)

    S.emit("dve", lambda e: e.memset(ones[:], 1.0), writes=[pfx + "ones"])
    S.emit("dve", lambda e: e.memset(onesm[:], 1.0 / DA), writes=[pfx + "onesm"])
    S.dma_group("sp", "cst", [(lambda e: e.dma_start(out=g_sb[:], in_=g_d), [], [pfx + "g"]),
                              (lambda e: e.dma_start(out=adw[:], in_=adw_d), [], [pfx + "adw"]),
                              (lambda e: e.dma_start(out=avec[:], in_=avec_d), [], [pfx + "avec"]),
                              (lambda e: e.dma_start(out=bdw[:], in_=bdw_d), [], [pfx + "bdw"])])
    S.dma_group("pool", "wA", [(lambda e, m=m: e.dma_start(out=win[:, m], in_=win_d[m]), [], [pfx + f"win{m}"]) for m in range(20)])
    S.dma_group("pool", "wB", [(lambda e, n=n: e.dma_start(out=wout[:, n], in_=wout_d[n]), [], [pfx + f"wout{n}"]) for n in range(KD)])

    out_toks = []
    it = 0
    for c in range(nch):
        xk = [pfx + f"x{k}" for k in range(KD)]
        hk = [pfx + f"h{k}" for k in range(KD)]
        S.dma_group("sp", "lx", [(lambda e, k=k, c=c: e.dma_start(out=x_sb[:, k, H:W], in_=xT_r[:, k, c * CH:(c + 1) * CH]), [], [xk[k]])
                                 for k in range(KD)])
        S.dma("sp", pfx + "lxh", lambda e, c=c: e.dma_start(out=x_sb[:, :, 0:H], in_=xh_r[:, :, c, :]), writes=[pfx + "xh"])
        emit_rmsnorm(cx, pfx, lambda k: x_sb[:, k, H:W], lambda k: hT[:, k, H:W], xk, hk, CH, sq, ones,
                     pstat[:, :], BK("pstat"), tmp, rstd, g_sb, pfx + "g")
        emit_rmsnorm(cx, pfx, lambda k: x_sb[:, k, 0:H], lambda k: hT[:, k, 0:H], [pfx + "xh"] * KD,
                     [pfx + f"hh{k}" for k in range(KD)], H, sq, ones, pmisc[:, 256:256 + H], BK("pmisc"), tmp, rstd, g_sb, pfx + "g")
        hh = [pfx + f"hh{k}" for k in range(KD)]

        def proj(m, pmain, bank, hcol=None):
            for k in range(KD):
                S.emit("pe", lambda e, k=k, m=m: e.matmul(pmain[:, :], win[:, m, k, :], hT[:, k, H:W], start=(k == 0), stop=(k == KD - 1)),
                       reads=[hk[k], pfx + f"win{m}"], banks=[bank], ms=(k == KD - 1))
            if hcol is not None:
                for k in range(KD):
                    S.emit("pe", lambda e, k=k, m=m: e.matmul(pmisc[:, hcol:hcol + H], win[:, m, k, :], hT[:, k, 0:H],
                                                              start=(k == 0), stop=(k == KD - 1)),
                           reads=[hh[k], pfx + f"win{m}"], banks=[BK("pmisc")], ms=(k == KD - 1))

        for m in range(MA):
            proj(m, pA, BK("pA"), 0)
            proj(MA + m, pB, BK("pB"), H)
            S.emit("act", lambda e: e.activation(out=sgm[:, H:W], in_=pB[:, :], func=AF.Sigmoid), banks=[BK("pB")], writes=[pfx + "sgm"])
            S.emit("act", lambda e: e.activation(out=sgm[:, 0:H], in_=pmisc[:, H:2 * H], func=AF.Sigmoid), banks=[BK("pmisc")], writes=[pfx + "sgm"])
            S.emit("dve", lambda e, m=m: e.tensor_tensor(out=glu[:, m, H:W], in0=pA[:, :], in1=sgm[:, H:W], op=ALU.mult),
                   reads=[pfx + "sgm"], banks=[BK("pA")], writes=[pfx + f"glu{m}"])
            S.emit("dve", lambda e, m=m: e.tensor_tensor(out=glu[:, m, 0:H], in0=pmisc[:, 0:H], in1=sgm[:, 0:H], op=ALU.mult),
                   reads=[pfx + "sgm"], banks=[BK("pmisc")], writes=[pfx + f"glu{m}"])
            S.emit("act", lambda e, m=m: e.activation(out=ca[:, m, :], in_=glu[:, m, H:W], func=AF.Identity,
                                                      scale=adw[:, m, KA - 1:KA], bias=avec[:, 0, m:m + 1]),
                   reads=[pfx + f"glu{m}", pfx + "adw", pfx + "avec"], writes=[pfx + f"ca{m}"])
            for k in range(KA - 1):
                S.emit("dve", lambda e, m=m, k=k: e.scalar_tensor_tensor(
                    out=ca[:, m, :], in0=glu[:, m, 2 + k:2 + k + CH], scalar=adw[:, m, k:k + 1], in1=ca[:, m, :],
                    op0=ALU.mult, op1=ALU.add), reads=[pfx + f"glu{m}", pfx + f"ca{m}", pfx + "adw"], writes=[pfx + f"ca{m}"])
            S.emit("act", lambda e, m=m: e.activation(out=cab[:, m, :], in_=ca[:, m, :], func=AF.Identity),
                   reads=[pfx + f"ca{m}"], writes=[pfx + f"cab{m}"])
        for m in range(MA):
            S.emit("pe", lambda e, m=m: e.matmul(pstat[:, :], onesm[:, :], cab[:, m, :], start=(m == 0), stop=(m == MA - 1)),
                   reads=[pfx + f"cab{m}", pfx + "onesm"], banks=[BK("pstat")], ms=(m == MA - 1))
        for m in range(MA):
            S.emit("dve", lambda e, m=m: e.tensor_tensor(out=ca[:, m, :], in0=ca[:, m, :], in1=pstat[:, :], op=ALU.subtract),
                   reads=[pfx + f"ca{m}"], banks=[BK("pstat")], writes=[pfx + f"ca{m}"])
            S.emit("act", lambda e, m=m: e.activation(out=cab[:, m, :], in_=ca[:, m, :], func=AF.Square),
                   reads=[pfx + f"ca{m}"], writes=[pfx + f"cab{m}"])
        for m in range(MA):
            S.emit("pe", lambda e, m=m: e.matmul(pvar[:, :], onesm[:, :], cab[:, m, :], start=(m == 0), stop=(m == MA - 1)),
                   reads=[pfx + f"cab{m}", pfx + "onesm"], banks=[BK("pvar")], ms=(m == MA - 1))
        S.emit("dve", lambda e: e.tensor_scalar(out=tmp[:, :], in0=pvar[:, :], scalar1=EPS, scalar2=None, op0=ALU.add),
               banks=[BK("pvar")], writes=[pfx + "tmp"])
        S.emit("act", lambda e: e.activation(out=tmp[:, :], in_=tmp[:, :], func=AF.Ln), reads=[pfx + "tmp"], writes=[pfx + "tmp"])
        S.emit("act", lambda e: e.activation(out=rstd[:, :], in_=tmp[:, :], func=AF.Exp, scale=-0.5), reads=[pfx + "tmp"], writes=[pfx + "rstd"])
        for m in range(MA):
            S.emit("dve", lambda e, m=m: e.scalar_tensor_tensor(out=ca[:, m, :], in0=ca[:, m, :], scalar=avec[:, 1, m:m + 1], in1=rstd[:, :],
                                                                op0=ALU.mult, op1=ALU.mult),
                   reads=[pfx + f"ca{m}", pfx + "rstd", pfx + "avec"], writes=[pfx + f"ca{m}"])
            S.emit("act", lambda e, m=m: e.activation(out=abT[:, m, :], in_=ca[:, m, :], func=AF.Silu, bias=avec[:, 2, m:m + 1]),
                   reads=[pfx + f"ca{m}", pfx + "avec"], writes=[pfx + f"ab{m}"])
        for m in range(MA):
            proj(3 * MA + m, pA, BK("pA"), 2 * H)
            proj(4 * MA + m, pB, BK("pB"), 3 * H)
            proj(2 * MA + m, pC, BK("pC"), None)
            S.emit("act", lambda e: e.activation(out=sgm[:, H:W], in_=pA[:, :], func=AF.Identity), banks=[BK("pA")], writes=[pfx + "sgm"])
            S.emit("act", lambda e: e.activation(out=sgm[:, 0:H], in_=pmisc[:, 2 * H:3 * H], func=AF.Identity), banks=[BK("pmisc")], writes=[pfx + "sgm"])
            S.emit("dve", lambda e: e.tensor_tensor(out=chb[:, H:W], in0=pB[:, :], in1=sgm[:, H:W], op=ALU.mult),
                   reads=[pfx + "sgm"], banks=[BK("pB")], writes=[pfx + "chb"])
            S.emit("dve", lambda e: e.tensor_tensor(out=chb[:, 0:H], in0=pmisc[:, 3 * H:4 * H], in1=sgm[:, 0:H], op=ALU.mult),
                   reads=[pfx + "sgm"], banks=[BK("pmisc")], writes=[pfx + "chb"])
            S.emit("act", lambda e, m=m: e.activation(out=accb[:, :], in_=chb[:, H:W], func=AF.Identity, scale=bdw[:, m, 2:3]),
                   reads=[pfx + "chb", pfx + "bdw"], writes=[pfx + "accb"])
            for k in range(2):
                S.emit("dve", lambda e, m=m, k=k: e.scalar_tensor_tensor(
                    out=accb[:, :], in0=chb[:, H - 2 + k:H - 2 + k + CH], scalar=bdw[:, m, k:k + 1], in1=accb[:, :],
                    op0=ALU.mult, op1=ALU.add), reads=[pfx + "chb", pfx + "accb", pfx + "bdw"], writes=[pfx + "accb"])
            S.emit("dve", lambda e, m=m: e.tensor_tensor(out=abT[:, MA + m, :], in0=pC[:, :], in1=accb[:, :], op=ALU.mult),
                   reads=[pfx + "accb"], banks=[BK("pC")], writes=[pfx + f"ab{MA + m}"])
        for n in range(KD):
            b = it % 2
            it += 1
            for k in range(2 * MA):
                S.emit("pe", lambda e, k=k, n=n, b=b: e.matmul(po[b][:, :], wout[:, n, k, :], abT[:, k, :], start=(k == 0), stop=(k == 2 * MA - 1)),
                       reads=[pfx + f"ab{k}", pfx + f"wout{n}"], banks=[BK(f"po{b}")], ms=(k == 2 * MA - 1))
            S.emit("dve", lambda e, b=b, n=n: e.tensor_tensor(out=xo[b][:, :], in0=po[b][:, :], in1=x_sb[:, n, H:W], op=ALU.add),
                   reads=[xk[n]], writes=[pfx + f"xo{b}"], banks=[BK(f"po{b}")])
            t = S.dma("sp", pfx + f"so{b}", lambda e, b=b, n=n, c=c: e.dma_start(out=xo_r[:, n, c * CH:(c + 1) * CH], in_=xo[b][:, :]),
                      reads=[pfx + f"xo{b}"], writes=[pfx + f"xoT.{n}.{c}"])
            out_toks.append(t)
    return out_toks


NH = 16
HP = NH // 2
NKB = TOK // 128


def qkv_phase(cx, pfx, xT, g_d, wq_d, wk_d, wv_d, qkg_d, qT_o, kT_o, V_o, nch=NCHUNK):
    S, nc = cx.S, cx.nc
    x_sb = cx.sb(pfx + "x", [128, KD, CH], F32)
    hT = cx.sb(pfx + "hT", [128, KD, CH], BF16)
    sq = cx.sb(pfx + "sq", [128, KD, CH], BF16)
    rstd = cx.sb(pfx + "rstd", [128, CH], F32)
    tmp = cx.sb(pfx + "tmp", [128, CH], F32)
    ones = cx.sb(pfx + "ones", [128, 128], BF16)
    bd = cx.sb(pfx + "bd", [128, 128], BF16)
    g_sb = cx.sb(pfx + "g", [128, KD], F32)
    qkg = cx.sb(pfx + "qkg", [128, 2], F32)
    epsc = cx.sb(pfx + "epsc", [128, 1], F32)
    wq = cx.sb(pfx + "wq", [128, HP, KD, 128], BF16)
    wk = cx.sb(pfx + "wk", [128, HP, KD, 128], BF16)
    wv = cx.sb(pfx + "wv", [128, 2, KD, 512], BF16)
    qf = [cx.sb(pfx + f"qf{i}", [128, CH], F32) for i in range(2)]
    sqq = [cx.sb(pfx + f"sqq{i}", [128, CH], BF16) for i in range(2)]
    rs = [cx.sb(pfx + f"rs{i}", [128, CH], F32) for i in range(2)]
    qn = [cx.sb(pfx + f"qn{i}", [128, CH], BF16) for i in range(2)]
    vt = [cx.sb(pfx + f"vt{i}", [128, 512], BF16) for i in range(2)]
    pq = [cx.ps(pfx + f"pq{i}") for i in range(2)]
    pms = [cx.ps(pfx + f"pms{i}") for i in range(2)]
    pvv = [cx.ps(pfx + f"pvv{i}") for i in range(2)]
    pstat = cx.ps(pfx + "pstat")
    BK = lambda n: pfx + "B." + n
    xT_r = xT.rearrange("(k p) t -> p k t", p=128)
    qT_r = qT_o.rearrange("(k p) t -> p k t", p=128)
    kT_r = kT_o.rearrange("(k p) t -> p k t", p=128)

    S.emit("dve", lambda e: e.memset(ones[:], 1.0), writes=[pfx + "ones"])
    S.emit("dve", lambda e: e.memset(epsc[:], EPS), writes=[pfx + "epsc"])
    S.emit("dve", lambda e: e.memset(bd[:], 0.0), writes=[pfx + "bd"])
    S.emit("dve", lambda e: e.memset(bd[0:64, 0:64], 1.0 / 64), writes=[pfx + "bd"])
    S.emit("dve", lambda e: e.memset(bd[64:128, 64:128], 1.0 / 64), writes=[pfx + "bd"])
    S.dma_group("sp", "cst", [(lambda e: e.dma_start(out=g_sb[:], in_=g_d), [], [pfx + "g"]),
                              (lambda e: e.dma_start(out=qkg[:], in_=qkg_d), [], [pfx + "qkg"])])
    S.emit("dve", lambda e: e.tensor_scalar(out=qkg[:, 0:1], in0=qkg[:, 0:1], scalar1=0.125, scalar2=None, op0=ALU.mult),
           reads=[pfx + "qkg"], writes=[pfx + "qkg"])
    S.dma_group("pool", "wA", [(lambda e, m=m: e.dma_start(out=wq[:, m], in_=wq_d[m]), [], [pfx + f"wq{m}"]) for m in range(HP)])
    S.dma_group("pool", "wB", [(lambda e, m=m: e.dma_start(out=wk[:, m], in_=wk_d[m]), [], [pfx + f"wk{m}"]) for m in range(HP)])
    S.dma_group("pool", "wC", [(lambda e, hf=hf: e.dma_start(out=wv[:, hf], in_=wv_d[hf]), [], [pfx + f"wv{hf}"]) for hf in range(2)])
    out_toks = []
    it = 0
    for c in range(nch):
        xk = [pfx + f"x{k}" for k in range(KD)]
        hk = [pfx + f"h{k}" for k in range(KD)]
        S.dma_group("sp", "lx", [(lambda e, k=k, c=c: e.dma_start(out=x_sb[:, k, :], in_=xT_r[:, k, c * CH:(c + 1) * CH]), [], [xk[k]])
                                 for k in range(KD)])
        emit_rmsnorm(cx, pfx, lambda k: x_sb[:, k, :], lambda k: hT[:, k, :], xk, hk, CH, sq, ones,
                     pstat[:, :], BK("pstat"), tmp, rstd, g_sb, pfx + "g")
        for which, (w_sb, wkey, gcol, o_r) in enumerate(((wq, "wq", 0, qT_r), (wk, "wk", 1, kT_r))):
            for m in range(HP):
                b = it % 2
                it += 1
                for k in range(KD):
                    S.emit("pe", lambda e, k=k, m=m, b=b, w_sb=w_sb: e.matmul(pq[b][:, :], w_sb[:, m, k, :], hT[:, k, :],
                                                                             start=(k == 0), stop=(k == KD - 1)),
                           reads=[hk[k], pfx + f"{wkey}{m}"], banks=[BK(f"pq{b}")], ms=(k == KD - 1))
                S.emit("act", lambda e, b=b: e.activation(out=sqq[b][:, :], in_=pq[b][:, :], func=AF.Square),
                       banks=[BK(f"pq{b}")], writes=[pfx + f"sqq{b}"])
                S.emit("dve", lambda e, b=b: e.tensor_copy(out=qf[b][:, :], in_=pq[b][:, :]),
                       banks=[BK(f"pq{b}")], writes=[pfx + f"qf{b}"])
                S.emit("pe", lambda e, b=b: e.matmul(pms[b][:, :], bd[:, :], sqq[b][:, :], start=True, stop=True),
                       reads=[pfx + f"sqq{b}", pfx + "bd"], banks=[BK(f"pms{b}")], ms=True)
                S.emit("act", lambda e, b=b: e.activation(out=rs[b][:, :], in_=pms[b][:, :], func=AF.Ln, bias=epsc[:, 0:1]),
                       reads=[pfx + "epsc"], banks=[BK(f"pms{b}")], writes=[pfx + f"rs{b}"])
                S.emit("act", lambda e, b=b: e.activation(out=rs[b][:, :], in_=rs[b][:, :], func=AF.Exp, scale=-0.5),
                       reads=[pfx + f"rs{b}"], writes=[pfx + f"rs{b}"])
                S.emit("dve", lambda e, b=b, gcol=gcol: e.scalar_tensor_tensor(out=qn[b][:, :], in0=qf[b][:, :], scalar=qkg[:, gcol:gcol + 1],
                                                                               in1=rs[b][:, :], op0=ALU.mult, op1=ALU.mult),
                       reads=[pfx + f"qf{b}", pfx + f"rs{b}", pfx + "qkg"], writes=[pfx + f"qn{b}"])
                t = S.dma("sp", pfx + f"sq{b}", lambda e, b=b, m=m, c=c, o_r=o_r: e.dma_start(out=o_r[:, m, c * CH:(c + 1) * CH], in_=qn[b][:, :]),
                          reads=[pfx + f"qn{b}"], writes=[pfx + f"o{which}.{m}.{c}"])
                out_toks.append(t)
        for tt in range(CH // 128):
            kb = c * (CH // 128) + tt
            for hf in range(2):
                b = it % 2
                it += 1
                for k in range(KD):
                    S.emit("pe", lambda e, k=k, tt=tt, hf=hf, b=b: e.matmul(pvv[b][:, :], hT[:, k, tt * 128:(tt + 1) * 128], wv[:, hf, k, :],
                                                                          start=(k == 0), stop=(k == KD - 1)),
                           reads=[hk[k], pfx + f"wv{hf}"], banks=[BK(f"pvv{b}")], ms=(k == KD - 1))
                S.emit("act", lambda e, b=b: e.activation(out=vt[b][:, :], in_=pvv[b][:, :], func=AF.Identity),
                       banks=[BK(f"pvv{b}")], writes=[pfx + f"vt{b}"])
                t = S.dma("sp", pfx + f"sv{b}", lambda e, b=b, hf=hf, kb=kb: e.dma_start(
                    out=V_o[hf * 4:(hf + 1) * 4, :, kb, :].rearrange("h p f -> p h f"),
                    in_=vt[b][:, :].rearrange("p (h f) -> p h f", f=128)),
                    reads=[pfx + f"vt{b}"], writes=[pfx + f"oV.{hf}.{kb}"])
                out_toks.append(t)
    return out_toks


def attn_phase(cx, pfx, xT, xoT, qT_i, kT_g, V_g, mask_d, tri_d, wo_d, nhp=HP, nq=NCHUNK):
    S, nc = cx.S, cx.nc
    kT_sb = cx.sb(pfx + "kT", [128, 2, TOK], BF16)
    V_sb = cx.sb(pfx + "V", [128, 2, NKB, 128], BF16)
    q_sb = cx.sb(pfx + "q", [128, TOK], BF16)
    oT = cx.sb(pfx + "oT", [128, HP, TOK], BF16)
    mask = cx.sb(pfx + "mask", [128, 2, 4, 1024], BF16)
    ntri = cx.sb(pfx + "ntri", [128, 128], BF16)
    ident = cx.sb(pfx + "ident", [128, 128], BF16)
    nones = cx.sb(pfx + "nones", [128, 128], BF16)
    one1 = cx.sb(pfx + "one1", [128, 1], F32)
    E = [cx.sb(pfx + f"E{i}", [128, 1024], F32) for i in range(3)]
    L = [cx.sb(pfx + f"L{i}", [128, 1024], BF16) for i in range(3)]
    Wt = [cx.sb(pfx + f"W{i}", [128, 1024], BF16) for i in range(2)]
    Ls = cx.sb(pfx + "Ls", [128, 1024], F32)
    Lsb = [cx.sb(pfx + f"Lsb{i}", [128, 1024], BF16) for i in range(2)]
    wo = cx.sb(pfx + "wo", [128, KD, KD, 128], BF16)
    xr = [cx.sb(pfx + f"xr{i}", [128, CH], F32) for i in range(2)]
    xo = [cx.sb(pfx + f"xo{i}", [128, CH], F32) for i in range(2)]
    pz = [cx.ps(pfx + f"pz{i}", (128, 1024)) for i in range(3)]
    po = cx.ps(pfx + "po")
    pf = cx.ps(pfx + "pf")
    BK = lambda n: pfx + "B." + n
    xT_r = xT.rearrange("(k p) t -> p k t", p=128)
    xo_r = xoT.rearrange("(k p) t -> p k t", p=128)
    qT_r = qT_i.rearrange("(k p) t -> p k t", p=128)
    kT_r = kT_g.rearrange("(h r p) t -> p h r t", r=2, p=128)
    V_r = V_g.rearrange("(h r p) (k f) -> h p r k f", r=2, p=128, f=128)

    S.dma("pool", pfx + "ct", lambda e: e.dma_start(out=ntri[:], in_=tri_d[:, 0:128]), writes=[pfx + "ntri"])
    S.dma("pool", pfx + "ct", lambda e: e.dma_start(out=ident[:], in_=tri_d[:, 128:256]), writes=[pfx + "ident"])
    S.emit("dve", lambda e: e.memset(nones[:], -1.0), writes=[pfx + "nones"])
    S.emit("dve", lambda e: e.memset(one1[:], 1.0), writes=[pfx + "one1"])
    S.dma("pool", pfx + "cm", lambda e: e.dma_start(out=mask[:], in_=mask_d), writes=[pfx + "mask"])
    S.dma_group("pool", "wA", [(lambda e, n=n: e.dma_start(out=wo[:, n], in_=wo_d[n]), [], [pfx + f"wo{n}"]) for n in range(KD)])

    step = 0
    for hp in range(nhp):
        S.dma("sp", pfx + "lk", lambda e, hp=hp: e.dma_start(out=kT_sb[:, :, :], in_=kT_r[:, hp, :, :]), writes=[pfx + "kT"])
        S.dma("sp", pfx + "lv", lambda e, hp=hp: e.dma_start(out=V_sb[:, :, :, :], in_=V_r[hp]),
              writes=[pfx + "V"])
        S.dma("sp", pfx + "lq", lambda e, hp=hp: e.dma_start(out=q_sb[:, :], in_=qT_r[:, hp, :]), writes=[pfx + "q"])
        for i in range(nq):
            blocks = []
            for j in range(2 * i + 1, -1, -1):
                for b4 in range(3, -1, -1):
                    mk = 1 if j == 2 * i + 1 else (0 if j == 2 * i else None)
                    blocks.append((j % 2, (j // 2) * 4 + b4, mk, b4))
            nb = len(blocks)

            def emit_z(s):
                r, kb, mk, b4 = blocks[s]
                z3 = (step + s) % 3
                for hd in range(2):
                    S.emit("pe", lambda e, hd=hd, r=r, kb=kb, z3=z3, i=i: e.matmul(
                        pz[z3][:, hd * 512:(hd + 1) * 512], kT_sb[hd * 64:(hd + 1) * 64, r, kb * 128:(kb + 1) * 128],
                        q_sb[hd * 64:(hd + 1) * 64, i * CH:(i + 1) * CH], start=True, stop=True, skip_group_check=True),
                        reads=[pfx + "kT", pfx + "q"], banks=[BK(f"pz{z3}")], ms=(hd == 1 and mk is None))
                if mk is not None:
                    for hd in range(2):
                        S.emit("pe", lambda e, hd=hd, z3=z3, mk=mk, b4=b4: e.matmul(
                            pz[z3][:, hd * 512:(hd + 1) * 512], ident[:, :], mask[:, mk, b4, hd * 512:(hd + 1) * 512],
                            start=False, stop=True, skip_group_check=True),
                            reads=[pfx + "ident", pfx + "mask"], banks=[BK(f"pz{z3}")], ms=(hd == 1))

            def emit_el(s):
                r, kb, mk, b4 = blocks[s]
                zb = (step + s) % 2
                z3 = (step + s) % 3
                S.emit("act", lambda e, zb=zb, z3=z3: e.activation(out=E[z3][:, :], in_=pz[z3][:, :], func=AF.Exp),
                       banks=[BK(f"pz{z3}")], writes=[pfx + f"E{z3}"])
                S.emit("act", lambda e, z3=z3: e.activation(out=L[z3][:, :], in_=E[z3][:, :], func=AF.Ln, bias=one1[:, 0:1]),
                       reads=[pfx + f"E{z3}", pfx + "one1"], writes=[pfx + f"L{z3}"])

            def emit_p2(s):
                zb = (step + s) % 2
                z3 = (step + s) % 3
                sls = [slice(hd * 512, (hd + 1) * 512) for hd in range(2)]
                for hd in range(2):
                    S.emit("pe", lambda e, sl=sls[hd], zb=zb, z3=z3, last=(s == 0): e.matmul(
                        pz[z3][:, sl], ntri[:, :], L[z3][:, sl], start=False, stop=last, skip_group_check=True),
                        reads=[pfx + f"L{z3}", pfx + "ntri"], banks=[BK(f"pz{z3}")], ms=(s == 0 and hd == 1))
                if s > 0:
                    lb = (step + s - 1) % 2
                    for hd in range(2):
                        S.emit("pe", lambda e, sl=sls[hd], lb=lb, z3=z3: e.matmul(
                            pz[z3][:, sl], nones[:, :], Lsb[lb][:, sl], start=False, stop=True, skip_group_check=True),
                            reads=[pfx + f"Lsb{lb}", pfx + "nones"], banks=[BK(f"pz{z3}")], ms=(hd == 1))

            def emit_w(s):
                r, kb, mk, b4 = blocks[s]
                zb = (step + s) % 2
                z3 = (step + s) % 3
                S.emit("act", lambda e, zb=zb, z3=z3: e.activation(out=Wt[zb][:, :], in_=pz[z3][:, :], func=AF.Exp),
                       banks=[BK(f"pz{z3}")], writes=[pfx + f"W{zb}"])
                if s + 1 < nb:
                    if s == 0:
                        S.emit("dve", lambda e, z3=z3: e.tensor_copy(out=Ls[:, :], in_=L[z3][:, :]), reads=[pfx + f"L{z3}"], writes=[pfx + "Ls"])
                    else:
                        S.emit("dve", lambda e, z3=z3: e.tensor_tensor(out=Ls[:, :], in0=Ls[:, :], in1=L[z3][:, :], op=ALU.add),
                               reads=[pfx + f"L{z3}", pfx + "Ls"], writes=[pfx + "Ls"])
                    S.emit("dve", lambda e, zb=zb: e.tensor_copy(out=Lsb[zb][:, :], in_=Ls[:, :]), reads=[pfx + "Ls"], writes=[pfx + f"Lsb{zb}"])

            def emit_pv(s):
                r, kb, mk, b4 = blocks[s]
                zb = (step + s) % 2
                for hd in range(2):
                    S.emit("pe", lambda e, hd=hd, r=r, kb=kb, zb=zb, st_=(s == 0), sp_=(s == nb - 1): e.matmul(
                        po[hd * 64:(hd + 1) * 64, :], V_sb[:, r, kb, hd * 64:(hd + 1) * 64], Wt[zb][:, hd * 512:(hd + 1) * 512],
                        start=st_, stop=sp_),
                        reads=[pfx + "V", pfx + f"W{zb}"], banks=[BK("po")], ms=(s == nb - 1 and hd == 1))

            emit_z(0)
            emit_el(0)
            emit_z(1)
            emit_el(1)
            for s in range(nb):
                emit_p2(s)
                emit_w(s)
                if s + 2 < nb:
                    emit_z(s + 2)
                    emit_el(s + 2)
                if s > 0:
                    emit_pv(s - 1)
            emit_pv(nb - 1)
            step += nb
            S.emit("act", lambda e, hp=hp, i=i: e.activation(out=oT[:, hp, i * CH:(i + 1) * CH], in_=po[:, :], func=AF.Identity),
                   banks=[BK("po")], writes=[pfx + f"oT{hp}.{i}"])
    out_toks = []
    it = 0
    for i in range(nq):
        for n in range(KD):
            b = it % 2
            it += 1
            S.dma("sp", pfx + f"lx{b}", lambda e, b=b, n=n, i=i: e.dma_start(out=xr[b][:, :], in_=xT_r[:, n, i * CH:(i + 1) * CH]),
                  writes=[pfx + f"xr{b}"])
            for k in range(nhp):
                S.emit("pe", lambda e, k=k, n=n, i=i: e.matmul(pf[:, :], wo[:, n, k, :], oT[:, k, i * CH:(i + 1) * CH],
                                                              start=(k == 0), stop=(k == nhp - 1)),
                       reads=[pfx + f"oT{k}.{i}", pfx + f"wo{n}"], banks=[BK("pf")], ms=(k == nhp - 1))
            S.emit("dve", lambda e, b=b: e.tensor_tensor(out=xo[b][:, :], in0=pf[:, :], in1=xr[b][:, :], op=ALU.add),
                   reads=[pfx + f"xr{b}"], writes=[pfx + f"xo{b}"], banks=[BK("pf")])
            t = S.dma("sp", pfx + f"so{b}", lambda e, b=b, n=n, i=i: e.dma_start(out=xo_r[:, n, i * CH:(i + 1) * CH], in_=xo[b][:, :]),
                      reads=[pfx + f"xo{b}"], writes=[pfx + f"xoT.{n}.{i}"])
            out_toks.append(t)
    return out_toks


NCORES = 8
SEQ = 8192
BATCH = 4


def _own_tokens(p):
    return np.concatenate([np.arange((2 * i + p) * CH, (2 * i + p + 1) * CH) for i in range(NCHUNK)])


def _halo_tokens(p):
    return np.concatenate([np.arange((2 * i + p) * CH, (2 * i + p) * CH + HALO) for i in range(NCHUNK)])


def _lay_vec(v, n):
    return np.ascontiguousarray(v.reshape(n, 128).T)


def _lay_w(w, kin, nout):
    return np.ascontiguousarray(w.reshape(kin, 128, nout, 128).transpose(2, 1, 0, 3))


def _lay_ffn(g, w_up, dw_w, dw_b, w_down):
    wu = w_up.reshape(KD, 128, 2, NJ, 128)
    return dict(g=_lay_vec(g, KD),
                wup=np.ascontiguousarray(wu.transpose(3, 1, 0, 2, 4).reshape(NJ, 128, KD, 256)),
                wdn=_lay_w(w_down, NJ, KD),
                dw=np.ascontiguousarray(dw_w.reshape(3, 2 * NJ, 128).transpose(2, 1, 0)),
                db=_lay_vec(dw_b, 2 * NJ))


def _lay_conv(g, w_in, a_dw_w, a_dw_b, a_ln_g, a_ln_b, b_dw_w, w_out):
    return dict(g=_lay_vec(g, KD), win=_lay_w(w_in, KD, 20), wout=_lay_w(w_out, KD, KD),
                adw=np.ascontiguousarray(a_dw_w.reshape(KA, MA, 128).transpose(2, 1, 0)),
                avec=np.ascontiguousarray(np.stack([a_dw_b, a_ln_g, a_ln_b]).reshape(3, MA, 128).transpose(2, 0, 1)),
                bdw=np.ascontiguousarray(b_dw_w.reshape(3, MA, 128).transpose(2, 1, 0)))


def _lay_attn(g, w_qkv, q_g, k_g, w_o):
    return dict(g=_lay_vec(g, KD), wq=_lay_w(w_qkv[:, :D], KD, HP), wk=_lay_w(w_qkv[:, D:2 * D], KD, HP),
                wv=np.ascontiguousarray(w_qkv[:, 2 * D:].reshape(KD, 128, 2, 512).transpose(2, 1, 0, 3)),
                qkg=np.ascontiguousarray(np.stack([np.tile(q_g, 2), np.tile(k_g, 2)], 1)),
                wo=_lay_w(w_o, KD, KD))


def _masks(p):
    ks = np.arange(CH)[:, None]
    tq = np.arange(CH)[None, :]
    diag = np.where(ks < tq, 0.0, -30000.0).astype(np.float32)
    A = diag if p == 0 else np.zeros((CH, CH), np.float32)
    B = np.full((CH, CH), -30000.0, np.float32) if p == 0 else diag
    m = np.stack([A, B]).reshape(2, 4, 128, CH).transpose(2, 0, 1, 3)
    return np.ascontiguousarray(np.concatenate([m, m], -1))


def _tri():
    j = np.arange(128)[:, None]
    s = np.arange(128)[None, :]
    return np.ascontiguousarray(np.concatenate([-(j >= s).astype(np.float32), (j == s).astype(np.float32)], 1))


_PROGS = {}


def _dram(nc, name, shape, dt=F32, kind="ExternalInput"):
    return nc.dram_tensor(name, list(shape), dt, kind=kind).ap()


def _prog(kind):
    if kind in _PROGS:
        return _PROGS[kind]
    nc = bass.Bass("TRN2", target_bir_lowering=False)
    S = Sched()
    with ExitStack() as st:
        cx = Ctx(nc, S, st)
        if kind == "conv":
            a = [_dram(nc, "xT", [D, TOK]), _dram(nc, "xh", [D, NCHUNK * HALO])]
            xo = _dram(nc, "xoT", [D, TOK], F32, "ExternalOutput")
            toks = conv_phase(cx, "c.", a[0], a[1], xo, _dram(nc, "g", [128, KD]), _dram(nc, "win", [20, 128, KD, 128]),
                              _dram(nc, "wout", [KD, 128, KD, 128]), _dram(nc, "adw", [128, MA, KA]),
                              _dram(nc, "avec", [128, 3, MA]), _dram(nc, "bdw", [128, MA, 3]))
        elif kind == "ffn":
            a = [_dram(nc, "xT", [D, TOK]), _dram(nc, "xh", [D, NCHUNK * HALO])]
            xo = _dram(nc, "xoT", [D, TOK], F32, "ExternalOutput")
            toks = ffn_phase(cx, "f.", a[0], a[1], xo, _dram(nc, "g", [128, KD]), _dram(nc, "wup", [NJ, 128, KD, 256]),
                             _dram(nc, "wdn", [KD, 128, NJ, 128]), _dram(nc, "dw", [128, 2 * NJ, 3]), _dram(nc, "db", [128, 2 * NJ]))
        elif kind == "qkv":
            toks = qkv_phase(cx, "q.", _dram(nc, "xT", [D, TOK]), _dram(nc, "g", [128, KD]), _dram(nc, "wq", [HP, 128, KD, 128]),
                             _dram(nc, "wk", [HP, 128, KD, 128]), _dram(nc, "wv", [2, 128, KD, 512]), _dram(nc, "qkg", [128, 2]),
                             _dram(nc, "qT", [D, TOK], BF16, "ExternalOutput"), _dram(nc, "kT", [D, TOK], BF16, "ExternalOutput"),
                             _dram(nc, "V", [HP, 128, NKB, 128], BF16, "ExternalOutput"))
        elif kind == "attn":
            toks = attn_phase(cx, "a.", _dram(nc, "xT", [D, TOK]), _dram(nc, "xoT", [D, TOK], F32, "ExternalOutput"),
                              _dram(nc, "qT", [D, TOK], BF16), _dram(nc, "kT", [2, D, TOK], BF16),
                              _dram(nc, "V", [2, HP, 128, NKB, 128], BF16), _dram(nc, "mask", [128, 2, 4, 1024]),
                              _dram(nc, "tri", [128, 256]), _dram(nc, "wo", [KD, 128, KD, 128]))
        S.wait_all("sp", toks)
        S.build(nc, st)
    _PROGS[kind] = nc
    return nc


def _run(kind, in_maps):
    res = run_bass_kernel_spmd(_prog(kind), in_maps, core_ids=list(range(NCORES)))
    return res.results


def kernel_unfused(**inputs):
    inp = {k: np.asarray(v) for k, v in inputs.items()}
    x = inp["x"]
    own = [_own_tokens(p) for p in range(2)]
    halo = [_halo_tokens(p) for p in range(2)]
    xT = [np.ascontiguousarray(x[c // 2][own[c % 2]].T) for c in range(NCORES)]

    def halos(xT):
        out = []
        for b in range(BATCH):
            full = np.zeros((D, HALO + SEQ), np.float32)
            for p in range(2):
                for i in range(NCHUNK):
                    g0 = (2 * i + p) * CH
                    full[:, HALO + g0:HALO + g0 + CH] = xT[2 * b + p][:, i * CH:(i + 1) * CH]
            for p in range(2):
                out.append(np.ascontiguousarray(full[:, halo[p]]))
        return out

    masks = [_masks(p) for p in range(2)]
    tri = _tri()
    for layer in range(4):
        i = layer // 2
        if layer % 2 == 0:
            lw = _lay_conv(inp["mix_norm_g"][layer], inp["conv_w_in"][i], inp["conv_a_dw_w"][i], inp["conv_a_dw_b"][i],
                           inp["conv_a_ln_g"][i], inp["conv_a_ln_b"][i], inp["conv_b_dw_w"][i], inp["conv_w_out"][i])
            xh = halos(xT)
            r = _run("conv", [dict(lw, xT=xT[c], xh=xh[c]) for c in range(NCORES)])
            xT = [r[c]["xoT"] for c in range(NCORES)]
        else:
            lw = _lay_attn(inp["mix_norm_g"][layer], inp["attn_w_qkv"][i], inp["attn_q_g"][i], inp["attn_k_g"][i], inp["attn_w_o"][i])
            r = _run("qkv", [dict(xT=xT[c], g=lw["g"], wq=lw["wq"], wk=lw["wk"], wv=lw["wv"], qkg=lw["qkg"]) for c in range(NCORES)])
            ins = []
            for c in range(NCORES):
                b = c // 2
                kT_g = np.stack([r[2 * b]["kT"], r[2 * b + 1]["kT"]])
                V_g = np.stack([r[2 * b]["V"], r[2 * b + 1]["V"]])
                ins.append(dict(xT=xT[c], qT=r[c]["qT"], kT=kT_g, V=V_g, mask=masks[c % 2], tri=tri, wo=lw["wo"]))
            r = _run("attn", ins)
            xT = [r[c]["xoT"] for c in range(NCORES)]
        lw = _lay_ffn(inp["ffn_norm_g"][layer], inp["ffn_w_up"][layer], inp["ffn_dw_w"][layer], inp["ffn_dw_b"][layer], inp["ffn_w_down"][layer])
        xh = halos(xT)
        r = _run("ffn", [dict(lw, xT=xT[c], xh=xh[c]) for c in range(NCORES)])
        xT = [r[c]["xoT"] for c in range(NCORES)]
    out = np.empty((BATCH, SEQ, D), np.float32)
    for c in range(NCORES):
        out[c // 2][own[c % 2]] = xT[c].T
    return out


PAIRS = [[0, 1], [2, 3], [4, 5], [6, 7]]


def halo_phase(cx, pfx, xT, tl, tg, xh, sel_d):
    S, nc = cx.S, cx.nc
    t_sb = cx.sb(pfx + "t", [128, KD, NCHUNK, HALO], F32)
    c0 = cx.sb(pfx + "c0", [128, KD, NCHUNK, HALO], F32)
    c1 = cx.sb(pfx + "c1", [128, KD, NCHUNK, HALO], F32)
    sel = cx.sb(pfx + "sel", [128, 2], F32)
    xT_r = xT.rearrange("(k p) (c t) -> p k c t", p=128, t=CH)
    tl_r = tl.rearrange("(k p) (c h) -> p k c h", p=128, h=HALO)
    tg_r = tg.rearrange("(r k p) (c h) -> p r k c h", p=128, k=KD, h=HALO)
    xh_r = xh.rearrange("(k p) (c h) -> p k c h", p=128, h=HALO)
    S.dma("sp", "cst", lambda e: e.dma_start(out=sel[:], in_=sel_d), writes=[pfx + "sel"])
    S.dma_group("sp", "h0", [(lambda e, k=k: e.dma_start(out=t_sb[:, k], in_=xT_r[:, k, :, CH - HALO:CH]), [], [pfx + f"t{k}"])
                             for k in range(KD)])
    S.dma_group("sp", "h1", [(lambda e, k=k: e.dma_start(out=tl_r[:, k], in_=t_sb[:, k]), [pfx + f"t{k}"], [pfx + f"tl{k}"])
                             for k in range(KD)])
    S.dma("pool", "cc", lambda e: e.collective_compute("AllGather", ALU.bypass, replica_groups=PAIRS, ins=[tl], outs=[tg]),
          reads=[pfx + f"tl{k}" for k in range(KD)], writes=[pfx + "tg"], inc=1)
    S.emit("dve", lambda e: e.memset(c0[:, :, 0, :], 0.0), writes=[pfx + "c0z"])
    S.dma_group("sp", "h2", [(lambda e, k=k: e.dma_start(out=c0[:, k, 1:NCHUNK, :], in_=tg_r[:, 1, k, 0:NCHUNK - 1, :]), [pfx + "tg"], [pfx + f"c0{k}"])
                             for k in range(KD)] +
                            [(lambda e, k=k: e.dma_start(out=c1[:, k], in_=tg_r[:, 0, k]), [pfx + "tg"], [pfx + f"c1{k}"])
                             for k in range(KD)])
    items = []
    for k in range(KD):
        S.emit("dve", lambda e, k=k: e.tensor_scalar(out=c0[:, k], in0=c0[:, k], scalar1=sel[:, 0:1], scalar2=None, op0=ALU.mult),
               reads=[pfx + f"c0{k}", pfx + "c0z", pfx + "sel"], writes=[pfx + f"c0{k}"])
        S.emit("dve", lambda e, k=k: e.scalar_tensor_tensor(out=c1[:, k], in0=c1[:, k], scalar=sel[:, 1:2], in1=c0[:, k],
                                                            op0=ALU.mult, op1=ALU.add),
               reads=[pfx + f"c0{k}", pfx + f"c1{k}", pfx + "sel"], writes=[pfx + f"c1{k}"])
        items.append((lambda e, k=k: e.dma_start(out=xh_r[:, k], in_=c1[:, k]), [pfx + f"c1{k}"], [pfx + f"xh{k}"]))
    t = S.dma_group("sp", "h3", items)
    return [t]


def gather_phase(cx, pfx, kT_l, kT_g, V_l, V_g):
    S = cx.S
    toks = []
    for hp in range(HP):
        for nm, a, b in (("k", kT_l, kT_g), ("v", V_l, V_g)):
            toks.append(S.dma("pool", "cc", lambda e, hp=hp, a=a, b=b: e.collective_compute(
                "AllGather", ALU.bypass, replica_groups=PAIRS, ins=[a[hp * 128:(hp + 1) * 128, :]], outs=[b[hp * 256:(hp + 1) * 256, :]]),
                writes=[pfx + f"{nm}g{hp}"], inc=1))
    return toks[-1:]


_FUSED = []


def _fused_prog():
    if _FUSED:
        return _FUSED[0]
    nc = bass.Bass("TRN2", target_bir_lowering=False)
    S = Sched()
    with ExitStack() as outer:
        x_in = _dram(nc, "xT", [D, TOK])
        out = _dram(nc, "out", [D, TOK], F32, "ExternalOutput")
        sel_d = _dram(nc, "sel", [128, 2])
        mask_d = _dram(nc, "mask", [128, 2, 4, 1024])
        tri_d = _dram(nc, "tri", [128, 256])
        xs = [nc.dram_tensor(f"xs{i}", [D, TOK], F32).ap() for i in range(2)]
        tl = nc.dram_tensor("tl", [D, NCHUNK * HALO], F32).ap()
        tg = nc.dram_tensor("tg", [2 * D, NCHUNK * HALO], F32).ap()
        xh = nc.dram_tensor("xh", [D, NCHUNK * HALO], F32).ap()
        qT = nc.dram_tensor("qT", [D, TOK], BF16).ap()
        kT_l = nc.dram_tensor("kTl", [D, TOK], BF16).ap()
        kT_g = nc.dram_tensor("kTg", [2 * D, TOK], BF16).ap()
        V_l = nc.dram_tensor("Vl", [HP * 128, NKB * 128], BF16).ap()
        V_g = nc.dram_tensor("Vg", [2 * HP * 128, NKB * 128], BF16).ap()
        V_l5 = V_l.rearrange("(h p) (k f) -> h p k f", p=128, f=128)

        def phase(fn):
            with ExitStack() as pst:
                cx = Ctx(nc, S, pst)
                toks = fn(cx)
                S.barrier(toks)
                S.build_phase(nc, pst, outer)

        cur = x_in
        nxt = 0
        for layer in range(4):
            L = f"L{layer}."
            if layer % 2 == 0:
                phase(lambda cx: halo_phase(cx, L + "h.", cur, tl, tg, xh, sel_d))
                dst = xs[nxt]
                phase(lambda cx: conv_phase(cx, L + "c.", cur, xh, dst, _dram(nc, L + "mg", [128, KD]), _dram(nc, L + "win", [20, 128, KD, 128]),
                                            _dram(nc, L + "wout", [KD, 128, KD, 128]), _dram(nc, L + "adw", [128, MA, KA]),
                                            _dram(nc, L + "avec", [128, 3, MA]), _dram(nc, L + "bdw", [128, MA, 3])))
            else:
                phase(lambda cx: qkv_phase(cx, L + "q.", cur, _dram(nc, L + "mg", [128, KD]), _dram(nc, L + "wq", [HP, 128, KD, 128]),
                                           _dram(nc, L + "wk", [HP, 128, KD, 128]), _dram(nc, L + "wv", [2, 128, KD, 512]),
                                           _dram(nc, L + "qkg", [128, 2]), qT, kT_l, V_l5))
                phase(lambda cx: gather_phase(cx, L + "g.", kT_l, kT_g, V_l, V_g))
                dst = xs[nxt]
                phase(lambda cx: attn_phase(cx, L + "a.", cur, dst, qT, kT_g, V_g, mask_d, tri_d, _dram(nc, L + "wo", [KD, 128, KD, 128])))
            cur = dst
            nxt ^= 1
            phase(lambda cx: halo_phase(cx, L + "hf.", cur, tl, tg, xh, sel_d))
            dst = out if layer == 3 else xs[nxt]
            phase(lambda cx: ffn_phase(cx, L + "f.", cur, xh, dst, _dram(nc, L + "fg", [128, KD]), _dram(nc, L + "wup", [NJ, 128, KD, 256]),
                                       _dram(nc, L + "wdn", [KD, 128, NJ, 128]), _dram(nc, L + "dw", [128, 2 * NJ, 3]),
                                       _dram(nc, L + "db", [128, 2 * NJ])))
            cur = dst
            nxt ^= 1
    _FUSED.append((nc, S))
    return _FUSED[0]


def kernel(**inputs):
    inp = {k: np.asarray(v) for k, v in inputs.items()}
    x = inp["x"]
    own = [_own_tokens(p) for p in range(2)]
    base = {"tri": _tri()}
    for layer in range(4):
        i = layer // 2
        L = f"L{layer}."
        if layer % 2 == 0:
            lw = _lay_conv(inp["mix_norm_g"][layer], inp["conv_w_in"][i], inp["conv_a_dw_w"][i], inp["conv_a_dw_b"][i],
                           inp["conv_a_ln_g"][i], inp["conv_a_ln_b"][i], inp["conv_b_dw_w"][i], inp["conv_w_out"][i])
        else:
            lw = _lay_attn(inp["mix_norm_g"][layer], inp["attn_w_qkv"][i], inp["attn_q_g"][i], inp["attn_k_g"][i], inp["attn_w_o"][i])
        lw["mg"] = lw.pop("g")
        lf = _lay_ffn(inp["ffn_norm_g"][layer], inp["ffn_w_up"][layer], inp["ffn_dw_w"][layer], inp["ffn_dw_b"][layer], inp["ffn_w_down"][layer])
        lf["fg"] = lf.pop("g")
        for k, v in list(lw.items()) + list(lf.items()):
            base[L + k] = v
    masks = [_masks(p) for p in range(2)]
    sels = [np.ascontiguousarray(np.tile(np.array([[1.0, 0.0]], np.float32) if p == 0 else np.array([[0.0, 1.0]], np.float32), (128, 1)))
            for p in range(2)]
    in_maps = []
    for c in range(NCORES):
        d = dict(base)
        d["xT"] = np.ascontiguousarray(x[c // 2][own[c % 2]].T)
        d["mask"] = masks[c % 2]
        d["sel"] = sels[c % 2]
        in_maps.append(d)
    nc, _ = _fused_prog()
    res = run_bass_kernel_spmd(nc, in_maps, core_ids=list(range(NCORES))).results
    out = np.empty((BATCH, SEQ, D), np.float32)
    for c in range(NCORES):
        out[c // 2][own[c % 2]] = res[c]["out"].T
    return out
```
